# Optimizing a Trainium2 kernel written in Bass

```python
import jax, jax.numpy as jnp
from jax import lax
import numpy as np

D_MODEL = 1024
BATCH = 2
SEQ = 16384
DEPTH = 1
DEC_BATCH = 8
DEC_SEQ = 16
PAST_LEN = 4096

CHUNK = 64
N_PREV_CHUNKS = 8
ATT_WINDOW = N_PREV_CHUNKS * CHUNK
N_HEADS_A = 8
HEAD_DIM_A = 64
D_A = N_HEADS_A * HEAD_DIM_A
REL_CLIP = 128
N_REL = CHUNK + REL_CLIP
N_HEADS_B = 4
HEAD_K_B = 128
HEAD_V_B = 128
D_B = N_HEADS_B * HEAD_K_B
D_B_V = N_HEADS_B * HEAD_V_B
D_FF = 2816
CONV_W = 3
EPS = 1e-6
IN_SPLITS = (D_A, D_A, D_A, D_B, D_B, D_B_V, D_B_V, D_MODEL, D_MODEL)
D_IN = sum(IN_SPLITS)

kernel_name = "hybrid_chunk_attn_hgrn2_convffn_step"


def rms_norm(x, g):
    xf = x.astype(jnp.float32)
    y = xf * lax.rsqrt(jnp.mean(xf * xf, axis=-1, keepdims=True) + EPS)
    return (y * g.astype(jnp.float32)).astype(x.dtype)


def split_in(h, w_in):
    z = h @ w_in
    offs = [int(o) for o in np.cumsum(IN_SPLITS)[:-1]]
    return jnp.split(z, offs, axis=-1)


def rel_bias_mask(qpos, kpos, table):
    d = qpos[..., :, None] - kpos[..., None, :]
    idx = jnp.clip(d, -(CHUNK - 1), REL_CLIP) + (CHUNK - 1)
    bias = jnp.moveaxis(table[:, idx], 0, -3)
    qc = (qpos // CHUNK)[..., :, None]
    kc = (kpos // CHUNK)[..., None, :]
    valid = (kpos[..., None, :] >= 0) & (kc <= qc) & (kc >= qc - N_PREV_CHUNKS)
    return bias, valid[..., None, :, :]


def band_attention(q, k, v, bias, valid):
    s = jnp.einsum('...hqd,...hkd->...hqk', q, k).astype(jnp.float32) * (HEAD_DIM_A ** -0.5)
    s = jnp.where(valid, s + bias.astype(jnp.float32), -jnp.inf)
    p = jax.nn.softmax(s, axis=-1).astype(v.dtype)
    return jnp.einsum('...hqk,...hkd->...hqd', p, v)


def attn_prompt(qa, ka, va, table):
    B, L, _ = qa.shape
    nC = L // CHUNK
    def to_chunks(t):
        return t.reshape(B, nC, CHUNK, N_HEADS_A, HEAD_DIM_A).transpose(0, 1, 3, 2, 4)
    qc, kc, vc = to_chunks(qa), to_chunks(ka), to_chunks(va)
    pad = ((0, 0), (N_PREV_CHUNKS, 0), (0, 0), (0, 0), (0, 0))
    kp, vp = jnp.pad(kc, pad), jnp.pad(vc, pad)
    kb = jnp.concatenate([kp[:, j:j + nC] for j in range(N_PREV_CHUNKS + 1)], axis=3)
    vb = jnp.concatenate([vp[:, j:j + nC] for j in range(N_PREV_CHUNKS + 1)], axis=3)
    qpos = jnp.arange(L).reshape(nC, CHUNK)
    kpos = (jnp.arange(nC)[:, None] - N_PREV_CHUNKS) * CHUNK + jnp.arange((N_PREV_CHUNKS + 1) * CHUNK)[None, :]
    bias, valid = rel_bias_mask(qpos, kpos, table)
    o = band_attention(qc, kb, vb, bias, valid)
    o = o.transpose(0, 1, 3, 2, 4).reshape(B, L, D_A)
    w = min(ATT_WINDOW, L)
    def last_rows(t):
        return t.reshape(B, L, N_HEADS_A, HEAD_DIM_A)[:, L - w:].transpose(0, 2, 1, 3)
    return o, last_rows(ka), last_rows(va)


def attn_sample(qa, ka, va, cache_k, cache_v, table):
    B, L, _ = qa.shape
    def heads(t):
        return t.reshape(B, L, N_HEADS_A, HEAD_DIM_A).transpose(0, 2, 1, 3)
    q, k, v = heads(qa), heads(ka), heads(va)
    kk = jnp.concatenate([cache_k.astype(k.dtype), k], axis=2)
    vv = jnp.concatenate([cache_v.astype(v.dtype), v], axis=2)
    w = cache_k.shape[2]
    qpos = PAST_LEN + jnp.arange(L)
    kpos = jnp.concatenate([PAST_LEN - w + jnp.arange(w), qpos])
    bias, valid = rel_bias_mask(qpos, kpos, table)
    o = band_attention(q, kk, vv, bias, valid)
    return o.transpose(0, 2, 1, 3).reshape(B, L, D_A), k, v


def hgrn2_prep(qb, fb, ib, lb):
    B, L, _ = qb.shape
    f = lb + (1.0 - lb) * jax.nn.sigmoid(fb.astype(jnp.float32))
    q = jax.nn.silu(qb.astype(jnp.float32))
    k = 1.0 - f
    logf = jnp.log(f)
    hk = lambda t: t.reshape(B, L, N_HEADS_B, HEAD_K_B)
    v = ib.astype(jnp.float32).reshape(B, L, N_HEADS_B, HEAD_V_B)
    return hk(q), hk(k), v, hk(logf)


def hgrn2_scan(q, k, v, logf, S0, block):
    B, L = q.shape[:2]
    n = L // block
    def blocks(t):
        return t.reshape(B, n, block, t.shape[2], t.shape[3]).transpose(1, 0, 3, 2, 4)
    tri = jnp.tril(jnp.ones((block, block), dtype=bool))
    def step(S, inp):
        qc, kc, vc, gc = inp
        b = jnp.cumsum(gc, axis=2)
        inter = jnp.einsum('bhtk,bhkv->bhtv', qc * jnp.exp(b), S)
        diff = b[:, :, :, None, :] - b[:, :, None, :, :]
        decay = jnp.exp(jnp.where(tri[:, :, None], diff, -jnp.inf))
        A = jnp.einsum('bhtk,bhsk,bhtsk->bhts', qc, kc, decay)
        intra = jnp.einsum('bhts,bhsv->bhtv', A, vc)
        bC = b[:, :, -1:, :]
        S_new = jnp.exp(bC[:, :, 0, :])[..., None] * S + jnp.einsum('bhsk,bhsv->bhkv', kc * jnp.exp(bC - b), vc)
        return S_new, inter + intra
    S_fin, o = lax.scan(step, S0.astype(jnp.float32), (blocks(q), blocks(k), blocks(v), blocks(logf)))
    o = o.transpose(1, 0, 3, 2, 4).reshape(B, L, N_HEADS_B, HEAD_V_B)
    return o, S_fin


def hgrn2_out(o, gb, g_norm):
    B, L = o.shape[:2]
    on = o * lax.rsqrt(jnp.mean(o * o, axis=-1, keepdims=True) + EPS) * g_norm.astype(jnp.float32)
    return on.reshape(B, L, D_B_V) * jax.nn.silu(gb.astype(jnp.float32))


def conv_ffn(h, prev, w_gate, w_up, conv_w, conv_b, w_down):
    a = h @ w_gate
    L = a.shape[1]
    ap = jnp.concatenate([prev.astype(a.dtype), a], axis=1)
    ac = conv_b
    for j in range(CONV_W):
        ac = ac + conv_w[j] * ap[:, j:j + L]
    y = (jax.nn.gelu(ac) * (h @ w_up)) @ w_down
    return y, ap[:, ap.shape[1] - (CONV_W - 1):]


def layer(x, lb, cache_k, cache_v, S0, conv_prev, block, norm_mix_g, w_in, rel_bias, hgrn_norm_g,
          w_branch_a, w_branch_b, w_out, norm_ffn_g, w_ffn_gate, w_ffn_up, ffn_conv_w, ffn_conv_b, w_ffn_down):
    h = rms_norm(x, norm_mix_g)
    qa, ka, va, qb, fb, ib, gb, za, zb = split_in(h, w_in)
    if cache_k is None:
        oa, k_rows, v_rows = attn_prompt(qa, ka, va, rel_bias)
    else:
        oa, k_rows, v_rows = attn_sample(qa, ka, va, cache_k, cache_v, rel_bias)
    q, k, v, logf = hgrn2_prep(qb, fb, ib, lb)
    ob, S_new = hgrn2_scan(q, k, v, logf, S0, block)
    ob = hgrn2_out(ob, gb, hgrn_norm_g).astype(x.dtype)
    merged = jax.nn.sigmoid(za) * (oa @ w_branch_a) + jax.nn.sigmoid(zb) * (ob @ w_branch_b)
    x = x + merged @ w_out
    y, conv_new = conv_ffn(rms_norm(x, norm_ffn_g), conv_prev, w_ffn_gate, w_ffn_up, ffn_conv_w, ffn_conv_b, w_ffn_down)
    return x + y, k_rows, v_rows, S_new, conv_new


def setup_inputs(seed: int = 0) -> dict:
    key = jax.random.key(seed)
    ks = jax.random.split(key, 24)
    nrm = lambda i, shape, s=1.0: jax.random.normal(ks[i], shape, jnp.float32) * s
    w_att = min(ATT_WINDOW, PAST_LEN)
    return {
        "x_prompt": nrm(0, (BATCH, SEQ, D_MODEL)),
        "x_sample": nrm(1, (DEC_BATCH, DEC_SEQ, D_MODEL)),
        "cache_attn_k": nrm(2, (DEPTH, DEC_BATCH, N_HEADS_A, w_att, HEAD_DIM_A)),
        "cache_attn_v": nrm(3, (DEPTH, DEC_BATCH, N_HEADS_A, w_att, HEAD_DIM_A)),
        "state_hgrn": nrm(4, (DEPTH, DEC_BATCH, N_HEADS_B, HEAD_K_B, HEAD_V_B)),
        "state_ffn_conv": nrm(5, (DEPTH, DEC_BATCH, CONV_W - 1, D_FF)),
        "norm_mix_g": 1.0 + nrm(6, (DEPTH, D_MODEL), 0.05),
        "w_in": nrm(7, (DEPTH, D_MODEL, D_IN), D_MODEL ** -0.5),
        "rel_bias": nrm(8, (DEPTH, N_HEADS_A, N_REL), 0.5),
        "hgrn_lb_logits": nrm(9, (DEPTH + 1, D_B)),
        "hgrn_norm_g": 1.0 + nrm(10, (DEPTH, HEAD_V_B), 0.05),
        "w_branch_a": nrm(11, (DEPTH, D_A, D_MODEL), D_A ** -0.5),
        "w_branch_b": nrm(12, (DEPTH, D_B_V, D_MODEL), D_B_V ** -0.5),
        "w_out": nrm(13, (DEPTH, D_MODEL, D_MODEL), D_MODEL ** -0.5),
        "norm_ffn_g": 1.0 + nrm(14, (DEPTH, D_MODEL), 0.05),
        "w_ffn_gate": nrm(15, (DEPTH, D_MODEL, D_FF), D_MODEL ** -0.5),
        "w_ffn_up": nrm(16, (DEPTH, D_MODEL, D_FF), D_MODEL ** -0.5),
        "ffn_conv_w": nrm(17, (DEPTH, CONV_W, D_FF), CONV_W ** -0.5),
        "ffn_conv_b": nrm(18, (DEPTH, D_FF), 0.01),
        "w_ffn_down": nrm(19, (DEPTH, D_FF, D_MODEL), D_FF ** -0.5),
        "norm_final_g": 1.0 + nrm(20, (D_MODEL,), 0.05),
    }


def reference(x_prompt, x_sample, cache_attn_k, cache_attn_v, state_hgrn, state_ffn_conv,
              norm_mix_g, w_in, rel_bias, hgrn_lb_logits, hgrn_norm_g, w_branch_a, w_branch_b, w_out,
              norm_ffn_g, w_ffn_gate, w_ffn_up, ffn_conv_w, ffn_conv_b, w_ffn_down, norm_final_g):
    lbs = jnp.cumsum(jax.nn.softmax(hgrn_lb_logits.astype(jnp.float32), axis=0), axis=0)
    xp, xs = x_prompt, x_sample
    Bp, Bs = x_prompt.shape[0], x_sample.shape[0]
    kp_l, vp_l, sp_l, cp_l, ks_l, vs_l, ss_l, cs_l = [], [], [], [], [], [], [], []
    for l in range(DEPTH):
        w = (norm_mix_g[l], w_in[l], rel_bias[l], hgrn_norm_g[l], w_branch_a[l], w_branch_b[l], w_out[l],
             norm_ffn_g[l], w_ffn_gate[l], w_ffn_up[l], ffn_conv_w[l], ffn_conv_b[l], w_ffn_down[l])
        S0p = jnp.zeros((Bp, N_HEADS_B, HEAD_K_B, HEAD_V_B), jnp.float32)
        conv0 = jnp.zeros((Bp, CONV_W - 1, D_FF), xp.dtype)
        xp, kr, vr, Sp, cp = layer(xp, lbs[l], None, None, S0p, conv0, CHUNK, *w)
        kp_l.append(kr); vp_l.append(vr); sp_l.append(Sp.astype(xp.dtype)); cp_l.append(cp)
        xs, kr, vr, Ss, cs = layer(xs, lbs[l], cache_attn_k[l], cache_attn_v[l], state_hgrn[l],
                                   state_ffn_conv[l], xs.shape[1], *w)
        ks_l.append(kr); vs_l.append(vr); ss_l.append(Ss.astype(state_hgrn.dtype)); cs_l.append(cs)
    y_prompt = rms_norm(xp, norm_final_g)
    y_sample = rms_norm(xs, norm_final_g)
    return (y_prompt, y_sample,
            jnp.stack(kp_l), jnp.stack(vp_l), jnp.stack(sp_l), jnp.stack(cp_l),
            jnp.stack(ks_l), jnp.stack(vs_l), jnp.stack(ss_l), jnp.stack(cs_l))
```

```python
import numpy as np
from contextlib import ExitStack
import concourse.bass as bass
import concourse.mybir as mybir
from concourse.bass import AP
from concourse.bass_utils import run_bass_kernel_spmd

F32 = mybir.dt.float32
BF16 = mybir.dt.bfloat16
AF = mybir.ActivationFunctionType
ALU = mybir.AluOpType
AX = mybir.AxisListType

D = 1024
DFF = 2816
NFT = 22
SEG = 4096
NPRE = 12288
PRE_T = 128
NTOK_IN = NPRE + PRE_T + SEG
T = 256
EPS = 1e-6
NSLOT = 6
STOP = 99
STOPMODE = 'main'
NOFK = False
SLOT_E = 4096


class Res:
    __slots__ = ("name", "w", "r", "excl")

    def __init__(self, name, excl=False):
        self.name = name
        self.w = None
        self.r = {}
        self.excl = excl


class Op:
    __slots__ = ("eng", "fn", "deps", "is_dma", "needs_inc", "sem", "val", "idx")


class Sched:
    ENGS = ("pe", "act", "dve", "pool", "sp")

    def __init__(self, nc, n_dma_sems=8):
        self.nc = nc
        self.ops = []
        self.n_dma_sems = n_dma_sems
        self.cap = None

    def capture(self, fn):
        assert self.cap is None
        self.cap = []
        try:
            fn()
            return self.cap
        finally:
            self.cap = None

    def emit_merged(self, a, b):
        i = j = 0
        while i < len(a) or j < len(b):
            if j >= len(b) or (i < len(a) and i * len(b) <= j * len(a)):
                self.emit(*a[i])
                i += 1
            else:
                self.emit(*b[j])
                j += 1

    def emit(self, eng, fn, reads=(), writes=(), dma=False):
        if self.cap is not None:
            self.cap.append((eng, fn, tuple(reads), tuple(writes), dma))
            return None
        op = Op()
        op.eng = eng
        op.fn = fn
        op.is_dma = dma
        op.needs_inc = dma
        op.sem = None
        op.val = 0
        op.idx = len(self.ops)
        deps = {}
        xr = [r for r in reads if r.excl]
        if xr:
            reads = [r for r in reads if not r.excl]
            writes = list(writes) + [r for r in xr if r not in writes]
        for r in reads:
            if r.w is not None:
                deps[r.w.idx] = r.w
        for w in writes:
            if w.w is not None:
                deps[w.w.idx] = w.w
            for o in w.r.values():
                deps[o.idx] = o
        op.deps = list(deps.values())
        for r in reads:
            key = (eng, op.idx) if dma else (eng, -1)
            r.r[key] = op
        for w in writes:
            w.w = op
            w.r = {}
        self.ops.append(op)
        return op

    def finalize(self, stack):
        nc = self.nc
        for op in self.ops:
            for d in op.deps:
                if d.is_dma or d.eng != op.eng or op.eng != "pe" or op.is_dma:
                    d.needs_inc = True
        csem = {e: stack.enter_context(nc.semaphore("cs_" + e)) for e in ("pe", "act", "dve", "pool")}
        dsem = {e: [stack.enter_context(nc.semaphore("ds_%s%d" % (e, i))) for i in range(self.n_dma_sems)]
                for e in ("sp", "pool", "act")}
        ccount = {e: 0 for e in csem}
        dstate = {e: [None] * self.n_dma_sems for e in dsem}
        duse = {e: [0] * self.n_dma_sems for e in dsem}
        drr = {e: 0 for e in dsem}
        waited = {e: {} for e in self.ENGS}
        streams = {e: [] for e in self.ENGS}
        for op in self.ops:
            waits = []
            e = op.eng
            extra = []
            if op.is_dma:
                k = drr[e] % self.n_dma_sems
                drr[e] += 1
                prev = dstate[e][k]
                if prev is not None:
                    extra.append(prev)
                duse[e][k] += 1
                op.sem = dsem[e][k]
                op.val = 16 * duse[e][k]
                dstate[e][k] = op
            elif op.needs_inc:
                ccount[e] += 1
                op.sem = csem[e]
                op.val = ccount[e]
            for d in op.deps + extra:
                if (not d.is_dma) and d.eng == e and e == "pe" and not op.is_dma:
                    continue
                key = id(d.sem)
                if waited[e].get(key, 0) >= d.val:
                    continue
                waited[e][key] = d.val
                waits.append((d.sem, d.val))
            streams[e].append((waits, op))
        fin = []
        for e in dsem:
            for k in range(self.n_dma_sems):
                if duse[e][k] and waited["sp"].get(id(dsem[e][k]), 0) < 16 * duse[e][k]:
                    fin.append((dsem[e][k], 16 * duse[e][k]))
        self.streams = streams
        self.fin = fin

    def run(self, block):
        streams = self.streams
        fin = self.fin

        def body(name):
            def f(eng):
                for waits, op in streams[name]:
                    for s, v in waits:
                        eng.wait_ge(s, v)
                    inst = op.fn(eng)
                    if op.sem is not None:
                        inst.then_inc(op.sem, 16 if op.is_dma else 1)
                if name == "sp":
                    for s, v in fin:
                        eng.wait_ge(s, v)
            return f
        block.tensor(body("pe"))
        block.scalar(body("act"))
        block.vector(body("dve"))
        block.gpsimd(body("pool"))
        block.sync(body("sp"))


def build_program(n_scan=NPRE // T, n_main=SEG // T, do_pre=True, do_sample=True, dumps=(), do_conv=True, init_stop=99):
    NPRE = n_scan * T
    NTOK_IN = NPRE + PRE_T + n_main * T
    nc = bass.Bass("TRN2", target_bir_lowering=False)
    st = ExitStack()
    S = Sched(nc)

    def din(name, shape):
        return nc.dram_tensor(name, list(shape), F32, kind="ExternalInput")

    def dout(name, shape):
        return nc.dram_tensor(name, list(shape), F32, kind="ExternalOutput")

    xin = din("xin", [NTOK_IN, D])
    hv_d = din("hv", [128, 1])
    xs_d = din("xs", [16, D])
    ckT_d = din("ckT", [128, 4, 512])
    cv_d = din("cv", [8, 512, 64])
    s0_d = din("s0", [4, 128, 128])
    cprev_d = din("cprev", [128, NFT, 2])
    w_in_d = din("w_in", [D, 5632])
    w_a_d = din("w_a", [512, D])
    w_b_d = din("w_b", [512, D])
    w_o_d = din("w_o", [D, D])
    w_g_d = din("w_g", [D, DFF])
    w_u_d = din("w_u", [D, DFF])
    w_d_d = din("w_d", [DFF, D])
    gmix_d = din("gmix", [128, 8])
    gffn_d = din("gffn", [128, 8])
    gfin_d = din("gfin", [D])
    rel_d = din("rel", [8, 192])
    lbl_d = din("lbl", [128, 2, 4])
    gn_d = din("gn", [128])
    cw_d = din("cw", [128, NFT, 3])
    cb_d = din("cb", [128, NFT])
    ident_d = din("ident", [128, 128])
    mask_d = din("mask", [128, 128])
    jmat_d = din("jmat", [128, 128])

    y_d = dout("y", [SEG, D])
    ys_d = dout("ys", [16, D])
    okT_d = dout("okT", [4, 128, 512])
    ov_d = dout("ov", [512, 512])
    oS_d = dout("oS", [4, 128, 128])
    oconv_d = dout("oconv", [128, NFT, 2])
    okTs_d = dout("okTs", [4, 128, 16])
    ovs_d = dout("ovs", [16, 512])
    oSs_d = dout("oSs", [4, 128, 128])
    oconvs_d = dout("oconvs", [128, NFT, 2])

    def scratch(name, shape, dt=BF16):
        return nc.dram_tensor(name, list(shape), dt, kind="Internal")

    wb_in = scratch("wb_in", [D, 5632])
    wb_a = scratch("wb_a", [512, D])
    wb_b = scratch("wb_b", [512, D])
    wb_o = scratch("wb_o", [D, D])
    wb_g = scratch("wb_g", [D, DFF])
    wb_u = scratch("wb_u", [D, DFF])
    wb_d = scratch("wb_d", [DFF, D])
    ext_d = scratch("ext_d", [8, 768], F32)

    def sb(name, shape, dt=F32):
        return st.enter_context(nc.sbuf_tensor(name, list(shape), dt))

    def R(name, excl=False):
        return Res(name, excl)

    def act(out, in_, func, reads, writes, **kw):
        S.emit("act", lambda e: e.activation(out=out, in_=in_, func=func, **kw), reads, writes)

    def tt(out, in0, in1, op, reads, writes, eng="dve"):
        S.emit(eng, lambda e: e.tensor_tensor(out=out, in0=in0, in1=in1, op=op), reads, writes)

    def ts(out, in0, s1, s2, op0, op1, reads, writes, eng="dve"):
        if op1 is None:
            S.emit(eng, lambda e: e.tensor_scalar(out=out, in0=in0, scalar1=s1, scalar2=None, op0=op0), reads, writes)
        else:
            S.emit(eng, lambda e: e.tensor_scalar(out=out, in0=in0, scalar1=s1, scalar2=s2, op0=op0, op1=op1), reads, writes)

    def stt(out, in0, scalar, in1, op0, op1, reads, writes):
        S.emit("dve", lambda e: e.scalar_tensor_tensor(out=out, in0=in0, scalar=scalar, in1=in1, op0=op0, op1=op1), reads, writes)

    def cp(out, in_, reads, writes, eng="dve"):
        if eng == "act":
            act(out, in_, AF.Copy, reads, writes)
        else:
            S.emit(eng, lambda e: e.tensor_copy(out=out, in_=in_), reads, writes)

    def recip(out, in_, reads, writes):
        S.emit("dve", lambda e: e.reciprocal(out=out, in_=in_), reads, writes)

    def mset(ap, val, writes, eng="pool"):
        S.emit(eng, lambda e: e.memset(ap, val), (), writes)

    def mm(out, lhsT, rhs, start, stop, reads, writes):
        S.emit("pe", lambda e: e.matmul(out, lhsT=lhsT, rhs=rhs, start=start, stop=stop), reads, writes)

    def dma(eng, out, in_, reads, writes):
        S.emit(eng, lambda e: e.dma_start(out=out, in_=in_), reads, writes, dma=True)

    identb = sb("identb", [128, 128], BF16); r_ident = R("ident")
    maskb = sb("maskb", [128, 128], BF16); r_mask = R("mask")
    jb = sb("jb", [128, 128], BF16); r_j = R("j")
    epst = sb("epst", [128, 1]); r_eps = R("eps")
    onesT = sb("onesT", [128, 8]); r_ones = R("ones")
    hvt = sb("hvt", [128, 1]); r_hv = R("hv")
    gmix = sb("gmix_s", [128, 8]); r_gmix = R("gmix")
    gffn = sb("gffn_s", [128, 8]); r_gffn = R("gffn")
    gfin = sb("gfin_s", [128, D]); r_gfin = R("gfin")
    gnrep = sb("gnrep", [128, 4, 128]); r_gn = R("gn")
    cw = sb("cw_s", [128, NFT, 3]); r_cw = R("cw")
    cb = sb("cb_s", [128, NFT]); r_cb = R("cb")
    lbl = sb("lbl_s", [128, 2, 4]); r_lbl = R("lbl")
    lb = sb("lb_s", [128, 4]); oml = sb("oml_s", [128, 4]); r_lb = R("lb")
    ET = sb("ET", [128, 5, 8, 128], BF16); r_ET = R("ET")

    dma("pool", identb[:], ident_d.ap(), (), [r_ident])
    dma("pool", maskb[:], mask_d.ap(), (), [r_mask])
    dma("pool", jb[:], jmat_d.ap(), (), [r_j])
    mset(epst[:], EPS, [r_eps])
    mset(onesT[:], 1.0, [r_ones])
    dma("sp", hvt[:], hv_d.ap(), (), [r_hv])
    dma("sp", gmix[:], gmix_d.ap(), (), [r_gmix])
    dma("sp", gffn[:], gffn_d.ap(), (), [r_gffn])
    dma("sp", gfin[:], AP(gfin_d, 0, [[0, 128], [1, D]]), (), [r_gfin])
    dma("sp", gnrep[:], AP(gn_d, 0, [[0, 128], [0, 4], [1, 128]]), (), [r_gn])
    dma("sp", cw[:], cw_d.ap(), (), [r_cw])
    dma("sp", cb[:], cb_d.ap(), (), [r_cb])
    dma("sp", lbl[:], lbl_d.ap(), (), [r_lbl])

    if init_stop <= 1:
        S.finalize(st)
        with nc.Block() as block:
            S.run(block)
        st.close()
        return nc
    ps_att = st.enter_context(nc.psum_tensor("ps_att", [128, 1536], F32))
    r_att = [R("att0", True), R("att1", True)]
    r_attB = R("attB", True)
    NGEN = 2
    ps_gen = [st.enter_context(nc.psum_tensor("ps_g%d" % i, [128, 512], F32)) for i in range(NGEN)]
    r_gen = [R("g%d" % i, True) for i in range(NGEN)]
    ps_o = st.enter_context(nc.psum_tensor("ps_o", [128, 512], F32))
    r_o = R("ps_o", True)
    ps_tr = [st.enter_context(nc.psum_tensor("ps_t%d" % i, [128, 1024], BF16)) for i in range(2)]
    r_tr = [R("t%d" % i, True) for i in range(2)]
    cnt = {"g": 0, "t": 0, "a": 0, "p": 0}

    gpool_full = [(ps_gen[i], r_gen[i]) for i in range(NGEN)]
    gpool_front = gpool_full + [(ps_o, r_o)]
    gpool_back = [(ps_att[:, 0:512], r_att[0]), (ps_att[:, 512:1024], r_att[1]), (ps_att[:, 1024:1536], r_attB)]
    gstate = {"pool": gpool_full, "tr": (0, 1)}

    def gbank():
        pool = gstate["pool"]
        i = cnt["g"] % len(pool)
        cnt["g"] += 1
        return pool[i]

    def tbank():
        sel = gstate["tr"]
        i = sel[cnt["t"] % len(sel)]
        cnt["t"] += 1
        return ps_tr[i], r_tr[i]

    lbt = sb("lbt", [128, 4])
    tt(lbt[:], lbl[:, 1, :], lbl[:, 0, :], ALU.subtract, [r_lbl], [r_lb])
    act(lbt[:], lbt[:], AF.Exp, [r_lb], [r_lb])
    ts(lbt[:], lbt[:], 1.0, None, ALU.add, None, [r_lb], [r_lb])
    recip(lb[:], lbt[:], [r_lb], [r_lb])
    ts(oml[:], lb[:], -1.0, 1.0, ALU.mult, ALU.add, [r_lb], [r_lb])

    if init_stop <= 2:
        S.finalize(st)
        with nc.Block() as block:
            S.run(block)
        st.close()
        return nc
    if init_stop <= 3:
        S.finalize(st)
        with nc.Block() as block:
            S.run(block)
        st.close()
        return nc
    r_wb = {}
    conv_jobs = []
    for name, src, dst, rows in (("in", w_in_d, wb_in, D), ("a", w_a_d, wb_a, 512), ("b", w_b_d, wb_b, 512),
                                 ("o", w_o_d, wb_o, D), ("g", w_g_d, wb_g, D), ("u", w_u_d, wb_u, D),
                                 ("d", w_d_d, wb_d, DFF)):
        r_wb[name] = R("wb_" + name)
        for r0 in range(0, rows, 128):
            conv_jobs.append((name, dst.ap()[r0:r0 + 128, :], src.ap()[r0:r0 + 128, :]))

    def emit_conv(n):
        for _ in range(n):
            if conv_jobs:
                name, d_, s_ = conv_jobs.pop(0)
                dma("pool", d_, s_, (), [r_wb[name]])

    wslot = [sb("wslot%d" % i, [128, SLOT_E], BF16) for i in range(NSLOT)]
    r_wslot = [R("wslot%d" % i) for i in range(NSLOT)]
    W1v = []
    for i, c0 in enumerate((2048, 2560, 512, 1024)):
        v_ = wslot[i][:, :].rearrange("p (k n) -> p k n", k=8)
        dma("pool", v_, w_in_d.ap()[:, c0:c0 + 512].rearrange("(k p) n -> p k n", p=128), (), [r_wslot[i]])
        W1v.append(v_)

    def wgroups(mode):
        g = []
        for i in (3, 4, 5, 6, 0, 1, 2, 7, 8, 9, 10):
            g.append(("in%d" % i, "in", wb_in.ap()[:, 512 * i:512 * i + 512].rearrange("(k p) n -> p k n", p=128), 8, 512))
        g.append(("a", "a", wb_a.ap().rearrange("(k p) n -> p k n", p=128), 4, 1024))
        g.append(("b", "b", wb_b.ap().rearrange("(k p) n -> p k n", p=128), 4, 1024))
        for n in range(2):
            g.append(("o%d" % n, "o", wb_o.ap()[:, 512 * n:512 * n + 512].rearrange("(k p) n -> p k n", p=128), 8, 512))
        for gi in range(6):
            nc_ = 512 if gi < 5 else 256
            g.append(("g%d" % gi, "g", wb_g.ap()[:, 512 * gi:512 * gi + nc_].rearrange("(k p) n -> p k n", p=128), 8, nc_))
            if mode != "pre":
                g.append(("u%d" % gi, "u", wb_u.ap()[:, 512 * gi:512 * gi + nc_].rearrange("(k p) n -> p k n", p=128), 8, nc_))
        if mode != "pre":
            for n in range(2):
                for gk in range(3):
                    nk = 8 if gk < 2 else 6
                    g.append(("d%d_%d" % (n, gk), "d",
                              wb_d.ap()[1024 * gk:1024 * gk + 128 * nk, 512 * n:512 * n + 512].rearrange("(k p) n -> p k n", p=128), nk, 512))
        return g

    wq = []
    tiles_plan = ([("pre", 0)] if do_pre else []) + [("main", m) for m in range(n_main)] + ([("sample", 0)] if do_sample else [])
    for mode, _ in tiles_plan:
        wq.extend(wgroups(mode))
    wstate = {"issued": 0, "consumed": 0, "slot": {}}

    def w_issue(slot):
        i = wstate["issued"]
        if i >= len(wq):
            return
        key, wname, src, nk, ncol = wq[i]
        view = wslot[slot][:, 0:nk * ncol].rearrange("p (k n) -> p k n", k=nk)
        dma("sp", view, src, [r_wb[wname]], [r_wslot[slot]])
        wstate["slot"][i] = slot
        wstate["issued"] += 1

    def w_next(key):
        i = wstate["consumed"]
        assert wq[i][0] == key, (wq[i][0], key)
        slot = wstate["slot"][i]
        _, _, _, nk, ncol = wq[i]
        view = wslot[slot][:, 0:nk * ncol].rearrange("p (k n) -> p k n", k=nk)
        return view, r_wslot[slot], slot

    def w_done(slot):
        wstate["consumed"] += 1
        w_issue(slot)

    if init_stop <= 4:
        S.finalize(st)
        with nc.Block() as block:
            S.run(block)
        st.close()
        return nc
    X = [sb("X%d" % i, [128, 2, D]) for i in range(2)]
    r_X = [[R("X%d_%d" % (i, j)) for j in range(2)] for i in range(2)]
    XN = sb("XN", [128, 2, D], BF16); r_XN = [R("XN0"), R("XN1")]
    ss = sb("ss", [128, 8]); r_ss = R("ss")
    hTs = [sb("hTa", [128, 8, T], BF16), sb("hTb", [128, 8, T], BF16), sb("h2T", [128, 8, T], BF16)]
    r_hTs = [[R("hTa0"), R("hTa1")], [R("hTb0"), R("hTb1")], [R("h2T0"), R("h2T1")]]
    qT = sb("qT", [128, 4, T], BF16); r_qT = R("qT")
    kTr = sb("kTr", [128, 4, 6, 128], BF16); r_kT = [R("kT%d" % i) for i in range(6)]
    Vr = sb("Vr", [128, 6, 8, 65], BF16); r_V = [R("V%d" % i) for i in range(6)]
    Pb = [sb("Pb%d" % i, [128, 5, 128], BF16) for i in range(3)]; r_Pb = [R("Pb%d" % i) for i in range(3)]
    oa = sb("oa", [128, 2, 512], BF16); r_oa = [R("oa0"), R("oa1")]
    oaT = sb("oaT", [128, 4, T], BF16); r_oaT = [R("oaT0"), R("oaT1")]
    rec = sb("rec", [128, 8]); r_rec = R("rec")
    sg = sb("sg", [128, 4, T]); r_sg = [R("sg%d" % i) for i in range(4)]
    siluq = sb("siluq", [128, 4, T]); r_sq = [R("siluq%d" % i) for i in range(4)]
    gF = sb("gF", [128, 4 * T]); r_gF = R("gF")
    gL = sb("gL", [128, 4 * T]); r_gL = R("gL")
    gB = sb("gB", [128, 4 * T]); r_gB = R("gB")
    gK = sb("gK", [128, 4 * T]); r_gK = R("gK")
    gE = sb("gE", [128, 4 * T]); r_gE = R("gE")
    dec = sb("dec", [128, 3, 4, 4]); r_dec = R("dec")
    Zq = sb("Zq", [128, 4, 4, 128], BF16); r_Zq = [R("Zq%d" % i) for i in range(4)]
    ktT = sb("ktT", [128, 4, T], BF16); r_ktT = [R("ktT%d" % i) for i in range(4)]
    khT = sb("khT", [128, 4, T], BF16); r_khT = [R("khT%d" % i) for i in range(4)]
    khtm = sb("khtm", [128, 2, 512], BF16); r_khtm = [R("khtm0"), R("khtm1")]
    vh = sb("vh", [128, 2, 512], BF16); r_vh = [R("vh0"), R("vh1")]
    sgb = sb("sgb", [128, 2, 512]); r_sgb = [R("sgb0"), R("sgb1")]
    Sst = sb("Sst", [128, 4, 128]); r_S = R("S")
    Sp = sb("Sp", [128, 2, 4, 128], BF16); r_Sp = [R("Sp0"), R("Sp1")]
    AT = sb("AT", [128, 4, 128], BF16); r_AT = R("AT")
    sqb = sb("sqb", [128, 512]); r_sqb = R("sqb")
    ssq = sb("ssq", [128, 4]); r_ssq = R("ssq")
    t1 = sb("t1", [128, 512]); r_t1 = R("t1")
    gG = sb("gG", [128, 512]); r_gG = R("gG")
    ob = sb("ob", [128, 512], BF16); r_ob = R("ob")
    obT = sb("obT", [128, 4, T], BF16); r_obT = [R("obT0"), R("obT1")]
    zs = sb("zs", [128, 16, T], BF16); r_zs = [R("zs%d" % i) for i in range(16)]
    mT = sb("mT", [128, 8, T], BF16); r_mT = R("mT")
    aT = [sb("aT%d" % i, [128, T + 2]) for i in range(2)]; r_aT = [R("aT0"), R("aT1")]
    c1 = [sb("c1_%d" % i, [128, T]) for i in range(2)]; r_c1 = [R("c1_0"), R("c1_1")]
    c2 = [sb("c2_%d" % i, [128, T]) for i in range(2)]; r_c2 = [R("c2_0"), R("c2_1")]
    m1, r_m1, m2, r_m2 = c1[0], r_c1[0], c2[0], r_c2[0]
    gT = sb("gT", [128, NFT, T], BF16); r_gT = [R("gT%d" % i) for i in range(NFT)]
    cprev = sb("cprev_s", [128, NFT, 2]); r_cprev = R("cprev")

    ext_ = siluq[0:8].rearrange("p h t -> p (h t)")[:, 0:768]; r_ext = r_sq; r_extd = R("extd")
    dma("sp", ext_[:, 64:256], rel_d.ap(), (), r_ext)
    act(ext_[:, 0:64], ext_[:, 64:65].to_broadcast([8, 64]), AF.Identity, r_ext, r_ext)
    act(ext_[:, 256:768], ext_[:, 255:256].to_broadcast([8, 512]), AF.Identity, r_ext, r_ext)
    act(ext_[:, :], ext_[:, :], AF.Exp, r_ext, r_ext)
    dma("sp", ext_d.ap(), ext_[:, :], r_ext, [r_extd])
    hk = sg[:].rearrange("p h t -> p (h t)").rearrange("p (a b) -> p a b", a=8); r_hk = r_sg
    hkb = ktT[:].rearrange("p h t -> p (h t)").rearrange("p (a b) -> p a b", a=8); r_hkb = r_ktT
    for kt in range(5):
        dma("sp", hk, AP(ext_d, 512 - 128 * kt, [[1, 128], [768, 8], [1, 128]]), [r_extd], r_hk)
        cp(hkb, hk, r_hk, r_hkb)
        for n in range(2):
            pb, rb = gbank()
            mm(pb[:, :], jb[:], hkb[:, 4 * n:4 * n + 4, :].rearrange("p h q -> p (h q)"), True, True, [r_j] + r_hkb, [rb])
            cp(ET[:, kt, 4 * n:4 * n + 4, :].rearrange("p h q -> p (h q)"), pb[:, :], [rb], [r_ET])
    mset(ET[0:64, 0, :, 64:128], 0.0, [r_ET])
    mset(ET[64:128, 4, :, 0:64], 0.0, [r_ET])

    kst = gT[:, 0:8, :].rearrange("p a b -> p (a b)").bitcast(F32).rearrange("p (h t) -> p h t", h=4)
    vst = gT[:, 8:16, :].rearrange("p a b -> p (a b)").bitcast(F32).rearrange("p (j c) -> p j c", j=2)
    rl_kst = r_gT[0:8]
    rl_vst = [r_gT[8:12], r_gT[12:16]]
    sg2 = gT[:, 0:8, :].rearrange("p a b -> p (a b)").bitcast(F32).rearrange("p (h t) -> p h t", h=4)
    r_sg2 = [[r_gT[2 * i], r_gT[2 * i + 1]] for i in range(4)]
    vh2 = gT[:, 8:12, :].rearrange("p a b -> p (a b)").rearrange("p (j c) -> p j c", j=2)
    r_vh2 = [[r_gT[8], r_gT[9]], [r_gT[10], r_gT[11]]]
    sgs = [(sg, [[r] for r in r_sg]), (sg2, r_sg2)]
    vhs = [(vh, [[r] for r in r_vh]), (vh2, r_vh2)]
    mset(Zq[:], 0.0, r_Zq)
    mset(Sst[:], 0.0, [r_S])
    mset(cprev[:], 0.0, [r_cprev])

    if init_stop <= 5:
        S.finalize(st)
        with nc.Block() as block:
            S.run(block)
        st.close()
        return nc
    def load_x(xsrc, ntok, xb):
        nsub = (ntok + 127) // 128
        for j in range(nsub):
            nt = min(128, ntok - 128 * j)
            dma("sp", X[xb][:nt, j, :], xsrc[j * 128:j * 128 + nt, :], (), [r_X[xb][j]])

    def norm_T(src_tile, rsrc, gcol, rg, col0, hsel, ntok):
        nsub = (ntok + 127) // 128
        dst, rdst = hTs[hsel], r_hTs[hsel]
        for j in range(nsub):
            nt = min(128, ntok - 128 * j)
            act(XN[:nt, j, :], src_tile[:nt, j, :], AF.Square, [rsrc[j]], [r_XN[j], r_ss], accum_out=ss[:nt, col0 + j:col0 + j + 1])
            act(ss[:nt, col0 + j:col0 + j + 1], ss[:nt, col0 + j:col0 + j + 1], AF.Sqrt, [r_ss, r_eps], [r_ss],
                bias=epst[:nt, 0:1], scale=1.0 / D)
            recip(ss[:nt, col0 + j:col0 + j + 1], ss[:nt, col0 + j:col0 + j + 1], [r_ss], [r_ss])
            act(XN[:nt, j, :], src_tile[:nt, j, :], AF.Copy, [rsrc[j], r_ss], [r_XN[j]], scale=ss[:nt, col0 + j:col0 + j + 1])
            pt, rt = tbank()
            for kc in range(8):
                S.emit("pe", lambda e, kc=kc, j=j, nt=nt, pt=pt: e.transpose(out=pt[:, kc * 128:kc * 128 + nt], in_=XN[:nt, j, kc * 128:(kc + 1) * 128],
                                                                              identity=identb[:nt, :nt]), [r_XN[j], r_ident], [rt])
            tt(dst[:, :, j * 128:j * 128 + nt], pt[:, :].rearrange("p (k t) -> p k t", k=8)[:, :, 0:nt],
               gcol[:, 0:8].unsqueeze(2).to_broadcast([128, 8, nt]), ALU.mult, [rt, rg], [rdst[j]])

    def macro_tile(mode, ntok, xb, gt0, hsel=0, normed=False, kv=False, out_row=None, final_kv=None, after_norm=None, prenorm=None,
                   part="all", bsel=0):
        sample = mode == "sample"
        nsub = (ntok + 127) // 128
        nts = [min(128, ntok - 128 * j) for j in range(nsub)]
        C = 16 if sample else 64
        nch = ntok // C
        cps = 1 if sample else 2
        Xb = X[xb]
        rX = r_X[xb]
        hT = hTs[hsel]
        r_hT = r_hTs[hsel]
        rhT_all = r_hT[:nsub]
        cur["hT"] = hT
        cur["r_hT"] = r_hT
        sg, rl_sg = sgs[bsel]
        vh, rl_vh = vhs[bsel]
        if mode == "scan":
            gstate["pool"] = gpool_front if part == "front" else gpool_back
            gstate["tr"] = (0,) if part == "front" else (1,)
        else:
            gstate["pool"] = gpool_full
            gstate["tr"] = (0, 1)

        if part != "back":
            if not normed:
                norm_T(Xb, rX, gmix, r_gmix, 0, hsel, ntok)
            if after_norm is not None:
                after_norm()

        def proj_fm(wv, rw, ct, evac):
            pb, rb = gbank()
            for kc in range(8):
                mm(pb[:, 0:ntok], wv[:, kc, ct * 128:(ct + 1) * 128], hT[:, kc, 0:ntok], kc == 0, kc == 7, [rw] + rhT_all, [rb])
            evac(pb, rb)

        def proj_tm(wv, rw, j, evac, ncol=512):
            pb, rb = gbank()
            nt = nts[j]
            for kc in range(8):
                mm(pb[:nt, 0:ncol], hT[:, kc, j * 128:j * 128 + nt], wv[:, kc, 0:ncol], kc == 0, kc == 7, [rw, r_hT[j]], [rb])
            evac(pb, rb, j, nt)

        def slot_of(j):
            return (gt0 + j) % 6 if not sample else 4

        def hgrn_gates_all():
            n4 = 4 * ntok
            nc4 = 4 * nch
            fl = lambda b: b[:, 0:n4]
            v3 = lambda b: b[:, 0:n4].rearrange("p (h t) -> p h t", h=4)
            ch = lambda b: b[:, 0:n4].rearrange("p (c t) -> p c t", t=C)
            hc = lambda b: b[:, 0:n4].rearrange("p (h c t) -> p h c t", h=4, t=C)
            rsg = [r for l_ in rl_sg for r in l_]
            sgf = sg.rearrange("p h t -> p (h t)")
            tt(v3(gF), v3(sgf), oml[:, 0:4].unsqueeze(2).to_broadcast([128, 4, ntok]), ALU.mult, rsg + [r_lb], [r_gF])
            tt(v3(gF), v3(gF), lb[:, 0:4].unsqueeze(2).to_broadcast([128, 4, ntok]), ALU.add, [r_gF, r_lb], [r_gF])
            act(fl(gL), fl(gF), AF.Ln, [r_gF], [r_gL])
            S.emit("dve", lambda e: e.tensor_tensor_scan(out=fl(gB), data0=fl(gL), data1=fl(gL), initial=0.0, op0=ALU.add, op1=ALU.min),
                   [r_gL], [r_gB])
            ts(fl(gK), fl(gF), -1.0, 1.0, ALU.mult, ALU.add, [r_gF], [r_gK], eng="pool")
            tt(ch(gL), ch(gB), ch(gB)[:, :, C // 2 - 1:C // 2].to_broadcast([128, nc4, C]), ALU.subtract, [r_gB, r_gL], [r_gL])
            act(fl(gE), fl(gL), AF.Exp, [r_gL], [r_gE], scale=-1.0)
            tt(ktT[:, :, 0:ntok], v3(gK), v3(gE), ALU.mult, [r_gK, r_gE], r_ktT, eng="pool")
            dsl = lambda i: dec[:, i, :, 0:nch]
            if mode != "scan":
                act(fl(gB), fl(gL), AF.Exp, [r_gL, r_gB], [r_gB])
                cp(dsl(2), hc(gB)[:, :, :, C - 1], [r_gB], [r_dec])
            else:
                act(dsl(2), hc(gL)[:, :, :, C - 1], AF.Exp, [r_gL], [r_dec])
            tt(dsl(0), hc(gE)[:, :, :, 0], hc(gF)[:, :, :, 0], ALU.mult, [r_gE, r_gF], [r_dec])
            tt(dsl(1), dsl(0), dsl(2), ALU.mult, [r_dec], [r_dec])
            k4 = lambda b: b[:, :, 0:ntok].rearrange("p h (c t) -> p h c t", t=C)
            tt(k4(khT), k4(ktT), dsl(2).unsqueeze(3).to_broadcast([128, 4, nch, C]), ALU.mult, r_ktT + [r_dec], r_khT)
            if mode != "scan":
                sqf = siluq.rearrange("p h t -> p (h t)")
                if sample:
                    tt(Zq[:, :, 0, 0:ntok], v3(sqf), v3(gB), ALU.mult, r_sq + [r_gB], r_Zq)
                else:
                    base = Zq[:, 0, 0, 0:64]
                    zout = AP(base.tensor, base.offset, [list(base.ap[0]), [512 // nsub, 4 * nsub], [192, 2], [1, 64]])
                    tt(zout, fl(sqf).rearrange("p (a c t) -> p a c t", c=2, t=64), fl(gB).rearrange("p (a c t) -> p a c t", c=2, t=64),
                       ALU.mult, r_sq + [r_gB], r_Zq)

        def khat_transpose(j):
            nt = nts[j]
            pt, rt = tbank()
            for hb in range(4):
                S.emit("pe", lambda e, hb=hb, pt=pt: e.transpose(out=pt[:nt, hb * 128:(hb + 1) * 128], in_=khT[:, hb, j * 128:j * 128 + nt],
                                                                  identity=identb[:, :]), [r_khT[hb], r_ident], [rt])
            cp(khtm[:nt, j, :], pt[:nt, 0:512], [rt], [r_khtm[j]], eng="act")

        def s_update(j, ci):
            p = ci % cps
            rows = slice(p * C, p * C + C)
            pb, rb = gbank()
            for hb in range(4):
                mm(pb[:, hb * 128:(hb + 1) * 128], khtm[rows, j, hb * 128:(hb + 1) * 128], vh[rows, j, hb * 128:(hb + 1) * 128],
                   True, True, [r_khtm[j]] + rl_vh[j], [rb])
            tt(Sst[:], Sst[:], dec[:, 1, :, ci:ci + 1].to_broadcast([128, 4, 128]), ALU.mult, [r_S, r_dec], [r_S])
            tt(Sst[:].rearrange("p h v -> p (h v)"), Sst[:].rearrange("p h v -> p (h v)"), pb[:, :], ALU.add, [r_S, rb], [r_S])

        if mode == "scan":
            wv_f, wv_i = W1v[0], W1v[1]
            if part != "back":
                for hb in range(4):
                    proj_fm(wv_f, r_wslot[0], hb, lambda pb, rb, hb=hb: act(sg.rearrange("p h t -> p (h t)")[:, hb * ntok:(hb + 1) * ntok], pb[:, 0:ntok], AF.Sigmoid, [rb], rl_sg[hb]))
                for j in range(nsub):
                    proj_tm(wv_i, r_wslot[1], j, lambda pb, rb, j, nt: cp(vh[:nt, j, :], pb[:nt, :], [rb], rl_vh[j], eng="act"))
                if kv:
                    kv_proj_only()
            if part != "front":
                hgrn_gates_all()
                for j in range(nsub):
                    khat_transpose(j)
                    for p in range(cps):
                        s_update(j, j * cps + p)
            return

        def ev_q(ct):
            return lambda pb, rb: act(qT[:, ct, 0:ntok], pb[:, 0:ntok], AF.Copy, [rb], [r_qT], scale=0.125)

        def ev_k(ct):
            def f(pb, rb):
                for j in range(nsub):
                    cp(kTr[:, ct, slot_of(j), 0:nts[j]], pb[:, j * 128:j * 128 + nts[j]], [rb], [r_kT[slot_of(j)]], eng="act")
                if final_kv is not None:
                    cp(kst[:, ct, 0:ntok], pb[:, 0:ntok], [rb], rl_kst)
            return f

        def ev_v(pb, rb, j, nt):
            s_ = slot_of(j)
            cp(Vr[:nt, s_, :, 0:64], pb[:nt, :].rearrange("p (h d) -> p h d", h=8), [rb], [r_V[s_]], eng="act")
            if mode == "main" or sample:
                cp(Vr[:nt, s_, :, 64:65], onesT[:nt, 0:8].unsqueeze(2), [r_ones], [r_V[s_]], eng="pool")
            else:
                cp(Vr[:nt, s_, :, 64:65], hvt[:nt, 0:1].unsqueeze(2).to_broadcast([nt, 8, 1]), [r_hv], [r_V[s_]], eng="pool")
            if final_kv is not None:
                cp(vst[:nt, j, :], pb[:nt, :], [rb], rl_vst[j])
                dma("act", final_kv[1].ap()[final_kv[2] + j * 128:final_kv[2] + j * 128 + nt, :], vst[:nt, j, :], rl_vst[j], ())

        wv, rw, sl = w_next("in3")
        for hb in range(4):
            proj_fm(wv, rw, hb, lambda pb, rb, hb=hb: act(siluq.rearrange("p h t -> p (h t)")[:, hb * ntok:(hb + 1) * ntok], pb[:, 0:ntok], AF.Silu, [rb], [r_sq[hb]]))
        w_done(sl)
        wv, rw, sl = w_next("in4")
        for hb in range(4):
            proj_fm(wv, rw, hb, lambda pb, rb, hb=hb: act(sg.rearrange("p h t -> p (h t)")[:, hb * ntok:(hb + 1) * ntok], pb[:, 0:ntok], AF.Sigmoid, [rb], rl_sg[hb]))
        w_done(sl)
        wv, rw, sl = w_next("in5")
        for j in range(nsub):
            proj_tm(wv, rw, j, lambda pb, rb, j, nt: cp(vh[:nt, j, :], pb[:nt, :], [rb], rl_vh[j], eng="act"))
        w_done(sl)
        wv, rw, sl = w_next("in6")
        for j in range(nsub):
            proj_tm(wv, rw, j, lambda pb, rb, j, nt: act(sgb[:nt, j, :], pb[:nt, :], AF.Silu, [rb], [r_sgb[j]]))
        w_done(sl)
        wv, rw, sl = w_next("in0")
        for ct in range(4):
            proj_fm(wv, rw, ct, ev_q(ct))
        w_done(sl)
        wv, rw, sl = w_next("in1")
        for ct in range(4):
            proj_fm(wv, rw, ct, ev_k(ct))
        w_done(sl)
        if final_kv is not None:
            for ct in range(4):
                dma("act", final_kv[0].ap()[ct, :, final_kv[2]:final_kv[2] + ntok], kst[:, ct, 0:ntok], rl_kst, ())
        wv, rw, sl = w_next("in2")
        for j in range(nsub):
            proj_tm(wv, rw, j, ev_v)
        w_done(sl)

        hgrn_gates_all()

        att_steps, z_steps, h_steps = [], [], []

        zst = {}

        def z_step(zi):
            gi, ct = divmod(zi, 4)
            if ct == 0:
                zst["w"] = w_next("in%d" % (7 + gi))
            wv_, rw_, sl_ = zst["w"]
            proj_fm(wv_, rw_, ct, lambda pb, rb: act(zs[:, zi, 0:ntok], pb[:, 0:ntok], AF.Sigmoid, [rb], [r_zs[zi]]))
            if ct == 3:
                w_done(sl_)
        for zi in range(16):
            z_steps.append(lambda zi=zi: z_step(zi))

        def make_att(j):
            nq = nts[j]
            if sample:
                ktiles = [(0, 128), (1, 128), (2, 128), (3, 128), (4, 16)]
            else:
                ktiles = [((gt0 + j - 4 + kt) % 6, 128) for kt in range(5)]
            pend = []

            def pv(h, pslot):
                for kt, (s_, nk) in enumerate(ktiles):
                    mm(ps_o[:nq, (h % 4) * 65:(h % 4) * 65 + 65], Pb[pslot][:nk, kt, 0:nq], Vr[:nk, s_, h, :], kt == 0, kt == 4,
                       [r_Pb[pslot], r_V[s_]], [r_o])

            def normalize(half):
                o3 = ps_o[:nq, 0:260].rearrange("p (h d) -> p h d", h=4)
                ts(rec[:nq, half * 4:half * 4 + 4].unsqueeze(2), o3[:, :, 64:65], 1e-30, None, ALU.max, None, [r_o], [r_rec])
                recip(rec[:nq, half * 4:half * 4 + 4], rec[:nq, half * 4:half * 4 + 4], [r_rec], [r_rec])
                tt(oa[:nq, j, half * 256:half * 256 + 256].rearrange("p (h d) -> p h d", h=4), o3[:, :, 0:64],
                   rec[:nq, half * 4:half * 4 + 4].unsqueeze(2).to_broadcast([nq, 4, 64]), ALU.mult, [r_o, r_rec], [r_oa[j]])

            def head(h):
                hp, r0 = h // 2, (h % 2) * 64
                ai = cnt["a"] % 2
                cnt["a"] += 1
                offA = ai * 512
                offB = 1024 + ai * 128
                for kt in (4, 0, 1, 2, 3):
                    s_, nk = ktiles[kt]
                    o_ = offB if kt == 4 else offA + kt * 128
                    mm(ps_att[:nk, o_:o_ + nq], kTr[r0:r0 + 64, hp, s_, 0:nk], qT[r0:r0 + 64, hp, j * 128:j * 128 + nq],
                       True, True, [r_kT[s_], r_qT], [r_attB if kt == 4 else r_att[ai]])
                pi = cnt["p"] % 3
                cnt["p"] += 1
                sattA = ps_att[:, offA:offA + 512].rearrange("p (k q) -> p k q", k=4)
                sattB = ps_att[:, offB:offB + 128]
                if sample:
                    act(Pb[pi][:16, 4, 0:nq], sattB[:16, 0:nq], AF.Exp, [r_attB], [r_Pb[pi]])
                    act(Pb[pi][:, 0:4, 0:nq], sattA[:, :, 0:nq], AF.Exp, [r_att[ai]], [r_Pb[pi]])
                    tt(Pb[pi][:, 0:4, 0:nq], Pb[pi][:, 0:4, 0:nq], ET[:, 0:4, h, 0:nq], ALU.mult, [r_Pb[pi], r_ET], [r_Pb[pi]], eng="pool")
                    tt(Pb[pi][:16, 4, 0:nq], Pb[pi][:16, 4, 0:nq], ET[:16, 4, h, 0:nq], ALU.mult, [r_Pb[pi], r_ET], [r_Pb[pi]], eng="pool")
                else:
                    act(Pb[pi][:, 4, :], sattB, AF.Exp, [r_attB], [r_Pb[pi]])
                    act(Pb[pi][:, 0:4, :], sattA, AF.Exp, [r_att[ai]], [r_Pb[pi]])
                    tt(Pb[pi][:, :, :], Pb[pi][:, :, :], ET[:, :, h, :], ALU.mult, [r_Pb[pi], r_ET], [r_Pb[pi]], eng="pool")
                if pend:
                    ph, ppi = pend.pop(0)
                    pv(ph, ppi)
                    if ph == 3:
                        normalize(0)
                pend.append((h, pi))

            def tail():
                ph, ppi = pend.pop(0)
                pv(ph, ppi)
                normalize(1)
                pt, rt = tbank()
                for kc in range(4):
                    S.emit("pe", lambda e, kc=kc, pt=pt: e.transpose(out=pt[:, kc * 128:kc * 128 + nq], in_=oa[:nq, j, kc * 128:(kc + 1) * 128],
                                                                      identity=identb[:nq, :nq]), [r_oa[j], r_ident], [rt])
                cp(oaT[:, :, j * 128:j * 128 + nq], pt[:, 0:512].rearrange("p (k t) -> p k t", k=4)[:, :, 0:nq], [rt], [r_oaT[j]], eng="act")
            for h in range(8):
                att_steps.append(lambda h=h: head(h))
            att_steps.append(tail)
        for j in range(nsub):
            make_att(j)

        def make_h(j):
            nt = nts[j]

            def h_at():
                pb, rb = gbank()
                for hb in range(4):
                    if sample:
                        qrhs = Zq[:, hb, 0, 0:nt]
                    else:
                        base = Zq[:, hb, 2 * j, 0:64]
                        qrhs = AP(base.tensor, base.offset, [list(base.ap[0]), [192, 2], [1, 64]])
                    mm(pb[:nt, hb * 128:hb * 128 + nt], ktT[:, hb, j * 128:j * 128 + nt], qrhs, True, True, [r_ktT[hb], r_Zq[hb]], [rb])
                tt(AT[:nt, :, 0:nt], pb[:nt, :].rearrange("p (h t) -> p h t", h=4)[:, :, 0:nt],
                   maskb[:nt, 0:nt].unsqueeze(1).to_broadcast([nt, 4, nt]), ALU.mult, [rb, r_mask], [r_AT])

            def h_chunk(p):
                ci = j * cps + p
                tt(Sp[:, p], Sst[:], dec[:, 0, :, ci:ci + 1].to_broadcast([128, 4, 128]), ALU.mult, [r_S, r_dec], [r_Sp[p]])
                s_update(j, ci)

            def h_out():
                ob_, rob = gbank()
                for hb in range(4):
                    for p in range(cps):
                        zl = Zq[:, hb, 0, 0:nt] if sample else Zq[:, hb, 2 * j + p, :]
                        mm(ob_[:nt, hb * 128:(hb + 1) * 128], zl, Sp[:, p, hb, :], p == 0, False, [r_Zq[hb], r_Sp[p]], [rob])
                    mm(ob_[:nt, hb * 128:(hb + 1) * 128], AT[:nt, hb, 0:nt], vh[:nt, j, hb * 128:(hb + 1) * 128], False, True, [r_AT] + rl_vh[j], [rob])
                act(sqb[:nt, :], ob_[:nt, :], AF.Square, [rob], [r_sqb])
                S.emit("dve", lambda e: e.tensor_reduce(out=ssq[:nt, 0:4], in_=sqb[:nt, :].rearrange("p (h v) -> p h v", h=4), axis=AX.X, op=ALU.add),
                       [r_sqb], [r_ssq])
                act(ssq[:nt, :], ssq[:nt, :], AF.Sqrt, [r_ssq, r_eps], [r_ssq], bias=epst[:nt, 0:1], scale=1.0 / 128)
                recip(ssq[:nt, :], ssq[:nt, :], [r_ssq], [r_ssq])
                tt(t1[:nt, :].rearrange("p (h v) -> p h v", h=4), ob_[:nt, :].rearrange("p (h v) -> p h v", h=4),
                   ssq[:nt, 0:4].unsqueeze(2).to_broadcast([nt, 4, 128]), ALU.mult, [rob, r_ssq], [r_t1])
                tt(gG[:nt, :], sgb[:nt, j, :], gnrep[:nt].rearrange("p h v -> p (h v)"), ALU.mult, [r_sgb[j], r_gn], [r_gG], eng="pool")
                tt(ob[:nt, :], t1[:nt, :], gG[:nt, :], ALU.mult, [r_t1, r_gG], [r_ob])

            def h_tr():
                pt, rt = tbank()
                for hb in range(4):
                    S.emit("pe", lambda e, hb=hb, pt=pt: e.transpose(out=pt[:, hb * 128:hb * 128 + nt], in_=ob[:nt, hb * 128:(hb + 1) * 128],
                                                                      identity=identb[:nt, :nt]), [r_ob, r_ident], [rt])
                cp(obT[:, :, j * 128:j * 128 + nt], pt[:, 0:512].rearrange("p (k t) -> p k t", k=4)[:, :, 0:nt], [rt], [r_obT[j]], eng="act")
            h_steps.append(lambda: khat_transpose(j))
            h_steps.append(h_at)
            for p in range(cps):
                h_steps.append(lambda p=p: h_chunk(p))
            h_steps.append(h_out)
            h_steps.append(h_tr)
        for j in range(nsub):
            make_h(j)

        for i in range(max(len(att_steps), len(z_steps), len(h_steps))):
            for lst in (att_steps, z_steps, h_steps):
                if i < len(lst):
                    lst[i]()

        wva, rwa, sla = w_next("a")
        wstate["consumed"] += 1
        wvb, rwb, slb = w_next("b")
        wstate["consumed"] -= 1
        for ct in range(8):
            pb, rb = gbank()
            for kc in range(4):
                mm(pb[:, 0:ntok], wva[:, kc, ct * 128:(ct + 1) * 128], oaT[:, kc, 0:ntok], kc == 0, kc == 3, [rwa] + r_oaT[:nsub], [rb])
            for kc in range(4):
                mm(pb[:, 256:256 + ntok], wvb[:, kc, ct * 128:(ct + 1) * 128], obT[:, kc, 0:ntok], kc == 0, kc == 3, [rwb] + r_obT[:nsub], [rb])
            tt(m1[:, 0:ntok], pb[:, 0:ntok], zs[:, ct, 0:ntok], ALU.mult, [rb, r_zs[ct]], [r_m1])
            tt(m2[:, 0:ntok], pb[:, 256:256 + ntok], zs[:, 8 + ct, 0:ntok], ALU.mult, [rb, r_zs[8 + ct]], [r_m2])
            tt(mT[:, ct, 0:ntok], m1[:, 0:ntok], m2[:, 0:ntok], ALU.add, [r_m1, r_m2], [r_mT], eng="pool")
        w_done(sla)
        w_done(slb)

        for n in range(2):
            wv, rw, sl = w_next("o%d" % n)
            for j in range(nsub):
                nt = nts[j]
                pb, rb = gbank()
                for kc in range(8):
                    mm(pb[:nt, :], mT[:, kc, j * 128:j * 128 + nt], wv[:, kc, :], kc == 0, kc == 7, [rw, r_mT], [rb])
                tt(Xb[:nt, j, n * 512:(n + 1) * 512], Xb[:nt, j, n * 512:(n + 1) * 512], pb[:nt, :], ALU.add, [rX[j], rb], [rX[j]])
            w_done(sl)
        norm_T(Xb, rX, gffn, r_gffn, 2, 2, ntok)
        hT = hTs[2]
        r_hT = r_hTs[2]
        rhT_all = r_hT[:nsub]

        ncol_f = 2 if mode == "pre" else ntok
        for gi in range(6):
            ntile = 4 if gi < 5 else 2
            if gi == 2 and prenorm is not None:
                prenorm()
            wvg, rwg, slg = w_next("g%d" % gi)
            if mode != "pre":
                wstate["consumed"] += 1
                wvu, rwu, slu = w_next("u%d" % gi)
                wstate["consumed"] -= 1
            for ct in range(ntile):
                ft = gi * 4 + ct
                pb, rb = gbank()
                if mode == "pre":
                    for kc in range(8):
                        mm(pb[:, 0:2], wvg[:, kc, ct * 128:(ct + 1) * 128], hT[:, kc, ntok - 2:ntok], kc == 0, kc == 7, [rwg] + rhT_all, [rb])
                    cp(cprev[:, ft, :], pb[:, 0:2], [rb], [r_cprev])
                    continue
                for kc in range(8):
                    mm(pb[:, 0:ntok], wvg[:, kc, ct * 128:(ct + 1) * 128], hT[:, kc, 0:ntok], kc == 0, kc == 7, [rwg] + rhT_all, [rb])
                for kc in range(8):
                    mm(pb[:, 256:256 + ntok], wvu[:, kc, ct * 128:(ct + 1) * 128], hT[:, kc, 0:ntok], kc == 0, kc == 7, [rwu] + rhT_all, [rb])
                bi = ft % 2
                a_ = aT[bi]
                cp(a_[:, 0:2], cprev[:, ft, :], [r_cprev], [r_aT[bi]], eng="pool")
                act(a_[:, 2:2 + ntok], pb[:, 0:ntok], AF.Copy, [rb], [r_aT[bi]])
                cp(cprev[:, ft, :], a_[:, ntok:ntok + 2], [r_aT[bi]], [r_cprev], eng="pool")
                act(c1[bi][:, 0:ntok], pb[:, 0:ntok], AF.Identity, [rb, r_cw, r_cb], [r_c1[bi]], scale=cw[:, ft, 2:3], bias=cb[:, ft:ft + 1])
                stt(c2[bi][:, 0:ntok], a_[:, 1:1 + ntok], cw[:, ft, 1:2], c1[bi][:, 0:ntok], ALU.mult, ALU.add, [r_aT[bi], r_c1[bi], r_cw], [r_c2[bi]])
                stt(c1[bi][:, 0:ntok], a_[:, 0:ntok], cw[:, ft, 0:1], c2[bi][:, 0:ntok], ALU.mult, ALU.add, [r_aT[bi], r_c2[bi], r_cw], [r_c1[bi]])
                act(c2[bi][:, 0:ntok], c1[bi][:, 0:ntok], AF.Gelu_apprx_tanh, [r_c1[bi]], [r_c2[bi]])
                tt(gT[:, ft, 0:ntok], c2[bi][:, 0:ntok], pb[:, 256:256 + ntok], ALU.mult, [r_c2[bi], rb], [r_gT[ft]])
            w_done(slg)
            if mode != "pre":
                w_done(slu)
        if mode == "pre":
            return

        for n in range(2):
            banks = [gbank() for _ in range(nsub)]
            for gk in range(3):
                wv, rw, sl = w_next("d%d_%d" % (n, gk))
                nk = 8 if gk < 2 else 6
                for kl in range(nk):
                    kc = gk * 8 + kl
                    for j in range(nsub):
                        nt = nts[j]
                        mm(banks[j][0][:nt, :], gT[:, kc, j * 128:j * 128 + nt], wv[:, kl, :], kc == 0, kc == NFT - 1, [rw, r_gT[kc]], [banks[j][1]])
                w_done(sl)
            for j in range(nsub):
                nt = nts[j]
                tt(Xb[:nt, j, n * 512:(n + 1) * 512], Xb[:nt, j, n * 512:(n + 1) * 512], banks[j][0][:nt, :], ALU.add, [rX[j], banks[j][1]], [rX[j]])
        for j in range(nsub):
            nt = nts[j]
            act(XN[:nt, j, :], Xb[:nt, j, :], AF.Square, [rX[j]], [r_XN[j], r_ss], accum_out=ss[:nt, 4 + j:5 + j])
            act(ss[:nt, 4 + j:5 + j], ss[:nt, 4 + j:5 + j], AF.Sqrt, [r_ss, r_eps], [r_ss], bias=epst[:nt, 0:1], scale=1.0 / D)
            recip(ss[:nt, 4 + j:5 + j], ss[:nt, 4 + j:5 + j], [r_ss], [r_ss])
            stt(Xb[:nt, j, :], Xb[:nt, j, :], ss[:nt, 4 + j:5 + j], gfin[:nt, :], ALU.mult, ALU.mult, [rX[j], r_ss, r_gfin], [rX[j]])
            dma("act", out_row[j * 128:j * 128 + nt, :], Xb[:nt, j, :], [rX[j]], ())

    cur = {}

    def kv_proj_only():
        ntok, gt0 = cur["ntok"], cur["gt0"]
        for ct in range(4):
            pb, rb = gbank()
            for kc in range(8):
                mm(pb[:, 0:ntok], W1v[2][:, kc, ct * 128:(ct + 1) * 128], cur["hT"][:, kc, 0:ntok], kc == 0, kc == 7, [r_wslot[2]] + cur["r_hT"], [rb])
            for j in range(2):
                s_ = (gt0 + j) % 6
                cp(kTr[:, ct, s_, :], pb[:, j * 128:(j + 1) * 128], [rb], [r_kT[s_]], eng="act")
        for j in range(2):
            s_ = (gt0 + j) % 6
            pb, rb = gbank()
            for kc in range(8):
                mm(pb[:, :], cur["hT"][:, kc, j * 128:(j + 1) * 128], W1v[3][:, kc, :], kc == 0, kc == 7, [r_wslot[3], cur["r_hT"][j]], [rb])
            cp(Vr[:, s_, :, 0:64], pb[:, :].rearrange("p (h d) -> p h d", h=8), [rb], [r_V[s_]], eng="act")
            cp(Vr[:, s_, :, 64:65], hvt[:, 0:1].unsqueeze(2).to_broadcast([128, 8, 1]), [r_hv], [r_V[s_]], eng="pool")


    xa = xin.ap()
    nscan = n_scan
    nmain = n_main
    plan = []
    for m in range(nscan):
        plan.append(("scan", xa[m * T:(m + 1) * T, :], T, dict(gt0=-5 + 2 * (m - (nscan - 2)) + 6, kv=m >= nscan - 2)))
    if do_pre:
        plan.append(("pre", xa[NPRE:NPRE + PRE_T, :], PRE_T, dict(gt0=5)))
    for m in range(nmain):
        fk = (okT_d, ov_d, (m - (nmain - 2)) * T) if (m >= nmain - 2 and not NOFK) else None
        r0 = NPRE + PRE_T + m * T
        plan.append(("main", xa[r0:r0 + T, :], T, dict(gt0=2 * m + 6, out_row=y_d.ap()[m * T:(m + 1) * T, :], final_kv=fk)))
    if do_sample:
        plan.append(("sample", xs_d.ap(), 16, dict(gt0=0, out_row=ys_d.ap(), final_kv=(okTs_d, ovs_d, 0))))
    if plan:
        load_x(plan[0][1], plan[0][2], 0)
    if not do_conv:
        conv_jobs.clear()
    for i, (mode, xsrc, ntok, kw) in enumerate(plan):
        xb = i % 2
        nxt = None
        pren = None
        if i + 1 < len(plan):
            nxt = (lambda p=plan[i + 1], b=(i + 1) % 2: load_x(p[1], p[2], b))
            if mode == "main":
                pren = (lambda p=plan[i + 1], b=(i + 1) % 2: norm_T(X[b], r_X[b], gmix, r_gmix, 0, b, p[2]))
        normed = i > 0 and plan[i - 1][0] == "main"
        if mode == "scan":
            def front(k):
                md, xs_, nt_, kw_ = plan[k]
                cur["ntok"] = nt_
                cur["gt0"] = kw_["gt0"]
                emit_conv(2)
                nx = (lambda p=plan[k + 1], b=(k + 1) % 2: load_x(p[1], p[2], b)) if k + 1 < len(plan) else None
                macro_tile("scan", nt_, k % 2, hsel=k % 2, after_norm=nx, part="front", bsel=k % 2, **kw_)
            if i == 0:
                front(0)
            back_ops = S.capture(lambda: macro_tile("scan", ntok, xb, hsel=xb, part="back", bsel=xb, **kw))
            front_ops = S.capture(lambda: front(i + 1)) if i + 1 < nscan else []
            S.emit_merged(back_ops, front_ops)
            continue
        if i == nscan:
            emit_conv(len(conv_jobs))
            for s_ in range(NSLOT):
                w_issue(s_)
        if mode == "sample":
            dma("act", oS_d.ap().rearrange("h k v -> k h v"), Sst[:], [r_S], ())
            dma("act", oconv_d.ap(), cprev[:], [r_cprev], ())
            dma("pool", kTr[:, :, 0:4, :], ckT_d.ap().rearrange("p c (t k) -> p c t k", t=4), (), r_kT[0:4])
            for t_ in range(4):
                dma("pool", Vr[:, t_, :, 0:64], cv_d.ap()[:, t_ * 128:(t_ + 1) * 128, :].rearrange("h p d -> p h d"), (), [r_V[t_]])
                cp(Vr[:, t_, :, 64:65], onesT[:, 0:8].unsqueeze(2), [r_ones], [r_V[t_]], eng="pool")
            dma("sp", Sst[:], s0_d.ap().rearrange("h k v -> k h v"), (), [r_S])
            dma("sp", cprev[:], cprev_d.ap(), (), [r_cprev])
        macro_tile(mode, ntok, xb, hsel=xb, normed=normed, after_norm=nxt, prenorm=pren, **kw)
    emit_conv(len(conv_jobs))
    if not do_sample:
        dma("act", oS_d.ap().rearrange("h k v -> k h v"), Sst[:], [r_S], ())
        dma("act", oconv_d.ap(), cprev[:], [r_cprev], ())
    else:
        dma("act", oSs_d.ap().rearrange("h k v -> k h v"), Sst[:], [r_S], ())
        dma("act", oconvs_d.ap(), cprev[:], [r_cprev], ())
    for nm, getter in dumps:
        ap_, res_ = getter(locals())
        d_ = nc.dram_tensor("dbg_" + nm, list(ap_.shape), ap_.dtype if hasattr(ap_, "dtype") else F32, kind="ExternalOutput")
        dma("pool", d_.ap(), ap_, res_, ())

    S.finalize(st)
    with nc.Block() as block:
        S.run(block)
    st.close()
    return nc


_NC_CACHE = {}


def kernel(x_prompt, x_sample, cache_attn_k, cache_attn_v, state_hgrn, state_ffn_conv,
           norm_mix_g, w_in, rel_bias, hgrn_lb_logits, hgrn_norm_g, w_branch_a, w_branch_b, w_out,
           norm_ffn_g, w_ffn_gate, w_ffn_up, ffn_conv_w, ffn_conv_b, w_ffn_down, norm_final_g):
    f32 = np.float32
    A = lambda a: np.ascontiguousarray(np.asarray(a, dtype=f32))
    x_prompt = A(x_prompt)
    if "nc" not in _NC_CACHE:
        _NC_CACHE["nc"] = build_program()
    nc = _NC_CACHE["nc"]
    s_idx = np.arange(128)
    mask = ((s_idx[:, None] // 64 == s_idx[None, :] // 64) & (s_idx[:, None] <= s_idx[None, :])).astype(f32)
    shared = {
        "w_in": A(w_in[0]), "w_a": A(w_branch_a[0]), "w_b": A(w_branch_b[0]), "w_o": A(w_out[0]),
        "w_g": A(w_ffn_gate[0]), "w_u": A(w_ffn_up[0]), "w_d": A(w_ffn_down[0]),
        "gmix": A(np.asarray(norm_mix_g[0]).reshape(8, 128).T), "gffn": A(np.asarray(norm_ffn_g[0]).reshape(8, 128).T),
        "gfin": A(norm_final_g), "rel": A(rel_bias[0]),
        "lbl": A(np.asarray(hgrn_lb_logits).reshape(2, 4, 128).transpose(2, 0, 1)),
        "gn": A(hgrn_norm_g[0]),
        "cw": A(np.asarray(ffn_conv_w[0]).reshape(3, NFT, 128).transpose(2, 1, 0)),
        "cb": A(np.asarray(ffn_conv_b[0]).reshape(NFT, 128).T),
        "ident": np.eye(128, dtype=f32), "mask": mask, "jmat": np.ascontiguousarray(np.eye(128, dtype=f32)[::-1]),
    }
    in_maps = []
    for c in range(8):
        b, j = divmod(c, 4)
        s = j * SEG
        lo = s - PRE_T - NPRE
        xin = np.zeros((NTOK_IN, D), f32)
        a0 = max(lo, 0)
        xin[a0 - lo:] = x_prompt[b, a0:s + SEG]
        m = dict(shared)
        m["xin"] = xin
        m["hv"] = np.full((128, 1), 1.0 if j > 0 else 0.0, f32)
        m["xs"] = A(x_sample[c])
        m["ckT"] = A(np.asarray(cache_attn_k[0, c]).transpose(0, 2, 1).reshape(4, 128, 512).transpose(1, 0, 2))
        m["cv"] = A(cache_attn_v[0, c])
        m["s0"] = A(state_hgrn[0, c])
        m["cprev"] = A(np.asarray(state_ffn_conv[0, c]).reshape(2, NFT, 128).transpose(2, 1, 0))
        in_maps.append(m)
    res = run_bass_kernel_spmd(nc, in_maps, core_ids=list(range(8)))
    R_ = res.results
    B = 2
    y_prompt = np.stack([np.concatenate([R_[b * 4 + j]["y"] for j in range(4)], axis=0) for b in range(B)])
    y_sample = np.stack([R_[c]["ys"] for c in range(8)])

    def kT_to_rows(a):
        n = a.shape[-1]
        return a.reshape(8, 64, n).transpose(0, 2, 1)

    def v_to_rows(a):
        n = a.shape[0]
        return a.reshape(n, 8, 64).transpose(1, 0, 2)

    def conv_rows(a):
        return a.transpose(2, 1, 0).reshape(2, DFF)

    last = [3, 7]
    new_k_p = np.stack([kT_to_rows(R_[c]["okT"]) for c in last])[None]
    new_v_p = np.stack([v_to_rows(R_[c]["ov"]) for c in last])[None]
    hg_p = np.stack([R_[c]["oS"] for c in last])[None]
    cv_p = np.stack([conv_rows(R_[c]["oconv"]) for c in last])[None]
    new_k_s = np.stack([kT_to_rows(R_[c]["okTs"]) for c in range(8)])[None]
    new_v_s = np.stack([v_to_rows(R_[c]["ovs"]) for c in range(8)])[None]
    hg_s = np.stack([R_[c]["oSs"] for c in range(8)])[None]
    cv_s = np.stack([conv_rows(R_[c]["oconvs"]) for c in range(8)])[None]
    outs = (y_prompt, y_sample, new_k_p, new_v_p, hg_p, cv_p, new_k_s, new_v_s, hg_s, cv_s)
    return tuple(np.ascontiguousarray(o, dtype=f32) for o in outs)
```

```python
import numpy as np
from contextlib import ExitStack
import concourse.bass as bass
import concourse.mybir as mybir
from concourse.bass import AP
from concourse.bass_utils import run_bass_kernel_spmd

F32 = mybir.dt.float32
BF16 = mybir.dt.bfloat16
AF = mybir.ActivationFunctionType
ALU = mybir.AluOpType
AX = mybir.AxisListType

D = 1024
DFF = 2816
NFT = 22
SEG = 4096
NPRE = 12288
PRE_T = 128
NTOK_IN = NPRE + PRE_T + SEG
T = 256
EPS = 1e-6
NSLOT = 6
STOP = 99
STOPMODE = 'main'
NOFK = False
SLOT_E = 4096


class Res:
    __slots__ = ("name", "w", "r", "excl")

    def __init__(self, name, excl=False):
        self.name = name
        self.w = None
        self.r = {}
        self.excl = excl


class Op:
    __slots__ = ("eng", "fn", "deps", "is_dma", "needs_inc", "sem", "val", "idx")


class Sched:
    ENGS = ("pe", "act", "dve", "pool", "sp")

    def __init__(self, nc, n_dma_sems=8):
        self.nc = nc
        self.ops = []
        self.n_dma_sems = n_dma_sems
        self.cap = None

    def capture(self, fn):
        assert self.cap is None
        self.cap = []
        try:
            fn()
            return self.cap
        finally:
            self.cap = None

    DUR = {"pe": 0.15, "act": 0.75, "dve": 0.85, "pool": 1.1, "sp": 0.3}

    def emit_merged(self, a, b):
        eng_free = {}
        ready = {}
        rdone = {}
        LAT = 0.25

        def start_of(op):
            eng, fn, reads, writes, dma = op
            t = eng_free.get(eng, 0.0)
            for r in reads:
                t = max(t, ready.get(id(r), 0.0) + LAT)
            for w in writes:
                t = max(t, ready.get(id(w), 0.0) + LAT, rdone.get(id(w), 0.0) + LAT)
            return t

        def commit(op, t):
            eng, fn, reads, writes, dma = op
            d = 2.5 if dma else self.DUR[eng]
            eng_free[eng] = t + (0.1 if dma else d)
            for r in reads:
                rdone[id(r)] = max(rdone.get(id(r), 0.0), t + d)
            for w in writes:
                ready[id(w)] = t + d
            self.emit(*op)

        i = j = 0
        while i < len(a) or j < len(b):
            if j >= len(b):
                commit(a[i], start_of(a[i])); i += 1
            elif i >= len(a):
                commit(b[j], start_of(b[j])); j += 1
            else:
                ta, tb = start_of(a[i]), start_of(b[j])
                if ta <= tb:
                    commit(a[i], ta); i += 1
                else:
                    commit(b[j], tb); j += 1

    def emit(self, eng, fn, reads=(), writes=(), dma=False):
        if self.cap is not None:
            self.cap.append((eng, fn, tuple(reads), tuple(writes), dma))
            return None
        op = Op()
        op.eng = eng
        op.fn = fn
        op.is_dma = dma
        op.needs_inc = dma
        op.sem = None
        op.val = 0
        op.idx = len(self.ops)
        deps = {}
        xr = [r for r in reads if r.excl]
        if xr:
            reads = [r for r in reads if not r.excl]
            writes = list(writes) + [r for r in xr if r not in writes]
        for r in reads:
            if r.w is not None:
                deps[r.w.idx] = r.w
        for w in writes:
            if w.w is not None:
                deps[w.w.idx] = w.w
            for o in w.r.values():
                deps[o.idx] = o
        op.deps = list(deps.values())
        for r in reads:
            key = (eng, op.idx) if dma else (eng, -1)
            r.r[key] = op
        for w in writes:
            w.w = op
            w.r = {}
        self.ops.append(op)
        return op

    def finalize(self, stack):
        nc = self.nc
        for op in self.ops:
            for d in op.deps:
                if d.is_dma or d.eng != op.eng or op.eng != "pe" or op.is_dma:
                    d.needs_inc = True
        csem = {e: stack.enter_context(nc.semaphore("cs_" + e)) for e in ("pe", "act", "dve", "pool")}
        dsem = {e: [stack.enter_context(nc.semaphore("ds_%s%d" % (e, i))) for i in range(self.n_dma_sems)]
                for e in ("sp", "pool", "act")}
        ccount = {e: 0 for e in csem}
        dstate = {e: [None] * self.n_dma_sems for e in dsem}
        duse = {e: [0] * self.n_dma_sems for e in dsem}
        drr = {e: 0 for e in dsem}
        waited = {e: {} for e in self.ENGS}
        streams = {e: [] for e in self.ENGS}
        for op in self.ops:
            waits = []
            e = op.eng
            extra = []
            if op.is_dma:
                k = drr[e] % self.n_dma_sems
                drr[e] += 1
                prev = dstate[e][k]
                if prev is not None:
                    extra.append(prev)
                duse[e][k] += 1
                op.sem = dsem[e][k]
                op.val = 16 * duse[e][k]
                dstate[e][k] = op
            elif op.needs_inc:
                ccount[e] += 1
                op.sem = csem[e]
                op.val = ccount[e]
            for d in op.deps + extra:
                if (not d.is_dma) and d.eng == e and e == "pe" and not op.is_dma:
                    continue
                key = id(d.sem)
                if waited[e].get(key, 0) >= d.val:
                    continue
                waited[e][key] = d.val
                waits.append((d.sem, d.val))
            streams[e].append((waits, op))
        fin = []
        for e in dsem:
            for k in range(self.n_dma_sems):
                if duse[e][k] and waited["sp"].get(id(dsem[e][k]), 0) < 16 * duse[e][k]:
                    fin.append((dsem[e][k], 16 * duse[e][k]))
        self.streams = streams
        self.fin = fin

    def run(self, block):
        streams = self.streams
        fin = self.fin

        def body(name):
            def f(eng):
                for waits, op in streams[name]:
                    for s, v in waits:
                        eng.wait_ge(s, v)
                    inst = op.fn(eng)
                    if op.sem is not None:
                        inst.then_inc(op.sem, 16 if op.is_dma else 1)
                if name == "sp":
                    for s, v in fin:
                        eng.wait_ge(s, v)
            return f
        block.tensor(body("pe"))
        block.scalar(body("act"))
        block.vector(body("dve"))
        block.gpsimd(body("pool"))
        block.sync(body("sp"))


def build_program(n_scan=NPRE // T, n_main=SEG // T, do_pre=True, do_sample=True, dumps=(), do_conv=True, init_stop=99):
    NPRE = n_scan * T
    NTOK_IN = NPRE + PRE_T + n_main * T
    nc = bass.Bass("TRN2", target_bir_lowering=False)
    st = ExitStack()
    S = Sched(nc)

    def din(name, shape):
        return nc.dram_tensor(name, list(shape), F32, kind="ExternalInput")

    def dout(name, shape):
        return nc.dram_tensor(name, list(shape), F32, kind="ExternalOutput")

    xin = din("xin", [NTOK_IN, D])
    hv_d = din("hv", [128, 1])
    xs_d = din("xs", [16, D])
    ckT_d = din("ckT", [128, 4, 512])
    cv_d = din("cv", [8, 512, 64])
    s0_d = din("s0", [4, 128, 128])
    cprev_d = din("cprev", [128, NFT, 2])
    w_in_d = din("w_in", [D, 5632])
    w_a_d = din("w_a", [512, D])
    w_b_d = din("w_b", [512, D])
    w_o_d = din("w_o", [D, D])
    w_g_d = din("w_g", [D, DFF])
    w_u_d = din("w_u", [D, DFF])
    w_d_d = din("w_d", [DFF, D])
    gmix_d = din("gmix", [128, 8])
    gffn_d = din("gffn", [128, 8])
    gfin_d = din("gfin", [D])
    rel_d = din("rel", [8, 192])
    lbl_d = din("lbl", [128, 2, 4])
    gn_d = din("gn", [128])
    cw_d = din("cw", [128, NFT, 3])
    cb_d = din("cb", [128, NFT])
    ident_d = din("ident", [128, 128])
    mask_d = din("mask", [128, 128])
    jmat_d = din("jmat", [128, 128])

    y_d = dout("y", [SEG, D])
    ys_d = dout("ys", [16, D])
    okT_d = dout("okT", [4, 128, 512])
    ov_d = dout("ov", [512, 512])
    oS_d = dout("oS", [4, 128, 128])
    oconv_d = dout("oconv", [128, NFT, 2])
    okTs_d = dout("okTs", [4, 128, 16])
    ovs_d = dout("ovs", [16, 512])
    oSs_d = dout("oSs", [4, 128, 128])
    oconvs_d = dout("oconvs", [128, NFT, 2])

    def scratch(name, shape, dt=BF16):
        return nc.dram_tensor(name, list(shape), dt, kind="Internal")

    wb_in = scratch("wb_in", [D, 5632])
    wb_a = scratch("wb_a", [512, D])
    wb_b = scratch("wb_b", [512, D])
    wb_o = scratch("wb_o", [D, D])
    wb_g = scratch("wb_g", [D, DFF])
    wb_u = scratch("wb_u", [D, DFF])
    wb_d = scratch("wb_d", [DFF, D])
    ext_d = scratch("ext_d", [8, 768], F32)

    def sb(name, shape, dt=F32):
        return st.enter_context(nc.sbuf_tensor(name, list(shape), dt))

    def R(name, excl=False):
        return Res(name, excl)

    def act(out, in_, func, reads, writes, **kw):
        S.emit("act", lambda e: e.activation(out=out, in_=in_, func=func, **kw), reads, writes)

    def tt(out, in0, in1, op, reads, writes, eng="dve"):
        S.emit(eng, lambda e: e.tensor_tensor(out=out, in0=in0, in1=in1, op=op), reads, writes)

    def ts(out, in0, s1, s2, op0, op1, reads, writes, eng="dve"):
        if op1 is None:
            S.emit(eng, lambda e: e.tensor_scalar(out=out, in0=in0, scalar1=s1, scalar2=None, op0=op0), reads, writes)
        else:
            S.emit(eng, lambda e: e.tensor_scalar(out=out, in0=in0, scalar1=s1, scalar2=s2, op0=op0, op1=op1), reads, writes)

    def stt(out, in0, scalar, in1, op0, op1, reads, writes):
        S.emit("dve", lambda e: e.scalar_tensor_tensor(out=out, in0=in0, scalar=scalar, in1=in1, op0=op0, op1=op1), reads, writes)

    def cp(out, in_, reads, writes, eng="dve"):
        if eng == "act":
            act(out, in_, AF.Copy, reads, writes)
        else:
            S.emit(eng, lambda e: e.tensor_copy(out=out, in_=in_), reads, writes)

    def recip(out, in_, reads, writes):
        S.emit("dve", lambda e: e.reciprocal(out=out, in_=in_), reads, writes)

    def mset(ap, val, writes, eng="pool"):
        S.emit(eng, lambda e: e.memset(ap, val), (), writes)

    def mm(out, lhsT, rhs, start, stop, reads, writes):
        S.emit("pe", lambda e: e.matmul(out, lhsT=lhsT, rhs=rhs, start=start, stop=stop), reads, writes)

    def dma(eng, out, in_, reads, writes):
        S.emit(eng, lambda e: e.dma_start(out=out, in_=in_), reads, writes, dma=True)

    identb = sb("identb", [128, 128], BF16); r_ident = R("ident")
    maskb = sb("maskb", [128, 128], BF16); r_mask = R("mask")
    jb = sb("jb", [128, 128], BF16); r_j = R("j")
    epst = sb("epst", [128, 1]); r_eps = R("eps")
    onesT = sb("onesT", [128, 8]); r_ones = R("ones")
    hvt = sb("hvt", [128, 1]); r_hv = R("hv")
    gmix = sb("gmix_s", [128, 8]); r_gmix = R("gmix")
    gffn = sb("gffn_s", [128, 8]); r_gffn = R("gffn")
    gfin = sb("gfin_s", [128, D]); r_gfin = R("gfin")
    gnrep = sb("gnrep", [128, 4, 128]); r_gn = R("gn")
    cw = sb("cw_s", [128, NFT, 3]); r_cw = R("cw")
    cb = sb("cb_s", [128, NFT]); r_cb = R("cb")
    lbl = sb("lbl_s", [128, 2, 4]); r_lbl = R("lbl")
    lb = sb("lb_s", [128, 4]); oml = sb("oml_s", [128, 4]); r_lb = R("lb")
    ET = sb("ET", [128, 5, 8, 128], BF16); r_ET = R("ET")

    dma("pool", identb[:], ident_d.ap(), (), [r_ident])
    dma("pool", maskb[:], mask_d.ap(), (), [r_mask])
    dma("pool", jb[:], jmat_d.ap(), (), [r_j])
    mset(epst[:], EPS, [r_eps])
    mset(onesT[:], 1.0, [r_ones])
    dma("sp", hvt[:], hv_d.ap(), (), [r_hv])
    dma("sp", gmix[:], gmix_d.ap(), (), [r_gmix])
    dma("sp", gffn[:], gffn_d.ap(), (), [r_gffn])
    dma("sp", gfin[:], AP(gfin_d, 0, [[0, 128], [1, D]]), (), [r_gfin])
    dma("sp", gnrep[:], AP(gn_d, 0, [[0, 128], [0, 4], [1, 128]]), (), [r_gn])
    dma("sp", cw[:], cw_d.ap(), (), [r_cw])
    dma("sp", cb[:], cb_d.ap(), (), [r_cb])
    dma("sp", lbl[:], lbl_d.ap(), (), [r_lbl])

    if init_stop <= 1:
        S.finalize(st)
        with nc.Block() as block:
            S.run(block)
        st.close()
        return nc
    ps_att = st.enter_context(nc.psum_tensor("ps_att", [128, 1536], F32))
    r_att = [R("att0", True), R("att1", True)]
    r_attB = R("attB", True)
    NGEN = 2
    ps_gen = [st.enter_context(nc.psum_tensor("ps_g%d" % i, [128, 512], F32)) for i in range(NGEN)]
    r_gen = [R("g%d" % i, True) for i in range(NGEN)]
    ps_o = st.enter_context(nc.psum_tensor("ps_o", [128, 512], F32))
    r_o = R("ps_o", True)
    ps_tr = [st.enter_context(nc.psum_tensor("ps_t%d" % i, [128, 1024], BF16)) for i in range(2)]
    r_tr = [R("t%d" % i, True) for i in range(2)]
    cnt = {"g": 0, "t": 0, "a": 0, "p": 0}

    gpool_full = [(ps_gen[i], r_gen[i]) for i in range(NGEN)]
    gpool_front = gpool_full + [(ps_o, r_o)]
    gpool_back = [(ps_att[:, 0:512], r_att[0]), (ps_att[:, 512:1024], r_att[1]), (ps_att[:, 1024:1536], r_attB)]
    gstate = {"pool": gpool_full, "tr": (0, 1)}

    def gbank():
        pool = gstate["pool"]
        i = cnt["g"] % len(pool)
        cnt["g"] += 1
        return pool[i]

    def tbank():
        sel = gstate["tr"]
        i = sel[cnt["t"] % len(sel)]
        cnt["t"] += 1
        return ps_tr[i], r_tr[i]

    lbt = sb("lbt", [128, 4])
    tt(lbt[:], lbl[:, 1, :], lbl[:, 0, :], ALU.subtract, [r_lbl], [r_lb])
    act(lbt[:], lbt[:], AF.Exp, [r_lb], [r_lb])
    ts(lbt[:], lbt[:], 1.0, None, ALU.add, None, [r_lb], [r_lb])
    recip(lb[:], lbt[:], [r_lb], [r_lb])
    ts(oml[:], lb[:], -1.0, 1.0, ALU.mult, ALU.add, [r_lb], [r_lb])

    if init_stop <= 2:
        S.finalize(st)
        with nc.Block() as block:
            S.run(block)
        st.close()
        return nc
    if init_stop <= 3:
        S.finalize(st)
        with nc.Block() as block:
            S.run(block)
        st.close()
        return nc
    r_wb = {}
    conv_jobs = []
    for name, src, dst, rows in (("in", w_in_d, wb_in, D), ("a", w_a_d, wb_a, 512), ("b", w_b_d, wb_b, 512),
                                 ("o", w_o_d, wb_o, D), ("g", w_g_d, wb_g, D), ("u", w_u_d, wb_u, D),
                                 ("d", w_d_d, wb_d, DFF)):
        r_wb[name] = R("wb_" + name)
        for r0 in range(0, rows, 128):
            conv_jobs.append((name, dst.ap()[r0:r0 + 128, :], src.ap()[r0:r0 + 128, :]))

    def emit_conv(n):
        for _ in range(n):
            if conv_jobs:
                name, d_, s_ = conv_jobs.pop(0)
                dma("pool", d_, s_, (), [r_wb[name]])

    wslot = [sb("wslot%d" % i, [128, SLOT_E], BF16) for i in range(NSLOT)]
    r_wslot = [R("wslot%d" % i) for i in range(NSLOT)]
    W1v = []
    for i, c0 in enumerate((2048, 2560, 512, 1024)):
        v_ = wslot[i][:, :].rearrange("p (k n) -> p k n", k=8)
        dma("pool", v_, w_in_d.ap()[:, c0:c0 + 512].rearrange("(k p) n -> p k n", p=128), (), [r_wslot[i]])
        W1v.append(v_)

    def wgroups(mode):
        g = []
        for i in (3, 4, 5, 6, 0, 1, 2, 7, 8, 9, 10):
            g.append(("in%d" % i, "in", wb_in.ap()[:, 512 * i:512 * i + 512].rearrange("(k p) n -> p k n", p=128), 8, 512))
        g.append(("a", "a", wb_a.ap().rearrange("(k p) n -> p k n", p=128), 4, 1024))
        g.append(("b", "b", wb_b.ap().rearrange("(k p) n -> p k n", p=128), 4, 1024))
        for n in range(2):
            g.append(("o%d" % n, "o", wb_o.ap()[:, 512 * n:512 * n + 512].rearrange("(k p) n -> p k n", p=128), 8, 512))
        for gi in range(6):
            nc_ = 512 if gi < 5 else 256
            g.append(("g%d" % gi, "g", wb_g.ap()[:, 512 * gi:512 * gi + nc_].rearrange("(k p) n -> p k n", p=128), 8, nc_))
            if mode != "pre":
                g.append(("u%d" % gi, "u", wb_u.ap()[:, 512 * gi:512 * gi + nc_].rearrange("(k p) n -> p k n", p=128), 8, nc_))
        if mode != "pre":
            for n in range(2):
                for gk in range(3):
                    nk = 8 if gk < 2 else 6
                    g.append(("d%d_%d" % (n, gk), "d",
                              wb_d.ap()[1024 * gk:1024 * gk + 128 * nk, 512 * n:512 * n + 512].rearrange("(k p) n -> p k n", p=128), nk, 512))
        return g

    wq = []
    tiles_plan = ([("pre", 0)] if do_pre else []) + [("main", m) for m in range(n_main)] + ([("sample", 0)] if do_sample else [])
    for mode, _ in tiles_plan:
        wq.extend(wgroups(mode))
    wstate = {"issued": 0, "consumed": 0, "slot": {}}

    def w_issue(slot):
        i = wstate["issued"]
        if i >= len(wq):
            return
        key, wname, src, nk, ncol = wq[i]
        view = wslot[slot][:, 0:nk * ncol].rearrange("p (k n) -> p k n", k=nk)
        dma("sp", view, src, [r_wb[wname]], [r_wslot[slot]])
        wstate["slot"][i] = slot
        wstate["issued"] += 1

    def w_next(key):
        i = wstate["consumed"]
        assert wq[i][0] == key, (wq[i][0], key)
        slot = wstate["slot"][i]
        _, _, _, nk, ncol = wq[i]
        view = wslot[slot][:, 0:nk * ncol].rearrange("p (k n) -> p k n", k=nk)
        return view, r_wslot[slot], slot

    def w_done(slot):
        wstate["consumed"] += 1
        w_issue(slot)

    if init_stop <= 4:
        S.finalize(st)
        with nc.Block() as block:
            S.run(block)
        st.close()
        return nc
    X = [sb("X%d" % i, [128, 2, D]) for i in range(2)]
    r_X = [[R("X%d_%d" % (i, j)) for j in range(2)] for i in range(2)]
    XN = sb("XN", [128, 2, D], BF16); r_XN = [R("XN0"), R("XN1")]
    ss = sb("ss", [128, 8]); r_ss = R("ss")
    hTs = [sb("hTa", [128, 8, T], BF16), sb("hTb", [128, 8, T], BF16), sb("h2T", [128, 8, T], BF16)]
    r_hTs = [[R("hTa0"), R("hTa1")], [R("hTb0"), R("hTb1")], [R("h2T0"), R("h2T1")]]
    qT = sb("qT", [128, 4, T], BF16); r_qT = R("qT")
    kTr = sb("kTr", [128, 4, 6, 128], BF16); r_kT = [R("kT%d" % i) for i in range(6)]
    Vr = sb("Vr", [128, 6, 8, 65], BF16); r_V = [R("V%d" % i) for i in range(6)]
    Pb = [sb("Pb%d" % i, [128, 5, 128], BF16) for i in range(3)]; r_Pb = [R("Pb%d" % i) for i in range(3)]
    oa = sb("oa", [128, 2, 512], BF16); r_oa = [R("oa0"), R("oa1")]
    oaT = sb("oaT", [128, 4, T], BF16); r_oaT = [R("oaT0"), R("oaT1")]
    rec = sb("rec", [128, 8]); r_rec = R("rec")
    sg = sb("sg", [128, 4, T]); r_sg = [R("sg%d" % i) for i in range(4)]
    siluq = sb("siluq", [128, 4, T]); r_sq = [R("siluq%d" % i) for i in range(4)]
    gF = sb("gF", [128, 4 * T]); r_gF = R("gF")
    gL = sb("gL", [128, 4 * T]); r_gL = R("gL")
    gB = sb("gB", [128, 4 * T]); r_gB = R("gB")
    gK = sb("gK", [128, 4 * T]); r_gK = R("gK")
    gE = sb("gE", [128, 4 * T]); r_gE = R("gE")
    dec = sb("dec", [128, 3, 4, 4]); r_dec = R("dec")
    Zq = sb("Zq", [128, 4, 4, 128], BF16); r_Zq = [R("Zq%d" % i) for i in range(4)]
    ktT = sb("ktT", [128, 4, T], BF16); r_ktT = [R("ktT%d" % i) for i in range(4)]
    khT = sb("khT", [128, 4, T], BF16); r_khT = [R("khT%d" % i) for i in range(4)]
    khtm = sb("khtm", [128, 2, 512], BF16); r_khtm = [R("khtm0"), R("khtm1")]
    vh = sb("vh", [128, 2, 512], BF16); r_vh = [R("vh0"), R("vh1")]
    sgb = sb("sgb", [128, 2, 512]); r_sgb = [R("sgb0"), R("sgb1")]
    Sst = sb("Sst", [128, 4, 128]); r_S = R("S")
    Sp = sb("Sp", [128, 2, 4, 128], BF16); r_Sp = [R("Sp0"), R("Sp1")]
    AT = sb("AT", [128, 4, 128], BF16); r_AT = R("AT")
    sqb = sb("sqb", [128, 512]); r_sqb = R("sqb")
    ssq = sb("ssq", [128, 4]); r_ssq = R("ssq")
    t1 = sb("t1", [128, 512]); r_t1 = R("t1")
    gG = sb("gG", [128, 512]); r_gG = R("gG")
    ob = sb("ob", [128, 512], BF16); r_ob = R("ob")
    obT = sb("obT", [128, 4, T], BF16); r_obT = [R("obT0"), R("obT1")]
    zs = sb("zs", [128, 16, T], BF16); r_zs = [R("zs%d" % i) for i in range(16)]
    mT = sb("mT", [128, 8, T], BF16); r_mT = R("mT")
    aT = [sb("aT%d" % i, [128, T + 2]) for i in range(2)]; r_aT = [R("aT0"), R("aT1")]
    c1 = [sb("c1_%d" % i, [128, T]) for i in range(2)]; r_c1 = [R("c1_0"), R("c1_1")]
    c2 = [sb("c2_%d" % i, [128, T]) for i in range(2)]; r_c2 = [R("c2_0"), R("c2_1")]
    m1, r_m1, m2, r_m2 = c1[0], r_c1[0], c2[0], r_c2[0]
    gT = sb("gT", [128, NFT, T], BF16); r_gT = [R("gT%d" % i) for i in range(NFT)]
    cprev = sb("cprev_s", [128, NFT, 2]); r_cprev = R("cprev")

    ext_ = siluq[0:8].rearrange("p h t -> p (h t)")[:, 0:768]; r_ext = r_sq; r_extd = R("extd")
    dma("sp", ext_[:, 64:256], rel_d.ap(), (), r_ext)
    act(ext_[:, 0:64], ext_[:, 64:65].to_broadcast([8, 64]), AF.Identity, r_ext, r_ext)
    act(ext_[:, 256:768], ext_[:, 255:256].to_broadcast([8, 512]), AF.Identity, r_ext, r_ext)
    act(ext_[:, :], ext_[:, :], AF.Exp, r_ext, r_ext)
    dma("sp", ext_d.ap(), ext_[:, :], r_ext, [r_extd])
    hk = sg[:].rearrange("p h t -> p (h t)").rearrange("p (a b) -> p a b", a=8); r_hk = r_sg
    hkb = ktT[:].rearrange("p h t -> p (h t)").rearrange("p (a b) -> p a b", a=8); r_hkb = r_ktT
    for kt in range(5):
        dma("sp", hk, AP(ext_d, 512 - 128 * kt, [[1, 128], [768, 8], [1, 128]]), [r_extd], r_hk)
        cp(hkb, hk, r_hk, r_hkb)
        for n in range(2):
            pb, rb = gbank()
            mm(pb[:, :], jb[:], hkb[:, 4 * n:4 * n + 4, :].rearrange("p h q -> p (h q)"), True, True, [r_j] + r_hkb, [rb])
            cp(ET[:, kt, 4 * n:4 * n + 4, :].rearrange("p h q -> p (h q)"), pb[:, :], [rb], [r_ET])
    mset(ET[0:64, 0, :, 64:128], 0.0, [r_ET])
    mset(ET[64:128, 4, :, 0:64], 0.0, [r_ET])

    kst = gT[:, 0:8, :].rearrange("p a b -> p (a b)").bitcast(F32).rearrange("p (h t) -> p h t", h=4)
    vst = gT[:, 8:16, :].rearrange("p a b -> p (a b)").bitcast(F32).rearrange("p (j c) -> p j c", j=2)
    rl_kst = r_gT[0:8]
    rl_vst = [r_gT[8:12], r_gT[12:16]]
    sg2 = gT[:, 0:8, :].rearrange("p a b -> p (a b)").bitcast(F32).rearrange("p (h t) -> p h t", h=4)
    r_sg2 = [[r_gT[2 * i], r_gT[2 * i + 1]] for i in range(4)]
    vh2 = gT[:, 8:12, :].rearrange("p a b -> p (a b)").rearrange("p (j c) -> p j c", j=2)
    r_vh2 = [[r_gT[8], r_gT[9]], [r_gT[10], r_gT[11]]]
    sgs = [(sg, [[r] for r in r_sg]), (sg2, r_sg2)]
    vhs = [(vh, [[r] for r in r_vh]), (vh2, r_vh2)]
    mset(Zq[:], 0.0, r_Zq)
    mset(Sst[:], 0.0, [r_S])
    mset(cprev[:], 0.0, [r_cprev])

    if init_stop <= 5:
        S.finalize(st)
        with nc.Block() as block:
            S.run(block)
        st.close()
        return nc
    def load_x(xsrc, ntok, xb):
        nsub = (ntok + 127) // 128
        for j in range(nsub):
            nt = min(128, ntok - 128 * j)
            dma("sp", X[xb][:nt, j, :], xsrc[j * 128:j * 128 + nt, :], (), [r_X[xb][j]])

    def norm_T(src_tile, rsrc, gcol, rg, col0, hsel, ntok):
        nsub = (ntok + 127) // 128
        dst, rdst = hTs[hsel], r_hTs[hsel]
        for j in range(nsub):
            nt = min(128, ntok - 128 * j)
            act(XN[:nt, j, :], src_tile[:nt, j, :], AF.Square, [rsrc[j]], [r_XN[j], r_ss], accum_out=ss[:nt, col0 + j:col0 + j + 1])
            act(ss[:nt, col0 + j:col0 + j + 1], ss[:nt, col0 + j:col0 + j + 1], AF.Sqrt, [r_ss, r_eps], [r_ss],
                bias=epst[:nt, 0:1], scale=1.0 / D)
            recip(ss[:nt, col0 + j:col0 + j + 1], ss[:nt, col0 + j:col0 + j + 1], [r_ss], [r_ss])
            act(XN[:nt, j, :], src_tile[:nt, j, :], AF.Copy, [rsrc[j], r_ss], [r_XN[j]], scale=ss[:nt, col0 + j:col0 + j + 1])
            pt, rt = tbank()
            for kc in range(8):
                S.emit("pe", lambda e, kc=kc, j=j, nt=nt, pt=pt: e.transpose(out=pt[:, kc * 128:kc * 128 + nt], in_=XN[:nt, j, kc * 128:(kc + 1) * 128],
                                                                              identity=identb[:nt, :nt]), [r_XN[j], r_ident], [rt])
            tt(dst[:, :, j * 128:j * 128 + nt], pt[:, :].rearrange("p (k t) -> p k t", k=8)[:, :, 0:nt],
               gcol[:, 0:8].unsqueeze(2).to_broadcast([128, 8, nt]), ALU.mult, [rt, rg], [rdst[j]])

    def macro_tile(mode, ntok, xb, gt0, hsel=0, normed=False, kv=False, out_row=None, final_kv=None, after_norm=None, prenorm=None,
                   part="all", bsel=0):
        sample = mode == "sample"
        nsub = (ntok + 127) // 128
        nts = [min(128, ntok - 128 * j) for j in range(nsub)]
        C = 16 if sample else 64
        nch = ntok // C
        cps = 1 if sample else 2
        Xb = X[xb]
        rX = r_X[xb]
        hT = hTs[hsel]
        r_hT = r_hTs[hsel]
        rhT_all = r_hT[:nsub]
        cur["hT"] = hT
        cur["r_hT"] = r_hT
        sg, rl_sg = sgs[bsel]
        vh, rl_vh = vhs[bsel]
        if mode == "scan":
            gstate["pool"] = gpool_front if part == "front" else gpool_back
            gstate["tr"] = (0,) if part == "front" else (1,)
        else:
            gstate["pool"] = gpool_front + gpool_back
            gstate["tr"] = (0, 1)

        if part != "back":
            if not normed:
                norm_T(Xb, rX, gmix, r_gmix, 0, hsel, ntok)
            if after_norm is not None:
                after_norm()

        def proj_fm(wv, rw, ct, evac):
            pb, rb = gbank()
            for kc in range(8):
                mm(pb[:, 0:ntok], wv[:, kc, ct * 128:(ct + 1) * 128], hT[:, kc, 0:ntok], kc == 0, kc == 7, [rw] + rhT_all, [rb])
            evac(pb, rb)

        def proj_tm(wv, rw, j, evac, ncol=512):
            pb, rb = gbank()
            nt = nts[j]
            for kc in range(8):
                mm(pb[:nt, 0:ncol], hT[:, kc, j * 128:j * 128 + nt], wv[:, kc, 0:ncol], kc == 0, kc == 7, [rw, r_hT[j]], [rb])
            evac(pb, rb, j, nt)

        def slot_of(j):
            return (gt0 + j) % 6 if not sample else 4

        def hgrn_gates_all():
            n4 = 4 * ntok
            nc4 = 4 * nch
            fl = lambda b: b[:, 0:n4]
            v3 = lambda b: b[:, 0:n4].rearrange("p (h t) -> p h t", h=4)
            ch = lambda b: b[:, 0:n4].rearrange("p (c t) -> p c t", t=C)
            hc = lambda b: b[:, 0:n4].rearrange("p (h c t) -> p h c t", h=4, t=C)
            rsg = [r for l_ in rl_sg for r in l_]
            sgf = sg.rearrange("p h t -> p (h t)")
            tt(v3(gF), v3(sgf), oml[:, 0:4].unsqueeze(2).to_broadcast([128, 4, ntok]), ALU.mult, rsg + [r_lb], [r_gF])
            tt(v3(gF), v3(gF), lb[:, 0:4].unsqueeze(2).to_broadcast([128, 4, ntok]), ALU.add, [r_gF, r_lb], [r_gF])
            act(fl(gL), fl(gF), AF.Ln, [r_gF], [r_gL])
            S.emit("dve", lambda e: e.tensor_tensor_scan(out=fl(gB), data0=fl(gL), data1=fl(gL), initial=0.0, op0=ALU.add, op1=ALU.min),
                   [r_gL], [r_gB])
            ts(fl(gK), fl(gF), -1.0, 1.0, ALU.mult, ALU.add, [r_gF], [r_gK], eng="pool")
            tt(ch(gL), ch(gB), ch(gB)[:, :, C // 2 - 1:C // 2].to_broadcast([128, nc4, C]), ALU.subtract, [r_gB, r_gL], [r_gL])
            act(fl(gE), fl(gL), AF.Exp, [r_gL], [r_gE], scale=-1.0)
            tt(ktT[:, :, 0:ntok], v3(gK), v3(gE), ALU.mult, [r_gK, r_gE], r_ktT, eng="pool")
            dsl = lambda i: dec[:, i, :, 0:nch]
            if mode != "scan":
                act(fl(gB), fl(gL), AF.Exp, [r_gL, r_gB], [r_gB])
                cp(dsl(2), hc(gB)[:, :, :, C - 1], [r_gB], [r_dec])
            else:
                act(dsl(2), hc(gL)[:, :, :, C - 1], AF.Exp, [r_gL], [r_dec])
            tt(dsl(0), hc(gE)[:, :, :, 0], hc(gF)[:, :, :, 0], ALU.mult, [r_gE, r_gF], [r_dec])
            tt(dsl(1), dsl(0), dsl(2), ALU.mult, [r_dec], [r_dec])
            k4 = lambda b: b[:, :, 0:ntok].rearrange("p h (c t) -> p h c t", t=C)
            tt(k4(khT), k4(ktT), dsl(2).unsqueeze(3).to_broadcast([128, 4, nch, C]), ALU.mult, r_ktT + [r_dec], r_khT)
            if mode != "scan":
                sqf = siluq.rearrange("p h t -> p (h t)")
                if sample:
                    tt(Zq[:, :, 0, 0:ntok], v3(sqf), v3(gB), ALU.mult, r_sq + [r_gB], r_Zq)
                else:
                    base = Zq[:, 0, 0, 0:64]
                    zout = AP(base.tensor, base.offset, [list(base.ap[0]), [512 // nsub, 4 * nsub], [192, 2], [1, 64]])
                    tt(zout, fl(sqf).rearrange("p (a c t) -> p a c t", c=2, t=64), fl(gB).rearrange("p (a c t) -> p a c t", c=2, t=64),
                       ALU.mult, r_sq + [r_gB], r_Zq)

        def khat_transpose(j):
            nt = nts[j]
            pt, rt = tbank()
            for hb in range(4):
                S.emit("pe", lambda e, hb=hb, pt=pt: e.transpose(out=pt[:nt, hb * 128:(hb + 1) * 128], in_=khT[:, hb, j * 128:j * 128 + nt],
                                                                  identity=identb[:, :]), [r_khT[hb], r_ident], [rt])
            cp(khtm[:nt, j, :], pt[:nt, 0:512], [rt], [r_khtm[j]], eng="act")

        def s_update(j, ci):
            p = ci % cps
            rows = slice(p * C, p * C + C)
            pb, rb = gbank()
            for hb in range(4):
                mm(pb[:, hb * 128:(hb + 1) * 128], khtm[rows, j, hb * 128:(hb + 1) * 128], vh[rows, j, hb * 128:(hb + 1) * 128],
                   True, True, [r_khtm[j]] + rl_vh[j], [rb])
            tt(Sst[:], Sst[:], dec[:, 1, :, ci:ci + 1].to_broadcast([128, 4, 128]), ALU.mult, [r_S, r_dec], [r_S])
            tt(Sst[:].rearrange("p h v -> p (h v)"), Sst[:].rearrange("p h v -> p (h v)"), pb[:, :], ALU.add, [r_S, rb], [r_S])

        if mode == "scan":
            wv_f, wv_i = W1v[0], W1v[1]
            if part != "back":
                for hb in range(4):
                    proj_fm(wv_f, r_wslot[0], hb, lambda pb, rb, hb=hb: act(sg.rearrange("p h t -> p (h t)")[:, hb * ntok:(hb + 1) * ntok], pb[:, 0:ntok], AF.Sigmoid, [rb], rl_sg[hb]))
                for j in range(nsub):
                    proj_tm(wv_i, r_wslot[1], j, lambda pb, rb, j, nt: cp(vh[:nt, j, :], pb[:nt, :], [rb], rl_vh[j], eng="act"))
                if kv:
                    kv_proj_only()
            if part != "front":
                hgrn_gates_all()
                for j in range(nsub):
                    khat_transpose(j)
                    for p in range(cps):
                        s_update(j, j * cps + p)
            return

        def ev_q(ct):
            return lambda pb, rb: act(qT[:, ct, 0:ntok], pb[:, 0:ntok], AF.Copy, [rb], [r_qT], scale=0.125)

        def ev_k(ct):
            def f(pb, rb):
                for j in range(nsub):
                    cp(kTr[:, ct, slot_of(j), 0:nts[j]], pb[:, j * 128:j * 128 + nts[j]], [rb], [r_kT[slot_of(j)]], eng="act")
                if final_kv is not None:
                    cp(kst[:, ct, 0:ntok], pb[:, 0:ntok], [rb], rl_kst)
            return f

        def ev_v(pb, rb, j, nt):
            s_ = slot_of(j)
            cp(Vr[:nt, s_, :, 0:64], pb[:nt, :].rearrange("p (h d) -> p h d", h=8), [rb], [r_V[s_]], eng="act")
            if mode == "main" or sample:
                cp(Vr[:nt, s_, :, 64:65], onesT[:nt, 0:8].unsqueeze(2), [r_ones], [r_V[s_]], eng="pool")
            else:
                cp(Vr[:nt, s_, :, 64:65], hvt[:nt, 0:1].unsqueeze(2).to_broadcast([nt, 8, 1]), [r_hv], [r_V[s_]], eng="pool")
            if final_kv is not None:
                cp(vst[:nt, j, :], pb[:nt, :], [rb], rl_vst[j])
                dma("act", final_kv[1].ap()[final_kv[2] + j * 128:final_kv[2] + j * 128 + nt, :], vst[:nt, j, :], rl_vst[j], ())

        wv, rw, sl = w_next("in3")
        for hb in range(4):
            proj_fm(wv, rw, hb, lambda pb, rb, hb=hb: act(siluq.rearrange("p h t -> p (h t)")[:, hb * ntok:(hb + 1) * ntok], pb[:, 0:ntok], AF.Silu, [rb], [r_sq[hb]]))
        w_done(sl)
        wv, rw, sl = w_next("in4")
        for hb in range(4):
            proj_fm(wv, rw, hb, lambda pb, rb, hb=hb: act(sg.rearrange("p h t -> p (h t)")[:, hb * ntok:(hb + 1) * ntok], pb[:, 0:ntok], AF.Sigmoid, [rb], rl_sg[hb]))
        w_done(sl)
        wv, rw, sl = w_next("in5")
        for j in range(nsub):
            proj_tm(wv, rw, j, lambda pb, rb, j, nt: cp(vh[:nt, j, :], pb[:nt, :], [rb], rl_vh[j], eng="act"))
        w_done(sl)
        wv, rw, sl = w_next("in6")
        for j in range(nsub):
            proj_tm(wv, rw, j, lambda pb, rb, j, nt: act(sgb[:nt, j, :], pb[:nt, :], AF.Silu, [rb], [r_sgb[j]]))
        w_done(sl)
        wv, rw, sl = w_next("in0")
        for ct in range(4):
            proj_fm(wv, rw, ct, ev_q(ct))
        w_done(sl)
        wv, rw, sl = w_next("in1")
        for ct in range(4):
            proj_fm(wv, rw, ct, ev_k(ct))
        w_done(sl)
        if final_kv is not None:
            for ct in range(4):
                dma("act", final_kv[0].ap()[ct, :, final_kv[2]:final_kv[2] + ntok], kst[:, ct, 0:ntok], rl_kst, ())
        wv, rw, sl = w_next("in2")
        for j in range(nsub):
            proj_tm(wv, rw, j, ev_v)
        w_done(sl)

        hgrn_gates_all()
        gstate["pool"] = gpool_full

        att_steps, z_steps, h_steps = [], [], []

        zst = {}

        def z_step(zi):
            gi, ct = divmod(zi, 4)
            if ct == 0:
                zst["w"] = w_next("in%d" % (7 + gi))
            wv_, rw_, sl_ = zst["w"]
            proj_fm(wv_, rw_, ct, lambda pb, rb: act(zs[:, zi, 0:ntok], pb[:, 0:ntok], AF.Sigmoid, [rb], [r_zs[zi]]))
            if ct == 3:
                w_done(sl_)
        for zi in range(16):
            z_steps.append(lambda zi=zi: z_step(zi))

        def make_att(j):
            nq = nts[j]
            if sample:
                ktiles = [(0, 128), (1, 128), (2, 128), (3, 128), (4, 16)]
            else:
                ktiles = [((gt0 + j - 4 + kt) % 6, 128) for kt in range(5)]
            pend = []

            def pv(h, pslot):
                for kt, (s_, nk) in enumerate(ktiles):
                    mm(ps_o[:nq, (h % 4) * 65:(h % 4) * 65 + 65], Pb[pslot][:nk, kt, 0:nq], Vr[:nk, s_, h, :], kt == 0, kt == 4,
                       [r_Pb[pslot], r_V[s_]], [r_o])

            def normalize(half):
                o3 = ps_o[:nq, 0:260].rearrange("p (h d) -> p h d", h=4)
                ts(rec[:nq, half * 4:half * 4 + 4].unsqueeze(2), o3[:, :, 64:65], 1e-30, None, ALU.max, None, [r_o], [r_rec])
                recip(rec[:nq, half * 4:half * 4 + 4], rec[:nq, half * 4:half * 4 + 4], [r_rec], [r_rec])
                tt(oa[:nq, j, half * 256:half * 256 + 256].rearrange("p (h d) -> p h d", h=4), o3[:, :, 0:64],
                   rec[:nq, half * 4:half * 4 + 4].unsqueeze(2).to_broadcast([nq, 4, 64]), ALU.mult, [r_o, r_rec], [r_oa[j]])

            def head(h):
                hp, r0 = h // 2, (h % 2) * 64
                ai = cnt["a"] % 2
                cnt["a"] += 1
                offA = ai * 512
                offB = 1024 + ai * 128
                for kt in (4, 0, 1, 2, 3):
                    s_, nk = ktiles[kt]
                    o_ = offB if kt == 4 else offA + kt * 128
                    mm(ps_att[:nk, o_:o_ + nq], kTr[r0:r0 + 64, hp, s_, 0:nk], qT[r0:r0 + 64, hp, j * 128:j * 128 + nq],
                       True, True, [r_kT[s_], r_qT], [r_attB if kt == 4 else r_att[ai]])
                pi = cnt["p"] % 3
                cnt["p"] += 1
                sattA = ps_att[:, offA:offA + 512].rearrange("p (k q) -> p k q", k=4)
                sattB = ps_att[:, offB:offB + 128]
                if sample:
                    act(Pb[pi][:16, 4, 0:nq], sattB[:16, 0:nq], AF.Exp, [r_attB], [r_Pb[pi]])
                    act(Pb[pi][:, 0:4, 0:nq], sattA[:, :, 0:nq], AF.Exp, [r_att[ai]], [r_Pb[pi]])
                    tt(Pb[pi][:, 0:4, 0:nq], Pb[pi][:, 0:4, 0:nq], ET[:, 0:4, h, 0:nq], ALU.mult, [r_Pb[pi], r_ET], [r_Pb[pi]], eng="pool")
                    tt(Pb[pi][:16, 4, 0:nq], Pb[pi][:16, 4, 0:nq], ET[:16, 4, h, 0:nq], ALU.mult, [r_Pb[pi], r_ET], [r_Pb[pi]], eng="pool")
                else:
                    act(Pb[pi][:, 4, :], sattB, AF.Exp, [r_attB], [r_Pb[pi]])
                    act(Pb[pi][:, 0:4, :], sattA, AF.Exp, [r_att[ai]], [r_Pb[pi]])
                    tt(Pb[pi][:, :, :], Pb[pi][:, :, :], ET[:, :, h, :], ALU.mult, [r_Pb[pi], r_ET], [r_Pb[pi]], eng="pool")
                if pend:
                    ph, ppi = pend.pop(0)
                    pv(ph, ppi)
                    if ph == 3:
                        normalize(0)
                pend.append((h, pi))

            def tail():
                ph, ppi = pend.pop(0)
                pv(ph, ppi)
                normalize(1)
                pt, rt = tbank()
                for kc in range(4):
                    S.emit("pe", lambda e, kc=kc, pt=pt: e.transpose(out=pt[:, kc * 128:kc * 128 + nq], in_=oa[:nq, j, kc * 128:(kc + 1) * 128],
                                                                      identity=identb[:nq, :nq]), [r_oa[j], r_ident], [rt])
                cp(oaT[:, :, j * 128:j * 128 + nq], pt[:, 0:512].rearrange("p (k t) -> p k t", k=4)[:, :, 0:nq], [rt], [r_oaT[j]], eng="act")
            for h in range(8):
                att_steps.append(lambda h=h: head(h))
            att_steps.append(tail)
        for j in range(nsub):
            make_att(j)

        def make_h(j):
            nt = nts[j]

            def h_at():
                pb, rb = gbank()
                for hb in range(4):
                    if sample:
                        qrhs = Zq[:, hb, 0, 0:nt]
                    else:
                        base = Zq[:, hb, 2 * j, 0:64]
                        qrhs = AP(base.tensor, base.offset, [list(base.ap[0]), [192, 2], [1, 64]])
                    mm(pb[:nt, hb * 128:hb * 128 + nt], ktT[:, hb, j * 128:j * 128 + nt], qrhs, True, True, [r_ktT[hb], r_Zq[hb]], [rb])
                tt(AT[:nt, :, 0:nt], pb[:nt, :].rearrange("p (h t) -> p h t", h=4)[:, :, 0:nt],
                   maskb[:nt, 0:nt].unsqueeze(1).to_broadcast([nt, 4, nt]), ALU.mult, [rb, r_mask], [r_AT])

            def h_chunk(p):
                ci = j * cps + p
                tt(Sp[:, p], Sst[:], dec[:, 0, :, ci:ci + 1].to_broadcast([128, 4, 128]), ALU.mult, [r_S, r_dec], [r_Sp[p]])
                s_update(j, ci)

            def h_out():
                ob_, rob = gbank()
                for hb in range(4):
                    for p in range(cps):
                        zl = Zq[:, hb, 0, 0:nt] if sample else Zq[:, hb, 2 * j + p, :]
                        mm(ob_[:nt, hb * 128:(hb + 1) * 128], zl, Sp[:, p, hb, :], p == 0, False, [r_Zq[hb], r_Sp[p]], [rob])
                    mm(ob_[:nt, hb * 128:(hb + 1) * 128], AT[:nt, hb, 0:nt], vh[:nt, j, hb * 128:(hb + 1) * 128], False, True, [r_AT] + rl_vh[j], [rob])
                act(sqb[:nt, :], ob_[:nt, :], AF.Square, [rob], [r_sqb])
                S.emit("dve", lambda e: e.tensor_reduce(out=ssq[:nt, 0:4], in_=sqb[:nt, :].rearrange("p (h v) -> p h v", h=4), axis=AX.X, op=ALU.add),
                       [r_sqb], [r_ssq])
                act(ssq[:nt, :], ssq[:nt, :], AF.Sqrt, [r_ssq, r_eps], [r_ssq], bias=epst[:nt, 0:1], scale=1.0 / 128)
                recip(ssq[:nt, :], ssq[:nt, :], [r_ssq], [r_ssq])
                tt(t1[:nt, :].rearrange("p (h v) -> p h v", h=4), ob_[:nt, :].rearrange("p (h v) -> p h v", h=4),
                   ssq[:nt, 0:4].unsqueeze(2).to_broadcast([nt, 4, 128]), ALU.mult, [rob, r_ssq], [r_t1])
                tt(gG[:nt, :], sgb[:nt, j, :], gnrep[:nt].rearrange("p h v -> p (h v)"), ALU.mult, [r_sgb[j], r_gn], [r_gG], eng="pool")
                tt(ob[:nt, :], t1[:nt, :], gG[:nt, :], ALU.mult, [r_t1, r_gG], [r_ob])

            def h_tr():
                pt, rt = tbank()
                for hb in range(4):
                    S.emit("pe", lambda e, hb=hb, pt=pt: e.transpose(out=pt[:, hb * 128:hb * 128 + nt], in_=ob[:nt, hb * 128:(hb + 1) * 128],
                                                                      identity=identb[:nt, :nt]), [r_ob, r_ident], [rt])
                cp(obT[:, :, j * 128:j * 128 + nt], pt[:, 0:512].rearrange("p (k t) -> p k t", k=4)[:, :, 0:nt], [rt], [r_obT[j]], eng="act")
            h_steps.append(lambda: khat_transpose(j))
            h_steps.append(h_at)
            for p in range(cps):
                h_steps.append(lambda p=p: h_chunk(p))
            h_steps.append(h_out)
            h_steps.append(h_tr)
        for j in range(nsub):
            make_h(j)

        for i in range(max(len(att_steps), len(z_steps), len(h_steps))):
            for lst in (att_steps, z_steps, h_steps):
                if i < len(lst):
                    lst[i]()

        gstate["pool"] = gpool_front + gpool_back
        wva, rwa, sla = w_next("a")
        wstate["consumed"] += 1
        wvb, rwb, slb = w_next("b")
        wstate["consumed"] -= 1
        for ct in range(8):
            pb, rb = gbank()
            for kc in range(4):
                mm(pb[:, 0:ntok], wva[:, kc, ct * 128:(ct + 1) * 128], oaT[:, kc, 0:ntok], kc == 0, kc == 3, [rwa] + r_oaT[:nsub], [rb])
            for kc in range(4):
                mm(pb[:, 256:256 + ntok], wvb[:, kc, ct * 128:(ct + 1) * 128], obT[:, kc, 0:ntok], kc == 0, kc == 3, [rwb] + r_obT[:nsub], [rb])
            tt(m1[:, 0:ntok], pb[:, 0:ntok], zs[:, ct, 0:ntok], ALU.mult, [rb, r_zs[ct]], [r_m1])
            tt(m2[:, 0:ntok], pb[:, 256:256 + ntok], zs[:, 8 + ct, 0:ntok], ALU.mult, [rb, r_zs[8 + ct]], [r_m2])
            tt(mT[:, ct, 0:ntok], m1[:, 0:ntok], m2[:, 0:ntok], ALU.add, [r_m1, r_m2], [r_mT], eng="pool")
        w_done(sla)
        w_done(slb)

        for n in range(2):
            wv, rw, sl = w_next("o%d" % n)
            for j in range(nsub):
                nt = nts[j]
                pb, rb = gbank()
                for kc in range(8):
                    mm(pb[:nt, :], mT[:, kc, j * 128:j * 128 + nt], wv[:, kc, :], kc == 0, kc == 7, [rw, r_mT], [rb])
                tt(Xb[:nt, j, n * 512:(n + 1) * 512], Xb[:nt, j, n * 512:(n + 1) * 512], pb[:nt, :], ALU.add, [rX[j], rb], [rX[j]])
            w_done(sl)
        norm_T(Xb, rX, gffn, r_gffn, 2, 2, ntok)
        hT = hTs[2]
        r_hT = r_hTs[2]
        rhT_all = r_hT[:nsub]

        ncol_f = 2 if mode == "pre" else ntok
        for gi in range(6):
            ntile = 4 if gi < 5 else 2
            if gi == 2 and prenorm is not None:
                prenorm()
            wvg, rwg, slg = w_next("g%d" % gi)
            if mode != "pre":
                wstate["consumed"] += 1
                wvu, rwu, slu = w_next("u%d" % gi)
                wstate["consumed"] -= 1
            for ct in range(ntile):
                ft = gi * 4 + ct
                pb, rb = gbank()
                if mode == "pre":
                    for kc in range(8):
                        mm(pb[:, 0:2], wvg[:, kc, ct * 128:(ct + 1) * 128], hT[:, kc, ntok - 2:ntok], kc == 0, kc == 7, [rwg] + rhT_all, [rb])
                    cp(cprev[:, ft, :], pb[:, 0:2], [rb], [r_cprev])
                    continue
                for kc in range(8):
                    mm(pb[:, 0:ntok], wvg[:, kc, ct * 128:(ct + 1) * 128], hT[:, kc, 0:ntok], kc == 0, kc == 7, [rwg] + rhT_all, [rb])
                for kc in range(8):
                    mm(pb[:, 256:256 + ntok], wvu[:, kc, ct * 128:(ct + 1) * 128], hT[:, kc, 0:ntok], kc == 0, kc == 7, [rwu] + rhT_all, [rb])
                bi = ft % 2
                a_ = aT[bi]
                cp(a_[:, 0:2], cprev[:, ft, :], [r_cprev], [r_aT[bi]], eng="pool")
                act(a_[:, 2:2 + ntok], pb[:, 0:ntok], AF.Copy, [rb], [r_aT[bi]])
                cp(cprev[:, ft, :], a_[:, ntok:ntok + 2], [r_aT[bi]], [r_cprev], eng="pool")
                act(c1[bi][:, 0:ntok], pb[:, 0:ntok], AF.Identity, [rb, r_cw, r_cb], [r_c1[bi]], scale=cw[:, ft, 2:3], bias=cb[:, ft:ft + 1])
                stt(c2[bi][:, 0:ntok], a_[:, 1:1 + ntok], cw[:, ft, 1:2], c1[bi][:, 0:ntok], ALU.mult, ALU.add, [r_aT[bi], r_c1[bi], r_cw], [r_c2[bi]])
                stt(c1[bi][:, 0:ntok], a_[:, 0:ntok], cw[:, ft, 0:1], c2[bi][:, 0:ntok], ALU.mult, ALU.add, [r_aT[bi], r_c2[bi], r_cw], [r_c1[bi]])
                act(c2[bi][:, 0:ntok], c1[bi][:, 0:ntok], AF.Gelu_apprx_tanh, [r_c1[bi]], [r_c2[bi]])
                tt(gT[:, ft, 0:ntok], c2[bi][:, 0:ntok], pb[:, 256:256 + ntok], ALU.mult, [r_c2[bi], rb], [r_gT[ft]])
            w_done(slg)
            if mode != "pre":
                w_done(slu)
        if mode == "pre":
            return

        for n in range(2):
            banks = [gbank() for _ in range(nsub)]
            for gk in range(3):
                wv, rw, sl = w_next("d%d_%d" % (n, gk))
                nk = 8 if gk < 2 else 6
                for kl in range(nk):
                    kc = gk * 8 + kl
                    for j in range(nsub):
                        nt = nts[j]
                        mm(banks[j][0][:nt, :], gT[:, kc, j * 128:j * 128 + nt], wv[:, kl, :], kc == 0, kc == NFT - 1, [rw, r_gT[kc]], [banks[j][1]])
                w_done(sl)
            for j in range(nsub):
                nt = nts[j]
                tt(Xb[:nt, j, n * 512:(n + 1) * 512], Xb[:nt, j, n * 512:(n + 1) * 512], banks[j][0][:nt, :], ALU.add, [rX[j], banks[j][1]], [rX[j]])
        for j in range(nsub):
            nt = nts[j]
            act(XN[:nt, j, :], Xb[:nt, j, :], AF.Square, [rX[j]], [r_XN[j], r_ss], accum_out=ss[:nt, 4 + j:5 + j])
            act(ss[:nt, 4 + j:5 + j], ss[:nt, 4 + j:5 + j], AF.Sqrt, [r_ss, r_eps], [r_ss], bias=epst[:nt, 0:1], scale=1.0 / D)
            recip(ss[:nt, 4 + j:5 + j], ss[:nt, 4 + j:5 + j], [r_ss], [r_ss])
            stt(Xb[:nt, j, :], Xb[:nt, j, :], ss[:nt, 4 + j:5 + j], gfin[:nt, :], ALU.mult, ALU.mult, [rX[j], r_ss, r_gfin], [rX[j]])
            dma("act", out_row[j * 128:j * 128 + nt, :], Xb[:nt, j, :], [rX[j]], ())

    cur = {}

    def kv_proj_only():
        ntok, gt0 = cur["ntok"], cur["gt0"]
        for ct in range(4):
            pb, rb = gbank()
            for kc in range(8):
                mm(pb[:, 0:ntok], W1v[2][:, kc, ct * 128:(ct + 1) * 128], cur["hT"][:, kc, 0:ntok], kc == 0, kc == 7, [r_wslot[2]] + cur["r_hT"], [rb])
            for j in range(2):
                s_ = (gt0 + j) % 6
                cp(kTr[:, ct, s_, :], pb[:, j * 128:(j + 1) * 128], [rb], [r_kT[s_]], eng="act")
        for j in range(2):
            s_ = (gt0 + j) % 6
            pb, rb = gbank()
            for kc in range(8):
                mm(pb[:, :], cur["hT"][:, kc, j * 128:(j + 1) * 128], W1v[3][:, kc, :], kc == 0, kc == 7, [r_wslot[3], cur["r_hT"][j]], [rb])
            cp(Vr[:, s_, :, 0:64], pb[:, :].rearrange("p (h d) -> p h d", h=8), [rb], [r_V[s_]], eng="act")
            cp(Vr[:, s_, :, 64:65], hvt[:, 0:1].unsqueeze(2).to_broadcast([128, 8, 1]), [r_hv], [r_V[s_]], eng="pool")


    xa = xin.ap()
    nscan = n_scan
    nmain = n_main
    plan = []
    for m in range(nscan):
        plan.append(("scan", xa[m * T:(m + 1) * T, :], T, dict(gt0=-5 + 2 * (m - (nscan - 2)) + 6, kv=m >= nscan - 2)))
    if do_pre:
        plan.append(("pre", xa[NPRE:NPRE + PRE_T, :], PRE_T, dict(gt0=5)))
    for m in range(nmain):
        fk = (okT_d, ov_d, (m - (nmain - 2)) * T) if (m >= nmain - 2 and not NOFK) else None
        r0 = NPRE + PRE_T + m * T
        plan.append(("main", xa[r0:r0 + T, :], T, dict(gt0=2 * m + 6, out_row=y_d.ap()[m * T:(m + 1) * T, :], final_kv=fk)))
    if do_sample:
        plan.append(("sample", xs_d.ap(), 16, dict(gt0=0, out_row=ys_d.ap(), final_kv=(okTs_d, ovs_d, 0))))
    if plan:
        load_x(plan[0][1], plan[0][2], 0)
    if not do_conv:
        conv_jobs.clear()
    for i, (mode, xsrc, ntok, kw) in enumerate(plan):
        xb = i % 2
        nxt = None
        pren = None
        if i + 1 < len(plan):
            nxt = (lambda p=plan[i + 1], b=(i + 1) % 2: load_x(p[1], p[2], b))
            if mode == "main":
                pren = (lambda p=plan[i + 1], b=(i + 1) % 2: norm_T(X[b], r_X[b], gmix, r_gmix, 0, b, p[2]))
        normed = i > 0 and plan[i - 1][0] == "main"
        if mode == "scan":
            def front(k):
                md, xs_, nt_, kw_ = plan[k]
                cur["ntok"] = nt_
                cur["gt0"] = kw_["gt0"]
                emit_conv(2)
                nx = (lambda p=plan[k + 1], b=(k + 1) % 2: load_x(p[1], p[2], b)) if k + 1 < len(plan) else None
                macro_tile("scan", nt_, k % 2, hsel=k % 2, after_norm=nx, part="front", bsel=k % 2, **kw_)
            if i == 0:
                front(0)
            back_ops = S.capture(lambda: macro_tile("scan", ntok, xb, hsel=xb, part="back", bsel=xb, **kw))
            front_ops = S.capture(lambda: front(i + 1)) if i + 1 < nscan else []
            S.emit_merged(back_ops, front_ops)
            continue
        if i == nscan:
            emit_conv(len(conv_jobs))
            for s_ in range(NSLOT):
                w_issue(s_)
        if mode == "sample":
            dma("act", oS_d.ap().rearrange("h k v -> k h v"), Sst[:], [r_S], ())
            dma("act", oconv_d.ap(), cprev[:], [r_cprev], ())
            dma("pool", kTr[:, :, 0:4, :], ckT_d.ap().rearrange("p c (t k) -> p c t k", t=4), (), r_kT[0:4])
            for t_ in range(4):
                dma("pool", Vr[:, t_, :, 0:64], cv_d.ap()[:, t_ * 128:(t_ + 1) * 128, :].rearrange("h p d -> p h d"), (), [r_V[t_]])
                cp(Vr[:, t_, :, 64:65], onesT[:, 0:8].unsqueeze(2), [r_ones], [r_V[t_]], eng="pool")
            dma("sp", Sst[:], s0_d.ap().rearrange("h k v -> k h v"), (), [r_S])
            dma("sp", cprev[:], cprev_d.ap(), (), [r_cprev])
        macro_tile(mode, ntok, xb, hsel=xb, normed=normed, after_norm=nxt, prenorm=pren, **kw)
    emit_conv(len(conv_jobs))
    if not do_sample:
        dma("act", oS_d.ap().rearrange("h k v -> k h v"), Sst[:], [r_S], ())
        dma("act", oconv_d.ap(), cprev[:], [r_cprev], ())
    else:
        dma("act", oSs_d.ap().rearrange("h k v -> k h v"), Sst[:], [r_S], ())
        dma("act", oconvs_d.ap(), cprev[:], [r_cprev], ())
    for nm, getter in dumps:
        ap_, res_ = getter(locals())
        d_ = nc.dram_tensor("dbg_" + nm, list(ap_.shape), ap_.dtype if hasattr(ap_, "dtype") else F32, kind="ExternalOutput")
        dma("pool", d_.ap(), ap_, res_, ())

    S.finalize(st)
    with nc.Block() as block:
        S.run(block)
    st.close()
    return nc


_NC_CACHE = {}


def kernel(x_prompt, x_sample, cache_attn_k, cache_attn_v, state_hgrn, state_ffn_conv,
           norm_mix_g, w_in, rel_bias, hgrn_lb_logits, hgrn_norm_g, w_branch_a, w_branch_b, w_out,
           norm_ffn_g, w_ffn_gate, w_ffn_up, ffn_conv_w, ffn_conv_b, w_ffn_down, norm_final_g):
    f32 = np.float32
    A = lambda a: np.ascontiguousarray(np.asarray(a, dtype=f32))
    x_prompt = A(x_prompt)
    if "nc" not in _NC_CACHE:
        _NC_CACHE["nc"] = build_program()
    nc = _NC_CACHE["nc"]
    s_idx = np.arange(128)
    mask = ((s_idx[:, None] // 64 == s_idx[None, :] // 64) & (s_idx[:, None] <= s_idx[None, :])).astype(f32)
    shared = {
        "w_in": A(w_in[0]), "w_a": A(w_branch_a[0]), "w_b": A(w_branch_b[0]), "w_o": A(w_out[0]),
        "w_g": A(w_ffn_gate[0]), "w_u": A(w_ffn_up[0]), "w_d": A(w_ffn_down[0]),
        "gmix": A(np.asarray(norm_mix_g[0]).reshape(8, 128).T), "gffn": A(np.asarray(norm_ffn_g[0]).reshape(8, 128).T),
        "gfin": A(norm_final_g), "rel": A(rel_bias[0]),
        "lbl": A(np.asarray(hgrn_lb_logits).reshape(2, 4, 128).transpose(2, 0, 1)),
        "gn": A(hgrn_norm_g[0]),
        "cw": A(np.asarray(ffn_conv_w[0]).reshape(3, NFT, 128).transpose(2, 1, 0)),
        "cb": A(np.asarray(ffn_conv_b[0]).reshape(NFT, 128).T),
        "ident": np.eye(128, dtype=f32), "mask": mask, "jmat": np.ascontiguousarray(np.eye(128, dtype=f32)[::-1]),
    }
    in_maps = []
    for c in range(8):
        b, j = divmod(c, 4)
        s = j * SEG
        lo = s - PRE_T - NPRE
        xin = np.zeros((NTOK_IN, D), f32)
        a0 = max(lo, 0)
        xin[a0 - lo:] = x_prompt[b, a0:s + SEG]
        m = dict(shared)
        m["xin"] = xin
        m["hv"] = np.full((128, 1), 1.0 if j > 0 else 0.0, f32)
        m["xs"] = A(x_sample[c])
        m["ckT"] = A(np.asarray(cache_attn_k[0, c]).transpose(0, 2, 1).reshape(4, 128, 512).transpose(1, 0, 2))
        m["cv"] = A(cache_attn_v[0, c])
        m["s0"] = A(state_hgrn[0, c])
        m["cprev"] = A(np.asarray(state_ffn_conv[0, c]).reshape(2, NFT, 128).transpose(2, 1, 0))
        in_maps.append(m)
    res = run_bass_kernel_spmd(nc, in_maps, core_ids=list(range(8)))
    R_ = res.results
    B = 2
    y_prompt = np.stack([np.concatenate([R_[b * 4 + j]["y"] for j in range(4)], axis=0) for b in range(B)])
    y_sample = np.stack([R_[c]["ys"] for c in range(8)])

    def kT_to_rows(a):
        n = a.shape[-1]
        return a.reshape(8, 64, n).transpose(0, 2, 1)

    def v_to_rows(a):
        n = a.shape[0]
        return a.reshape(n, 8, 64).transpose(1, 0, 2)

    def conv_rows(a):
        return a.transpose(2, 1, 0).reshape(2, DFF)

    last = [3, 7]
    new_k_p = np.stack([kT_to_rows(R_[c]["okT"]) for c in last])[None]
    new_v_p = np.stack([v_to_rows(R_[c]["ov"]) for c in last])[None]
    hg_p = np.stack([R_[c]["oS"] for c in last])[None]
    cv_p = np.stack([conv_rows(R_[c]["oconv"]) for c in last])[None]
    new_k_s = np.stack([kT_to_rows(R_[c]["okTs"]) for c in range(8)])[None]
    new_v_s = np.stack([v_to_rows(R_[c]["ovs"]) for c in range(8)])[None]
    hg_s = np.stack([R_[c]["oSs"] for c in range(8)])[None]
    cv_s = np.stack([conv_rows(R_[c]["oconvs"]) for c in range(8)])[None]
    outs = (y_prompt, y_sample, new_k_p, new_v_p, hg_p, cv_p, new_k_s, new_v_s, hg_s, cv_s)
    return tuple(np.ascontiguousarray(o, dtype=f32) for o in outs)
```

```python
import numpy as np
from contextlib import ExitStack
import concourse.bass as bass
import concourse.mybir as mybir
from concourse.bass import AP
from concourse.bass_utils import run_bass_kernel_spmd

F32 = mybir.dt.float32
BF16 = mybir.dt.bfloat16
AF = mybir.ActivationFunctionType
ALU = mybir.AluOpType
AX = mybir.AxisListType

D = 1024
DFF = 2816
NFT = 22
SEG = 4096
NPRE = 12288
PRE_T = 128
NTOK_IN = NPRE + PRE_T + SEG
T = 256
EPS = 1e-6
NSLOT = 6
STOP = 99
STOPMODE = 'main'
NOFK = False
SLOT_E = 4096


class Res:
    __slots__ = ("name", "w", "r", "excl")

    def __init__(self, name, excl=False):
        self.name = name
        self.w = None
        self.r = {}
        self.excl = excl


class Op:
    __slots__ = ("eng", "fn", "deps", "is_dma", "needs_inc", "sem", "val", "idx")


class Sched:
    ENGS = ("pe", "act", "dve", "pool", "sp")

    def __init__(self, nc, n_dma_sems=8):
        self.nc = nc
        self.ops = []
        self.n_dma_sems = n_dma_sems
        self.cap = None

    def capture(self, fn):
        assert self.cap is None
        self.cap = []
        try:
            fn()
            return self.cap
        finally:
            self.cap = None

    DUR = {"pe": 0.15, "act": 0.75, "dve": 0.85, "pool": 1.1, "sp": 0.3}

    def emit_merged(self, *lists):
        eng_free = {}
        ready = {}
        rdone = {}
        LAT = 0.25

        def start_of(op):
            eng, fn, reads, writes, dma, cost = op
            t = eng_free.get(eng, 0.0)
            for r in reads:
                t = max(t, ready.get(id(r), 0.0) + LAT)
            for w in writes:
                t = max(t, ready.get(id(w), 0.0) + LAT, rdone.get(id(w), 0.0) + LAT)
            return t

        def commit(op, t):
            eng, fn, reads, writes, dma, cost = op
            d = 2.5 if dma else (cost if cost is not None else self.DUR[eng])
            eng_free[eng] = t + (0.1 if dma else d)
            for r in reads:
                rdone[id(r)] = max(rdone.get(id(r), 0.0), t + d)
            for w in writes:
                ready[id(w)] = t + d
            self.emit(eng, fn, reads, writes, dma)

        pos = [0] * len(lists)
        while True:
            best, bt = -1, None
            for k, l in enumerate(lists):
                if pos[k] < len(l):
                    t = start_of(l[pos[k]])
                    if bt is None or t < bt:
                        best, bt = k, t
            if best < 0:
                break
            commit(lists[best][pos[best]], bt)
            pos[best] += 1

    def emit(self, eng, fn, reads=(), writes=(), dma=False, cost=None):
        if self.cap is not None:
            self.cap.append((eng, fn, tuple(reads), tuple(writes), dma, cost))
            return None
        op = Op()
        op.eng = eng
        op.fn = fn
        op.is_dma = dma
        op.needs_inc = dma
        op.sem = None
        op.val = 0
        op.idx = len(self.ops)
        deps = {}
        xr = [r for r in reads if r.excl]
        if xr:
            reads = [r for r in reads if not r.excl]
            writes = list(writes) + [r for r in xr if r not in writes]
        for r in reads:
            if r.w is not None:
                deps[r.w.idx] = r.w
        for w in writes:
            if w.w is not None:
                deps[w.w.idx] = w.w
            for o in w.r.values():
                deps[o.idx] = o
        op.deps = list(deps.values())
        for r in reads:
            key = (eng, op.idx) if dma else (eng, -1)
            r.r[key] = op
        for w in writes:
            w.w = op
            w.r = {}
        self.ops.append(op)
        return op

    def finalize(self, stack):
        nc = self.nc
        for op in self.ops:
            for d in op.deps:
                if d.is_dma or d.eng != op.eng or op.eng != "pe" or op.is_dma:
                    d.needs_inc = True
        csem = {e: stack.enter_context(nc.semaphore("cs_" + e)) for e in ("pe", "act", "dve", "pool")}
        dsem = {e: [stack.enter_context(nc.semaphore("ds_%s%d" % (e, i))) for i in range(self.n_dma_sems)]
                for e in ("sp", "pool", "act")}
        ccount = {e: 0 for e in csem}
        dstate = {e: [None] * self.n_dma_sems for e in dsem}
        duse = {e: [0] * self.n_dma_sems for e in dsem}
        drr = {e: 0 for e in dsem}
        waited = {e: {} for e in self.ENGS}
        streams = {e: [] for e in self.ENGS}
        for op in self.ops:
            waits = []
            e = op.eng
            extra = []
            if op.is_dma:
                k = drr[e] % self.n_dma_sems
                drr[e] += 1
                prev = dstate[e][k]
                if prev is not None:
                    extra.append(prev)
                duse[e][k] += 1
                op.sem = dsem[e][k]
                op.val = 16 * duse[e][k]
                dstate[e][k] = op
            elif op.needs_inc:
                ccount[e] += 1
                op.sem = csem[e]
                op.val = ccount[e]
            for d in op.deps + extra:
                if (not d.is_dma) and d.eng == e and e == "pe" and not op.is_dma:
                    continue
                key = id(d.sem)
                if waited[e].get(key, 0) >= d.val:
                    continue
                waited[e][key] = d.val
                waits.append((d.sem, d.val))
            streams[e].append((waits, op))
        fin = []
        for e in dsem:
            for k in range(self.n_dma_sems):
                if duse[e][k] and waited["sp"].get(id(dsem[e][k]), 0) < 16 * duse[e][k]:
                    fin.append((dsem[e][k], 16 * duse[e][k]))
        self.streams = streams
        self.fin = fin

    def run(self, block):
        streams = self.streams
        fin = self.fin

        def body(name):
            def f(eng):
                for waits, op in streams[name]:
                    for s, v in waits:
                        eng.wait_ge(s, v)
                    inst = op.fn(eng)
                    if op.sem is not None:
                        inst.then_inc(op.sem, 16 if op.is_dma else 1)
                if name == "sp":
                    for s, v in fin:
                        eng.wait_ge(s, v)
            return f
        block.tensor(body("pe"))
        block.scalar(body("act"))
        block.vector(body("dve"))
        block.gpsimd(body("pool"))
        block.sync(body("sp"))


def build_program(n_scan=NPRE // T, n_main=SEG // T, do_pre=True, do_sample=True, dumps=(), do_conv=True, init_stop=99):
    NPRE = n_scan * T
    NTOK_IN = NPRE + PRE_T + n_main * T
    nc = bass.Bass("TRN2", target_bir_lowering=False)
    st = ExitStack()
    S = Sched(nc)

    def din(name, shape):
        return nc.dram_tensor(name, list(shape), F32, kind="ExternalInput")

    def dout(name, shape):
        return nc.dram_tensor(name, list(shape), F32, kind="ExternalOutput")

    xin = din("xin", [NTOK_IN, D])
    hv_d = din("hv", [128, 1])
    xs_d = din("xs", [16, D])
    ckT_d = din("ckT", [128, 4, 512])
    cv_d = din("cv", [8, 512, 64])
    s0_d = din("s0", [4, 128, 128])
    cprev_d = din("cprev", [128, NFT, 2])
    w_in_d = din("w_in", [D, 5632])
    w_a_d = din("w_a", [512, D])
    w_b_d = din("w_b", [512, D])
    w_o_d = din("w_o", [D, D])
    w_g_d = din("w_g", [D, DFF])
    w_u_d = din("w_u", [D, DFF])
    w_d_d = din("w_d", [DFF, D])
    gmix_d = din("gmix", [128, 8])
    gffn_d = din("gffn", [128, 8])
    gfin_d = din("gfin", [D])
    rel_d = din("rel", [8, 192])
    lbl_d = din("lbl", [128, 2, 4])
    gn_d = din("gn", [128])
    cw_d = din("cw", [128, NFT, 3])
    cb_d = din("cb", [128, NFT])
    ident_d = din("ident", [128, 128])
    mask_d = din("mask", [128, 128])
    jmat_d = din("jmat", [128, 128])

    y_d = dout("y", [SEG, D])
    ys_d = dout("ys", [16, D])
    okT_d = dout("okT", [4, 128, 512])
    ov_d = dout("ov", [512, 512])
    oS_d = dout("oS", [4, 128, 128])
    oconv_d = dout("oconv", [128, NFT, 2])
    okTs_d = dout("okTs", [4, 128, 16])
    ovs_d = dout("ovs", [16, 512])
    oSs_d = dout("oSs", [4, 128, 128])
    oconvs_d = dout("oconvs", [128, NFT, 2])

    def scratch(name, shape, dt=BF16):
        return nc.dram_tensor(name, list(shape), dt, kind="Internal")

    wb_in = scratch("wb_in", [D, 5632])
    wb_a = scratch("wb_a", [512, D])
    wb_b = scratch("wb_b", [512, D])
    wb_o = scratch("wb_o", [D, D])
    wb_g = scratch("wb_g", [D, DFF])
    wb_u = scratch("wb_u", [D, DFF])
    wb_d = scratch("wb_d", [DFF, D])
    ext_d = scratch("ext_d", [8, 768], F32)

    def sb(name, shape, dt=F32):
        return st.enter_context(nc.sbuf_tensor(name, list(shape), dt))

    def R(name, excl=False):
        return Res(name, excl)

    def fsz(ap):
        n = 1
        for d_ in list(ap.shape)[1:]:
            n *= int(d_)
        return n

    def ecost(eng, ap):
        n = fsz(ap)
        return {"act": 0.25 + n / 1000.0, "dve": 0.1 + n / 850.0, "pool": 0.2 + n / 480.0}[eng]

    def act(out, in_, func, reads, writes, **kw):
        S.emit("act", lambda e: e.activation(out=out, in_=in_, func=func, **kw), reads, writes, cost=ecost("act", out))

    def tt(out, in0, in1, op, reads, writes, eng="dve"):
        S.emit(eng, lambda e: e.tensor_tensor(out=out, in0=in0, in1=in1, op=op), reads, writes, cost=ecost(eng, out))

    def ts(out, in0, s1, s2, op0, op1, reads, writes, eng="dve"):
        if op1 is None:
            S.emit(eng, lambda e: e.tensor_scalar(out=out, in0=in0, scalar1=s1, scalar2=None, op0=op0), reads, writes, cost=ecost(eng, out))
        else:
            S.emit(eng, lambda e: e.tensor_scalar(out=out, in0=in0, scalar1=s1, scalar2=s2, op0=op0, op1=op1), reads, writes, cost=ecost(eng, out))

    def stt(out, in0, scalar, in1, op0, op1, reads, writes):
        S.emit("dve", lambda e: e.scalar_tensor_tensor(out=out, in0=in0, scalar=scalar, in1=in1, op0=op0, op1=op1), reads, writes, cost=ecost("dve", out))

    def cp(out, in_, reads, writes, eng="dve"):
        if eng == "act":
            act(out, in_, AF.Copy, reads, writes)
        else:
            S.emit(eng, lambda e: e.tensor_copy(out=out, in_=in_), reads, writes, cost=ecost(eng, out))

    def recip(out, in_, reads, writes):
        S.emit("dve", lambda e: e.reciprocal(out=out, in_=in_), reads, writes)

    def mset(ap, val, writes, eng="pool"):
        S.emit(eng, lambda e: e.memset(ap, val), (), writes)

    def mm(out, lhsT, rhs, start, stop, reads, writes):
        S.emit("pe", lambda e: e.matmul(out, lhsT=lhsT, rhs=rhs, start=start, stop=stop), reads, writes, cost=0.05 + fsz(rhs) / 2000.0)

    def dma(eng, out, in_, reads, writes):
        S.emit(eng, lambda e: e.dma_start(out=out, in_=in_), reads, writes, dma=True)

    identb = sb("identb", [128, 128], BF16); r_ident = R("ident")
    maskb = sb("maskb", [128, 128], BF16); r_mask = R("mask")
    jb = sb("jb", [128, 128], BF16); r_j = R("j")
    epst = sb("epst", [128, 1]); r_eps = R("eps")
    onesT = sb("onesT", [128, 8]); r_ones = R("ones")
    hvt = sb("hvt", [128, 1]); r_hv = R("hv")
    gmix = sb("gmix_s", [128, 8]); r_gmix = R("gmix")
    gffn = sb("gffn_s", [128, 8]); r_gffn = R("gffn")
    gfin = sb("gfin_s", [128, D]); r_gfin = R("gfin")
    gnrep = sb("gnrep", [128, 4, 128]); r_gn = R("gn")
    cw = sb("cw_s", [128, NFT, 3]); r_cw = R("cw")
    cb = sb("cb_s", [128, NFT]); r_cb = R("cb")
    lbl = sb("lbl_s", [128, 2, 4]); r_lbl = R("lbl")
    lb = sb("lb_s", [128, 4]); oml = sb("oml_s", [128, 4]); r_lb = R("lb")
    ET = sb("ET", [128, 5, 8, 128], BF16); r_ET = R("ET")

    dma("pool", identb[:], ident_d.ap(), (), [r_ident])
    dma("pool", maskb[:], mask_d.ap(), (), [r_mask])
    dma("pool", jb[:], jmat_d.ap(), (), [r_j])
    mset(epst[:], EPS, [r_eps])
    mset(onesT[:], 1.0, [r_ones])
    dma("sp", hvt[:], hv_d.ap(), (), [r_hv])
    dma("sp", gmix[:], gmix_d.ap(), (), [r_gmix])
    dma("sp", gffn[:], gffn_d.ap(), (), [r_gffn])
    dma("sp", gfin[:], AP(gfin_d, 0, [[0, 128], [1, D]]), (), [r_gfin])
    dma("sp", gnrep[:], AP(gn_d, 0, [[0, 128], [0, 4], [1, 128]]), (), [r_gn])
    dma("sp", cw[:], cw_d.ap(), (), [r_cw])
    dma("sp", cb[:], cb_d.ap(), (), [r_cb])
    dma("sp", lbl[:], lbl_d.ap(), (), [r_lbl])

    if init_stop <= 1:
        S.finalize(st)
        with nc.Block() as block:
            S.run(block)
        st.close()
        return nc
    ps_att = st.enter_context(nc.psum_tensor("ps_att", [128, 1536], F32))
    r_att = [R("att0", True), R("att1", True)]
    r_attB = R("attB", True)
    NGEN = 2
    ps_gen = [st.enter_context(nc.psum_tensor("ps_g%d" % i, [128, 512], F32)) for i in range(NGEN)]
    r_gen = [R("g%d" % i, True) for i in range(NGEN)]
    ps_o = st.enter_context(nc.psum_tensor("ps_o", [128, 512], F32))
    r_o = R("ps_o", True)
    ps_tr = [st.enter_context(nc.psum_tensor("ps_t%d" % i, [128, 1024], BF16)) for i in range(2)]
    r_tr = [R("t%d" % i, True) for i in range(2)]
    cnt = {"g": 0, "t": 0, "a": 0, "p": 0}

    gpool_full = [(ps_gen[i], r_gen[i]) for i in range(NGEN)]
    gpool_front = gpool_full + [(ps_o, r_o)]
    gpool_back = [(ps_att[:, 0:512], r_att[0]), (ps_att[:, 512:1024], r_att[1]), (ps_att[:, 1024:1536], r_attB)]
    gstate = {"pool": gpool_full, "tr": (0, 1)}

    def gbank():
        pool = gstate["pool"]
        i = cnt["g"] % len(pool)
        cnt["g"] += 1
        return pool[i]

    def tbank():
        sel = gstate["tr"]
        i = sel[cnt["t"] % len(sel)]
        cnt["t"] += 1
        return ps_tr[i], r_tr[i]

    lbt = sb("lbt", [128, 4])
    tt(lbt[:], lbl[:, 1, :], lbl[:, 0, :], ALU.subtract, [r_lbl], [r_lb])
    act(lbt[:], lbt[:], AF.Exp, [r_lb], [r_lb])
    ts(lbt[:], lbt[:], 1.0, None, ALU.add, None, [r_lb], [r_lb])
    recip(lb[:], lbt[:], [r_lb], [r_lb])
    ts(oml[:], lb[:], -1.0, 1.0, ALU.mult, ALU.add, [r_lb], [r_lb])

    if init_stop <= 2:
        S.finalize(st)
        with nc.Block() as block:
            S.run(block)
        st.close()
        return nc
    if init_stop <= 3:
        S.finalize(st)
        with nc.Block() as block:
            S.run(block)
        st.close()
        return nc
    r_wb = {}
    conv_jobs = []
    for name, src, dst, rows in (("in", w_in_d, wb_in, D), ("a", w_a_d, wb_a, 512), ("b", w_b_d, wb_b, 512),
                                 ("o", w_o_d, wb_o, D), ("g", w_g_d, wb_g, D), ("u", w_u_d, wb_u, D),
                                 ("d", w_d_d, wb_d, DFF)):
        r_wb[name] = R("wb_" + name)
        for r0 in range(0, rows, 128):
            conv_jobs.append((name, dst.ap()[r0:r0 + 128, :], src.ap()[r0:r0 + 128, :]))

    def emit_conv(n):
        for _ in range(n):
            if conv_jobs:
                name, d_, s_ = conv_jobs.pop(0)
                dma("pool", d_, s_, (), [r_wb[name]])

    wslot = [sb("wslot%d" % i, [128, SLOT_E], BF16) for i in range(NSLOT)]
    r_wslot = [R("wslot%d" % i) for i in range(NSLOT)]
    W1v = []
    for i, c0 in enumerate((2048, 2560, 512, 1024)):
        v_ = wslot[i][:, :].rearrange("p (k n) -> p k n", k=8)
        dma("pool", v_, w_in_d.ap()[:, c0:c0 + 512].rearrange("(k p) n -> p k n", p=128), (), [r_wslot[i]])
        W1v.append(v_)

    def wgroups(mode):
        g = []
        for i in (3, 4, 5, 6, 0, 1, 2, 7, 8, 9, 10):
            g.append(("in%d" % i, "in", wb_in.ap()[:, 512 * i:512 * i + 512].rearrange("(k p) n -> p k n", p=128), 8, 512))
        g.append(("a", "a", wb_a.ap().rearrange("(k p) n -> p k n", p=128), 4, 1024))
        g.append(("b", "b", wb_b.ap().rearrange("(k p) n -> p k n", p=128), 4, 1024))
        for n in range(2):
            g.append(("o%d" % n, "o", wb_o.ap()[:, 512 * n:512 * n + 512].rearrange("(k p) n -> p k n", p=128), 8, 512))
        for gi in range(6):
            nc_ = 512 if gi < 5 else 256
            g.append(("g%d" % gi, "g", wb_g.ap()[:, 512 * gi:512 * gi + nc_].rearrange("(k p) n -> p k n", p=128), 8, nc_))
            if mode != "pre":
                g.append(("u%d" % gi, "u", wb_u.ap()[:, 512 * gi:512 * gi + nc_].rearrange("(k p) n -> p k n", p=128), 8, nc_))
        if mode != "pre":
            for n in range(2):
                for gk in range(3):
                    nk = 8 if gk < 2 else 6
                    g.append(("d%d_%d" % (n, gk), "d",
                              wb_d.ap()[1024 * gk:1024 * gk + 128 * nk, 512 * n:512 * n + 512].rearrange("(k p) n -> p k n", p=128), nk, 512))
        return g

    wq = []
    tiles_plan = ([("pre", 0)] if do_pre else []) + [("main", m) for m in range(n_main)] + ([("sample", 0)] if do_sample else [])
    for mode, _ in tiles_plan:
        wq.extend(wgroups(mode))
    wstate = {"issued": 0, "consumed": 0, "slot": {}}

    def w_issue(slot):
        i = wstate["issued"]
        if i >= len(wq):
            return
        key, wname, src, nk, ncol = wq[i]
        view = wslot[slot][:, 0:nk * ncol].rearrange("p (k n) -> p k n", k=nk)
        dma("sp", view, src, [r_wb[wname]], [r_wslot[slot]])
        wstate["slot"][i] = slot
        wstate["issued"] += 1

    def w_next(key):
        i = wstate["consumed"]
        assert wq[i][0] == key, (wq[i][0], key)
        slot = wstate["slot"][i]
        _, _, _, nk, ncol = wq[i]
        view = wslot[slot][:, 0:nk * ncol].rearrange("p (k n) -> p k n", k=nk)
        return view, r_wslot[slot], slot

    def w_done(slot):
        wstate["consumed"] += 1
        w_issue(slot)

    if init_stop <= 4:
        S.finalize(st)
        with nc.Block() as block:
            S.run(block)
        st.close()
        return nc
    X = [sb("X%d" % i, [128, 2, D]) for i in range(2)]
    r_X = [[R("X%d_%d" % (i, j)) for j in range(2)] for i in range(2)]
    XN = sb("XN", [128, 2, D], BF16); r_XN = [R("XN0"), R("XN1")]
    ss = sb("ss", [128, 8]); r_ss = R("ss")
    hTs = [sb("hTa", [128, 8, T], BF16), sb("hTb", [128, 8, T], BF16), sb("h2T", [128, 8, T], BF16)]
    r_hTs = [[R("hTa0"), R("hTa1")], [R("hTb0"), R("hTb1")], [R("h2T0"), R("h2T1")]]
    qT = sb("qT", [128, 4, T], BF16); r_qT = R("qT")
    kTr = sb("kTr", [128, 4, 6, 128], BF16); r_kT = [R("kT%d" % i) for i in range(6)]
    Vr = sb("Vr", [128, 6, 8, 65], BF16); r_V = [R("V%d" % i) for i in range(6)]
    Pb = [sb("Pb%d" % i, [128, 5, 128], BF16) for i in range(3)]; r_Pb = [R("Pb%d" % i) for i in range(3)]
    oa = sb("oa", [128, 2, 512], BF16); r_oa = [R("oa0"), R("oa1")]
    oaT = sb("oaT", [128, 4, T], BF16); r_oaT = [R("oaT0"), R("oaT1")]
    rec = sb("rec", [128, 8]); r_rec = R("rec")
    sg = sb("sg", [128, 4, T]); r_sg = [R("sg%d" % i) for i in range(4)]
    siluq = sb("siluq", [128, 4, T]); r_sq = [R("siluq%d" % i) for i in range(4)]
    gF = sb("gF", [128, 4 * T]); r_gF = R("gF")
    gL = sb("gL", [128, 4 * T]); r_gL = R("gL")
    gB = sb("gB", [128, 4 * T]); r_gB = R("gB")
    gK = sb("gK", [128, 4 * T]); r_gK = R("gK")
    gE = sb("gE", [128, 4 * T]); r_gE = R("gE")
    dec = sb("dec", [128, 3, 4, 4]); r_dec = R("dec")
    Zq = sb("Zq", [128, 4, 4, 128], BF16); r_Zq = [R("Zq%d" % i) for i in range(4)]
    ktT = sb("ktT", [128, 4, T], BF16); r_ktT = [R("ktT%d" % i) for i in range(4)]
    khT = sb("khT", [128, 4, T], BF16); r_khT = [R("khT%d" % i) for i in range(4)]
    khtm = sb("khtm", [128, 2, 512], BF16); r_khtm = [R("khtm0"), R("khtm1")]
    vh = sb("vh", [128, 2, 512], BF16); r_vh = [R("vh0"), R("vh1")]
    sgb = sb("sgb", [128, 2, 512]); r_sgb = [R("sgb0"), R("sgb1")]
    Sst = sb("Sst", [128, 4, 128]); r_S = R("S")
    Sp = sb("Sp", [128, 2, 4, 128], BF16); r_Sp = [R("Sp0"), R("Sp1")]
    AT = sb("AT", [128, 4, 128], BF16); r_AT = R("AT")
    sqb = sb("sqb", [128, 512]); r_sqb = R("sqb")
    ssq = sb("ssq", [128, 4]); r_ssq = R("ssq")
    t1 = sb("t1", [128, 512]); r_t1 = R("t1")
    gG = sb("gG", [128, 512]); r_gG = R("gG")
    ob = sb("ob", [128, 512], BF16); r_ob = R("ob")
    obT = sb("obT", [128, 4, T], BF16); r_obT = [R("obT0"), R("obT1")]
    zs = sb("zs", [128, 16, T], BF16); r_zs = [R("zs%d" % i) for i in range(16)]
    mT = sb("mT", [128, 8, T], BF16); r_mT = R("mT")
    aT = [sb("aT%d" % i, [128, T + 2]) for i in range(2)]; r_aT = [R("aT0"), R("aT1")]
    c1 = [sb("c1_%d" % i, [128, T]) for i in range(2)]; r_c1 = [R("c1_0"), R("c1_1")]
    c2 = [sb("c2_%d" % i, [128, T]) for i in range(2)]; r_c2 = [R("c2_0"), R("c2_1")]
    m1, r_m1, m2, r_m2 = c1[0], r_c1[0], c2[0], r_c2[0]
    gT = sb("gT", [128, NFT, T], BF16); r_gT = [R("gT%d" % i) for i in range(NFT)]
    cprev = sb("cprev_s", [128, NFT, 2]); r_cprev = R("cprev")

    ext_ = siluq[0:8].rearrange("p h t -> p (h t)")[:, 0:768]; r_ext = r_sq; r_extd = R("extd")
    dma("sp", ext_[:, 64:256], rel_d.ap(), (), r_ext)
    act(ext_[:, 0:64], ext_[:, 64:65].to_broadcast([8, 64]), AF.Identity, r_ext, r_ext)
    act(ext_[:, 256:768], ext_[:, 255:256].to_broadcast([8, 512]), AF.Identity, r_ext, r_ext)
    act(ext_[:, :], ext_[:, :], AF.Exp, r_ext, r_ext)
    dma("sp", ext_d.ap(), ext_[:, :], r_ext, [r_extd])
    hk = sg[:].rearrange("p h t -> p (h t)").rearrange("p (a b) -> p a b", a=8); r_hk = r_sg
    hkb = ktT[:].rearrange("p h t -> p (h t)").rearrange("p (a b) -> p a b", a=8); r_hkb = r_ktT
    for kt in range(5):
        dma("sp", hk, AP(ext_d, 512 - 128 * kt, [[1, 128], [768, 8], [1, 128]]), [r_extd], r_hk)
        cp(hkb, hk, r_hk, r_hkb)
        for n in range(2):
            pb, rb = gbank()
            mm(pb[:, :], jb[:], hkb[:, 4 * n:4 * n + 4, :].rearrange("p h q -> p (h q)"), True, True, [r_j] + r_hkb, [rb])
            cp(ET[:, kt, 4 * n:4 * n + 4, :].rearrange("p h q -> p (h q)"), pb[:, :], [rb], [r_ET])
    mset(ET[0:64, 0, :, 64:128], 0.0, [r_ET])
    mset(ET[64:128, 4, :, 0:64], 0.0, [r_ET])

    kst = gT[:, 0:8, :].rearrange("p a b -> p (a b)").bitcast(F32).rearrange("p (h t) -> p h t", h=4)
    vst = gT[:, 8:16, :].rearrange("p a b -> p (a b)").bitcast(F32).rearrange("p (j c) -> p j c", j=2)
    rl_kst = r_gT[0:8]
    rl_vst = [r_gT[8:12], r_gT[12:16]]
    sg2 = gT[:, 0:8, :].rearrange("p a b -> p (a b)").bitcast(F32).rearrange("p (h t) -> p h t", h=4)
    r_sg2 = [[r_gT[2 * i], r_gT[2 * i + 1]] for i in range(4)]
    vh2 = gT[:, 8:12, :].rearrange("p a b -> p (a b)").rearrange("p (j c) -> p j c", j=2)
    r_vh2 = [[r_gT[8], r_gT[9]], [r_gT[10], r_gT[11]]]
    sgs = [(sg, [[r] for r in r_sg]), (sg2, r_sg2)]
    vhs = [(vh, [[r] for r in r_vh]), (vh2, r_vh2)]
    mset(Zq[:], 0.0, r_Zq)
    mset(Sst[:], 0.0, [r_S])
    mset(cprev[:], 0.0, [r_cprev])

    if init_stop <= 5:
        S.finalize(st)
        with nc.Block() as block:
            S.run(block)
        st.close()
        return nc
    def load_x(xsrc, ntok, xb):
        nsub = (ntok + 127) // 128
        for j in range(nsub):
            nt = min(128, ntok - 128 * j)
            dma("sp", X[xb][:nt, j, :], xsrc[j * 128:j * 128 + nt, :], (), [r_X[xb][j]])

    def norm_T(src_tile, rsrc, gcol, rg, col0, hsel, ntok):
        nsub = (ntok + 127) // 128
        dst, rdst = hTs[hsel], r_hTs[hsel]
        for j in range(nsub):
            nt = min(128, ntok - 128 * j)
            act(XN[:nt, j, :], src_tile[:nt, j, :], AF.Square, [rsrc[j]], [r_XN[j], r_ss], accum_out=ss[:nt, col0 + j:col0 + j + 1])
            act(ss[:nt, col0 + j:col0 + j + 1], ss[:nt, col0 + j:col0 + j + 1], AF.Sqrt, [r_ss, r_eps], [r_ss],
                bias=epst[:nt, 0:1], scale=1.0 / D)
            recip(ss[:nt, col0 + j:col0 + j + 1], ss[:nt, col0 + j:col0 + j + 1], [r_ss], [r_ss])
            act(XN[:nt, j, :], src_tile[:nt, j, :], AF.Copy, [rsrc[j], r_ss], [r_XN[j]], scale=ss[:nt, col0 + j:col0 + j + 1])
            pt, rt = tbank()
            for kc in range(8):
                S.emit("pe", lambda e, kc=kc, j=j, nt=nt, pt=pt: e.transpose(out=pt[:, kc * 128:kc * 128 + nt], in_=XN[:nt, j, kc * 128:(kc + 1) * 128],
                                                                              identity=identb[:nt, :nt]), [r_XN[j], r_ident], [rt])
            tt(dst[:, :, j * 128:j * 128 + nt], pt[:, :].rearrange("p (k t) -> p k t", k=8)[:, :, 0:nt],
               gcol[:, 0:8].unsqueeze(2).to_broadcast([128, 8, nt]), ALU.mult, [rt, rg], [rdst[j]])

    def macro_tile(mode, ntok, xb, gt0, hsel=0, normed=False, kv=False, out_row=None, final_kv=None, after_norm=None, prenorm=None,
                   part="all", bsel=0):
        sample = mode == "sample"
        nsub = (ntok + 127) // 128
        nts = [min(128, ntok - 128 * j) for j in range(nsub)]
        C = 16 if sample else 64
        nch = ntok // C
        cps = 1 if sample else 2
        Xb = X[xb]
        rX = r_X[xb]
        hT = hTs[hsel]
        r_hT = r_hTs[hsel]
        rhT_all = r_hT[:nsub]
        cur["hT"] = hT
        cur["r_hT"] = r_hT
        sg, rl_sg = sgs[bsel]
        vh, rl_vh = vhs[bsel]
        if mode == "scan":
            gstate["pool"] = gpool_front if part == "front" else gpool_back
            gstate["tr"] = (0,) if part == "front" else (1,)
        else:
            gstate["pool"] = gpool_front + gpool_back
            gstate["tr"] = (0, 1)

        if part != "back":
            if not normed:
                norm_T(Xb, rX, gmix, r_gmix, 0, hsel, ntok)
            if after_norm is not None:
                after_norm()

        def proj_fm(wv, rw, ct, evac):
            pb, rb = gbank()
            for kc in range(8):
                mm(pb[:, 0:ntok], wv[:, kc, ct * 128:(ct + 1) * 128], hT[:, kc, 0:ntok], kc == 0, kc == 7, [rw] + rhT_all, [rb])
            evac(pb, rb)

        def proj_tm(wv, rw, j, evac, ncol=512):
            pb, rb = gbank()
            nt = nts[j]
            for kc in range(8):
                mm(pb[:nt, 0:ncol], hT[:, kc, j * 128:j * 128 + nt], wv[:, kc, 0:ncol], kc == 0, kc == 7, [rw, r_hT[j]], [rb])
            evac(pb, rb, j, nt)

        def slot_of(j):
            return (gt0 + j) % 6 if not sample else 4

        def hgrn_gates_all():
            n4 = 4 * ntok
            nc4 = 4 * nch
            fl = lambda b: b[:, 0:n4]
            v3 = lambda b: b[:, 0:n4].rearrange("p (h t) -> p h t", h=4)
            ch = lambda b: b[:, 0:n4].rearrange("p (c t) -> p c t", t=C)
            hc = lambda b: b[:, 0:n4].rearrange("p (h c t) -> p h c t", h=4, t=C)
            rsg = [r for l_ in rl_sg for r in l_]
            sgf = sg.rearrange("p h t -> p (h t)")
            tt(v3(gF), v3(sgf), oml[:, 0:4].unsqueeze(2).to_broadcast([128, 4, ntok]), ALU.mult, rsg + [r_lb], [r_gF])
            tt(v3(gF), v3(gF), lb[:, 0:4].unsqueeze(2).to_broadcast([128, 4, ntok]), ALU.add, [r_gF, r_lb], [r_gF])
            act(fl(gL), fl(gF), AF.Ln, [r_gF], [r_gL])
            S.emit("dve", lambda e: e.tensor_tensor_scan(out=fl(gB), data0=fl(gL), data1=fl(gL), initial=0.0, op0=ALU.add, op1=ALU.min),
                   [r_gL], [r_gB], cost=0.1 + 2 * n4 / 900.0)
            ts(fl(gK), fl(gF), -1.0, 1.0, ALU.mult, ALU.add, [r_gF], [r_gK], eng="pool")
            tt(ch(gL), ch(gB), ch(gB)[:, :, C // 2 - 1:C // 2].to_broadcast([128, nc4, C]), ALU.subtract, [r_gB, r_gL], [r_gL])
            act(fl(gE), fl(gL), AF.Exp, [r_gL], [r_gE], scale=-1.0)
            tt(ktT[:, :, 0:ntok], v3(gK), v3(gE), ALU.mult, [r_gK, r_gE], r_ktT, eng="pool")
            dsl = lambda i: dec[:, i, :, 0:nch]
            if mode != "scan":
                act(fl(gB), fl(gL), AF.Exp, [r_gL, r_gB], [r_gB])
                cp(dsl(2), hc(gB)[:, :, :, C - 1], [r_gB], [r_dec])
            else:
                act(dsl(2), hc(gL)[:, :, :, C - 1], AF.Exp, [r_gL], [r_dec])
            tt(dsl(0), hc(gE)[:, :, :, 0], hc(gF)[:, :, :, 0], ALU.mult, [r_gE, r_gF], [r_dec])
            tt(dsl(1), dsl(0), dsl(2), ALU.mult, [r_dec], [r_dec])
            k4 = lambda b: b[:, :, 0:ntok].rearrange("p h (c t) -> p h c t", t=C)
            tt(k4(khT), k4(ktT), dsl(2).unsqueeze(3).to_broadcast([128, 4, nch, C]), ALU.mult, r_ktT + [r_dec], r_khT)
            if mode != "scan":
                sqf = siluq.rearrange("p h t -> p (h t)")
                if sample:
                    tt(Zq[:, :, 0, 0:ntok], v3(sqf), v3(gB), ALU.mult, r_sq + [r_gB], r_Zq)
                else:
                    base = Zq[:, 0, 0, 0:64]
                    zout = AP(base.tensor, base.offset, [list(base.ap[0]), [512 // nsub, 4 * nsub], [192, 2], [1, 64]])
                    tt(zout, fl(sqf).rearrange("p (a c t) -> p a c t", c=2, t=64), fl(gB).rearrange("p (a c t) -> p a c t", c=2, t=64),
                       ALU.mult, r_sq + [r_gB], r_Zq)

        def khat_transpose(j):
            nt = nts[j]
            pt, rt = tbank()
            for hb in range(4):
                S.emit("pe", lambda e, hb=hb, pt=pt: e.transpose(out=pt[:nt, hb * 128:(hb + 1) * 128], in_=khT[:, hb, j * 128:j * 128 + nt],
                                                                  identity=identb[:, :]), [r_khT[hb], r_ident], [rt])
            cp(khtm[:nt, j, :], pt[:nt, 0:512], [rt], [r_khtm[j]], eng="act")

        def s_update(j, ci):
            p = ci % cps
            rows = slice(p * C, p * C + C)
            pb, rb = gbank()
            for hb in range(4):
                mm(pb[:, hb * 128:(hb + 1) * 128], khtm[rows, j, hb * 128:(hb + 1) * 128], vh[rows, j, hb * 128:(hb + 1) * 128],
                   True, True, [r_khtm[j]] + rl_vh[j], [rb])
            tt(Sst[:], Sst[:], dec[:, 1, :, ci:ci + 1].to_broadcast([128, 4, 128]), ALU.mult, [r_S, r_dec], [r_S])
            tt(Sst[:].rearrange("p h v -> p (h v)"), Sst[:].rearrange("p h v -> p (h v)"), pb[:, :], ALU.add, [r_S, rb], [r_S])

        if mode == "scan":
            wv_f, wv_i = W1v[0], W1v[1]
            if part != "back":
                for hb in range(4):
                    proj_fm(wv_f, r_wslot[0], hb, lambda pb, rb, hb=hb: act(sg.rearrange("p h t -> p (h t)")[:, hb * ntok:(hb + 1) * ntok], pb[:, 0:ntok], AF.Sigmoid, [rb], rl_sg[hb]))
                for j in range(nsub):
                    proj_tm(wv_i, r_wslot[1], j, lambda pb, rb, j, nt: cp(vh[:nt, j, :], pb[:nt, :], [rb], rl_vh[j], eng="act"))
                if kv:
                    kv_proj_only()
            if part != "front":
                hgrn_gates_all()
                for j in range(nsub):
                    khat_transpose(j)
                    for p in range(cps):
                        s_update(j, j * cps + p)
            return

        def ev_q(ct):
            return lambda pb, rb: act(qT[:, ct, 0:ntok], pb[:, 0:ntok], AF.Copy, [rb], [r_qT], scale=0.125)

        def ev_k(ct):
            def f(pb, rb):
                for j in range(nsub):
                    cp(kTr[:, ct, slot_of(j), 0:nts[j]], pb[:, j * 128:j * 128 + nts[j]], [rb], [r_kT[slot_of(j)]], eng="act")
                if final_kv is not None:
                    cp(kst[:, ct, 0:ntok], pb[:, 0:ntok], [rb], rl_kst)
            return f

        def ev_v(pb, rb, j, nt):
            s_ = slot_of(j)
            cp(Vr[:nt, s_, :, 0:64], pb[:nt, :].rearrange("p (h d) -> p h d", h=8), [rb], [r_V[s_]], eng="act")
            if mode == "main" or sample:
                cp(Vr[:nt, s_, :, 64:65], onesT[:nt, 0:8].unsqueeze(2), [r_ones], [r_V[s_]], eng="pool")
            else:
                cp(Vr[:nt, s_, :, 64:65], hvt[:nt, 0:1].unsqueeze(2).to_broadcast([nt, 8, 1]), [r_hv], [r_V[s_]], eng="pool")
            if final_kv is not None:
                cp(vst[:nt, j, :], pb[:nt, :], [rb], rl_vst[j])
                dma("act", final_kv[1].ap()[final_kv[2] + j * 128:final_kv[2] + j * 128 + nt, :], vst[:nt, j, :], rl_vst[j], ())

        wv, rw, sl = w_next("in3")
        for hb in range(4):
            proj_fm(wv, rw, hb, lambda pb, rb, hb=hb: act(siluq.rearrange("p h t -> p (h t)")[:, hb * ntok:(hb + 1) * ntok], pb[:, 0:ntok], AF.Silu, [rb], [r_sq[hb]]))
        w_done(sl)
        wv, rw, sl = w_next("in4")
        for hb in range(4):
            proj_fm(wv, rw, hb, lambda pb, rb, hb=hb: act(sg.rearrange("p h t -> p (h t)")[:, hb * ntok:(hb + 1) * ntok], pb[:, 0:ntok], AF.Sigmoid, [rb], rl_sg[hb]))
        w_done(sl)
        wv, rw, sl = w_next("in5")
        for j in range(nsub):
            proj_tm(wv, rw, j, lambda pb, rb, j, nt: cp(vh[:nt, j, :], pb[:nt, :], [rb], rl_vh[j], eng="act"))
        w_done(sl)
        wv, rw, sl = w_next("in6")
        for j in range(nsub):
            proj_tm(wv, rw, j, lambda pb, rb, j, nt: act(sgb[:nt, j, :], pb[:nt, :], AF.Silu, [rb], [r_sgb[j]]))
        w_done(sl)
        wv, rw, sl = w_next("in0")
        for ct in range(4):
            proj_fm(wv, rw, ct, ev_q(ct))
        w_done(sl)
        wv, rw, sl = w_next("in1")
        for ct in range(4):
            proj_fm(wv, rw, ct, ev_k(ct))
        w_done(sl)
        if final_kv is not None:
            for ct in range(4):
                dma("act", final_kv[0].ap()[ct, :, final_kv[2]:final_kv[2] + ntok], kst[:, ct, 0:ntok], rl_kst, ())
        wv, rw, sl = w_next("in2")
        for j in range(nsub):
            proj_tm(wv, rw, j, ev_v)
        w_done(sl)

        att_steps, z_steps, h_steps = [], [], []

        zst = {}

        def z_step(zi):
            gi, ct = divmod(zi, 4)
            if ct == 0:
                zst["w"] = w_next("in%d" % (7 + gi))
            wv_, rw_, sl_ = zst["w"]
            proj_fm(wv_, rw_, ct, lambda pb, rb: act(zs[:, zi, 0:ntok], pb[:, 0:ntok], AF.Sigmoid, [rb], [r_zs[zi]]))
            if ct == 3:
                w_done(sl_)
        for zi in range(16):
            z_steps.append(lambda zi=zi: z_step(zi))

        def make_att(j):
            nq = nts[j]
            if sample:
                ktiles = [(0, 128), (1, 128), (2, 128), (3, 128), (4, 16)]
            else:
                ktiles = [((gt0 + j - 4 + kt) % 6, 128) for kt in range(5)]
            pend = []

            def pv(h, pslot):
                for kt, (s_, nk) in enumerate(ktiles):
                    mm(ps_o[:nq, (h % 4) * 65:(h % 4) * 65 + 65], Pb[pslot][:nk, kt, 0:nq], Vr[:nk, s_, h, :], kt == 0, kt == 4,
                       [r_Pb[pslot], r_V[s_]], [r_o])

            def normalize(half):
                o3 = ps_o[:nq, 0:260].rearrange("p (h d) -> p h d", h=4)
                ts(rec[:nq, half * 4:half * 4 + 4].unsqueeze(2), o3[:, :, 64:65], 1e-30, None, ALU.max, None, [r_o], [r_rec])
                recip(rec[:nq, half * 4:half * 4 + 4], rec[:nq, half * 4:half * 4 + 4], [r_rec], [r_rec])
                tt(oa[:nq, j, half * 256:half * 256 + 256].rearrange("p (h d) -> p h d", h=4), o3[:, :, 0:64],
                   rec[:nq, half * 4:half * 4 + 4].unsqueeze(2).to_broadcast([nq, 4, 64]), ALU.mult, [r_o, r_rec], [r_oa[j]])

            def head(h):
                hp, r0 = h // 2, (h % 2) * 64
                ai = cnt["a"] % 2
                cnt["a"] += 1
                offA = ai * 512
                offB = 1024 + ai * 128
                for kt in (4, 0, 1, 2, 3):
                    s_, nk = ktiles[kt]
                    o_ = offB if kt == 4 else offA + kt * 128
                    mm(ps_att[:nk, o_:o_ + nq], kTr[r0:r0 + 64, hp, s_, 0:nk], qT[r0:r0 + 64, hp, j * 128:j * 128 + nq],
                       True, True, [r_kT[s_], r_qT], [r_attB if kt == 4 else r_att[ai]])
                pi = cnt["p"] % 3
                cnt["p"] += 1
                sattA = ps_att[:, offA:offA + 512].rearrange("p (k q) -> p k q", k=4)
                sattB = ps_att[:, offB:offB + 128]
                if sample:
                    act(Pb[pi][:16, 4, 0:nq], sattB[:16, 0:nq], AF.Exp, [r_attB], [r_Pb[pi]])
                    act(Pb[pi][:, 0:4, 0:nq], sattA[:, :, 0:nq], AF.Exp, [r_att[ai]], [r_Pb[pi]])
                    tt(Pb[pi][:, 0:4, 0:nq], Pb[pi][:, 0:4, 0:nq], ET[:, 0:4, h, 0:nq], ALU.mult, [r_Pb[pi], r_ET], [r_Pb[pi]], eng="pool")
                    tt(Pb[pi][:16, 4, 0:nq], Pb[pi][:16, 4, 0:nq], ET[:16, 4, h, 0:nq], ALU.mult, [r_Pb[pi], r_ET], [r_Pb[pi]], eng="pool")
                else:
                    act(Pb[pi][:, 4, :], sattB, AF.Exp, [r_attB], [r_Pb[pi]])
                    act(Pb[pi][:, 0:4, :], sattA, AF.Exp, [r_att[ai]], [r_Pb[pi]])
                    tt(Pb[pi][:, :, :], Pb[pi][:, :, :], ET[:, :, h, :], ALU.mult, [r_Pb[pi], r_ET], [r_Pb[pi]], eng="pool")
                if pend:
                    ph, ppi = pend.pop(0)
                    pv(ph, ppi)
                    if ph == 3:
                        normalize(0)
                pend.append((h, pi))

            def tail():
                ph, ppi = pend.pop(0)
                pv(ph, ppi)
                normalize(1)
                pt, rt = tbank()
                for kc in range(4):
                    S.emit("pe", lambda e, kc=kc, pt=pt: e.transpose(out=pt[:, kc * 128:kc * 128 + nq], in_=oa[:nq, j, kc * 128:(kc + 1) * 128],
                                                                      identity=identb[:nq, :nq]), [r_oa[j], r_ident], [rt])
                cp(oaT[:, :, j * 128:j * 128 + nq], pt[:, 0:512].rearrange("p (k t) -> p k t", k=4)[:, :, 0:nq], [rt], [r_oaT[j]], eng="act")
            for h in range(8):
                att_steps.append(lambda h=h: head(h))
            att_steps.append(tail)
        for j in range(nsub):
            make_att(j)

        def make_h(j):
            nt = nts[j]

            def h_at():
                pb, rb = gbank()
                for hb in range(4):
                    if sample:
                        qrhs = Zq[:, hb, 0, 0:nt]
                    else:
                        base = Zq[:, hb, 2 * j, 0:64]
                        qrhs = AP(base.tensor, base.offset, [list(base.ap[0]), [192, 2], [1, 64]])
                    mm(pb[:nt, hb * 128:hb * 128 + nt], ktT[:, hb, j * 128:j * 128 + nt], qrhs, True, True, [r_ktT[hb], r_Zq[hb]], [rb])
                tt(AT[:nt, :, 0:nt], pb[:nt, :].rearrange("p (h t) -> p h t", h=4)[:, :, 0:nt],
                   maskb[:nt, 0:nt].unsqueeze(1).to_broadcast([nt, 4, nt]), ALU.mult, [rb, r_mask], [r_AT])

            def h_chunk(p):
                ci = j * cps + p
                tt(Sp[:, p], Sst[:], dec[:, 0, :, ci:ci + 1].to_broadcast([128, 4, 128]), ALU.mult, [r_S, r_dec], [r_Sp[p]])
                s_update(j, ci)

            def h_out():
                ob_, rob = gbank()
                for hb in range(4):
                    for p in range(cps):
                        zl = Zq[:, hb, 0, 0:nt] if sample else Zq[:, hb, 2 * j + p, :]
                        mm(ob_[:nt, hb * 128:(hb + 1) * 128], zl, Sp[:, p, hb, :], p == 0, False, [r_Zq[hb], r_Sp[p]], [rob])
                    mm(ob_[:nt, hb * 128:(hb + 1) * 128], AT[:nt, hb, 0:nt], vh[:nt, j, hb * 128:(hb + 1) * 128], False, True, [r_AT] + rl_vh[j], [rob])
                act(sqb[:nt, :], ob_[:nt, :], AF.Square, [rob], [r_sqb])
                S.emit("dve", lambda e: e.tensor_reduce(out=ssq[:nt, 0:4], in_=sqb[:nt, :].rearrange("p (h v) -> p h v", h=4), axis=AX.X, op=ALU.add),
                       [r_sqb], [r_ssq])
                act(ssq[:nt, :], ssq[:nt, :], AF.Sqrt, [r_ssq, r_eps], [r_ssq], bias=epst[:nt, 0:1], scale=1.0 / 128)
                recip(ssq[:nt, :], ssq[:nt, :], [r_ssq], [r_ssq])
                tt(t1[:nt, :].rearrange("p (h v) -> p h v", h=4), ob_[:nt, :].rearrange("p (h v) -> p h v", h=4),
                   ssq[:nt, 0:4].unsqueeze(2).to_broadcast([nt, 4, 128]), ALU.mult, [rob, r_ssq], [r_t1])
                tt(gG[:nt, :], sgb[:nt, j, :], gnrep[:nt].rearrange("p h v -> p (h v)"), ALU.mult, [r_sgb[j], r_gn], [r_gG], eng="pool")
                tt(ob[:nt, :], t1[:nt, :], gG[:nt, :], ALU.mult, [r_t1, r_gG], [r_ob])

            def h_tr():
                pt, rt = tbank()
                for hb in range(4):
                    S.emit("pe", lambda e, hb=hb, pt=pt: e.transpose(out=pt[:, hb * 128:hb * 128 + nt], in_=ob[:nt, hb * 128:(hb + 1) * 128],
                                                                      identity=identb[:nt, :nt]), [r_ob, r_ident], [rt])
                cp(obT[:, :, j * 128:j * 128 + nt], pt[:, 0:512].rearrange("p (k t) -> p k t", k=4)[:, :, 0:nt], [rt], [r_obT[j]], eng="act")
            h_steps.append(lambda: khat_transpose(j))
            h_steps.append(h_at)
            for p in range(cps):
                h_steps.append(lambda p=p: h_chunk(p))
            h_steps.append(h_out)
            h_steps.append(h_tr)
        for j in range(nsub):
            make_h(j)

        def run_steps(steps, pool, tr, pre=None):
            def f():
                gstate["pool"] = pool
                gstate["tr"] = tr
                if pre is not None:
                    pre()
                for st_ in steps:
                    st_()
            return f
        att_ops = S.capture(run_steps(att_steps, [], (0,)))
        z_ops = S.capture(run_steps(z_steps, gpool_full[0:1], (0,)))
        h_ops = S.capture(run_steps(h_steps, gpool_full[1:2], (1,), pre=hgrn_gates_all))
        S.emit_merged(att_ops, z_ops, h_ops)
        gstate["tr"] = (0, 1)

        gstate["pool"] = gpool_front + gpool_back
        wva, rwa, sla = w_next("a")
        wstate["consumed"] += 1
        wvb, rwb, slb = w_next("b")
        wstate["consumed"] -= 1
        for ct in range(8):
            pb, rb = gbank()
            for kc in range(4):
                mm(pb[:, 0:ntok], wva[:, kc, ct * 128:(ct + 1) * 128], oaT[:, kc, 0:ntok], kc == 0, kc == 3, [rwa] + r_oaT[:nsub], [rb])
            for kc in range(4):
                mm(pb[:, 256:256 + ntok], wvb[:, kc, ct * 128:(ct + 1) * 128], obT[:, kc, 0:ntok], kc == 0, kc == 3, [rwb] + r_obT[:nsub], [rb])
            tt(m1[:, 0:ntok], pb[:, 0:ntok], zs[:, ct, 0:ntok], ALU.mult, [rb, r_zs[ct]], [r_m1])
            tt(m2[:, 0:ntok], pb[:, 256:256 + ntok], zs[:, 8 + ct, 0:ntok], ALU.mult, [rb, r_zs[8 + ct]], [r_m2])
            tt(mT[:, ct, 0:ntok], m1[:, 0:ntok], m2[:, 0:ntok], ALU.add, [r_m1, r_m2], [r_mT], eng="pool")
        w_done(sla)
        w_done(slb)

        for n in range(2):
            wv, rw, sl = w_next("o%d" % n)
            for j in range(nsub):
                nt = nts[j]
                pb, rb = gbank()
                for kc in range(8):
                    mm(pb[:nt, :], mT[:, kc, j * 128:j * 128 + nt], wv[:, kc, :], kc == 0, kc == 7, [rw, r_mT], [rb])
                tt(Xb[:nt, j, n * 512:(n + 1) * 512], Xb[:nt, j, n * 512:(n + 1) * 512], pb[:nt, :], ALU.add, [rX[j], rb], [rX[j]])
            w_done(sl)
        norm_T(Xb, rX, gffn, r_gffn, 2, 2, ntok)
        hT = hTs[2]
        r_hT = r_hTs[2]
        rhT_all = r_hT[:nsub]

        ffn_pend = []
        for gi in range(6):
            ntile = 4 if gi < 5 else 2
            if gi == 2 and prenorm is not None:
                prenorm()
            wvg, rwg, slg = w_next("g%d" % gi)
            if mode != "pre":
                wstate["consumed"] += 1
                wvu, rwu, slu = w_next("u%d" % gi)
                wstate["consumed"] -= 1
            for ct in range(ntile):
                ft = gi * 4 + ct
                pb, rb = gbank()
                if mode == "pre":
                    for kc in range(8):
                        mm(pb[:, 0:2], wvg[:, kc, ct * 128:(ct + 1) * 128], hT[:, kc, ntok - 2:ntok], kc == 0, kc == 7, [rwg] + rhT_all, [rb])
                    cp(cprev[:, ft, :], pb[:, 0:2], [rb], [r_cprev])
                    continue
                for kc in range(8):
                    mm(pb[:, 0:ntok], wvg[:, kc, ct * 128:(ct + 1) * 128], hT[:, kc, 0:ntok], kc == 0, kc == 7, [rwg] + rhT_all, [rb])
                for kc in range(8):
                    mm(pb[:, 256:256 + ntok], wvu[:, kc, ct * 128:(ct + 1) * 128], hT[:, kc, 0:ntok], kc == 0, kc == 7, [rwu] + rhT_all, [rb])
                bi = ft % 2
                a_ = aT[bi]
                cp(a_[:, 0:2], cprev[:, ft, :], [r_cprev], [r_aT[bi]], eng="pool")
                act(a_[:, 2:2 + ntok], pb[:, 0:ntok], AF.Copy, [rb], [r_aT[bi]])
                cp(cprev[:, ft, :], a_[:, ntok:ntok + 2], [r_aT[bi]], [r_cprev], eng="pool")
                act(c1[bi][:, 0:ntok], pb[:, 0:ntok], AF.Identity, [rb, r_cw, r_cb], [r_c1[bi]], scale=cw[:, ft, 2:3], bias=cb[:, ft:ft + 1])
                stt(c2[bi][:, 0:ntok], a_[:, 1:1 + ntok], cw[:, ft, 1:2], c1[bi][:, 0:ntok], ALU.mult, ALU.add, [r_aT[bi], r_c1[bi], r_cw], [r_c2[bi]])
                stt(c1[bi][:, 0:ntok], a_[:, 0:ntok], cw[:, ft, 0:1], c2[bi][:, 0:ntok], ALU.mult, ALU.add, [r_aT[bi], r_c2[bi], r_cw], [r_c1[bi]])
                def fin(bi=bi, ft=ft, pb=pb, rb=rb):
                    act(c2[bi][:, 0:ntok], c1[bi][:, 0:ntok], AF.Gelu_apprx_tanh, [r_c1[bi]], [r_c2[bi]])
                    tt(gT[:, ft, 0:ntok], c2[bi][:, 0:ntok], pb[:, 256:256 + ntok], ALU.mult, [r_c2[bi], rb], [r_gT[ft]])
                if ffn_pend:
                    ffn_pend.pop(0)()
                ffn_pend.append(fin)
            w_done(slg)
            if mode != "pre":
                w_done(slu)
        if mode == "pre":
            return
        while ffn_pend:
            ffn_pend.pop(0)()

        for n in range(2):
            banks = [gbank() for _ in range(nsub)]
            for gk in range(3):
                wv, rw, sl = w_next("d%d_%d" % (n, gk))
                nk = 8 if gk < 2 else 6
                for kl in range(nk):
                    kc = gk * 8 + kl
                    for j in range(nsub):
                        nt = nts[j]
                        mm(banks[j][0][:nt, :], gT[:, kc, j * 128:j * 128 + nt], wv[:, kl, :], kc == 0, kc == NFT - 1, [rw, r_gT[kc]], [banks[j][1]])
                w_done(sl)
            for j in range(nsub):
                nt = nts[j]
                tt(Xb[:nt, j, n * 512:(n + 1) * 512], Xb[:nt, j, n * 512:(n + 1) * 512], banks[j][0][:nt, :], ALU.add, [rX[j], banks[j][1]], [rX[j]])
        for j in range(nsub):
            nt = nts[j]
            act(XN[:nt, j, :], Xb[:nt, j, :], AF.Square, [rX[j]], [r_XN[j], r_ss], accum_out=ss[:nt, 4 + j:5 + j])
            act(ss[:nt, 4 + j:5 + j], ss[:nt, 4 + j:5 + j], AF.Sqrt, [r_ss, r_eps], [r_ss], bias=epst[:nt, 0:1], scale=1.0 / D)
            recip(ss[:nt, 4 + j:5 + j], ss[:nt, 4 + j:5 + j], [r_ss], [r_ss])
            stt(Xb[:nt, j, :], Xb[:nt, j, :], ss[:nt, 4 + j:5 + j], gfin[:nt, :], ALU.mult, ALU.mult, [rX[j], r_ss, r_gfin], [rX[j]])
            dma("act", out_row[j * 128:j * 128 + nt, :], Xb[:nt, j, :], [rX[j]], ())

    cur = {}

    def kv_proj_only():
        ntok, gt0 = cur["ntok"], cur["gt0"]
        for ct in range(4):
            pb, rb = gbank()
            for kc in range(8):
                mm(pb[:, 0:ntok], W1v[2][:, kc, ct * 128:(ct + 1) * 128], cur["hT"][:, kc, 0:ntok], kc == 0, kc == 7, [r_wslot[2]] + cur["r_hT"], [rb])
            for j in range(2):
                s_ = (gt0 + j) % 6
                cp(kTr[:, ct, s_, :], pb[:, j * 128:(j + 1) * 128], [rb], [r_kT[s_]], eng="act")
        for j in range(2):
            s_ = (gt0 + j) % 6
            pb, rb = gbank()
            for kc in range(8):
                mm(pb[:, :], cur["hT"][:, kc, j * 128:(j + 1) * 128], W1v[3][:, kc, :], kc == 0, kc == 7, [r_wslot[3], cur["r_hT"][j]], [rb])
            cp(Vr[:, s_, :, 0:64], pb[:, :].rearrange("p (h d) -> p h d", h=8), [rb], [r_V[s_]], eng="act")
            cp(Vr[:, s_, :, 64:65], hvt[:, 0:1].unsqueeze(2).to_broadcast([128, 8, 1]), [r_hv], [r_V[s_]], eng="pool")


    xa = xin.ap()
    nscan = n_scan
    nmain = n_main
    plan = []
    for m in range(nscan):
        plan.append(("scan", xa[m * T:(m + 1) * T, :], T, dict(gt0=-5 + 2 * (m - (nscan - 2)) + 6, kv=m >= nscan - 2)))
    if do_pre:
        plan.append(("pre", xa[NPRE:NPRE + PRE_T, :], PRE_T, dict(gt0=5)))
    for m in range(nmain):
        fk = (okT_d, ov_d, (m - (nmain - 2)) * T) if (m >= nmain - 2 and not NOFK) else None
        r0 = NPRE + PRE_T + m * T
        plan.append(("main", xa[r0:r0 + T, :], T, dict(gt0=2 * m + 6, out_row=y_d.ap()[m * T:(m + 1) * T, :], final_kv=fk)))
    if do_sample:
        plan.append(("sample", xs_d.ap(), 16, dict(gt0=0, out_row=ys_d.ap(), final_kv=(okTs_d, ovs_d, 0))))
    if plan:
        load_x(plan[0][1], plan[0][2], 0)
    if not do_conv:
        conv_jobs.clear()
    for i, (mode, xsrc, ntok, kw) in enumerate(plan):
        xb = i % 2
        nxt = None
        pren = None
        if i + 1 < len(plan):
            nxt = (lambda p=plan[i + 1], b=(i + 1) % 2: load_x(p[1], p[2], b))
            if mode == "main":
                pren = (lambda p=plan[i + 1], b=(i + 1) % 2: norm_T(X[b], r_X[b], gmix, r_gmix, 0, b, p[2]))
        normed = i > 0 and plan[i - 1][0] == "main"
        if mode == "scan":
            def front(k):
                md, xs_, nt_, kw_ = plan[k]
                cur["ntok"] = nt_
                cur["gt0"] = kw_["gt0"]
                emit_conv(2)
                nx = (lambda p=plan[k + 1], b=(k + 1) % 2: load_x(p[1], p[2], b)) if k + 1 < len(plan) else None
                macro_tile("scan", nt_, k % 2, hsel=k % 2, after_norm=nx, part="front", bsel=k % 2, **kw_)
            if i == 0:
                front(0)
            back_ops = S.capture(lambda: macro_tile("scan", ntok, xb, hsel=xb, part="back", bsel=xb, **kw))
            front_ops = S.capture(lambda: front(i + 1)) if i + 1 < nscan else []
            S.emit_merged(back_ops, front_ops)
            continue
        if i == nscan:
            emit_conv(len(conv_jobs))
            for s_ in range(NSLOT):
                w_issue(s_)
        if mode == "sample":
            dma("act", oS_d.ap().rearrange("h k v -> k h v"), Sst[:], [r_S], ())
            dma("act", oconv_d.ap(), cprev[:], [r_cprev], ())
            dma("pool", kTr[:, :, 0:4, :], ckT_d.ap().rearrange("p c (t k) -> p c t k", t=4), (), r_kT[0:4])
            for t_ in range(4):
                dma("pool", Vr[:, t_, :, 0:64], cv_d.ap()[:, t_ * 128:(t_ + 1) * 128, :].rearrange("h p d -> p h d"), (), [r_V[t_]])
                cp(Vr[:, t_, :, 64:65], onesT[:, 0:8].unsqueeze(2), [r_ones], [r_V[t_]], eng="pool")
            dma("sp", Sst[:], s0_d.ap().rearrange("h k v -> k h v"), (), [r_S])
            dma("sp", cprev[:], cprev_d.ap(), (), [r_cprev])
        macro_tile(mode, ntok, xb, hsel=xb, normed=normed, after_norm=nxt, prenorm=pren, **kw)
    emit_conv(len(conv_jobs))
    if not do_sample:
        dma("act", oS_d.ap().rearrange("h k v -> k h v"), Sst[:], [r_S], ())
        dma("act", oconv_d.ap(), cprev[:], [r_cprev], ())
    else:
        dma("act", oSs_d.ap().rearrange("h k v -> k h v"), Sst[:], [r_S], ())
        dma("act", oconvs_d.ap(), cprev[:], [r_cprev], ())
    for nm, getter in dumps:
        ap_, res_ = getter(locals())
        d_ = nc.dram_tensor("dbg_" + nm, list(ap_.shape), ap_.dtype if hasattr(ap_, "dtype") else F32, kind="ExternalOutput")
        dma("pool", d_.ap(), ap_, res_, ())

    S.finalize(st)
    with nc.Block() as block:
        S.run(block)
    st.close()
    return nc


_NC_CACHE = {}


def kernel(x_prompt, x_sample, cache_attn_k, cache_attn_v, state_hgrn, state_ffn_conv,
           norm_mix_g, w_in, rel_bias, hgrn_lb_logits, hgrn_norm_g, w_branch_a, w_branch_b, w_out,
           norm_ffn_g, w_ffn_gate, w_ffn_up, ffn_conv_w, ffn_conv_b, w_ffn_down, norm_final_g):
    f32 = np.float32
    A = lambda a: np.ascontiguousarray(np.asarray(a, dtype=f32))
    x_prompt = A(x_prompt)
    if "nc" not in _NC_CACHE:
        _NC_CACHE["nc"] = build_program()
    nc = _NC_CACHE["nc"]
    s_idx = np.arange(128)
    mask = ((s_idx[:, None] // 64 == s_idx[None, :] // 64) & (s_idx[:, None] <= s_idx[None, :])).astype(f32)
    shared = {
        "w_in": A(w_in[0]), "w_a": A(w_branch_a[0]), "w_b": A(w_branch_b[0]), "w_o": A(w_out[0]),
        "w_g": A(w_ffn_gate[0]), "w_u": A(w_ffn_up[0]), "w_d": A(w_ffn_down[0]),
        "gmix": A(np.asarray(norm_mix_g[0]).reshape(8, 128).T), "gffn": A(np.asarray(norm_ffn_g[0]).reshape(8, 128).T),
        "gfin": A(norm_final_g), "rel": A(rel_bias[0]),
        "lbl": A(np.asarray(hgrn_lb_logits).reshape(2, 4, 128).transpose(2, 0, 1)),
        "gn": A(hgrn_norm_g[0]),
        "cw": A(np.asarray(ffn_conv_w[0]).reshape(3, NFT, 128).transpose(2, 1, 0)),
        "cb": A(np.asarray(ffn_conv_b[0]).reshape(NFT, 128).T),
        "ident": np.eye(128, dtype=f32), "mask": mask, "jmat": np.ascontiguousarray(np.eye(128, dtype=f32)[::-1]),
    }
    in_maps = []
    for c in range(8):
        b, j = divmod(c, 4)
        s = j * SEG
        lo = s - PRE_T - NPRE
        xin = np.zeros((NTOK_IN, D), f32)
        a0 = max(lo, 0)
        xin[a0 - lo:] = x_prompt[b, a0:s + SEG]
        m = dict(shared)
        m["xin"] = xin
        m["hv"] = np.full((128, 1), 1.0 if j > 0 else 0.0, f32)
        m["xs"] = A(x_sample[c])
        m["ckT"] = A(np.asarray(cache_attn_k[0, c]).transpose(0, 2, 1).reshape(4, 128, 512).transpose(1, 0, 2))
        m["cv"] = A(cache_attn_v[0, c])
        m["s0"] = A(state_hgrn[0, c])
        m["cprev"] = A(np.asarray(state_ffn_conv[0, c]).reshape(2, NFT, 128).transpose(2, 1, 0))
        in_maps.append(m)
    res = run_bass_kernel_spmd(nc, in_maps, core_ids=list(range(8)))
    R_ = res.results
    B = 2
    y_prompt = np.stack([np.concatenate([R_[b * 4 + j]["y"] for j in range(4)], axis=0) for b in range(B)])
    y_sample = np.stack([R_[c]["ys"] for c in range(8)])

    def kT_to_rows(a):
        n = a.shape[-1]
        return a.reshape(8, 64, n).transpose(0, 2, 1)

    def v_to_rows(a):
        n = a.shape[0]
        return a.reshape(n, 8, 64).transpose(1, 0, 2)

    def conv_rows(a):
        return a.transpose(2, 1, 0).reshape(2, DFF)

    last = [3, 7]
    new_k_p = np.stack([kT_to_rows(R_[c]["okT"]) for c in last])[None]
    new_v_p = np.stack([v_to_rows(R_[c]["ov"]) for c in last])[None]
    hg_p = np.stack([R_[c]["oS"] for c in last])[None]
    cv_p = np.stack([conv_rows(R_[c]["oconv"]) for c in last])[None]
    new_k_s = np.stack([kT_to_rows(R_[c]["okTs"]) for c in range(8)])[None]
    new_v_s = np.stack([v_to_rows(R_[c]["ovs"]) for c in range(8)])[None]
    hg_s = np.stack([R_[c]["oSs"] for c in range(8)])[None]
    cv_s = np.stack([conv_rows(R_[c]["oconvs"]) for c in range(8)])[None]
    outs = (y_prompt, y_sample, new_k_p, new_v_p, hg_p, cv_p, new_k_s, new_v_s, hg_s, cv_s)
    return tuple(np.ascontiguousarray(o, dtype=f32) for o in outs)
```

```python
import numpy as np
from contextlib import ExitStack
import concourse.bass as bass
import concourse.mybir as mybir
from concourse.bass import AP
from concourse.bass_utils import run_bass_kernel_spmd

F32 = mybir.dt.float32
BF16 = mybir.dt.bfloat16
AF = mybir.ActivationFunctionType
ALU = mybir.AluOpType
AX = mybir.AxisListType

D = 1024
DFF = 2816
NFT = 22
SEG = 4096
NPRE = 12288
PRE_T = 128
NTOK_IN = NPRE + PRE_T + SEG
T = 256
EPS = 1e-6
NSLOT = 6
STOP = 99
STOPMODE = 'main'
NOFK = False
SLOT_E = 4096


class Res:
    __slots__ = ("name", "w", "r", "excl")

    def __init__(self, name, excl=False):
        self.name = name
        self.w = None
        self.r = {}
        self.excl = excl


class Op:
    __slots__ = ("eng", "fn", "deps", "is_dma", "needs_inc", "sem", "val", "idx")


class Sched:
    ENGS = ("pe", "act", "dve", "pool", "sp")

    def __init__(self, nc, n_dma_sems=8):
        self.nc = nc
        self.ops = []
        self.n_dma_sems = n_dma_sems
        self.cap = None

    def capture(self, fn):
        assert self.cap is None
        self.cap = []
        try:
            fn()
            return self.cap
        finally:
            self.cap = None

    DUR = {"pe": 0.15, "act": 0.75, "dve": 0.85, "pool": 1.1, "sp": 0.3}

    def emit_merged(self, *lists):
        eng_free = {}
        ready = {}
        rdone = {}
        LAT = 0.25

        def start_of(op):
            eng, fn, reads, writes, dma, cost = op
            t = eng_free.get(eng, 0.0)
            for r in reads:
                t = max(t, ready.get(id(r), 0.0) + LAT)
            for w in writes:
                t = max(t, ready.get(id(w), 0.0) + LAT, rdone.get(id(w), 0.0) + LAT)
            return t

        def commit(op, t):
            eng, fn, reads, writes, dma, cost = op
            d = 2.5 if dma else (cost if cost is not None else self.DUR[eng])
            eng_free[eng] = t + (0.1 if dma else d)
            for r in reads:
                rdone[id(r)] = max(rdone.get(id(r), 0.0), t + d)
            for w in writes:
                ready[id(w)] = t + d
            self.emit(eng, fn, reads, writes, dma)

        pos = [0] * len(lists)
        while True:
            best, bt = -1, None
            for k, l in enumerate(lists):
                if pos[k] < len(l):
                    t = start_of(l[pos[k]])
                    if bt is None or t < bt:
                        best, bt = k, t
            if best < 0:
                break
            commit(lists[best][pos[best]], bt)
            pos[best] += 1

    def emit(self, eng, fn, reads=(), writes=(), dma=False, cost=None):
        if self.cap is not None:
            self.cap.append((eng, fn, tuple(reads), tuple(writes), dma, cost))
            return None
        op = Op()
        op.eng = eng
        op.fn = fn
        op.is_dma = dma
        op.needs_inc = dma
        op.sem = None
        op.val = 0
        op.idx = len(self.ops)
        deps = {}
        xr = [r for r in reads if r.excl]
        if xr:
            reads = [r for r in reads if not r.excl]
            writes = list(writes) + [r for r in xr if r not in writes]
        for r in reads:
            if r.w is not None:
                deps[r.w.idx] = r.w
        for w in writes:
            if w.w is not None:
                deps[w.w.idx] = w.w
            for o in w.r.values():
                deps[o.idx] = o
        op.deps = list(deps.values())
        for r in reads:
            key = (eng, op.idx) if dma else (eng, -1)
            r.r[key] = op
        for w in writes:
            w.w = op
            w.r = {}
        self.ops.append(op)
        return op

    def finalize(self, stack):
        nc = self.nc
        for op in self.ops:
            for d in op.deps:
                if d.is_dma or d.eng != op.eng or op.eng != "pe" or op.is_dma:
                    d.needs_inc = True
        csem = {e: stack.enter_context(nc.semaphore("cs_" + e)) for e in ("pe", "act", "dve", "pool")}
        dsem = {e: [stack.enter_context(nc.semaphore("ds_%s%d" % (e, i))) for i in range(self.n_dma_sems)]
                for e in ("sp", "pool", "act")}
        ccount = {e: 0 for e in csem}
        dstate = {e: [None] * self.n_dma_sems for e in dsem}
        duse = {e: [0] * self.n_dma_sems for e in dsem}
        drr = {e: 0 for e in dsem}
        waited = {e: {} for e in self.ENGS}
        streams = {e: [] for e in self.ENGS}
        for op in self.ops:
            waits = []
            e = op.eng
            extra = []
            if op.is_dma:
                k = drr[e] % self.n_dma_sems
                drr[e] += 1
                prev = dstate[e][k]
                if prev is not None:
                    extra.append(prev)
                duse[e][k] += 1
                op.sem = dsem[e][k]
                op.val = 16 * duse[e][k]
                dstate[e][k] = op
            elif op.needs_inc:
                ccount[e] += 1
                op.sem = csem[e]
                op.val = ccount[e]
            for d in op.deps + extra:
                if (not d.is_dma) and d.eng == e and e == "pe" and not op.is_dma:
                    continue
                key = id(d.sem)
                if waited[e].get(key, 0) >= d.val:
                    continue
                waited[e][key] = d.val
                waits.append((d.sem, d.val))
            streams[e].append((waits, op))
        fin = []
        for e in dsem:
            for k in range(self.n_dma_sems):
                if duse[e][k] and waited["sp"].get(id(dsem[e][k]), 0) < 16 * duse[e][k]:
                    fin.append((dsem[e][k], 16 * duse[e][k]))
        self.streams = streams
        self.fin = fin

    def run(self, block):
        streams = self.streams
        fin = self.fin

        def body(name):
            def f(eng):
                for waits, op in streams[name]:
                    for s, v in waits:
                        eng.wait_ge(s, v)
                    inst = op.fn(eng)
                    if op.sem is not None:
                        inst.then_inc(op.sem, 16 if op.is_dma else 1)
                if name == "sp":
                    for s, v in fin:
                        eng.wait_ge(s, v)
            return f
        block.tensor(body("pe"))
        block.scalar(body("act"))
        block.vector(body("dve"))
        block.gpsimd(body("pool"))
        block.sync(body("sp"))


def build_program(n_scan=NPRE // T, n_main=SEG // T, do_pre=True, do_sample=True, dumps=(), do_conv=True, init_stop=99):
    NPRE = n_scan * T
    NTOK_IN = NPRE + PRE_T + n_main * T
    nc = bass.Bass("TRN2", target_bir_lowering=False)
    st = ExitStack()
    S = Sched(nc)

    def din(name, shape):
        return nc.dram_tensor(name, list(shape), F32, kind="ExternalInput")

    def dout(name, shape):
        return nc.dram_tensor(name, list(shape), F32, kind="ExternalOutput")

    xin = din("xin", [NTOK_IN, D])
    hv_d = din("hv", [128, 1])
    xs_d = din("xs", [16, D])
    ckT_d = din("ckT", [128, 4, 512])
    cv_d = din("cv", [8, 512, 64])
    s0_d = din("s0", [4, 128, 128])
    cprev_d = din("cprev", [128, NFT, 2])
    w_in_d = din("w_in", [D, 5632])
    w_a_d = din("w_a", [512, D])
    w_b_d = din("w_b", [512, D])
    w_o_d = din("w_o", [D, D])
    w_g_d = din("w_g", [D, DFF])
    w_u_d = din("w_u", [D, DFF])
    w_d_d = din("w_d", [DFF, D])
    gmix_d = din("gmix", [128, 8])
    gffn_d = din("gffn", [128, 8])
    gfin_d = din("gfin", [D])
    rel_d = din("rel", [8, 192])
    lbl_d = din("lbl", [128, 2, 4])
    gn_d = din("gn", [128])
    cw_d = din("cw", [128, NFT, 3])
    cb_d = din("cb", [128, NFT])
    ident_d = din("ident", [128, 128])
    mask_d = din("mask", [128, 128])
    jmat_d = din("jmat", [128, 128])

    y_d = dout("y", [SEG, D])
    ys_d = dout("ys", [16, D])
    okT_d = dout("okT", [4, 128, 512])
    ov_d = dout("ov", [512, 512])
    oS_d = dout("oS", [4, 128, 128])
    oconv_d = dout("oconv", [128, NFT, 2])
    okTs_d = dout("okTs", [4, 128, 16])
    ovs_d = dout("ovs", [16, 512])
    oSs_d = dout("oSs", [4, 128, 128])
    oconvs_d = dout("oconvs", [128, NFT, 2])

    def scratch(name, shape, dt=BF16):
        return nc.dram_tensor(name, list(shape), dt, kind="Internal")

    wb_in = scratch("wb_in", [D, 5632])
    wb_a = scratch("wb_a", [512, D])
    wb_b = scratch("wb_b", [512, D])
    wb_o = scratch("wb_o", [D, D])
    wb_g = scratch("wb_g", [D, DFF])
    wb_u = scratch("wb_u", [D, DFF])
    wb_d = scratch("wb_d", [DFF, D])
    ext_d = scratch("ext_d", [8, 768], F32)

    def sb(name, shape, dt=F32):
        return st.enter_context(nc.sbuf_tensor(name, list(shape), dt))

    def R(name, excl=False):
        return Res(name, excl)

    def fsz(ap):
        n = 1
        for d_ in list(ap.shape)[1:]:
            n *= int(d_)
        return n

    def ecost(eng, ap):
        n = fsz(ap)
        return {"act": 0.25 + n / 1000.0, "dve": 0.1 + n / 850.0, "pool": 0.2 + n / 480.0}[eng]

    def act(out, in_, func, reads, writes, **kw):
        S.emit("act", lambda e: e.activation(out=out, in_=in_, func=func, **kw), reads, writes, cost=ecost("act", out))

    def tt(out, in0, in1, op, reads, writes, eng="dve"):
        S.emit(eng, lambda e: e.tensor_tensor(out=out, in0=in0, in1=in1, op=op), reads, writes, cost=ecost(eng, out))

    def ts(out, in0, s1, s2, op0, op1, reads, writes, eng="dve"):
        if op1 is None:
            S.emit(eng, lambda e: e.tensor_scalar(out=out, in0=in0, scalar1=s1, scalar2=None, op0=op0), reads, writes, cost=ecost(eng, out))
        else:
            S.emit(eng, lambda e: e.tensor_scalar(out=out, in0=in0, scalar1=s1, scalar2=s2, op0=op0, op1=op1), reads, writes, cost=ecost(eng, out))

    def stt(out, in0, scalar, in1, op0, op1, reads, writes):
        S.emit("dve", lambda e: e.scalar_tensor_tensor(out=out, in0=in0, scalar=scalar, in1=in1, op0=op0, op1=op1), reads, writes, cost=ecost("dve", out))

    def cp(out, in_, reads, writes, eng="dve"):
        if eng == "act":
            act(out, in_, AF.Copy, reads, writes)
        else:
            S.emit(eng, lambda e: e.tensor_copy(out=out, in_=in_), reads, writes, cost=ecost(eng, out))

    def recip(out, in_, reads, writes):
        S.emit("dve", lambda e: e.reciprocal(out=out, in_=in_), reads, writes)

    def mset(ap, val, writes, eng="pool"):
        S.emit(eng, lambda e: e.memset(ap, val), (), writes)

    def mm(out, lhsT, rhs, start, stop, reads, writes):
        S.emit("pe", lambda e: e.matmul(out, lhsT=lhsT, rhs=rhs, start=start, stop=stop), reads, writes, cost=0.05 + fsz(rhs) / 2000.0)

    def dma(eng, out, in_, reads, writes):
        S.emit(eng, lambda e: e.dma_start(out=out, in_=in_), reads, writes, dma=True)

    identb = sb("identb", [128, 128], BF16); r_ident = R("ident")
    maskb = sb("maskb", [128, 128], BF16); r_mask = R("mask")
    jb = sb("jb", [128, 128], BF16); r_j = R("j")
    epst = sb("epst", [128, 1]); r_eps = R("eps")
    onesT = sb("onesT", [128, 8]); r_ones = R("ones")
    hvt = sb("hvt", [128, 1]); r_hv = R("hv")
    gmix = sb("gmix_s", [128, 8]); r_gmix = R("gmix")
    gffn = sb("gffn_s", [128, 8]); r_gffn = R("gffn")
    gfin = sb("gfin_s", [128, D]); r_gfin = R("gfin")
    gnrep = sb("gnrep", [128, 4, 128]); r_gn = R("gn")
    cw = sb("cw_s", [128, NFT, 3]); r_cw = R("cw")
    cb = sb("cb_s", [128, NFT]); r_cb = R("cb")
    lbl = sb("lbl_s", [128, 2, 4]); r_lbl = R("lbl")
    lb = sb("lb_s", [128, 4]); oml = sb("oml_s", [128, 4]); r_lb = R("lb")
    ET = sb("ET", [128, 5, 8, 128], BF16); r_ET = R("ET")

    dma("pool", identb[:], ident_d.ap(), (), [r_ident])
    dma("pool", maskb[:], mask_d.ap(), (), [r_mask])
    dma("pool", jb[:], jmat_d.ap(), (), [r_j])
    mset(epst[:], EPS, [r_eps])
    nhalf = sb("nhalf", [128, 8]); r_nh = R("nhalf")
    mset(nhalf[:], -0.5, [r_nh])
    mset(onesT[:], 1.0, [r_ones])
    dma("sp", hvt[:], hv_d.ap(), (), [r_hv])
    dma("sp", gmix[:], gmix_d.ap(), (), [r_gmix])
    dma("sp", gffn[:], gffn_d.ap(), (), [r_gffn])
    dma("sp", gfin[:], AP(gfin_d, 0, [[0, 128], [1, D]]), (), [r_gfin])
    dma("sp", gnrep[:], AP(gn_d, 0, [[0, 128], [0, 4], [1, 128]]), (), [r_gn])
    dma("sp", cw[:], cw_d.ap(), (), [r_cw])
    dma("sp", cb[:], cb_d.ap(), (), [r_cb])
    dma("sp", lbl[:], lbl_d.ap(), (), [r_lbl])

    if init_stop <= 1:
        S.finalize(st)
        with nc.Block() as block:
            S.run(block)
        st.close()
        return nc
    ps_att = st.enter_context(nc.psum_tensor("ps_att", [128, 1536], F32))
    r_att = [R("att0", True), R("att1", True)]
    r_attB = R("attB", True)
    NGEN = 2
    ps_gen = [st.enter_context(nc.psum_tensor("ps_g%d" % i, [128, 512], F32)) for i in range(NGEN)]
    r_gen = [R("g%d" % i, True) for i in range(NGEN)]
    ps_o = st.enter_context(nc.psum_tensor("ps_o", [128, 512], F32))
    r_o = R("ps_o", True)
    ps_tr = [st.enter_context(nc.psum_tensor("ps_t%d" % i, [128, 1024], BF16)) for i in range(2)]
    r_tr = [R("t%d" % i, True) for i in range(2)]
    cnt = {"g": 0, "t": 0, "a": 0, "p": 0}

    gpool_full = [(ps_gen[i], r_gen[i]) for i in range(NGEN)]
    gpool_front = gpool_full + [(ps_o, r_o)]
    gpool_back = [(ps_att[:, 0:512], r_att[0]), (ps_att[:, 512:1024], r_att[1]), (ps_att[:, 1024:1536], r_attB)]
    gstate = {"pool": gpool_full, "tr": (0, 1)}

    def gbank():
        pool = gstate["pool"]
        i = cnt["g"] % len(pool)
        cnt["g"] += 1
        return pool[i]

    def tbank():
        sel = gstate["tr"]
        i = sel[cnt["t"] % len(sel)]
        cnt["t"] += 1
        return ps_tr[i], r_tr[i]

    lbt = sb("lbt", [128, 4])
    tt(lbt[:], lbl[:, 1, :], lbl[:, 0, :], ALU.subtract, [r_lbl], [r_lb])
    act(lbt[:], lbt[:], AF.Exp, [r_lb], [r_lb])
    ts(lbt[:], lbt[:], 1.0, None, ALU.add, None, [r_lb], [r_lb])
    recip(lb[:], lbt[:], [r_lb], [r_lb])
    ts(oml[:], lb[:], -1.0, 1.0, ALU.mult, ALU.add, [r_lb], [r_lb])
    omlh = sb("omlh_s", [128, 4]); lbh = sb("lbh_s", [128, 4])
    ts(omlh[:], oml[:], 0.5, None, ALU.mult, None, [r_lb], [r_lb])
    tt(lbh[:], lb[:], omlh[:], ALU.add, [r_lb], [r_lb])

    if init_stop <= 2:
        S.finalize(st)
        with nc.Block() as block:
            S.run(block)
        st.close()
        return nc
    if init_stop <= 3:
        S.finalize(st)
        with nc.Block() as block:
            S.run(block)
        st.close()
        return nc
    r_wb = {}
    conv_jobs = []
    for name, src, dst, rows in (("in", w_in_d, wb_in, D), ("a", w_a_d, wb_a, 512), ("b", w_b_d, wb_b, 512),
                                 ("o", w_o_d, wb_o, D), ("g", w_g_d, wb_g, D), ("u", w_u_d, wb_u, D),
                                 ("d", w_d_d, wb_d, DFF)):
        r_wb[name] = R("wb_" + name)
        for r0 in range(0, rows, 128):
            conv_jobs.append((name, dst.ap()[r0:r0 + 128, :], src.ap()[r0:r0 + 128, :]))

    def emit_conv(n):
        for _ in range(n):
            if conv_jobs:
                name, d_, s_ = conv_jobs.pop(0)
                dma("pool", d_, s_, (), [r_wb[name]])

    wslot = [sb("wslot%d" % i, [128, SLOT_E], BF16) for i in range(NSLOT)]
    r_wslot = [R("wslot%d" % i) for i in range(NSLOT)]
    W1v = []
    for i, c0 in enumerate((2048, 2560, 512, 1024)):
        v_ = wslot[i][:, :].rearrange("p (k n) -> p k n", k=8)
        dma("pool", v_, w_in_d.ap()[:, c0:c0 + 512].rearrange("(k p) n -> p k n", p=128), (), [r_wslot[i]])
        W1v.append(v_)

    def wgroups(mode):
        g = []
        for i in (3, 4, 5, 6, 0, 1, 2, 7, 8, 9, 10):
            g.append(("in%d" % i, "in", wb_in.ap()[:, 512 * i:512 * i + 512].rearrange("(k p) n -> p k n", p=128), 8, 512))
        g.append(("a", "a", wb_a.ap().rearrange("(k p) n -> p k n", p=128), 4, 1024))
        g.append(("b", "b", wb_b.ap().rearrange("(k p) n -> p k n", p=128), 4, 1024))
        for n in range(2):
            g.append(("o%d" % n, "o", wb_o.ap()[:, 512 * n:512 * n + 512].rearrange("(k p) n -> p k n", p=128), 8, 512))
        for gi in range(6):
            nc_ = 512 if gi < 5 else 256
            g.append(("g%d" % gi, "g", wb_g.ap()[:, 512 * gi:512 * gi + nc_].rearrange("(k p) n -> p k n", p=128), 8, nc_))
            if mode != "pre":
                g.append(("u%d" % gi, "u", wb_u.ap()[:, 512 * gi:512 * gi + nc_].rearrange("(k p) n -> p k n", p=128), 8, nc_))
        if mode != "pre":
            for n in range(2):
                for gk in range(3):
                    nk = 8 if gk < 2 else 6
                    g.append(("d%d_%d" % (n, gk), "d",
                              wb_d.ap()[1024 * gk:1024 * gk + 128 * nk, 512 * n:512 * n + 512].rearrange("(k p) n -> p k n", p=128), nk, 512))
        return g

    wq = []
    tiles_plan = ([("pre", 0)] if do_pre else []) + [("main", m) for m in range(n_main)] + ([("sample", 0)] if do_sample else [])
    for mode, _ in tiles_plan:
        wq.extend(wgroups(mode))
    wstate = {"issued": 0, "consumed": 0, "slot": {}}

    def w_issue(slot):
        i = wstate["issued"]
        if i >= len(wq):
            return
        key, wname, src, nk, ncol = wq[i]
        view = wslot[slot][:, 0:nk * ncol].rearrange("p (k n) -> p k n", k=nk)
        dma("sp", view, src, [r_wb[wname]], [r_wslot[slot]])
        wstate["slot"][i] = slot
        wstate["issued"] += 1

    def w_next(key):
        i = wstate["consumed"]
        assert wq[i][0] == key, (wq[i][0], key)
        slot = wstate["slot"][i]
        _, _, _, nk, ncol = wq[i]
        view = wslot[slot][:, 0:nk * ncol].rearrange("p (k n) -> p k n", k=nk)
        return view, r_wslot[slot], slot

    def w_done(slot):
        wstate["consumed"] += 1
        w_issue(slot)

    if init_stop <= 4:
        S.finalize(st)
        with nc.Block() as block:
            S.run(block)
        st.close()
        return nc
    X = [sb("X%d" % i, [128, 2, D]) for i in range(2)]
    r_X = [[R("X%d_%d" % (i, j)) for j in range(2)] for i in range(2)]
    XN = sb("XN", [128, 2, D], BF16); r_XN = [R("XN0"), R("XN1")]
    ss = sb("ss", [128, 8]); r_ss = R("ss")
    hTs = [sb("hTa", [128, 8, T], BF16), sb("hTb", [128, 8, T], BF16), sb("h2T", [128, 8, T], BF16)]
    r_hTs = [[R("hTa0"), R("hTa1")], [R("hTb0"), R("hTb1")], [R("h2T0"), R("h2T1")]]
    qT = sb("qT", [128, 4, T], BF16); r_qT = R("qT")
    kTr = sb("kTr", [128, 4, 6, 128], BF16); r_kT = [R("kT%d" % i) for i in range(6)]
    Vr = sb("Vr", [128, 6, 8, 65], BF16); r_V = [R("V%d" % i) for i in range(6)]
    Pb = [sb("Pb%d" % i, [128, 5, 128], BF16) for i in range(3)]; r_Pb = [R("Pb%d" % i) for i in range(3)]
    oa = sb("oa", [128, 2, 512], BF16); r_oa = [R("oa0"), R("oa1")]
    oaT = sb("oaT", [128, 4, T], BF16); r_oaT = [R("oaT0"), R("oaT1")]
    rec = sb("rec", [128, 8]); r_rec = R("rec")
    sg = sb("sg", [128, 4, T]); r_sg = [R("sg%d" % i) for i in range(4)]
    siluq = sb("siluq", [128, 4, T]); r_sq = [R("siluq%d" % i) for i in range(4)]
    gF = sb("gF", [128, 4 * T]); r_gF = R("gF")
    gL = sb("gL", [128, 4 * T]); r_gL = R("gL")
    gB = sb("gB", [128, 4 * T]); r_gB = R("gB")
    gK = sb("gK", [128, 4 * T]); r_gK = R("gK")
    gE = sb("gE", [128, 4 * T]); r_gE = R("gE")
    dec = sb("dec", [128, 3, 4, 4]); r_dec = R("dec")
    Zq = sb("Zq", [128, 4, 4, 128], BF16); r_Zq = [R("Zq%d" % i) for i in range(4)]
    ktT = sb("ktT", [128, 4, T], BF16); r_ktT = [R("ktT%d" % i) for i in range(4)]
    khT = sb("khT", [128, 4, T], BF16); r_khT = [R("khT%d" % i) for i in range(4)]
    khtm = sb("khtm", [128, 2, 512], BF16); r_khtm = [R("khtm0"), R("khtm1")]
    vh = sb("vh", [128, 2, 512], BF16); r_vh = [R("vh0"), R("vh1")]
    sgb = sb("sgb", [128, 2, 512]); r_sgb = [R("sgb0"), R("sgb1")]
    Sst = sb("Sst", [128, 4, 128]); r_S = R("S")
    Sp = sb("Sp", [128, 2, 4, 128], BF16); r_Sp = [R("Sp0"), R("Sp1")]
    AT = sb("AT", [128, 4, 128], BF16); r_AT = R("AT")
    sqb = sb("sqb", [128, 512]); r_sqb = R("sqb")
    ssq = sb("ssq", [128, 4]); r_ssq = R("ssq")
    t1 = sb("t1", [128, 512]); r_t1 = R("t1")
    gG = sb("gG", [128, 512]); r_gG = R("gG")
    ob = sb("ob", [128, 512], BF16); r_ob = R("ob")
    obT = sb("obT", [128, 4, T], BF16); r_obT = [R("obT0"), R("obT1")]
    zs = sb("zs", [128, 16, T], BF16); r_zs = [R("zs%d" % i) for i in range(16)]
    mT = sb("mT", [128, 8, T], BF16); r_mT = R("mT")
    aT = [sb("aT%d" % i, [128, T + 2]) for i in range(2)]; r_aT = [R("aT0"), R("aT1")]
    c1 = [sb("c1_%d" % i, [128, T]) for i in range(2)]; r_c1 = [R("c1_0"), R("c1_1")]
    c2 = [sb("c2_%d" % i, [128, T]) for i in range(2)]; r_c2 = [R("c2_0"), R("c2_1")]
    m1, r_m1, m2, r_m2 = c1[0], r_c1[0], c2[0], r_c2[0]
    gT = sb("gT", [128, NFT, T], BF16); r_gT = [R("gT%d" % i) for i in range(NFT)]
    cprev = sb("cprev_s", [128, NFT, 2]); r_cprev = R("cprev")

    ext_ = siluq[0:8].rearrange("p h t -> p (h t)")[:, 0:768]; r_ext = r_sq; r_extd = R("extd")
    dma("sp", ext_[:, 64:256], rel_d.ap(), (), r_ext)
    act(ext_[:, 0:64], ext_[:, 64:65].to_broadcast([8, 64]), AF.Identity, r_ext, r_ext)
    act(ext_[:, 256:768], ext_[:, 255:256].to_broadcast([8, 512]), AF.Identity, r_ext, r_ext)
    act(ext_[:, :], ext_[:, :], AF.Exp, r_ext, r_ext)
    dma("sp", ext_d.ap(), ext_[:, :], r_ext, [r_extd])
    hk = sg[:].rearrange("p h t -> p (h t)").rearrange("p (a b) -> p a b", a=8); r_hk = r_sg
    hkb = ktT[:].rearrange("p h t -> p (h t)").rearrange("p (a b) -> p a b", a=8); r_hkb = r_ktT
    for kt in range(5):
        dma("sp", hk, AP(ext_d, 512 - 128 * kt, [[1, 128], [768, 8], [1, 128]]), [r_extd], r_hk)
        cp(hkb, hk, r_hk, r_hkb)
        for n in range(2):
            pb, rb = gbank()
            mm(pb[:, :], jb[:], hkb[:, 4 * n:4 * n + 4, :].rearrange("p h q -> p (h q)"), True, True, [r_j] + r_hkb, [rb])
            cp(ET[:, kt, 4 * n:4 * n + 4, :].rearrange("p h q -> p (h q)"), pb[:, :], [rb], [r_ET])
    mset(ET[0:64, 0, :, 64:128], 0.0, [r_ET])
    mset(ET[64:128, 4, :, 0:64], 0.0, [r_ET])

    kst = gT[:, 0:8, :].rearrange("p a b -> p (a b)").bitcast(F32).rearrange("p (h t) -> p h t", h=4)
    vst = gT[:, 8:16, :].rearrange("p a b -> p (a b)").bitcast(F32).rearrange("p (j c) -> p j c", j=2)
    rl_kst = r_gT[0:8]
    rl_vst = [r_gT[8:12], r_gT[12:16]]
    sg2 = gT[:, 0:8, :].rearrange("p a b -> p (a b)").bitcast(F32).rearrange("p (h t) -> p h t", h=4)
    r_sg2 = [[r_gT[2 * i], r_gT[2 * i + 1]] for i in range(4)]
    vh2 = gT[:, 8:12, :].rearrange("p a b -> p (a b)").rearrange("p (j c) -> p j c", j=2)
    r_vh2 = [[r_gT[8], r_gT[9]], [r_gT[10], r_gT[11]]]
    sgs = [(sg, [[r] for r in r_sg]), (sg2, r_sg2)]
    vhs = [(vh, [[r] for r in r_vh]), (vh2, r_vh2)]
    mset(Zq[:], 0.0, r_Zq)
    mset(Sst[:], 0.0, [r_S])
    mset(cprev[:], 0.0, [r_cprev])

    if init_stop <= 5:
        S.finalize(st)
        with nc.Block() as block:
            S.run(block)
        st.close()
        return nc
    def load_x(xsrc, ntok, xb):
        nsub = (ntok + 127) // 128
        for j in range(nsub):
            nt = min(128, ntok - 128 * j)
            dma("sp", X[xb][:nt, j, :], xsrc[j * 128:j * 128 + nt, :], (), [r_X[xb][j]])

    def norm_T(src_tile, rsrc, gcol, rg, col0, hsel, ntok):
        nsub = (ntok + 127) // 128
        dst, rdst = hTs[hsel], r_hTs[hsel]
        for j in range(nsub):
            nt = min(128, ntok - 128 * j)
            act(XN[:nt, j, :], src_tile[:nt, j, :], AF.Square, [rsrc[j]], [r_XN[j], r_ss], accum_out=ss[:nt, col0 + j:col0 + j + 1])
            ts(ss[:nt, col0 + j:col0 + j + 1], ss[:nt, col0 + j:col0 + j + 1], 1.0 / D, EPS, ALU.mult, ALU.add, [r_ss], [r_ss], eng="pool")
            tt(ss[:nt, col0 + j:col0 + j + 1], ss[:nt, col0 + j:col0 + j + 1], nhalf[:nt, 0:1], ALU.pow, [r_ss, r_nh], [r_ss], eng="pool")
            act(XN[:nt, j, :], src_tile[:nt, j, :], AF.Copy, [rsrc[j], r_ss], [r_XN[j]], scale=ss[:nt, col0 + j:col0 + j + 1])
            pt, rt = tbank()
            for kc in range(8):
                S.emit("pe", lambda e, kc=kc, j=j, nt=nt, pt=pt: e.transpose(out=pt[:, kc * 128:kc * 128 + nt], in_=XN[:nt, j, kc * 128:(kc + 1) * 128],
                                                                              identity=identb[:nt, :nt]), [r_XN[j], r_ident], [rt])
            tt(dst[:, :, j * 128:j * 128 + nt], pt[:, :].rearrange("p (k t) -> p k t", k=8)[:, :, 0:nt],
               gcol[:, 0:8].unsqueeze(2).to_broadcast([128, 8, nt]), ALU.mult, [rt, rg], [rdst[j]])

    def macro_tile(mode, ntok, xb, gt0, hsel=0, normed=False, kv=False, out_row=None, final_kv=None, after_norm=None, prenorm=None,
                   part="all", bsel=0):
        sample = mode == "sample"
        nsub = (ntok + 127) // 128
        nts = [min(128, ntok - 128 * j) for j in range(nsub)]
        C = 16 if sample else 64
        nch = ntok // C
        cps = 1 if sample else 2
        Xb = X[xb]
        rX = r_X[xb]
        hT = hTs[hsel]
        r_hT = r_hTs[hsel]
        rhT_all = r_hT[:nsub]
        cur["hT"] = hT
        cur["r_hT"] = r_hT
        sg, rl_sg = sgs[bsel]
        vh, rl_vh = vhs[bsel]
        if mode == "scan":
            gstate["pool"] = gpool_front if part == "front" else gpool_back
            gstate["tr"] = (0,) if part == "front" else (1,)
        else:
            gstate["pool"] = gpool_front + gpool_back
            gstate["tr"] = (0, 1)

        if part != "back":
            if not normed:
                norm_T(Xb, rX, gmix, r_gmix, 0, hsel, ntok)
            if after_norm is not None:
                after_norm()

        def proj_fm(wv, rw, ct, evac):
            pb, rb = gbank()
            for kc in range(8):
                mm(pb[:, 0:ntok], wv[:, kc, ct * 128:(ct + 1) * 128], hT[:, kc, 0:ntok], kc == 0, kc == 7, [rw] + rhT_all, [rb])
            evac(pb, rb)

        def proj_tm(wv, rw, j, evac, ncol=512):
            pb, rb = gbank()
            nt = nts[j]
            for kc in range(8):
                mm(pb[:nt, 0:ncol], hT[:, kc, j * 128:j * 128 + nt], wv[:, kc, 0:ncol], kc == 0, kc == 7, [rw, r_hT[j]], [rb])
            evac(pb, rb, j, nt)

        def slot_of(j):
            return (gt0 + j) % 6 if not sample else 4

        def hgrn_gates_all():
            n4 = 4 * ntok
            nc4 = 4 * nch
            fl = lambda b: b[:, 0:n4]
            v3 = lambda b: b[:, 0:n4].rearrange("p (h t) -> p h t", h=4)
            ch = lambda b: b[:, 0:n4].rearrange("p (c t) -> p c t", t=C)
            hc = lambda b: b[:, 0:n4].rearrange("p (h c t) -> p h c t", h=4, t=C)
            rsg = [r for l_ in rl_sg for r in l_]
            sgf = sg.rearrange("p h t -> p (h t)")
            tt(v3(gF), v3(sgf), omlh[:, 0:4].unsqueeze(2).to_broadcast([128, 4, ntok]), ALU.mult, rsg + [r_lb], [r_gF])
            tt(v3(gF), v3(gF), lbh[:, 0:4].unsqueeze(2).to_broadcast([128, 4, ntok]), ALU.add, [r_gF, r_lb], [r_gF])
            act(fl(gL), fl(gF), AF.Ln, [r_gF], [r_gL])
            S.emit("dve", lambda e: e.tensor_tensor_scan(out=fl(gB), data0=fl(gL), data1=fl(gL), initial=0.0, op0=ALU.add, op1=ALU.min),
                   [r_gL], [r_gB], cost=0.1 + 2 * n4 / 900.0)
            ts(fl(gK), fl(gF), -1.0, 1.0, ALU.mult, ALU.add, [r_gF], [r_gK], eng="pool")
            tt(ch(gL), ch(gB), ch(gB)[:, :, C // 2 - 1:C // 2].to_broadcast([128, nc4, C]), ALU.subtract, [r_gB, r_gL], [r_gL])
            act(fl(gE), fl(gL), AF.Exp, [r_gL], [r_gE], scale=-1.0)
            tt(ktT[:, :, 0:ntok], v3(gK), v3(gE), ALU.mult, [r_gK, r_gE], r_ktT, eng="pool")
            dsl = lambda i: dec[:, i, :, 0:nch]
            if mode != "scan":
                act(fl(gB), fl(gL), AF.Exp, [r_gL, r_gB], [r_gB])
                cp(dsl(2), hc(gB)[:, :, :, C - 1], [r_gB], [r_dec])
            else:
                act(dsl(2), hc(gL)[:, :, :, C - 1], AF.Exp, [r_gL], [r_dec])
            tt(dsl(0), hc(gE)[:, :, :, 0], hc(gF)[:, :, :, 0], ALU.mult, [r_gE, r_gF], [r_dec])
            tt(dsl(1), dsl(0), dsl(2), ALU.mult, [r_dec], [r_dec])
            k4 = lambda b: b[:, :, 0:ntok].rearrange("p h (c t) -> p h c t", t=C)
            tt(k4(khT), k4(ktT), dsl(2).unsqueeze(3).to_broadcast([128, 4, nch, C]), ALU.mult, r_ktT + [r_dec], r_khT)
            if mode != "scan":
                sqf = siluq.rearrange("p h t -> p (h t)")
                if sample:
                    tt(Zq[:, :, 0, 0:ntok], v3(sqf), v3(gB), ALU.mult, r_sq + [r_gB], r_Zq)
                else:
                    base = Zq[:, 0, 0, 0:64]
                    zout = AP(base.tensor, base.offset, [list(base.ap[0]), [512 // nsub, 4 * nsub], [192, 2], [1, 64]])
                    tt(zout, fl(sqf).rearrange("p (a c t) -> p a c t", c=2, t=64), fl(gB).rearrange("p (a c t) -> p a c t", c=2, t=64),
                       ALU.mult, r_sq + [r_gB], r_Zq)

        def khat_transpose(j):
            nt = nts[j]
            pt, rt = tbank()
            for hb in range(4):
                S.emit("pe", lambda e, hb=hb, pt=pt: e.transpose(out=pt[:nt, hb * 128:(hb + 1) * 128], in_=khT[:, hb, j * 128:j * 128 + nt],
                                                                  identity=identb[:, :]), [r_khT[hb], r_ident], [rt])
            cp(khtm[:nt, j, :], pt[:nt, 0:512], [rt], [r_khtm[j]], eng="act")

        def s_update(j, ci):
            p = ci % cps
            rows = slice(p * C, p * C + C)
            pb, rb = gbank()
            for hb in range(4):
                mm(pb[:, hb * 128:(hb + 1) * 128], khtm[rows, j, hb * 128:(hb + 1) * 128], vh[rows, j, hb * 128:(hb + 1) * 128],
                   True, True, [r_khtm[j]] + rl_vh[j], [rb])
            tt(Sst[:], Sst[:], dec[:, 1, :, ci:ci + 1].to_broadcast([128, 4, 128]), ALU.mult, [r_S, r_dec], [r_S])
            tt(Sst[:].rearrange("p h v -> p (h v)"), Sst[:].rearrange("p h v -> p (h v)"), pb[:, :], ALU.add, [r_S, rb], [r_S])

        if mode == "scan":
            wv_f, wv_i = W1v[0], W1v[1]
            if part != "back":
                for hb in range(4):
                    proj_fm(wv_f, r_wslot[0], hb, lambda pb, rb, hb=hb: act(sg.rearrange("p h t -> p (h t)")[:, hb * ntok:(hb + 1) * ntok], pb[:, 0:ntok], AF.Tanh, [rb], rl_sg[hb], scale=0.5))
                for j in range(nsub):
                    proj_tm(wv_i, r_wslot[1], j, lambda pb, rb, j, nt: cp(vh[:nt, j, :], pb[:nt, :], [rb], rl_vh[j], eng="act"))
                if kv:
                    kv_proj_only()
            if part != "front":
                hgrn_gates_all()
                for j in range(nsub):
                    khat_transpose(j)
                    for p in range(cps):
                        s_update(j, j * cps + p)
            return

        def ev_q(ct):
            return lambda pb, rb: act(qT[:, ct, 0:ntok], pb[:, 0:ntok], AF.Copy, [rb], [r_qT], scale=0.125)

        def ev_k(ct):
            def f(pb, rb):
                for j in range(nsub):
                    cp(kTr[:, ct, slot_of(j), 0:nts[j]], pb[:, j * 128:j * 128 + nts[j]], [rb], [r_kT[slot_of(j)]], eng="act")
                if final_kv is not None:
                    cp(kst[:, ct, 0:ntok], pb[:, 0:ntok], [rb], rl_kst)
            return f

        def ev_v(pb, rb, j, nt):
            s_ = slot_of(j)
            cp(Vr[:nt, s_, :, 0:64], pb[:nt, :].rearrange("p (h d) -> p h d", h=8), [rb], [r_V[s_]], eng="act")
            if mode == "main" or sample:
                cp(Vr[:nt, s_, :, 64:65], onesT[:nt, 0:8].unsqueeze(2), [r_ones], [r_V[s_]], eng="pool")
            else:
                cp(Vr[:nt, s_, :, 64:65], hvt[:nt, 0:1].unsqueeze(2).to_broadcast([nt, 8, 1]), [r_hv], [r_V[s_]], eng="pool")
            if final_kv is not None:
                cp(vst[:nt, j, :], pb[:nt, :], [rb], rl_vst[j])
                dma("act", final_kv[1].ap()[final_kv[2] + j * 128:final_kv[2] + j * 128 + nt, :], vst[:nt, j, :], rl_vst[j], ())

        wv, rw, sl = w_next("in3")
        for hb in range(4):
            proj_fm(wv, rw, hb, lambda pb, rb, hb=hb: act(siluq.rearrange("p h t -> p (h t)")[:, hb * ntok:(hb + 1) * ntok], pb[:, 0:ntok], AF.Silu, [rb], [r_sq[hb]]))
        w_done(sl)
        wv, rw, sl = w_next("in4")
        for hb in range(4):
            proj_fm(wv, rw, hb, lambda pb, rb, hb=hb: act(sg.rearrange("p h t -> p (h t)")[:, hb * ntok:(hb + 1) * ntok], pb[:, 0:ntok], AF.Tanh, [rb], rl_sg[hb], scale=0.5))
        w_done(sl)
        wv, rw, sl = w_next("in5")
        for j in range(nsub):
            proj_tm(wv, rw, j, lambda pb, rb, j, nt: cp(vh[:nt, j, :], pb[:nt, :], [rb], rl_vh[j], eng="act"))
        w_done(sl)
        wv, rw, sl = w_next("in6")
        for j in range(nsub):
            proj_tm(wv, rw, j, lambda pb, rb, j, nt: act(sgb[:nt, j, :], pb[:nt, :], AF.Silu, [rb], [r_sgb[j]]))
        w_done(sl)
        wv, rw, sl = w_next("in0")
        for ct in range(4):
            proj_fm(wv, rw, ct, ev_q(ct))
        w_done(sl)
        wv, rw, sl = w_next("in1")
        for ct in range(4):
            proj_fm(wv, rw, ct, ev_k(ct))
        w_done(sl)
        if final_kv is not None:
            for ct in range(4):
                dma("act", final_kv[0].ap()[ct, :, final_kv[2]:final_kv[2] + ntok], kst[:, ct, 0:ntok], rl_kst, ())
        wv, rw, sl = w_next("in2")
        for j in range(nsub):
            proj_tm(wv, rw, j, ev_v)
        w_done(sl)

        att_steps, z_steps, h_steps = [], [], []

        zst = {}

        def z_step(zi):
            gi, ct = divmod(zi, 4)
            if ct == 0:
                zst["w"] = w_next("in%d" % (7 + gi))
            wv_, rw_, sl_ = zst["w"]
            proj_fm(wv_, rw_, ct, lambda pb, rb: act(zs[:, zi, 0:ntok], pb[:, 0:ntok], AF.Tanh, [rb], [r_zs[zi]], scale=0.5))
            if ct == 3:
                w_done(sl_)
        for zi in range(16):
            z_steps.append(lambda zi=zi: z_step(zi))

        def make_att(j):
            nq = nts[j]
            if sample:
                ktiles = [(0, 128), (1, 128), (2, 128), (3, 128), (4, 16)]
            else:
                ktiles = [((gt0 + j - 4 + kt) % 6, 128) for kt in range(5)]
            pend = []

            def pv(h, pslot):
                for kt, (s_, nk) in enumerate(ktiles):
                    mm(ps_o[:nq, (h % 4) * 65:(h % 4) * 65 + 65], Pb[pslot][:nk, kt, 0:nq], Vr[:nk, s_, h, :], kt == 0, kt == 4,
                       [r_Pb[pslot], r_V[s_]], [r_o])

            def normalize(half):
                o3 = ps_o[:nq, 0:260].rearrange("p (h d) -> p h d", h=4)
                ts(rec[:nq, half * 4:half * 4 + 4].unsqueeze(2), o3[:, :, 64:65], 1e-30, None, ALU.max, None, [r_o], [r_rec])
                recip(rec[:nq, half * 4:half * 4 + 4], rec[:nq, half * 4:half * 4 + 4], [r_rec], [r_rec])
                tt(oa[:nq, j, half * 256:half * 256 + 256].rearrange("p (h d) -> p h d", h=4), o3[:, :, 0:64],
                   rec[:nq, half * 4:half * 4 + 4].unsqueeze(2).to_broadcast([nq, 4, 64]), ALU.mult, [r_o, r_rec], [r_oa[j]])

            def head(h):
                hp, r0 = h // 2, (h % 2) * 64
                ai = cnt["a"] % 2
                cnt["a"] += 1
                offA = ai * 512
                offB = 1024 + ai * 128
                for kt in (4, 0, 1, 2, 3):
                    s_, nk = ktiles[kt]
                    o_ = offB if kt == 4 else offA + kt * 128
                    mm(ps_att[:nk, o_:o_ + nq], kTr[r0:r0 + 64, hp, s_, 0:nk], qT[r0:r0 + 64, hp, j * 128:j * 128 + nq],
                       True, True, [r_kT[s_], r_qT], [r_attB if kt == 4 else r_att[ai]])
                pi = cnt["p"] % 3
                cnt["p"] += 1
                sattA = ps_att[:, offA:offA + 512].rearrange("p (k q) -> p k q", k=4)
                sattB = ps_att[:, offB:offB + 128]
                if sample:
                    act(Pb[pi][:16, 4, 0:nq], sattB[:16, 0:nq], AF.Exp, [r_attB], [r_Pb[pi]])
                    act(Pb[pi][:, 0:4, 0:nq], sattA[:, :, 0:nq], AF.Exp, [r_att[ai]], [r_Pb[pi]])
                    tt(Pb[pi][:, 0:4, 0:nq], Pb[pi][:, 0:4, 0:nq], ET[:, 0:4, h, 0:nq], ALU.mult, [r_Pb[pi], r_ET], [r_Pb[pi]], eng="pool")
                    tt(Pb[pi][:16, 4, 0:nq], Pb[pi][:16, 4, 0:nq], ET[:16, 4, h, 0:nq], ALU.mult, [r_Pb[pi], r_ET], [r_Pb[pi]], eng="pool")
                else:
                    act(Pb[pi][:, 4, :], sattB, AF.Exp, [r_attB], [r_Pb[pi]])
                    act(Pb[pi][:, 0:4, :], sattA, AF.Exp, [r_att[ai]], [r_Pb[pi]])
                    tt(Pb[pi][:, :, :], Pb[pi][:, :, :], ET[:, :, h, :], ALU.mult, [r_Pb[pi], r_ET], [r_Pb[pi]], eng="pool")
                if pend:
                    ph, ppi = pend.pop(0)
                    pv(ph, ppi)
                    if ph == 3:
                        normalize(0)
                pend.append((h, pi))

            def tail():
                ph, ppi = pend.pop(0)
                pv(ph, ppi)
                normalize(1)
                pt, rt = tbank()
                for kc in range(4):
                    S.emit("pe", lambda e, kc=kc, pt=pt: e.transpose(out=pt[:, kc * 128:kc * 128 + nq], in_=oa[:nq, j, kc * 128:(kc + 1) * 128],
                                                                      identity=identb[:nq, :nq]), [r_oa[j], r_ident], [rt])
                cp(oaT[:, :, j * 128:j * 128 + nq], pt[:, 0:512].rearrange("p (k t) -> p k t", k=4)[:, :, 0:nq], [rt], [r_oaT[j]], eng="act")
            for h in range(8):
                att_steps.append(lambda h=h: head(h))
            att_steps.append(tail)
        for j in range(nsub):
            make_att(j)

        def make_h(j):
            nt = nts[j]

            def h_at():
                pb, rb = gbank()
                for hb in range(4):
                    if sample:
                        qrhs = Zq[:, hb, 0, 0:nt]
                    else:
                        base = Zq[:, hb, 2 * j, 0:64]
                        qrhs = AP(base.tensor, base.offset, [list(base.ap[0]), [192, 2], [1, 64]])
                    mm(pb[:nt, hb * 128:hb * 128 + nt], ktT[:, hb, j * 128:j * 128 + nt], qrhs, True, True, [r_ktT[hb], r_Zq[hb]], [rb])
                tt(AT[:nt, :, 0:nt], pb[:nt, :].rearrange("p (h t) -> p h t", h=4)[:, :, 0:nt],
                   maskb[:nt, 0:nt].unsqueeze(1).to_broadcast([nt, 4, nt]), ALU.mult, [rb, r_mask], [r_AT])

            def h_chunk(p):
                ci = j * cps + p
                tt(Sp[:, p], Sst[:], dec[:, 0, :, ci:ci + 1].to_broadcast([128, 4, 128]), ALU.mult, [r_S, r_dec], [r_Sp[p]])
                s_update(j, ci)

            def h_out():
                ob_, rob = gbank()
                for hb in range(4):
                    for p in range(cps):
                        zl = Zq[:, hb, 0, 0:nt] if sample else Zq[:, hb, 2 * j + p, :]
                        mm(ob_[:nt, hb * 128:(hb + 1) * 128], zl, Sp[:, p, hb, :], p == 0, False, [r_Zq[hb], r_Sp[p]], [rob])
                    mm(ob_[:nt, hb * 128:(hb + 1) * 128], AT[:nt, hb, 0:nt], vh[:nt, j, hb * 128:(hb + 1) * 128], False, True, [r_AT] + rl_vh[j], [rob])
                act(sqb[:nt, :], ob_[:nt, :], AF.Square, [rob], [r_sqb])
                S.emit("dve", lambda e: e.tensor_reduce(out=ssq[:nt, 0:4], in_=sqb[:nt, :].rearrange("p (h v) -> p h v", h=4), axis=AX.X, op=ALU.add),
                       [r_sqb], [r_ssq])
                ts(ssq[:nt, :], ssq[:nt, :], 1.0 / 128, EPS, ALU.mult, ALU.add, [r_ssq], [r_ssq], eng="pool")
                tt(ssq[:nt, :], ssq[:nt, :], nhalf[:nt, 0:4], ALU.pow, [r_ssq, r_nh], [r_ssq], eng="pool")
                tt(t1[:nt, :].rearrange("p (h v) -> p h v", h=4), ob_[:nt, :].rearrange("p (h v) -> p h v", h=4),
                   ssq[:nt, 0:4].unsqueeze(2).to_broadcast([nt, 4, 128]), ALU.mult, [rob, r_ssq], [r_t1])
                tt(gG[:nt, :], sgb[:nt, j, :], gnrep[:nt].rearrange("p h v -> p (h v)"), ALU.mult, [r_sgb[j], r_gn], [r_gG], eng="pool")
                tt(ob[:nt, :], t1[:nt, :], gG[:nt, :], ALU.mult, [r_t1, r_gG], [r_ob])

            def h_tr():
                pt, rt = tbank()
                for hb in range(4):
                    S.emit("pe", lambda e, hb=hb, pt=pt: e.transpose(out=pt[:, hb * 128:hb * 128 + nt], in_=ob[:nt, hb * 128:(hb + 1) * 128],
                                                                      identity=identb[:nt, :nt]), [r_ob, r_ident], [rt])
                cp(obT[:, :, j * 128:j * 128 + nt], pt[:, 0:512].rearrange("p (k t) -> p k t", k=4)[:, :, 0:nt], [rt], [r_obT[j]], eng="act")
            h_steps.append(lambda: khat_transpose(j))
            h_steps.append(h_at)
            for p in range(cps):
                h_steps.append(lambda p=p: h_chunk(p))
            h_steps.append(h_out)
            h_steps.append(h_tr)
        for j in range(nsub):
            make_h(j)

        def run_steps(steps, pool, tr, pre=None):
            def f():
                gstate["pool"] = pool
                gstate["tr"] = tr
                if pre is not None:
                    pre()
                for st_ in steps:
                    st_()
            return f
        att_ops = S.capture(run_steps(att_steps, [], (0,)))
        z_ops = S.capture(run_steps(z_steps, gpool_full[0:1], (0,)))
        h_ops = S.capture(run_steps(h_steps, gpool_full[1:2], (1,), pre=hgrn_gates_all))
        S.emit_merged(att_ops, z_ops, h_ops)
        gstate["tr"] = (0, 1)

        gstate["pool"] = gpool_front + gpool_back
        wva, rwa, sla = w_next("a")
        wstate["consumed"] += 1
        wvb, rwb, slb = w_next("b")
        wstate["consumed"] -= 1
        for ct in range(8):
            pb, rb = gbank()
            for kc in range(4):
                mm(pb[:, 0:ntok], wva[:, kc, ct * 128:(ct + 1) * 128], oaT[:, kc, 0:ntok], kc == 0, kc == 3, [rwa] + r_oaT[:nsub], [rb])
            for kc in range(4):
                mm(pb[:, 256:256 + ntok], wvb[:, kc, ct * 128:(ct + 1) * 128], obT[:, kc, 0:ntok], kc == 0, kc == 3, [rwb] + r_obT[:nsub], [rb])
            stt(m1[:, 0:ntok], zs[:, ct, 0:ntok], 1.0, pb[:, 0:ntok], ALU.add, ALU.mult, [rb, r_zs[ct]], [r_m1])
            stt(m2[:, 0:ntok], zs[:, 8 + ct, 0:ntok], 1.0, pb[:, 256:256 + ntok], ALU.add, ALU.mult, [rb, r_zs[8 + ct]], [r_m2])
            tt(mT[:, ct, 0:ntok], m1[:, 0:ntok], m2[:, 0:ntok], ALU.add, [r_m1, r_m2], [r_mT], eng="pool")
        w_done(sla)
        w_done(slb)

        for n in range(2):
            wv, rw, sl = w_next("o%d" % n)
            for j in range(nsub):
                nt = nts[j]
                pb, rb = gbank()
                for kc in range(8):
                    mm(pb[:nt, :], mT[:, kc, j * 128:j * 128 + nt], wv[:, kc, :], kc == 0, kc == 7, [rw, r_mT], [rb])
                stt(Xb[:nt, j, n * 512:(n + 1) * 512], pb[:nt, :], 0.5, Xb[:nt, j, n * 512:(n + 1) * 512], ALU.mult, ALU.add, [rX[j], rb], [rX[j]])
            w_done(sl)
        norm_T(Xb, rX, gffn, r_gffn, 2, 2, ntok)
        hT = hTs[2]
        r_hT = r_hTs[2]
        rhT_all = r_hT[:nsub]

        ffn_pend = []
        for gi in range(6):
            ntile = 4 if gi < 5 else 2
            if gi == 2 and prenorm is not None:
                prenorm()
            wvg, rwg, slg = w_next("g%d" % gi)
            if mode != "pre":
                wstate["consumed"] += 1
                wvu, rwu, slu = w_next("u%d" % gi)
                wstate["consumed"] -= 1
            for ct in range(ntile):
                ft = gi * 4 + ct
                pb, rb = gbank()
                if mode == "pre":
                    for kc in range(8):
                        mm(pb[:, 0:2], wvg[:, kc, ct * 128:(ct + 1) * 128], hT[:, kc, ntok - 2:ntok], kc == 0, kc == 7, [rwg] + rhT_all, [rb])
                    cp(cprev[:, ft, :], pb[:, 0:2], [rb], [r_cprev])
                    continue
                for kc in range(8):
                    mm(pb[:, 0:ntok], wvg[:, kc, ct * 128:(ct + 1) * 128], hT[:, kc, 0:ntok], kc == 0, kc == 7, [rwg] + rhT_all, [rb])
                for kc in range(8):
                    mm(pb[:, 256:256 + ntok], wvu[:, kc, ct * 128:(ct + 1) * 128], hT[:, kc, 0:ntok], kc == 0, kc == 7, [rwu] + rhT_all, [rb])
                bi = ft % 2
                a_ = aT[bi]
                cp(a_[:, 0:2], cprev[:, ft, :], [r_cprev], [r_aT[bi]], eng="pool")
                act(a_[:, 2:2 + ntok], pb[:, 0:ntok], AF.Copy, [rb], [r_aT[bi]])
                cp(cprev[:, ft, :], a_[:, ntok:ntok + 2], [r_aT[bi]], [r_cprev], eng="pool")
                act(c1[bi][:, 0:ntok], pb[:, 0:ntok], AF.Identity, [rb, r_cw, r_cb], [r_c1[bi]], scale=cw[:, ft, 2:3], bias=cb[:, ft:ft + 1])
                stt(c2[bi][:, 0:ntok], a_[:, 1:1 + ntok], cw[:, ft, 1:2], c1[bi][:, 0:ntok], ALU.mult, ALU.add, [r_aT[bi], r_c1[bi], r_cw], [r_c2[bi]])
                stt(c1[bi][:, 0:ntok], a_[:, 0:ntok], cw[:, ft, 0:1], c2[bi][:, 0:ntok], ALU.mult, ALU.add, [r_aT[bi], r_c2[bi], r_cw], [r_c1[bi]])
                def fin(bi=bi, ft=ft, pb=pb, rb=rb):
                    act(c2[bi][:, 0:ntok], c1[bi][:, 0:ntok], AF.Gelu_apprx_tanh, [r_c1[bi]], [r_c2[bi]])
                    tt(gT[:, ft, 0:ntok], c2[bi][:, 0:ntok], pb[:, 256:256 + ntok], ALU.mult, [r_c2[bi], rb], [r_gT[ft]])
                if ffn_pend:
                    ffn_pend.pop(0)()
                ffn_pend.append(fin)
            w_done(slg)
            if mode != "pre":
                w_done(slu)
        if mode == "pre":
            return
        while ffn_pend:
            ffn_pend.pop(0)()

        for n in range(2):
            banks = [gbank() for _ in range(nsub)]
            for gk in range(3):
                wv, rw, sl = w_next("d%d_%d" % (n, gk))
                nk = 8 if gk < 2 else 6
                for kl in range(nk):
                    kc = gk * 8 + kl
                    for j in range(nsub):
                        nt = nts[j]
                        mm(banks[j][0][:nt, :], gT[:, kc, j * 128:j * 128 + nt], wv[:, kl, :], kc == 0, kc == NFT - 1, [rw, r_gT[kc]], [banks[j][1]])
                w_done(sl)
            for j in range(nsub):
                nt = nts[j]
                tt(Xb[:nt, j, n * 512:(n + 1) * 512], Xb[:nt, j, n * 512:(n + 1) * 512], banks[j][0][:nt, :], ALU.add, [rX[j], banks[j][1]], [rX[j]])
        for j in range(nsub):
            nt = nts[j]
            act(XN[:nt, j, :], Xb[:nt, j, :], AF.Square, [rX[j]], [r_XN[j], r_ss], accum_out=ss[:nt, 4 + j:5 + j])
            ts(ss[:nt, 4 + j:5 + j], ss[:nt, 4 + j:5 + j], 1.0 / D, EPS, ALU.mult, ALU.add, [r_ss], [r_ss], eng="pool")
            tt(ss[:nt, 4 + j:5 + j], ss[:nt, 4 + j:5 + j], nhalf[:nt, 0:1], ALU.pow, [r_ss, r_nh], [r_ss], eng="pool")
            stt(Xb[:nt, j, :], Xb[:nt, j, :], ss[:nt, 4 + j:5 + j], gfin[:nt, :], ALU.mult, ALU.mult, [rX[j], r_ss, r_gfin], [rX[j]])
            dma("act", out_row[j * 128:j * 128 + nt, :], Xb[:nt, j, :], [rX[j]], ())

    cur = {}

    def kv_proj_only():
        ntok, gt0 = cur["ntok"], cur["gt0"]
        for ct in range(4):
            pb, rb = gbank()
            for kc in range(8):
                mm(pb[:, 0:ntok], W1v[2][:, kc, ct * 128:(ct + 1) * 128], cur["hT"][:, kc, 0:ntok], kc == 0, kc == 7, [r_wslot[2]] + cur["r_hT"], [rb])
            for j in range(2):
                s_ = (gt0 + j) % 6
                cp(kTr[:, ct, s_, :], pb[:, j * 128:(j + 1) * 128], [rb], [r_kT[s_]], eng="act")
        for j in range(2):
            s_ = (gt0 + j) % 6
            pb, rb = gbank()
            for kc in range(8):
                mm(pb[:, :], cur["hT"][:, kc, j * 128:(j + 1) * 128], W1v[3][:, kc, :], kc == 0, kc == 7, [r_wslot[3], cur["r_hT"][j]], [rb])
            cp(Vr[:, s_, :, 0:64], pb[:, :].rearrange("p (h d) -> p h d", h=8), [rb], [r_V[s_]], eng="act")
            cp(Vr[:, s_, :, 64:65], hvt[:, 0:1].unsqueeze(2).to_broadcast([128, 8, 1]), [r_hv], [r_V[s_]], eng="pool")


    xa = xin.ap()
    nscan = n_scan
    nmain = n_main
    plan = []
    for m in range(nscan):
        plan.append(("scan", xa[m * T:(m + 1) * T, :], T, dict(gt0=-5 + 2 * (m - (nscan - 2)) + 6, kv=m >= nscan - 2)))
    if do_pre:
        plan.append(("pre", xa[NPRE:NPRE + PRE_T, :], PRE_T, dict(gt0=5)))
    for m in range(nmain):
        fk = (okT_d, ov_d, (m - (nmain - 2)) * T) if (m >= nmain - 2 and not NOFK) else None
        r0 = NPRE + PRE_T + m * T
        plan.append(("main", xa[r0:r0 + T, :], T, dict(gt0=2 * m + 6, out_row=y_d.ap()[m * T:(m + 1) * T, :], final_kv=fk)))
    if do_sample:
        plan.append(("sample", xs_d.ap(), 16, dict(gt0=0, out_row=ys_d.ap(), final_kv=(okTs_d, ovs_d, 0))))
    if plan:
        load_x(plan[0][1], plan[0][2], 0)
    if not do_conv:
        conv_jobs.clear()
    for i, (mode, xsrc, ntok, kw) in enumerate(plan):
        xb = i % 2
        nxt = None
        pren = None
        if i + 1 < len(plan):
            nxt = (lambda p=plan[i + 1], b=(i + 1) % 2: load_x(p[1], p[2], b))
            if mode == "main":
                pren = (lambda p=plan[i + 1], b=(i + 1) % 2: norm_T(X[b], r_X[b], gmix, r_gmix, 0, b, p[2]))
        normed = i > 0 and plan[i - 1][0] == "main"
        if mode == "scan":
            def front(k):
                md, xs_, nt_, kw_ = plan[k]
                cur["ntok"] = nt_
                cur["gt0"] = kw_["gt0"]
                emit_conv(2)
                nx = (lambda p=plan[k + 1], b=(k + 1) % 2: load_x(p[1], p[2], b)) if k + 1 < len(plan) else None
                macro_tile("scan", nt_, k % 2, hsel=k % 2, after_norm=nx, part="front", bsel=k % 2, **kw_)
            if i == 0:
                front(0)
            back_ops = S.capture(lambda: macro_tile("scan", ntok, xb, hsel=xb, part="back", bsel=xb, **kw))
            front_ops = S.capture(lambda: front(i + 1)) if i + 1 < nscan else []
            S.emit_merged(back_ops, front_ops)
            continue
        if i == nscan:
            emit_conv(len(conv_jobs))
            for s_ in range(NSLOT):
                w_issue(s_)
        if mode == "sample":
            dma("act", oS_d.ap().rearrange("h k v -> k h v"), Sst[:], [r_S], ())
            dma("act", oconv_d.ap(), cprev[:], [r_cprev], ())
            dma("pool", kTr[:, :, 0:4, :], ckT_d.ap().rearrange("p c (t k) -> p c t k", t=4), (), r_kT[0:4])
            for t_ in range(4):
                dma("pool", Vr[:, t_, :, 0:64], cv_d.ap()[:, t_ * 128:(t_ + 1) * 128, :].rearrange("h p d -> p h d"), (), [r_V[t_]])
                cp(Vr[:, t_, :, 64:65], onesT[:, 0:8].unsqueeze(2), [r_ones], [r_V[t_]], eng="pool")
            dma("sp", Sst[:], s0_d.ap().rearrange("h k v -> k h v"), (), [r_S])
            dma("sp", cprev[:], cprev_d.ap(), (), [r_cprev])
        macro_tile(mode, ntok, xb, hsel=xb, normed=normed, after_norm=nxt, prenorm=pren, **kw)
    emit_conv(len(conv_jobs))
    if not do_sample:
        dma("act", oS_d.ap().rearrange("h k v -> k h v"), Sst[:], [r_S], ())
        dma("act", oconv_d.ap(), cprev[:], [r_cprev], ())
    else:
        dma("act", oSs_d.ap().rearrange("h k v -> k h v"), Sst[:], [r_S], ())
        dma("act", oconvs_d.ap(), cprev[:], [r_cprev], ())
    for nm, getter in dumps:
        ap_, res_ = getter(locals())
        d_ = nc.dram_tensor("dbg_" + nm, list(ap_.shape), ap_.dtype if hasattr(ap_, "dtype") else F32, kind="ExternalOutput")
        dma("pool", d_.ap(), ap_, res_, ())

    S.finalize(st)
    with nc.Block() as block:
        S.run(block)
    st.close()
    return nc


_NC_CACHE = {}


def kernel(x_prompt, x_sample, cache_attn_k, cache_attn_v, state_hgrn, state_ffn_conv,
           norm_mix_g, w_in, rel_bias, hgrn_lb_logits, hgrn_norm_g, w_branch_a, w_branch_b, w_out,
           norm_ffn_g, w_ffn_gate, w_ffn_up, ffn_conv_w, ffn_conv_b, w_ffn_down, norm_final_g):
    f32 = np.float32
    A = lambda a: np.ascontiguousarray(np.asarray(a, dtype=f32))
    x_prompt = A(x_prompt)
    if "nc" not in _NC_CACHE:
        _NC_CACHE["nc"] = build_program()
    nc = _NC_CACHE["nc"]
    s_idx = np.arange(128)
    mask = ((s_idx[:, None] // 64 == s_idx[None, :] // 64) & (s_idx[:, None] <= s_idx[None, :])).astype(f32)
    shared = {
        "w_in": A(w_in[0]), "w_a": A(w_branch_a[0]), "w_b": A(w_branch_b[0]), "w_o": A(w_out[0]),
        "w_g": A(w_ffn_gate[0]), "w_u": A(w_ffn_up[0]), "w_d": A(w_ffn_down[0]),
        "gmix": A(np.asarray(norm_mix_g[0]).reshape(8, 128).T), "gffn": A(np.asarray(norm_ffn_g[0]).reshape(8, 128).T),
        "gfin": A(norm_final_g), "rel": A(rel_bias[0]),
        "lbl": A(np.asarray(hgrn_lb_logits).reshape(2, 4, 128).transpose(2, 0, 1)),
        "gn": A(hgrn_norm_g[0]),
        "cw": A(np.asarray(ffn_conv_w[0]).reshape(3, NFT, 128).transpose(2, 1, 0)),
        "cb": A(np.asarray(ffn_conv_b[0]).reshape(NFT, 128).T),
        "ident": np.eye(128, dtype=f32), "mask": mask, "jmat": np.ascontiguousarray(np.eye(128, dtype=f32)[::-1]),
    }
    in_maps = []
    for c in range(8):
        b, j = divmod(c, 4)
        s = j * SEG
        lo = s - PRE_T - NPRE
        xin = np.zeros((NTOK_IN, D), f32)
        a0 = max(lo, 0)
        xin[a0 - lo:] = x_prompt[b, a0:s + SEG]
        m = dict(shared)
        m["xin"] = xin
        m["hv"] = np.full((128, 1), 1.0 if j > 0 else 0.0, f32)
        m["xs"] = A(x_sample[c])
        m["ckT"] = A(np.asarray(cache_attn_k[0, c]).transpose(0, 2, 1).reshape(4, 128, 512).transpose(1, 0, 2))
        m["cv"] = A(cache_attn_v[0, c])
        m["s0"] = A(state_hgrn[0, c])
        m["cprev"] = A(np.asarray(state_ffn_conv[0, c]).reshape(2, NFT, 128).transpose(2, 1, 0))
        in_maps.append(m)
    res = run_bass_kernel_spmd(nc, in_maps, core_ids=list(range(8)))
    R_ = res.results
    B = 2
    y_prompt = np.stack([np.concatenate([R_[b * 4 + j]["y"] for j in range(4)], axis=0) for b in range(B)])
    y_sample = np.stack([R_[c]["ys"] for c in range(8)])

    def kT_to_rows(a):
        n = a.shape[-1]
        return a.reshape(8, 64, n).transpose(0, 2, 1)

    def v_to_rows(a):
        n = a.shape[0]
        return a.reshape(n, 8, 64).transpose(1, 0, 2)

    def conv_rows(a):
        return a.transpose(2, 1, 0).reshape(2, DFF)

    last = [3, 7]
    new_k_p = np.stack([kT_to_rows(R_[c]["okT"]) for c in last])[None]
    new_v_p = np.stack([v_to_rows(R_[c]["ov"]) for c in last])[None]
    hg_p = np.stack([R_[c]["oS"] for c in last])[None]
    cv_p = np.stack([conv_rows(R_[c]["oconv"]) for c in last])[None]
    new_k_s = np.stack([kT_to_rows(R_[c]["okTs"]) for c in range(8)])[None]
    new_v_s = np.stack([v_to_rows(R_[c]["ovs"]) for c in range(8)])[None]
    hg_s = np.stack([R_[c]["oSs"] for c in range(8)])[None]
    cv_s = np.stack([conv_rows(R_[c]["oconvs"]) for c in range(8)])[None]
    outs = (y_prompt, y_sample, new_k_p, new_v_p, hg_p, cv_p, new_k_s, new_v_s, hg_s, cv_s)
    return tuple(np.ascontiguousarray(o, dtype=f32) for o in outs)
```

```python
import numpy as np
from contextlib import ExitStack
import concourse.bass as bass
import concourse.mybir as mybir
from concourse.bass import AP
from concourse.bass_utils import run_bass_kernel_spmd

F32 = mybir.dt.float32
BF16 = mybir.dt.bfloat16
AF = mybir.ActivationFunctionType
ALU = mybir.AluOpType
AX = mybir.AxisListType

D = 1024
DFF = 2816
NFT = 22
SEG = 4096
NPRE = 12288
PRE_T = 128
NTOK_IN = NPRE + PRE_T + SEG
T = 256
EPS = 1e-6
NSLOT = 6
STOP = 99
STOPMODE = 'main'
NOFK = False
SLOT_E = 4096


class Res:
    __slots__ = ("name", "w", "r", "excl")

    def __init__(self, name, excl=False):
        self.name = name
        self.w = None
        self.r = {}
        self.excl = excl


class Op:
    __slots__ = ("eng", "fn", "deps", "is_dma", "needs_inc", "sem", "val", "idx")


class Sched:
    ENGS = ("pe", "act", "dve", "pool", "sp")

    def __init__(self, nc, n_dma_sems=8):
        self.nc = nc
        self.ops = []
        self.n_dma_sems = n_dma_sems
        self.cap = None

    def capture(self, fn):
        assert self.cap is None
        self.cap = []
        try:
            fn()
            return self.cap
        finally:
            self.cap = None

    DUR = {"pe": 0.15, "act": 0.75, "dve": 0.85, "pool": 1.1, "sp": 0.3}

    def emit_merged(self, *lists):
        eng_free = {}
        ready = {}
        rdone = {}
        LAT = 0.25

        def start_of(op):
            eng, fn, reads, writes, dma, cost = op
            t = eng_free.get(eng, 0.0)
            for r in reads:
                t = max(t, ready.get(id(r), 0.0) + LAT)
            for w in writes:
                t = max(t, ready.get(id(w), 0.0) + LAT, rdone.get(id(w), 0.0) + LAT)
            return t

        def commit(op, t):
            eng, fn, reads, writes, dma, cost = op
            d = 2.5 if dma else (cost if cost is not None else self.DUR[eng])
            eng_free[eng] = t + (0.1 if dma else d)
            for r in reads:
                rdone[id(r)] = max(rdone.get(id(r), 0.0), t + d)
            for w in writes:
                ready[id(w)] = t + d
            self.emit(eng, fn, reads, writes, dma)

        pos = [0] * len(lists)
        while True:
            best, bt = -1, None
            for k, l in enumerate(lists):
                if pos[k] < len(l):
                    t = start_of(l[pos[k]])
                    if bt is None or t < bt:
                        best, bt = k, t
            if best < 0:
                break
            commit(lists[best][pos[best]], bt)
            pos[best] += 1

    def emit(self, eng, fn, reads=(), writes=(), dma=False, cost=None):
        if self.cap is not None:
            self.cap.append((eng, fn, tuple(reads), tuple(writes), dma, cost))
            return None
        op = Op()
        op.eng = eng
        op.fn = fn
        op.is_dma = dma
        op.needs_inc = dma
        op.sem = None
        op.val = 0
        op.idx = len(self.ops)
        deps = {}
        xr = [r for r in reads if r.excl]
        if xr:
            reads = [r for r in reads if not r.excl]
            writes = list(writes) + [r for r in xr if r not in writes]
        for r in reads:
            if r.w is not None:
                deps[r.w.idx] = r.w
        for w in writes:
            if w.w is not None:
                deps[w.w.idx] = w.w
            for o in w.r.values():
                deps[o.idx] = o
        op.deps = list(deps.values())
        for r in reads:
            key = (eng, op.idx) if dma else (eng, -1)
            r.r[key] = op
        for w in writes:
            w.w = op
            w.r = {}
        self.ops.append(op)
        return op

    def finalize(self, stack):
        nc = self.nc
        for op in self.ops:
            for d in op.deps:
                if d.is_dma or d.eng != op.eng or op.eng != "pe" or op.is_dma:
                    d.needs_inc = True
        csem = {e: stack.enter_context(nc.semaphore("cs_" + e)) for e in ("pe", "act", "dve", "pool")}
        dsem = {e: [stack.enter_context(nc.semaphore("ds_%s%d" % (e, i))) for i in range(self.n_dma_sems)]
                for e in ("sp", "pool", "act")}
        ccount = {e: 0 for e in csem}
        dstate = {e: [None] * self.n_dma_sems for e in dsem}
        duse = {e: [0] * self.n_dma_sems for e in dsem}
        drr = {e: 0 for e in dsem}
        waited = {e: {} for e in self.ENGS}
        streams = {e: [] for e in self.ENGS}
        for op in self.ops:
            waits = []
            e = op.eng
            extra = []
            if op.is_dma:
                k = drr[e] % self.n_dma_sems
                drr[e] += 1
                prev = dstate[e][k]
                if prev is not None:
                    extra.append(prev)
                duse[e][k] += 1
                op.sem = dsem[e][k]
                op.val = 16 * duse[e][k]
                dstate[e][k] = op
            elif op.needs_inc:
                ccount[e] += 1
                op.sem = csem[e]
                op.val = ccount[e]
            for d in op.deps + extra:
                if (not d.is_dma) and d.eng == e and e == "pe" and not op.is_dma:
                    continue
                key = id(d.sem)
                if waited[e].get(key, 0) >= d.val:
                    continue
                waited[e][key] = d.val
                waits.append((d.sem, d.val))
            streams[e].append((waits, op))
        fin = []
        for e in dsem:
            for k in range(self.n_dma_sems):
                if duse[e][k] and waited["sp"].get(id(dsem[e][k]), 0) < 16 * duse[e][k]:
                    fin.append((dsem[e][k], 16 * duse[e][k]))
        self.streams = streams
        self.fin = fin

    def run(self, block):
        streams = self.streams
        fin = self.fin

        def body(name):
            def f(eng):
                for waits, op in streams[name]:
                    for s, v in waits:
                        eng.wait_ge(s, v)
                    inst = op.fn(eng)
                    if op.sem is not None:
                        inst.then_inc(op.sem, 16 if op.is_dma else 1)
                if name == "sp":
                    for s, v in fin:
                        eng.wait_ge(s, v)
            return f
        block.tensor(body("pe"))
        block.scalar(body("act"))
        block.vector(body("dve"))
        block.gpsimd(body("pool"))
        block.sync(body("sp"))


def build_program(n_scan=NPRE // T, n_main=SEG // T, do_pre=True, do_sample=True, dumps=(), do_conv=True, init_stop=99):
    NPRE = n_scan * T
    NTOK_IN = NPRE + PRE_T + n_main * T
    nc = bass.Bass("TRN2", target_bir_lowering=False)
    st = ExitStack()
    S = Sched(nc)

    def din(name, shape):
        return nc.dram_tensor(name, list(shape), F32, kind="ExternalInput")

    def dout(name, shape):
        return nc.dram_tensor(name, list(shape), F32, kind="ExternalOutput")

    xin = din("xin", [NTOK_IN, D])
    hv_d = din("hv", [128, 1])
    xs_d = din("xs", [16, D])
    ckT_d = din("ckT", [128, 4, 512])
    cv_d = din("cv", [8, 512, 64])
    s0_d = din("s0", [4, 128, 128])
    cprev_d = din("cprev", [128, NFT, 2])
    w_in_d = din("w_in", [D, 5632])
    w_a_d = din("w_a", [512, D])
    w_b_d = din("w_b", [512, D])
    w_o_d = din("w_o", [D, D])
    w_g_d = din("w_g", [D, DFF])
    w_u_d = din("w_u", [D, DFF])
    w_d_d = din("w_d", [DFF, D])
    gmix_d = din("gmix", [128, 8])
    gffn_d = din("gffn", [128, 8])
    gfin_d = din("gfin", [D])
    rel_d = din("rel", [8, 192])
    lbl_d = din("lbl", [128, 2, 4])
    gn_d = din("gn", [128])
    cw_d = din("cw", [128, NFT, 3])
    cb_d = din("cb", [128, NFT])
    ident_d = din("ident", [128, 128])
    mask_d = din("mask", [128, 128])
    jmat_d = din("jmat", [128, 128])

    y_d = dout("y", [SEG, D])
    ys_d = dout("ys", [16, D])
    okT_d = dout("okT", [4, 128, 512])
    ov_d = dout("ov", [512, 512])
    oS_d = dout("oS", [4, 128, 128])
    oconv_d = dout("oconv", [128, NFT, 2])
    okTs_d = dout("okTs", [4, 128, 16])
    ovs_d = dout("ovs", [16, 512])
    oSs_d = dout("oSs", [4, 128, 128])
    oconvs_d = dout("oconvs", [128, NFT, 2])

    def scratch(name, shape, dt=BF16):
        return nc.dram_tensor(name, list(shape), dt, kind="Internal")

    wb_in = scratch("wb_in", [D, 5632])
    wb_a = scratch("wb_a", [512, D])
    wb_b = scratch("wb_b", [512, D])
    wb_o = scratch("wb_o", [D, D])
    wb_g = scratch("wb_g", [D, DFF])
    wb_u = scratch("wb_u", [D, DFF])
    wb_d = scratch("wb_d", [DFF, D])
    ext_d = scratch("ext_d", [8, 768], F32)

    def sb(name, shape, dt=F32):
        return st.enter_context(nc.sbuf_tensor(name, list(shape), dt))

    def R(name, excl=False):
        return Res(name, excl)

    def fsz(ap):
        n = 1
        for d_ in list(ap.shape)[1:]:
            n *= int(d_)
        return n

    def ecost(eng, ap):
        n = fsz(ap)
        return {"act": 0.25 + n / 1000.0, "dve": 0.1 + n / 850.0, "pool": 0.2 + n / 480.0}[eng]

    def act(out, in_, func, reads, writes, **kw):
        S.emit("act", lambda e: e.activation(out=out, in_=in_, func=func, **kw), reads, writes, cost=ecost("act", out))

    def tt(out, in0, in1, op, reads, writes, eng="dve"):
        S.emit(eng, lambda e: e.tensor_tensor(out=out, in0=in0, in1=in1, op=op), reads, writes, cost=ecost(eng, out))

    def ts(out, in0, s1, s2, op0, op1, reads, writes, eng="dve"):
        if op1 is None:
            S.emit(eng, lambda e: e.tensor_scalar(out=out, in0=in0, scalar1=s1, scalar2=None, op0=op0), reads, writes, cost=ecost(eng, out))
        else:
            S.emit(eng, lambda e: e.tensor_scalar(out=out, in0=in0, scalar1=s1, scalar2=s2, op0=op0, op1=op1), reads, writes, cost=ecost(eng, out))

    def stt(out, in0, scalar, in1, op0, op1, reads, writes):
        S.emit("dve", lambda e: e.scalar_tensor_tensor(out=out, in0=in0, scalar=scalar, in1=in1, op0=op0, op1=op1), reads, writes, cost=ecost("dve", out))

    def cp(out, in_, reads, writes, eng="dve"):
        if eng == "act":
            act(out, in_, AF.Copy, reads, writes)
        else:
            S.emit(eng, lambda e: e.tensor_copy(out=out, in_=in_), reads, writes, cost=ecost(eng, out))

    def recip(out, in_, reads, writes):
        S.emit("dve", lambda e: e.reciprocal(out=out, in_=in_), reads, writes)

    def mset(ap, val, writes, eng="pool"):
        S.emit(eng, lambda e: e.memset(ap, val), (), writes)

    def mm(out, lhsT, rhs, start, stop, reads, writes):
        S.emit("pe", lambda e: e.matmul(out, lhsT=lhsT, rhs=rhs, start=start, stop=stop), reads, writes, cost=0.05 + fsz(rhs) / 2000.0)

    def dma(eng, out, in_, reads, writes):
        S.emit(eng, lambda e: e.dma_start(out=out, in_=in_), reads, writes, dma=True)

    identb = sb("identb", [128, 128], BF16); r_ident = R("ident")
    maskb = sb("maskb", [128, 128], BF16); r_mask = R("mask")
    jb = sb("jb", [128, 128], BF16); r_j = R("j")
    epst = sb("epst", [128, 1]); r_eps = R("eps")
    onesT = sb("onesT", [128, 8]); r_ones = R("ones")
    hvt = sb("hvt", [128, 1]); r_hv = R("hv")
    gmix = sb("gmix_s", [128, 8]); r_gmix = R("gmix")
    gffn = sb("gffn_s", [128, 8]); r_gffn = R("gffn")
    gfin = sb("gfin_s", [128, D]); r_gfin = R("gfin")
    gnrep = sb("gnrep", [128, 4, 128]); r_gn = R("gn")
    cw = sb("cw_s", [128, NFT, 3]); r_cw = R("cw")
    cb = sb("cb_s", [128, NFT]); r_cb = R("cb")
    lbl = sb("lbl_s", [128, 2, 4]); r_lbl = R("lbl")
    lb = sb("lb_s", [128, 4]); oml = sb("oml_s", [128, 4]); r_lb = R("lb")
    ET = sb("ET", [128, 5, 8, 128], BF16); r_ET = R("ET")

    dma("pool", identb[:], ident_d.ap(), (), [r_ident])
    dma("pool", maskb[:], mask_d.ap(), (), [r_mask])
    dma("pool", jb[:], jmat_d.ap(), (), [r_j])
    mset(epst[:], EPS, [r_eps])
    nhalf = sb("nhalf", [128, 8]); r_nh = R("nhalf")
    mset(nhalf[:], -0.5, [r_nh])
    mset(onesT[:], 1.0, [r_ones])
    dma("sp", hvt[:], hv_d.ap(), (), [r_hv])
    dma("sp", gmix[:], gmix_d.ap(), (), [r_gmix])
    dma("sp", gffn[:], gffn_d.ap(), (), [r_gffn])
    dma("sp", gfin[:], AP(gfin_d, 0, [[0, 128], [1, D]]), (), [r_gfin])
    dma("sp", gnrep[:], AP(gn_d, 0, [[0, 128], [0, 4], [1, 128]]), (), [r_gn])
    dma("sp", cw[:], cw_d.ap(), (), [r_cw])
    dma("sp", cb[:], cb_d.ap(), (), [r_cb])
    dma("sp", lbl[:], lbl_d.ap(), (), [r_lbl])

    if init_stop <= 1:
        S.finalize(st)
        with nc.Block() as block:
            S.run(block)
        st.close()
        return nc
    ps_att = st.enter_context(nc.psum_tensor("ps_att", [128, 1536], F32))
    r_att = [R("att0", True), R("att1", True)]
    r_attB = R("attB", True)
    NGEN = 2
    ps_gen = [st.enter_context(nc.psum_tensor("ps_g%d" % i, [128, 512], F32)) for i in range(NGEN)]
    r_gen = [R("g%d" % i, True) for i in range(NGEN)]
    ps_o = st.enter_context(nc.psum_tensor("ps_o", [128, 512], F32))
    r_o = R("ps_o", True)
    ps_tr = [st.enter_context(nc.psum_tensor("ps_t%d" % i, [128, 1024], BF16)) for i in range(2)]
    r_tr = [R("t%d" % i, True) for i in range(2)]
    cnt = {"g": 0, "t": 0, "a": 0, "p": 0}

    gpool_full = [(ps_gen[i], r_gen[i]) for i in range(NGEN)]
    gpool_front = gpool_full + [(ps_o, r_o)]
    gpool_back = [(ps_att[:, 0:512], r_att[0]), (ps_att[:, 512:1024], r_att[1]), (ps_att[:, 1024:1536], r_attB)]
    gstate = {"pool": gpool_full, "tr": (0, 1)}

    def gbank():
        pool = gstate["pool"]
        i = cnt["g"] % len(pool)
        cnt["g"] += 1
        return pool[i]

    def tbank():
        sel = gstate["tr"]
        i = sel[cnt["t"] % len(sel)]
        cnt["t"] += 1
        return ps_tr[i], r_tr[i]

    lbt = sb("lbt", [128, 4])
    tt(lbt[:], lbl[:, 1, :], lbl[:, 0, :], ALU.subtract, [r_lbl], [r_lb])
    act(lbt[:], lbt[:], AF.Exp, [r_lb], [r_lb])
    ts(lbt[:], lbt[:], 1.0, None, ALU.add, None, [r_lb], [r_lb])
    recip(lb[:], lbt[:], [r_lb], [r_lb])
    ts(oml[:], lb[:], -1.0, 1.0, ALU.mult, ALU.add, [r_lb], [r_lb])
    omlh = sb("omlh_s", [128, 4]); lbh = sb("lbh_s", [128, 4])
    ts(omlh[:], oml[:], 0.5, None, ALU.mult, None, [r_lb], [r_lb])
    tt(lbh[:], lb[:], omlh[:], ALU.add, [r_lb], [r_lb])

    if init_stop <= 2:
        S.finalize(st)
        with nc.Block() as block:
            S.run(block)
        st.close()
        return nc
    if init_stop <= 3:
        S.finalize(st)
        with nc.Block() as block:
            S.run(block)
        st.close()
        return nc
    r_wb = {}
    conv_jobs = []
    for name, src, dst, rows in (("in", w_in_d, wb_in, D), ("a", w_a_d, wb_a, 512), ("b", w_b_d, wb_b, 512),
                                 ("o", w_o_d, wb_o, D), ("g", w_g_d, wb_g, D), ("u", w_u_d, wb_u, D),
                                 ("d", w_d_d, wb_d, DFF)):
        r_wb[name] = R("wb_" + name)
        for r0 in range(0, rows, 128):
            conv_jobs.append((name, dst.ap()[r0:r0 + 128, :], src.ap()[r0:r0 + 128, :]))

    def emit_conv(n):
        for _ in range(n):
            if conv_jobs:
                name, d_, s_ = conv_jobs.pop(0)
                dma("pool", d_, s_, (), [r_wb[name]])

    wslot = [sb("wslot%d" % i, [128, SLOT_E], BF16) for i in range(NSLOT)]
    r_wslot = [R("wslot%d" % i) for i in range(NSLOT)]
    W1v = []
    for i, c0 in enumerate((2048, 2560, 512, 1024)):
        v_ = wslot[i][:, :].rearrange("p (k n) -> p k n", k=8)
        dma("pool", v_, w_in_d.ap()[:, c0:c0 + 512].rearrange("(k p) n -> p k n", p=128), (), [r_wslot[i]])
        W1v.append(v_)

    def wgroups(mode):
        g = []
        for i in (3, 4, 5, 6, 0, 1, 2, 7, 8, 9, 10):
            g.append(("in%d" % i, "in", wb_in.ap()[:, 512 * i:512 * i + 512].rearrange("(k p) n -> p k n", p=128), 8, 512))
        g.append(("a", "a", wb_a.ap().rearrange("(k p) n -> p k n", p=128), 4, 1024))
        g.append(("b", "b", wb_b.ap().rearrange("(k p) n -> p k n", p=128), 4, 1024))
        for n in range(2):
            g.append(("o%d" % n, "o", wb_o.ap()[:, 512 * n:512 * n + 512].rearrange("(k p) n -> p k n", p=128), 8, 512))
        for gi in range(6):
            nc_ = 512 if gi < 5 else 256
            g.append(("g%d" % gi, "g", wb_g.ap()[:, 512 * gi:512 * gi + nc_].rearrange("(k p) n -> p k n", p=128), 8, nc_))
            if mode != "pre":
                g.append(("u%d" % gi, "u", wb_u.ap()[:, 512 * gi:512 * gi + nc_].rearrange("(k p) n -> p k n", p=128), 8, nc_))
        if mode != "pre":
            for n in range(2):
                for gk in range(3):
                    nk = 8 if gk < 2 else 6
                    g.append(("d%d_%d" % (n, gk), "d",
                              wb_d.ap()[1024 * gk:1024 * gk + 128 * nk, 512 * n:512 * n + 512].rearrange("(k p) n -> p k n", p=128), nk, 512))
        return g

    wq = []
    tiles_plan = ([("pre", 0)] if do_pre else []) + [("main", m) for m in range(n_main)] + ([("sample", 0)] if do_sample else [])
    for mode, _ in tiles_plan:
        wq.extend(wgroups(mode))
    wstate = {"issued": 0, "consumed": 0, "slot": {}}

    def w_issue(slot):
        i = wstate["issued"]
        if i >= len(wq):
            return
        key, wname, src, nk, ncol = wq[i]
        view = wslot[slot][:, 0:nk * ncol].rearrange("p (k n) -> p k n", k=nk)
        dma("sp", view, src, [r_wb[wname]], [r_wslot[slot]])
        wstate["slot"][i] = slot
        wstate["issued"] += 1

    def w_next(key):
        i = wstate["consumed"]
        assert wq[i][0] == key, (wq[i][0], key)
        slot = wstate["slot"][i]
        _, _, _, nk, ncol = wq[i]
        view = wslot[slot][:, 0:nk * ncol].rearrange("p (k n) -> p k n", k=nk)
        return view, r_wslot[slot], slot

    def w_done(slot):
        wstate["consumed"] += 1
        w_issue(slot)

    if init_stop <= 4:
        S.finalize(st)
        with nc.Block() as block:
            S.run(block)
        st.close()
        return nc
    X = [sb("X%d" % i, [128, 2, D]) for i in range(2)]
    r_X = [[R("X%d_%d" % (i, j)) for j in range(2)] for i in range(2)]
    XN = sb("XN", [128, 2, D], BF16); r_XN = [R("XN0"), R("XN1")]
    ss = sb("ss", [128, 8]); r_ss = R("ss")
    hTs = [sb("hTa", [128, 8, T], BF16), sb("hTb", [128, 8, T], BF16), sb("h2T", [128, 8, T], BF16)]
    r_hTs = [[R("hTa0"), R("hTa1")], [R("hTb0"), R("hTb1")], [R("h2T0"), R("h2T1")]]
    qT = sb("qT", [128, 4, T], BF16); r_qT = R("qT")
    kTr = sb("kTr", [128, 4, 6, 128], BF16); r_kT = [R("kT%d" % i) for i in range(6)]
    Vr = sb("Vr", [128, 6, 8, 65], BF16); r_V = [R("V%d" % i) for i in range(6)]
    Pb = [sb("Pb%d" % i, [128, 5, 128], BF16) for i in range(3)]; r_Pb = [R("Pb%d" % i) for i in range(3)]
    oa = sb("oa", [128, 2, 512], BF16); r_oa = [R("oa0"), R("oa1")]
    oaT = sb("oaT", [128, 4, T], BF16); r_oaT = [R("oaT0"), R("oaT1")]
    rec = sb("rec", [128, 8]); r_rec = R("rec")
    sg = sb("sg", [128, 4, T]); r_sg = [R("sg%d" % i) for i in range(4)]
    siluq = sb("siluq", [128, 4, T]); r_sq = [R("siluq%d" % i) for i in range(4)]
    gF = sb("gF", [128, 4 * T]); r_gF = R("gF")
    gL = sb("gL", [128, 4 * T]); r_gL = R("gL")
    gB = sb("gB", [128, 4 * T]); r_gB = R("gB")
    gK = sb("gK", [128, 4 * T]); r_gK = R("gK")
    gE = sb("gE", [128, 4 * T]); r_gE = R("gE")
    dec = sb("dec", [128, 3, 4, 4]); r_dec = R("dec")
    Zq = sb("Zq", [128, 4, 4, 128], BF16); r_Zq = [R("Zq%d" % i) for i in range(4)]
    ktT = sb("ktT", [128, 4, T], BF16); r_ktT = [R("ktT%d" % i) for i in range(4)]
    khT = sb("khT", [128, 4, T], BF16); r_khT = [R("khT%d" % i) for i in range(4)]
    khtm = sb("khtm", [128, 2, 512], BF16); r_khtm = [R("khtm0"), R("khtm1")]
    vh = sb("vh", [128, 2, 512], BF16); r_vh = [R("vh0"), R("vh1")]
    sgb = sb("sgb", [128, 2, 512]); r_sgb = [R("sgb0"), R("sgb1")]
    Sst = sb("Sst", [128, 4, 128]); r_S = R("S")
    Sp = sb("Sp", [128, 2, 4, 128], BF16); r_Sp = [R("Sp0"), R("Sp1")]
    AT = sb("AT", [128, 4, 128], BF16); r_AT = R("AT")
    sqb = sb("sqb", [128, 512]); r_sqb = R("sqb")
    ssq = sb("ssq", [128, 4]); r_ssq = R("ssq")
    t1 = sb("t1", [128, 512]); r_t1 = R("t1")
    gG = sb("gG", [128, 512]); r_gG = R("gG")
    ob = sb("ob", [128, 512], BF16); r_ob = R("ob")
    obT = sb("obT", [128, 4, T], BF16); r_obT = [R("obT0"), R("obT1")]
    zs = sb("zs", [128, 16, T], BF16); r_zs = [R("zs%d" % i) for i in range(16)]
    mT = sb("mT", [128, 8, T], BF16); r_mT = [R("mT%d" % i) for i in range(8)]
    aT = [sb("aT%d" % i, [128, T + 2]) for i in range(2)]; r_aT = [R("aT0"), R("aT1")]
    c1 = [sb("c1_%d" % i, [128, T]) for i in range(2)]; r_c1 = [R("c1_0"), R("c1_1")]
    c2 = [sb("c2_%d" % i, [128, T]) for i in range(2)]; r_c2 = [R("c2_0"), R("c2_1")]
    m1, r_m1, m2, r_m2 = c1[0], r_c1[0], c2[0], r_c2[0]
    gT = sb("gT", [128, NFT, T], BF16); r_gT = [R("gT%d" % i) for i in range(NFT)]
    cprev = sb("cprev_s", [128, NFT, 2]); r_cprev = R("cprev")

    ext_ = siluq[0:8].rearrange("p h t -> p (h t)")[:, 0:768]; r_ext = r_sq; r_extd = R("extd")
    dma("sp", ext_[:, 64:256], rel_d.ap(), (), r_ext)
    act(ext_[:, 0:64], ext_[:, 64:65].to_broadcast([8, 64]), AF.Identity, r_ext, r_ext)
    act(ext_[:, 256:768], ext_[:, 255:256].to_broadcast([8, 512]), AF.Identity, r_ext, r_ext)
    act(ext_[:, :], ext_[:, :], AF.Exp, r_ext, r_ext)
    dma("sp", ext_d.ap(), ext_[:, :], r_ext, [r_extd])
    hk = sg[:].rearrange("p h t -> p (h t)").rearrange("p (a b) -> p a b", a=8); r_hk = r_sg
    hkb = ktT[:].rearrange("p h t -> p (h t)").rearrange("p (a b) -> p a b", a=8); r_hkb = r_ktT
    for kt in range(5):
        dma("sp", hk, AP(ext_d, 512 - 128 * kt, [[1, 128], [768, 8], [1, 128]]), [r_extd], r_hk)
        cp(hkb, hk, r_hk, r_hkb)
        for n in range(2):
            pb, rb = gbank()
            mm(pb[:, :], jb[:], hkb[:, 4 * n:4 * n + 4, :].rearrange("p h q -> p (h q)"), True, True, [r_j] + r_hkb, [rb])
            cp(ET[:, kt, 4 * n:4 * n + 4, :].rearrange("p h q -> p (h q)"), pb[:, :], [rb], [r_ET])
    mset(ET[0:64, 0, :, 64:128], 0.0, [r_ET])
    mset(ET[64:128, 4, :, 0:64], 0.0, [r_ET])

    kst = gT[:, 0:8, :].rearrange("p a b -> p (a b)").bitcast(F32).rearrange("p (h t) -> p h t", h=4)
    vst = gT[:, 8:16, :].rearrange("p a b -> p (a b)").bitcast(F32).rearrange("p (j c) -> p j c", j=2)
    rl_kst = r_gT[0:8]
    rl_vst = [r_gT[8:12], r_gT[12:16]]
    sg2 = gT[:, 0:8, :].rearrange("p a b -> p (a b)").bitcast(F32).rearrange("p (h t) -> p h t", h=4)
    r_sg2 = [[r_gT[2 * i], r_gT[2 * i + 1]] for i in range(4)]
    vh2 = gT[:, 8:12, :].rearrange("p a b -> p (a b)").rearrange("p (j c) -> p j c", j=2)
    r_vh2 = [[r_gT[8], r_gT[9]], [r_gT[10], r_gT[11]]]
    sgs = [(sg, [[r] for r in r_sg]), (sg2, r_sg2)]
    vhs = [(vh, [[r] for r in r_vh]), (vh2, r_vh2)]
    mset(Zq[:], 0.0, r_Zq)
    mset(Sst[:], 0.0, [r_S])
    mset(cprev[:], 0.0, [r_cprev])

    if init_stop <= 5:
        S.finalize(st)
        with nc.Block() as block:
            S.run(block)
        st.close()
        return nc
    def load_x(xsrc, ntok, xb):
        nsub = (ntok + 127) // 128
        for j in range(nsub):
            nt = min(128, ntok - 128 * j)
            dma("sp", X[xb][:nt, j, :], xsrc[j * 128:j * 128 + nt, :], (), [r_X[xb][j]])

    def norm_T(src_tile, rsrc, gcol, rg, col0, hsel, ntok):
        nsub = (ntok + 127) // 128
        dst, rdst = hTs[hsel], r_hTs[hsel]
        for j in range(nsub):
            nt = min(128, ntok - 128 * j)
            act(XN[:nt, j, :], src_tile[:nt, j, :], AF.Square, [rsrc[j]], [r_XN[j], r_ss], accum_out=ss[:nt, col0 + j:col0 + j + 1])
            ts(ss[:nt, col0 + j:col0 + j + 1], ss[:nt, col0 + j:col0 + j + 1], 1.0 / D, EPS, ALU.mult, ALU.add, [r_ss], [r_ss], eng="pool")
            tt(ss[:nt, col0 + j:col0 + j + 1], ss[:nt, col0 + j:col0 + j + 1], nhalf[:nt, 0:1], ALU.pow, [r_ss, r_nh], [r_ss], eng="pool")
            act(XN[:nt, j, :], src_tile[:nt, j, :], AF.Copy, [rsrc[j], r_ss], [r_XN[j]], scale=ss[:nt, col0 + j:col0 + j + 1])
            pt, rt = tbank()
            for kc in range(8):
                S.emit("pe", lambda e, kc=kc, j=j, nt=nt, pt=pt: e.transpose(out=pt[:, kc * 128:kc * 128 + nt], in_=XN[:nt, j, kc * 128:(kc + 1) * 128],
                                                                              identity=identb[:nt, :nt]), [r_XN[j], r_ident], [rt])
            tt(dst[:, :, j * 128:j * 128 + nt], pt[:, :].rearrange("p (k t) -> p k t", k=8)[:, :, 0:nt],
               gcol[:, 0:8].unsqueeze(2).to_broadcast([128, 8, nt]), ALU.mult, [rt, rg], [rdst[j]])

    def macro_tile(mode, ntok, xb, gt0, hsel=0, normed=False, kv=False, out_row=None, final_kv=None, after_norm=None, prenorm=None,
                   part="all", bsel=0):
        sample = mode == "sample"
        nsub = (ntok + 127) // 128
        nts = [min(128, ntok - 128 * j) for j in range(nsub)]
        C = 16 if sample else (ntok if mode == "scan" else 64)
        nch = ntok // C
        ri = C - 1 if mode == "scan" else C // 2 - 1
        cps = 1 if sample else 2
        Xb = X[xb]
        rX = r_X[xb]
        hT = hTs[hsel]
        r_hT = r_hTs[hsel]
        rhT_all = r_hT[:nsub]
        cur["hT"] = hT
        cur["r_hT"] = r_hT
        sg, rl_sg = sgs[bsel]
        vh, rl_vh = vhs[bsel]
        if mode == "scan":
            gstate["pool"] = gpool_front if part == "front" else gpool_back
            gstate["tr"] = (0,) if part == "front" else (1,)
        else:
            gstate["pool"] = gpool_front + gpool_back
            gstate["tr"] = (0, 1)

        if part != "back":
            if not normed:
                norm_T(Xb, rX, gmix, r_gmix, 0, hsel, ntok)
            if after_norm is not None:
                after_norm()

        def proj_fm(wv, rw, ct, evac):
            pb, rb = gbank()
            for kc in range(8):
                mm(pb[:, 0:ntok], wv[:, kc, ct * 128:(ct + 1) * 128], hT[:, kc, 0:ntok], kc == 0, kc == 7, [rw] + rhT_all, [rb])
            evac(pb, rb)

        def proj_tm(wv, rw, j, evac, ncol=512):
            pb, rb = gbank()
            nt = nts[j]
            for kc in range(8):
                mm(pb[:nt, 0:ncol], hT[:, kc, j * 128:j * 128 + nt], wv[:, kc, 0:ncol], kc == 0, kc == 7, [rw, r_hT[j]], [rb])
            evac(pb, rb, j, nt)

        def slot_of(j):
            return (gt0 + j) % 6 if not sample else 4

        def hgrn_gates_all():
            n4 = 4 * ntok
            nc4 = 4 * nch
            fl = lambda b: b[:, 0:n4]
            v3 = lambda b: b[:, 0:n4].rearrange("p (h t) -> p h t", h=4)
            ch = lambda b: b[:, 0:n4].rearrange("p (c t) -> p c t", t=C)
            hc = lambda b: b[:, 0:n4].rearrange("p (h c t) -> p h c t", h=4, t=C)
            rsg = [r for l_ in rl_sg for r in l_]
            sgf = sg.rearrange("p h t -> p (h t)")
            tt(v3(gF), v3(sgf), omlh[:, 0:4].unsqueeze(2).to_broadcast([128, 4, ntok]), ALU.mult, rsg + [r_lb], [r_gF])
            tt(v3(gF), v3(gF), lbh[:, 0:4].unsqueeze(2).to_broadcast([128, 4, ntok]), ALU.add, [r_gF, r_lb], [r_gF])
            act(fl(gL), fl(gF), AF.Ln, [r_gF], [r_gL])
            S.emit("dve", lambda e: e.tensor_tensor_scan(out=fl(gB), data0=fl(gL), data1=fl(gL), initial=0.0, op0=ALU.add, op1=ALU.min),
                   [r_gL], [r_gB], cost=0.1 + 2 * n4 / 900.0)
            ts(fl(gK), fl(gF), -1.0, 1.0, ALU.mult, ALU.add, [r_gF], [r_gK], eng="pool")
            tt(ch(gL), ch(gB), ch(gB)[:, :, ri:ri + 1].to_broadcast([128, nc4, C]), ALU.subtract, [r_gB, r_gL], [r_gL])
            act(fl(gE), fl(gL), AF.Exp, [r_gL], [r_gE], scale=-1.0)
            tt(ktT[:, :, 0:ntok], v3(gK), v3(gE), ALU.mult, [r_gK, r_gE], r_ktT, eng="pool")
            dsl = lambda i: dec[:, i, :, 0:nch]
            if mode != "scan":
                act(fl(gB), fl(gL), AF.Exp, [r_gL, r_gB], [r_gB])
                cp(dsl(2), hc(gB)[:, :, :, C - 1], [r_gB], [r_dec])
            tt(dsl(0), hc(gE)[:, :, :, 0], hc(gF)[:, :, :, 0], ALU.mult, [r_gE, r_gF], [r_dec])
            if mode == "scan":
                return
            tt(dsl(1), dsl(0), dsl(2), ALU.mult, [r_dec], [r_dec])
            k4 = lambda b: b[:, :, 0:ntok].rearrange("p h (c t) -> p h c t", t=C)
            tt(k4(khT), k4(ktT), dsl(2).unsqueeze(3).to_broadcast([128, 4, nch, C]), ALU.mult, r_ktT + [r_dec], r_khT)
            if mode != "scan":
                sqf = siluq.rearrange("p h t -> p (h t)")
                if sample:
                    tt(Zq[:, :, 0, 0:ntok], v3(sqf), v3(gB), ALU.mult, r_sq + [r_gB], r_Zq)
                else:
                    base = Zq[:, 0, 0, 0:64]
                    zout = AP(base.tensor, base.offset, [list(base.ap[0]), [512 // nsub, 4 * nsub], [192, 2], [1, 64]])
                    tt(zout, fl(sqf).rearrange("p (a c t) -> p a c t", c=2, t=64), fl(gB).rearrange("p (a c t) -> p a c t", c=2, t=64),
                       ALU.mult, r_sq + [r_gB], r_Zq)

        def khat_transpose(j):
            nt = nts[j]
            pt, rt = tbank()
            ksrc, rks = (ktT, r_ktT) if mode == "scan" else (khT, r_khT)
            for hb in range(4):
                S.emit("pe", lambda e, hb=hb, pt=pt: e.transpose(out=pt[:nt, hb * 128:(hb + 1) * 128], in_=ksrc[:, hb, j * 128:j * 128 + nt],
                                                                  identity=identb[:, :]), [rks[hb], r_ident], [rt])
            cp(khtm[:nt, j, :], pt[:nt, 0:512], [rt], [r_khtm[j]], eng="act")

        def s_update(j, ci):
            p = ci % cps
            rows = slice(p * C, p * C + C)
            pb, rb = gbank()
            for hb in range(4):
                mm(pb[:, hb * 128:(hb + 1) * 128], khtm[rows, j, hb * 128:(hb + 1) * 128], vh[rows, j, hb * 128:(hb + 1) * 128],
                   True, True, [r_khtm[j]] + rl_vh[j], [rb])
            tt(Sst[:], Sst[:], dec[:, 1, :, ci:ci + 1].to_broadcast([128, 4, 128]), ALU.mult, [r_S, r_dec], [r_S])
            tt(Sst[:].rearrange("p h v -> p (h v)"), Sst[:].rearrange("p h v -> p (h v)"), pb[:, :], ALU.add, [r_S, rb], [r_S])

        if mode == "scan":
            wv_f, wv_i = W1v[0], W1v[1]
            if part != "back":
                for hb in range(4):
                    proj_fm(wv_f, r_wslot[0], hb, lambda pb, rb, hb=hb: act(sg.rearrange("p h t -> p (h t)")[:, hb * ntok:(hb + 1) * ntok], pb[:, 0:ntok], AF.Tanh, [rb], rl_sg[hb], scale=0.5))
                for j in range(nsub):
                    proj_tm(wv_i, r_wslot[1], j, lambda pb, rb, j, nt: cp(vh[:nt, j, :], pb[:nt, :], [rb], rl_vh[j], eng="act"))
                if kv:
                    kv_proj_only()
            if part != "front":
                hgrn_gates_all()
                for j in range(nsub):
                    khat_transpose(j)
                pb, rb = gbank()
                for hb in range(4):
                    for j in range(nsub):
                        mm(pb[:, hb * 128:(hb + 1) * 128], khtm[:, j, hb * 128:(hb + 1) * 128], vh[:, j, hb * 128:(hb + 1) * 128],
                           j == 0, j == nsub - 1, [r_khtm[j]] + rl_vh[j], [rb])
                tt(Sst[:], Sst[:], dec[:, 0, :, 0:1].to_broadcast([128, 4, 128]), ALU.mult, [r_S, r_dec], [r_S])
                tt(Sst[:].rearrange("p h v -> p (h v)"), Sst[:].rearrange("p h v -> p (h v)"), pb[:, :], ALU.add, [r_S, rb], [r_S])
            return

        def ev_q(ct):
            return lambda pb, rb: act(qT[:, ct, 0:ntok], pb[:, 0:ntok], AF.Copy, [rb], [r_qT], scale=0.125)

        def ev_k(ct):
            def f(pb, rb):
                for j in range(nsub):
                    cp(kTr[:, ct, slot_of(j), 0:nts[j]], pb[:, j * 128:j * 128 + nts[j]], [rb], [r_kT[slot_of(j)]], eng="act")
                if final_kv is not None:
                    cp(kst[:, ct, 0:ntok], pb[:, 0:ntok], [rb], rl_kst)
            return f

        def ev_v(pb, rb, j, nt):
            s_ = slot_of(j)
            cp(Vr[:nt, s_, :, 0:64], pb[:nt, :].rearrange("p (h d) -> p h d", h=8), [rb], [r_V[s_]], eng="act")
            if mode == "main" or sample:
                cp(Vr[:nt, s_, :, 64:65], onesT[:nt, 0:8].unsqueeze(2), [r_ones], [r_V[s_]], eng="pool")
            else:
                cp(Vr[:nt, s_, :, 64:65], hvt[:nt, 0:1].unsqueeze(2).to_broadcast([nt, 8, 1]), [r_hv], [r_V[s_]], eng="pool")
            if final_kv is not None:
                cp(vst[:nt, j, :], pb[:nt, :], [rb], rl_vst[j])
                dma("act", final_kv[1].ap()[final_kv[2] + j * 128:final_kv[2] + j * 128 + nt, :], vst[:nt, j, :], rl_vst[j], ())

        wv, rw, sl = w_next("in3")
        for hb in range(4):
            proj_fm(wv, rw, hb, lambda pb, rb, hb=hb: act(siluq.rearrange("p h t -> p (h t)")[:, hb * ntok:(hb + 1) * ntok], pb[:, 0:ntok], AF.Silu, [rb], [r_sq[hb]]))
        w_done(sl)
        wv, rw, sl = w_next("in4")
        for hb in range(4):
            proj_fm(wv, rw, hb, lambda pb, rb, hb=hb: act(sg.rearrange("p h t -> p (h t)")[:, hb * ntok:(hb + 1) * ntok], pb[:, 0:ntok], AF.Tanh, [rb], rl_sg[hb], scale=0.5))
        w_done(sl)
        wv, rw, sl = w_next("in5")
        for j in range(nsub):
            proj_tm(wv, rw, j, lambda pb, rb, j, nt: cp(vh[:nt, j, :], pb[:nt, :], [rb], rl_vh[j], eng="act"))
        w_done(sl)
        wv, rw, sl = w_next("in6")
        for j in range(nsub):
            proj_tm(wv, rw, j, lambda pb, rb, j, nt: act(sgb[:nt, j, :], pb[:nt, :], AF.Silu, [rb], [r_sgb[j]]))
        w_done(sl)
        wv, rw, sl = w_next("in0")
        for ct in range(4):
            proj_fm(wv, rw, ct, ev_q(ct))
        w_done(sl)
        wv, rw, sl = w_next("in1")
        for ct in range(4):
            proj_fm(wv, rw, ct, ev_k(ct))
        w_done(sl)
        if final_kv is not None:
            for ct in range(4):
                dma("act", final_kv[0].ap()[ct, :, final_kv[2]:final_kv[2] + ntok], kst[:, ct, 0:ntok], rl_kst, ())
        wv, rw, sl = w_next("in2")
        for j in range(nsub):
            proj_tm(wv, rw, j, ev_v)
        w_done(sl)

        att_steps, z_steps, h_steps = [], [], []

        zst = {}

        def z_step(zi):
            gi, ct = divmod(zi, 4)
            if ct == 0:
                zst["w"] = w_next("in%d" % (7 + gi))
            wv_, rw_, sl_ = zst["w"]
            proj_fm(wv_, rw_, ct, lambda pb, rb: act(zs[:, zi, 0:ntok], pb[:, 0:ntok], AF.Tanh, [rb], [r_zs[zi]], scale=0.5))
            if ct == 3:
                w_done(sl_)
        for zi in range(16):
            z_steps.append(lambda zi=zi: z_step(zi))

        def make_att(j):
            nq = nts[j]
            if sample:
                ktiles = [(0, 128), (1, 128), (2, 128), (3, 128), (4, 16)]
            else:
                ktiles = [((gt0 + j - 4 + kt) % 6, 128) for kt in range(5)]
            pend = []

            def pv(h, pslot):
                for kt, (s_, nk) in enumerate(ktiles):
                    mm(ps_o[:nq, (h % 4) * 65:(h % 4) * 65 + 65], Pb[pslot][:nk, kt, 0:nq], Vr[:nk, s_, h, :], kt == 0, kt == 4,
                       [r_Pb[pslot], r_V[s_]], [r_o])

            def normalize(half):
                o3 = ps_o[:nq, 0:260].rearrange("p (h d) -> p h d", h=4)
                ts(rec[:nq, half * 4:half * 4 + 4].unsqueeze(2), o3[:, :, 64:65], 1e-30, None, ALU.max, None, [r_o], [r_rec])
                recip(rec[:nq, half * 4:half * 4 + 4], rec[:nq, half * 4:half * 4 + 4], [r_rec], [r_rec])
                tt(oa[:nq, j, half * 256:half * 256 + 256].rearrange("p (h d) -> p h d", h=4), o3[:, :, 0:64],
                   rec[:nq, half * 4:half * 4 + 4].unsqueeze(2).to_broadcast([nq, 4, 64]), ALU.mult, [r_o, r_rec], [r_oa[j]])

            def head(h):
                hp, r0 = h // 2, (h % 2) * 64
                ai = cnt["a"] % 2
                cnt["a"] += 1
                offA = ai * 512
                offB = 1024 + ai * 128
                for kt in (4, 0, 1, 2, 3):
                    s_, nk = ktiles[kt]
                    o_ = offB if kt == 4 else offA + kt * 128
                    mm(ps_att[:nk, o_:o_ + nq], kTr[r0:r0 + 64, hp, s_, 0:nk], qT[r0:r0 + 64, hp, j * 128:j * 128 + nq],
                       True, True, [r_kT[s_], r_qT], [r_attB if kt == 4 else r_att[ai]])
                pi = cnt["p"] % 3
                cnt["p"] += 1
                sattA = ps_att[:, offA:offA + 512].rearrange("p (k q) -> p k q", k=4)
                sattB = ps_att[:, offB:offB + 128]
                if sample:
                    act(Pb[pi][:16, 4, 0:nq], sattB[:16, 0:nq], AF.Exp, [r_attB], [r_Pb[pi]])
                    act(Pb[pi][:, 0:4, 0:nq], sattA[:, :, 0:nq], AF.Exp, [r_att[ai]], [r_Pb[pi]])
                    tt(Pb[pi][:, 0:4, 0:nq], Pb[pi][:, 0:4, 0:nq], ET[:, 0:4, h, 0:nq], ALU.mult, [r_Pb[pi], r_ET], [r_Pb[pi]], eng="pool")
                    tt(Pb[pi][:16, 4, 0:nq], Pb[pi][:16, 4, 0:nq], ET[:16, 4, h, 0:nq], ALU.mult, [r_Pb[pi], r_ET], [r_Pb[pi]], eng="pool")
                else:
                    act(Pb[pi][:, 4, :], sattB, AF.Exp, [r_attB], [r_Pb[pi]])
                    act(Pb[pi][:, 0:4, :], sattA, AF.Exp, [r_att[ai]], [r_Pb[pi]])
                    tt(Pb[pi][:, :, :], Pb[pi][:, :, :], ET[:, :, h, :], ALU.mult, [r_Pb[pi], r_ET], [r_Pb[pi]], eng="pool")
                if pend:
                    ph, ppi = pend.pop(0)
                    pv(ph, ppi)
                    if ph == 3:
                        normalize(0)
                pend.append((h, pi))

            def tail():
                ph, ppi = pend.pop(0)
                pv(ph, ppi)
                normalize(1)
                pt, rt = tbank()
                for kc in range(4):
                    S.emit("pe", lambda e, kc=kc, pt=pt: e.transpose(out=pt[:, kc * 128:kc * 128 + nq], in_=oa[:nq, j, kc * 128:(kc + 1) * 128],
                                                                      identity=identb[:nq, :nq]), [r_oa[j], r_ident], [rt])
                cp(oaT[:, :, j * 128:j * 128 + nq], pt[:, 0:512].rearrange("p (k t) -> p k t", k=4)[:, :, 0:nq], [rt], [r_oaT[j]], eng="act")
            for h in range(8):
                att_steps.append(lambda h=h: head(h))
            att_steps.append(tail)
        for j in range(nsub):
            make_att(j)

        def make_h(j):
            nt = nts[j]

            def h_at():
                pb, rb = gbank()
                for hb in range(4):
                    if sample:
                        qrhs = Zq[:, hb, 0, 0:nt]
                    else:
                        base = Zq[:, hb, 2 * j, 0:64]
                        qrhs = AP(base.tensor, base.offset, [list(base.ap[0]), [192, 2], [1, 64]])
                    mm(pb[:nt, hb * 128:hb * 128 + nt], ktT[:, hb, j * 128:j * 128 + nt], qrhs, True, True, [r_ktT[hb], r_Zq[hb]], [rb])
                tt(AT[:nt, :, 0:nt], pb[:nt, :].rearrange("p (h t) -> p h t", h=4)[:, :, 0:nt],
                   maskb[:nt, 0:nt].unsqueeze(1).to_broadcast([nt, 4, nt]), ALU.mult, [rb, r_mask], [r_AT])

            def h_chunk(p):
                ci = j * cps + p
                tt(Sp[:, p], Sst[:], dec[:, 0, :, ci:ci + 1].to_broadcast([128, 4, 128]), ALU.mult, [r_S, r_dec], [r_Sp[p]])
                s_update(j, ci)

            def h_out():
                ob_, rob = gbank()
                for hb in range(4):
                    for p in range(cps):
                        zl = Zq[:, hb, 0, 0:nt] if sample else Zq[:, hb, 2 * j + p, :]
                        mm(ob_[:nt, hb * 128:(hb + 1) * 128], zl, Sp[:, p, hb, :], p == 0, False, [r_Zq[hb], r_Sp[p]], [rob])
                    mm(ob_[:nt, hb * 128:(hb + 1) * 128], AT[:nt, hb, 0:nt], vh[:nt, j, hb * 128:(hb + 1) * 128], False, True, [r_AT] + rl_vh[j], [rob])
                act(sqb[:nt, :], ob_[:nt, :], AF.Square, [rob], [r_sqb])
                S.emit("dve", lambda e: e.tensor_reduce(out=ssq[:nt, 0:4], in_=sqb[:nt, :].rearrange("p (h v) -> p h v", h=4), axis=AX.X, op=ALU.add),
                       [r_sqb], [r_ssq])
                ts(ssq[:nt, :], ssq[:nt, :], 1.0 / 128, EPS, ALU.mult, ALU.add, [r_ssq], [r_ssq], eng="pool")
                tt(ssq[:nt, :], ssq[:nt, :], nhalf[:nt, 0:4], ALU.pow, [r_ssq, r_nh], [r_ssq], eng="pool")
                tt(t1[:nt, :].rearrange("p (h v) -> p h v", h=4), ob_[:nt, :].rearrange("p (h v) -> p h v", h=4),
                   ssq[:nt, 0:4].unsqueeze(2).to_broadcast([nt, 4, 128]), ALU.mult, [rob, r_ssq], [r_t1])
                tt(gG[:nt, :], sgb[:nt, j, :], gnrep[:nt].rearrange("p h v -> p (h v)"), ALU.mult, [r_sgb[j], r_gn], [r_gG], eng="pool")
                tt(ob[:nt, :], t1[:nt, :], gG[:nt, :], ALU.mult, [r_t1, r_gG], [r_ob])

            def h_tr():
                pt, rt = tbank()
                for hb in range(4):
                    S.emit("pe", lambda e, hb=hb, pt=pt: e.transpose(out=pt[:, hb * 128:hb * 128 + nt], in_=ob[:nt, hb * 128:(hb + 1) * 128],
                                                                      identity=identb[:nt, :nt]), [r_ob, r_ident], [rt])
                cp(obT[:, :, j * 128:j * 128 + nt], pt[:, 0:512].rearrange("p (k t) -> p k t", k=4)[:, :, 0:nt], [rt], [r_obT[j]], eng="act")
            h_steps.append(lambda: khat_transpose(j))
            h_steps.append(h_at)
            for p in range(cps):
                h_steps.append(lambda p=p: h_chunk(p))
            h_steps.append(h_out)
            h_steps.append(h_tr)
        for j in range(nsub):
            make_h(j)

        def run_steps(steps, pool, tr, pre=None):
            def f():
                gstate["pool"] = pool
                gstate["tr"] = tr
                if pre is not None:
                    pre()
                for st_ in steps:
                    st_()
            return f
        att_ops = S.capture(run_steps(att_steps, [], (0,)))
        z_ops = S.capture(run_steps(z_steps, gpool_full[0:1], (0,)))
        h_ops = S.capture(run_steps(h_steps, gpool_full[1:2], (1,), pre=hgrn_gates_all))
        S.emit_merged(att_ops, z_ops, h_ops)
        gstate["tr"] = (0, 1)

        gstate["pool"] = gpool_front + gpool_back
        wva, rwa, sla = w_next("a")
        wstate["consumed"] += 1
        wvb, rwb, slb = w_next("b")
        wstate["consumed"] -= 1
        for ct in range(8):
            pb, rb = gbank()
            for kc in range(4):
                mm(pb[:, 0:ntok], wva[:, kc, ct * 128:(ct + 1) * 128], oaT[:, kc, 0:ntok], kc == 0, kc == 3, [rwa] + r_oaT[:nsub], [rb])
            for kc in range(4):
                mm(pb[:, 256:256 + ntok], wvb[:, kc, ct * 128:(ct + 1) * 128], obT[:, kc, 0:ntok], kc == 0, kc == 3, [rwb] + r_obT[:nsub], [rb])
            stt(m1[:, 0:ntok], zs[:, ct, 0:ntok], 1.0, pb[:, 0:ntok], ALU.add, ALU.mult, [rb, r_zs[ct]], [r_m1])
            stt(m2[:, 0:ntok], zs[:, 8 + ct, 0:ntok], 1.0, pb[:, 256:256 + ntok], ALU.add, ALU.mult, [rb, r_zs[8 + ct]], [r_m2])
            tt(mT[:, ct, 0:ntok], m1[:, 0:ntok], m2[:, 0:ntok], ALU.add, [r_m1, r_m2], [r_mT[ct]], eng="pool")
        w_done(sla)
        w_done(slb)

        for n in range(2):
            wv, rw, sl = w_next("o%d" % n)
            for j in range(nsub):
                nt = nts[j]
                pb, rb = gbank()
                for kc in range(8):
                    mm(pb[:nt, :], mT[:, kc, j * 128:j * 128 + nt], wv[:, kc, :], kc == 0, kc == 7, [rw, r_mT[kc]], [rb])
                stt(Xb[:nt, j, n * 512:(n + 1) * 512], pb[:nt, :], 0.5, Xb[:nt, j, n * 512:(n + 1) * 512], ALU.mult, ALU.add, [rX[j], rb], [rX[j]])
            w_done(sl)
        norm_T(Xb, rX, gffn, r_gffn, 2, 2, ntok)
        hT = hTs[2]
        r_hT = r_hTs[2]
        rhT_all = r_hT[:nsub]

        ffn_pend = []
        for gi in range(6):
            ntile = 4 if gi < 5 else 2
            if gi == 2 and prenorm is not None:
                prenorm()
            wvg, rwg, slg = w_next("g%d" % gi)
            if mode != "pre":
                wstate["consumed"] += 1
                wvu, rwu, slu = w_next("u%d" % gi)
                wstate["consumed"] -= 1
            for ct in range(ntile):
                ft = gi * 4 + ct
                pb, rb = gbank()
                if mode == "pre":
                    for kc in range(8):
                        mm(pb[:, 0:2], wvg[:, kc, ct * 128:(ct + 1) * 128], hT[:, kc, ntok - 2:ntok], kc == 0, kc == 7, [rwg] + rhT_all, [rb])
                    cp(cprev[:, ft, :], pb[:, 0:2], [rb], [r_cprev])
                    continue
                for kc in range(8):
                    mm(pb[:, 0:ntok], wvg[:, kc, ct * 128:(ct + 1) * 128], hT[:, kc, 0:ntok], kc == 0, kc == 7, [rwg] + rhT_all, [rb])
                for kc in range(8):
                    mm(pb[:, 256:256 + ntok], wvu[:, kc, ct * 128:(ct + 1) * 128], hT[:, kc, 0:ntok], kc == 0, kc == 7, [rwu] + rhT_all, [rb])
                bi = ft % 2
                a_ = aT[bi]
                cp(a_[:, 0:2], cprev[:, ft, :], [r_cprev], [r_aT[bi]], eng="pool")
                act(a_[:, 2:2 + ntok], pb[:, 0:ntok], AF.Copy, [rb], [r_aT[bi]])
                cp(cprev[:, ft, :], a_[:, ntok:ntok + 2], [r_aT[bi]], [r_cprev], eng="pool")
                act(c1[bi][:, 0:ntok], pb[:, 0:ntok], AF.Identity, [rb, r_cw, r_cb], [r_c1[bi]], scale=cw[:, ft, 2:3], bias=cb[:, ft:ft + 1])
                stt(c2[bi][:, 0:ntok], a_[:, 1:1 + ntok], cw[:, ft, 1:2], c1[bi][:, 0:ntok], ALU.mult, ALU.add, [r_aT[bi], r_c1[bi], r_cw], [r_c2[bi]])
                stt(c1[bi][:, 0:ntok], a_[:, 0:ntok], cw[:, ft, 0:1], c2[bi][:, 0:ntok], ALU.mult, ALU.add, [r_aT[bi], r_c2[bi], r_cw], [r_c1[bi]])
                def fin(bi=bi, ft=ft, pb=pb, rb=rb):
                    act(c2[bi][:, 0:ntok], c1[bi][:, 0:ntok], AF.Gelu_apprx_tanh, [r_c1[bi]], [r_c2[bi]])
                    tt(gT[:, ft, 0:ntok], c2[bi][:, 0:ntok], pb[:, 256:256 + ntok], ALU.mult, [r_c2[bi], rb], [r_gT[ft]])
                if ffn_pend:
                    ffn_pend.pop(0)()
                ffn_pend.append(fin)
            w_done(slg)
            if mode != "pre":
                w_done(slu)
        if mode == "pre":
            return
        while ffn_pend:
            ffn_pend.pop(0)()

        for n in range(2):
            banks = [gbank() for _ in range(nsub)]
            for gk in range(3):
                wv, rw, sl = w_next("d%d_%d" % (n, gk))
                nk = 8 if gk < 2 else 6
                for kl in range(nk):
                    kc = gk * 8 + kl
                    for j in range(nsub):
                        nt = nts[j]
                        mm(banks[j][0][:nt, :], gT[:, kc, j * 128:j * 128 + nt], wv[:, kl, :], kc == 0, kc == NFT - 1, [rw, r_gT[kc]], [banks[j][1]])
                w_done(sl)
            for j in range(nsub):
                nt = nts[j]
                tt(Xb[:nt, j, n * 512:(n + 1) * 512], Xb[:nt, j, n * 512:(n + 1) * 512], banks[j][0][:nt, :], ALU.add, [rX[j], banks[j][1]], [rX[j]])
        for j in range(nsub):
            nt = nts[j]
            act(XN[:nt, j, :], Xb[:nt, j, :], AF.Square, [rX[j]], [r_XN[j], r_ss], accum_out=ss[:nt, 4 + j:5 + j])
            ts(ss[:nt, 4 + j:5 + j], ss[:nt, 4 + j:5 + j], 1.0 / D, EPS, ALU.mult, ALU.add, [r_ss], [r_ss], eng="pool")
            tt(ss[:nt, 4 + j:5 + j], ss[:nt, 4 + j:5 + j], nhalf[:nt, 0:1], ALU.pow, [r_ss, r_nh], [r_ss], eng="pool")
            stt(Xb[:nt, j, :], Xb[:nt, j, :], ss[:nt, 4 + j:5 + j], gfin[:nt, :], ALU.mult, ALU.mult, [rX[j], r_ss, r_gfin], [rX[j]])
            dma("act", out_row[j * 128:j * 128 + nt, :], Xb[:nt, j, :], [rX[j]], ())

    cur = {}

    def kv_proj_only():
        ntok, gt0 = cur["ntok"], cur["gt0"]
        for ct in range(4):
            pb, rb = gbank()
            for kc in range(8):
                mm(pb[:, 0:ntok], W1v[2][:, kc, ct * 128:(ct + 1) * 128], cur["hT"][:, kc, 0:ntok], kc == 0, kc == 7, [r_wslot[2]] + cur["r_hT"], [rb])
            for j in range(2):
                s_ = (gt0 + j) % 6
                cp(kTr[:, ct, s_, :], pb[:, j * 128:(j + 1) * 128], [rb], [r_kT[s_]], eng="act")
        for j in range(2):
            s_ = (gt0 + j) % 6
            pb, rb = gbank()
            for kc in range(8):
                mm(pb[:, :], cur["hT"][:, kc, j * 128:(j + 1) * 128], W1v[3][:, kc, :], kc == 0, kc == 7, [r_wslot[3], cur["r_hT"][j]], [rb])
            cp(Vr[:, s_, :, 0:64], pb[:, :].rearrange("p (h d) -> p h d", h=8), [rb], [r_V[s_]], eng="act")
            cp(Vr[:, s_, :, 64:65], hvt[:, 0:1].unsqueeze(2).to_broadcast([128, 8, 1]), [r_hv], [r_V[s_]], eng="pool")


    xa = xin.ap()
    nscan = n_scan
    nmain = n_main
    plan = []
    for m in range(nscan):
        plan.append(("scan", xa[m * T:(m + 1) * T, :], T, dict(gt0=-5 + 2 * (m - (nscan - 2)) + 6, kv=m >= nscan - 2)))
    if do_pre:
        plan.append(("pre", xa[NPRE:NPRE + PRE_T, :], PRE_T, dict(gt0=5)))
    for m in range(nmain):
        fk = (okT_d, ov_d, (m - (nmain - 2)) * T) if (m >= nmain - 2 and not NOFK) else None
        r0 = NPRE + PRE_T + m * T
        plan.append(("main", xa[r0:r0 + T, :], T, dict(gt0=2 * m + 6, out_row=y_d.ap()[m * T:(m + 1) * T, :], final_kv=fk)))
    if do_sample:
        plan.append(("sample", xs_d.ap(), 16, dict(gt0=0, out_row=ys_d.ap(), final_kv=(okTs_d, ovs_d, 0))))
    if plan:
        load_x(plan[0][1], plan[0][2], 0)
    if not do_conv:
        conv_jobs.clear()
    for i, (mode, xsrc, ntok, kw) in enumerate(plan):
        xb = i % 2
        nxt = None
        pren = None
        if i + 1 < len(plan):
            nxt = (lambda p=plan[i + 1], b=(i + 1) % 2: load_x(p[1], p[2], b))
            if mode == "main":
                pren = (lambda p=plan[i + 1], b=(i + 1) % 2: norm_T(X[b], r_X[b], gmix, r_gmix, 0, b, p[2]))
        normed = i > 0 and plan[i - 1][0] == "main"
        if mode == "scan":
            def front(k):
                md, xs_, nt_, kw_ = plan[k]
                cur["ntok"] = nt_
                cur["gt0"] = kw_["gt0"]
                emit_conv(2)
                nx = (lambda p=plan[k + 1], b=(k + 1) % 2: load_x(p[1], p[2], b)) if k + 1 < len(plan) else None
                macro_tile("scan", nt_, k % 2, hsel=k % 2, after_norm=nx, part="front", bsel=k % 2, **kw_)
            if i == 0:
                front(0)
            back_ops = S.capture(lambda: macro_tile("scan", ntok, xb, hsel=xb, part="back", bsel=xb, **kw))
            front_ops = S.capture(lambda: front(i + 1)) if i + 1 < nscan else []
            S.emit_merged(back_ops, front_ops)
            continue
        if i == nscan:
            emit_conv(len(conv_jobs))
            for s_ in range(NSLOT):
                w_issue(s_)
        if mode == "sample":
            dma("act", oS_d.ap().rearrange("h k v -> k h v"), Sst[:], [r_S], ())
            dma("act", oconv_d.ap(), cprev[:], [r_cprev], ())
            dma("pool", kTr[:, :, 0:4, :], ckT_d.ap().rearrange("p c (t k) -> p c t k", t=4), (), r_kT[0:4])
            for t_ in range(4):
                dma("pool", Vr[:, t_, :, 0:64], cv_d.ap()[:, t_ * 128:(t_ + 1) * 128, :].rearrange("h p d -> p h d"), (), [r_V[t_]])
                cp(Vr[:, t_, :, 64:65], onesT[:, 0:8].unsqueeze(2), [r_ones], [r_V[t_]], eng="pool")
            dma("sp", Sst[:], s0_d.ap().rearrange("h k v -> k h v"), (), [r_S])
            dma("sp", cprev[:], cprev_d.ap(), (), [r_cprev])
        macro_tile(mode, ntok, xb, hsel=xb, normed=normed, after_norm=nxt, prenorm=pren, **kw)
    emit_conv(len(conv_jobs))
    if not do_sample:
        dma("act", oS_d.ap().rearrange("h k v -> k h v"), Sst[:], [r_S], ())
        dma("act", oconv_d.ap(), cprev[:], [r_cprev], ())
    else:
        dma("act", oSs_d.ap().rearrange("h k v -> k h v"), Sst[:], [r_S], ())
        dma("act", oconvs_d.ap(), cprev[:], [r_cprev], ())
    for nm, getter in dumps:
        ap_, res_ = getter(locals())
        d_ = nc.dram_tensor("dbg_" + nm, list(ap_.shape), ap_.dtype if hasattr(ap_, "dtype") else F32, kind="ExternalOutput")
        dma("pool", d_.ap(), ap_, res_, ())

    S.finalize(st)
    with nc.Block() as block:
        S.run(block)
    st.close()
    return nc


_NC_CACHE = {}


def kernel(x_prompt, x_sample, cache_attn_k, cache_attn_v, state_hgrn, state_ffn_conv,
           norm_mix_g, w_in, rel_bias, hgrn_lb_logits, hgrn_norm_g, w_branch_a, w_branch_b, w_out,
           norm_ffn_g, w_ffn_gate, w_ffn_up, ffn_conv_w, ffn_conv_b, w_ffn_down, norm_final_g):
    f32 = np.float32
    A = lambda a: np.ascontiguousarray(np.asarray(a, dtype=f32))
    x_prompt = A(x_prompt)
    if "nc" not in _NC_CACHE:
        _NC_CACHE["nc"] = build_program()
    nc = _NC_CACHE["nc"]
    s_idx = np.arange(128)
    mask = ((s_idx[:, None] // 64 == s_idx[None, :] // 64) & (s_idx[:, None] <= s_idx[None, :])).astype(f32)
    shared = {
        "w_in": A(w_in[0]), "w_a": A(w_branch_a[0]), "w_b": A(w_branch_b[0]), "w_o": A(w_out[0]),
        "w_g": A(w_ffn_gate[0]), "w_u": A(w_ffn_up[0]), "w_d": A(w_ffn_down[0]),
        "gmix": A(np.asarray(norm_mix_g[0]).reshape(8, 128).T), "gffn": A(np.asarray(norm_ffn_g[0]).reshape(8, 128).T),
        "gfin": A(norm_final_g), "rel": A(rel_bias[0]),
        "lbl": A(np.asarray(hgrn_lb_logits).reshape(2, 4, 128).transpose(2, 0, 1)),
        "gn": A(hgrn_norm_g[0]),
        "cw": A(np.asarray(ffn_conv_w[0]).reshape(3, NFT, 128).transpose(2, 1, 0)),
        "cb": A(np.asarray(ffn_conv_b[0]).reshape(NFT, 128).T),
        "ident": np.eye(128, dtype=f32), "mask": mask, "jmat": np.ascontiguousarray(np.eye(128, dtype=f32)[::-1]),
    }
    in_maps = []
    for c in range(8):
        b, j = divmod(c, 4)
        s = j * SEG
        lo = s - PRE_T - NPRE
        xin = np.zeros((NTOK_IN, D), f32)
        a0 = max(lo, 0)
        xin[a0 - lo:] = x_prompt[b, a0:s + SEG]
        m = dict(shared)
        m["xin"] = xin
        m["hv"] = np.full((128, 1), 1.0 if j > 0 else 0.0, f32)
        m["xs"] = A(x_sample[c])
        m["ckT"] = A(np.asarray(cache_attn_k[0, c]).transpose(0, 2, 1).reshape(4, 128, 512).transpose(1, 0, 2))
        m["cv"] = A(cache_attn_v[0, c])
        m["s0"] = A(state_hgrn[0, c])
        m["cprev"] = A(np.asarray(state_ffn_conv[0, c]).reshape(2, NFT, 128).transpose(2, 1, 0))
        in_maps.append(m)
    res = run_bass_kernel_spmd(nc, in_maps, core_ids=list(range(8)))
    R_ = res.results
    B = 2
    y_prompt = np.stack([np.concatenate([R_[b * 4 + j]["y"] for j in range(4)], axis=0) for b in range(B)])
    y_sample = np.stack([R_[c]["ys"] for c in range(8)])

    def kT_to_rows(a):
        n = a.shape[-1]
        return a.reshape(8, 64, n).transpose(0, 2, 1)

    def v_to_rows(a):
        n = a.shape[0]
        return a.reshape(n, 8, 64).transpose(1, 0, 2)

    def conv_rows(a):
        return a.transpose(2, 1, 0).reshape(2, DFF)

    last = [3, 7]
    new_k_p = np.stack([kT_to_rows(R_[c]["okT"]) for c in last])[None]
    new_v_p = np.stack([v_to_rows(R_[c]["ov"]) for c in last])[None]
    hg_p = np.stack([R_[c]["oS"] for c in last])[None]
    cv_p = np.stack([conv_rows(R_[c]["oconv"]) for c in last])[None]
    new_k_s = np.stack([kT_to_rows(R_[c]["okTs"]) for c in range(8)])[None]
    new_v_s = np.stack([v_to_rows(R_[c]["ovs"]) for c in range(8)])[None]
    hg_s = np.stack([R_[c]["oSs"] for c in range(8)])[None]
    cv_s = np.stack([conv_rows(R_[c]["oconvs"]) for c in range(8)])[None]
    outs = (y_prompt, y_sample, new_k_p, new_v_p, hg_p, cv_p, new_k_s, new_v_s, hg_s, cv_s)
    return tuple(np.ascontiguousarray(o, dtype=f32) for o in outs)
```

```python
import numpy as np
from contextlib import ExitStack
import concourse.bass as bass
import concourse.mybir as mybir
from concourse.bass import AP
from concourse.bass_utils import run_bass_kernel_spmd

F32 = mybir.dt.float32
BF16 = mybir.dt.bfloat16
AF = mybir.ActivationFunctionType
ALU = mybir.AluOpType
AX = mybir.AxisListType

D = 1024
DFF = 2816
NFT = 22
SEG = 4096
NPRE = 12288
PRE_T = 128
NTOK_IN = NPRE + PRE_T + SEG
T = 256
EPS = 1e-6
NSLOT = 6
STOP = 99
STOPMODE = 'main'
NOFK = False
SLOT_E = 4096


class Res:
    __slots__ = ("name", "w", "r", "excl")

    def __init__(self, name, excl=False):
        self.name = name
        self.w = None
        self.r = {}
        self.excl = excl


class Op:
    __slots__ = ("eng", "fn", "deps", "is_dma", "needs_inc", "sem", "val", "idx")


class Sched:
    ENGS = ("pe", "act", "dve", "pool", "sp")

    def __init__(self, nc, n_dma_sems=8):
        self.nc = nc
        self.ops = []
        self.n_dma_sems = n_dma_sems
        self.cap = None

    def capture(self, fn):
        assert self.cap is None
        self.cap = []
        try:
            fn()
            return self.cap
        finally:
            self.cap = None

    DUR = {"pe": 0.15, "act": 0.75, "dve": 0.85, "pool": 1.1, "sp": 0.3}

    def emit_merged(self, *lists):
        eng_free = {}
        ready = {}
        rdone = {}
        LAT = 0.25

        def start_of(op):
            eng, fn, reads, writes, dma, cost = op
            t = eng_free.get(eng, 0.0)
            for r in reads:
                t = max(t, ready.get(id(r), 0.0) + LAT)
            for w in writes:
                t = max(t, ready.get(id(w), 0.0) + LAT, rdone.get(id(w), 0.0) + LAT)
            return t

        def commit(op, t):
            eng, fn, reads, writes, dma, cost = op
            d = 2.5 if dma else (cost if cost is not None else self.DUR[eng])
            eng_free[eng] = t + (0.1 if dma else d)
            for r in reads:
                rdone[id(r)] = max(rdone.get(id(r), 0.0), t + d)
            for w in writes:
                ready[id(w)] = t + d
            self.emit(eng, fn, reads, writes, dma)

        pos = [0] * len(lists)
        while True:
            best, bt = -1, None
            for k, l in enumerate(lists):
                if pos[k] < len(l):
                    t = start_of(l[pos[k]])
                    if bt is None or t < bt:
                        best, bt = k, t
            if best < 0:
                break
            commit(lists[best][pos[best]], bt)
            pos[best] += 1

    def emit(self, eng, fn, reads=(), writes=(), dma=False, cost=None):
        if self.cap is not None:
            self.cap.append((eng, fn, tuple(reads), tuple(writes), dma, cost))
            return None
        op = Op()
        op.eng = eng
        op.fn = fn
        op.is_dma = dma
        op.needs_inc = dma
        op.sem = None
        op.val = 0
        op.idx = len(self.ops)
        deps = {}
        xr = [r for r in reads if r.excl]
        if xr:
            reads = [r for r in reads if not r.excl]
            writes = list(writes) + [r for r in xr if r not in writes]
        for r in reads:
            if r.w is not None:
                deps[r.w.idx] = r.w
        for w in writes:
            if w.w is not None:
                deps[w.w.idx] = w.w
            for o in w.r.values():
                deps[o.idx] = o
        op.deps = list(deps.values())
        for r in reads:
            key = (eng, op.idx) if dma else (eng, -1)
            r.r[key] = op
        for w in writes:
            w.w = op
            w.r = {}
        self.ops.append(op)
        return op

    def finalize(self, stack):
        nc = self.nc
        for op in self.ops:
            for d in op.deps:
                if d.is_dma or d.eng != op.eng or op.eng != "pe" or op.is_dma:
                    d.needs_inc = True
        csem = {e: stack.enter_context(nc.semaphore("cs_" + e)) for e in ("pe", "act", "dve", "pool")}
        dsem = {e: [stack.enter_context(nc.semaphore("ds_%s%d" % (e, i))) for i in range(self.n_dma_sems)]
                for e in ("sp", "pool", "act")}
        ccount = {e: 0 for e in csem}
        dstate = {e: [None] * self.n_dma_sems for e in dsem}
        duse = {e: [0] * self.n_dma_sems for e in dsem}
        drr = {e: 0 for e in dsem}
        waited = {e: {} for e in self.ENGS}
        streams = {e: [] for e in self.ENGS}
        for op in self.ops:
            waits = []
            e = op.eng
            extra = []
            if op.is_dma:
                k = drr[e] % self.n_dma_sems
                drr[e] += 1
                prev = dstate[e][k]
                if prev is not None:
                    extra.append(prev)
                duse[e][k] += 1
                op.sem = dsem[e][k]
                op.val = 16 * duse[e][k]
                dstate[e][k] = op
            elif op.needs_inc:
                ccount[e] += 1
                op.sem = csem[e]
                op.val = ccount[e]
            for d in op.deps + extra:
                if (not d.is_dma) and d.eng == e and e == "pe" and not op.is_dma:
                    continue
                key = id(d.sem)
                if waited[e].get(key, 0) >= d.val:
                    continue
                waited[e][key] = d.val
                waits.append((d.sem, d.val))
            streams[e].append((waits, op))
        fin = []
        for e in dsem:
            for k in range(self.n_dma_sems):
                if duse[e][k] and waited["sp"].get(id(dsem[e][k]), 0) < 16 * duse[e][k]:
                    fin.append((dsem[e][k], 16 * duse[e][k]))
        self.streams = streams
        self.fin = fin

    def run(self, block):
        streams = self.streams
        fin = self.fin

        def body(name):
            def f(eng):
                for waits, op in streams[name]:
                    for s, v in waits:
                        eng.wait_ge(s, v)
                    inst = op.fn(eng)
                    if op.sem is not None:
                        inst.then_inc(op.sem, 16 if op.is_dma else 1)
                if name == "sp":
                    for s, v in fin:
                        eng.wait_ge(s, v)
            return f
        block.tensor(body("pe"))
        block.scalar(body("act"))
        block.vector(body("dve"))
        block.gpsimd(body("pool"))
        block.sync(body("sp"))


def build_program(n_scan=NPRE // T, n_main=SEG // T, do_pre=True, do_sample=True, dumps=(), do_conv=True, init_stop=99):
    NPRE = n_scan * T
    NTOK_IN = NPRE + PRE_T + n_main * T
    nc = bass.Bass("TRN2", target_bir_lowering=False)
    st = ExitStack()
    S = Sched(nc)

    def din(name, shape):
        return nc.dram_tensor(name, list(shape), F32, kind="ExternalInput")

    def dout(name, shape):
        return nc.dram_tensor(name, list(shape), F32, kind="ExternalOutput")

    xin = din("xin", [NTOK_IN, D])
    hv_d = din("hv", [128, 1])
    xs_d = din("xs", [16, D])
    ckT_d = din("ckT", [128, 4, 512])
    cv_d = din("cv", [8, 512, 64])
    s0_d = din("s0", [4, 128, 128])
    cprev_d = din("cprev", [128, NFT, 2])
    w_in_d = din("w_in", [D, 5632])
    w_a_d = din("w_a", [512, D])
    w_b_d = din("w_b", [512, D])
    w_o_d = din("w_o", [D, D])
    w_g_d = din("w_g", [D, DFF])
    w_u_d = din("w_u", [D, DFF])
    w_d_d = din("w_d", [DFF, D])
    gmix_d = din("gmix", [128, 8])
    gffn_d = din("gffn", [128, 8])
    gfin_d = din("gfin", [D])
    rel_d = din("rel", [8, 192])
    lbl_d = din("lbl", [128, 2, 4])
    gn_d = din("gn", [128])
    cw_d = din("cw", [128, NFT, 3])
    cb_d = din("cb", [128, NFT])
    ident_d = din("ident", [128, 128])
    mask_d = din("mask", [128, 128])
    jmat_d = din("jmat", [128, 128])

    y_d = dout("y", [SEG, D])
    ys_d = dout("ys", [16, D])
    okT_d = dout("okT", [4, 128, 512])
    ov_d = dout("ov", [512, 512])
    oS_d = dout("oS", [4, 128, 128])
    oconv_d = dout("oconv", [128, NFT, 2])
    okTs_d = dout("okTs", [4, 128, 16])
    ovs_d = dout("ovs", [16, 512])
    oSs_d = dout("oSs", [4, 128, 128])
    oconvs_d = dout("oconvs", [128, NFT, 2])

    def scratch(name, shape, dt=BF16):
        return nc.dram_tensor(name, list(shape), dt, kind="Internal")

    wb_in = scratch("wb_in", [D, 5632])
    wb_a = scratch("wb_a", [512, D])
    wb_b = scratch("wb_b", [512, D])
    wb_o = scratch("wb_o", [D, D])
    wb_g = scratch("wb_g", [D, DFF])
    wb_u = scratch("wb_u", [D, DFF])
    wb_d = scratch("wb_d", [DFF, D])
    ext_d = scratch("ext_d", [8, 768], F32)

    def sb(name, shape, dt=F32):
        return st.enter_context(nc.sbuf_tensor(name, list(shape), dt))

    def R(name, excl=False):
        return Res(name, excl)

    def fsz(ap):
        n = 1
        for d_ in list(ap.shape)[1:]:
            n *= int(d_)
        return n

    def ecost(eng, ap):
        n = fsz(ap)
        return {"act": 0.25 + n / 1000.0, "dve": 0.1 + n / 850.0, "pool": 0.2 + n / 480.0}[eng]

    def act(out, in_, func, reads, writes, **kw):
        S.emit("act", lambda e: e.activation(out=out, in_=in_, func=func, **kw), reads, writes, cost=ecost("act", out))

    def tt(out, in0, in1, op, reads, writes, eng="dve"):
        S.emit(eng, lambda e: e.tensor_tensor(out=out, in0=in0, in1=in1, op=op), reads, writes, cost=ecost(eng, out))

    def ts(out, in0, s1, s2, op0, op1, reads, writes, eng="dve"):
        if op1 is None:
            S.emit(eng, lambda e: e.tensor_scalar(out=out, in0=in0, scalar1=s1, scalar2=None, op0=op0), reads, writes, cost=ecost(eng, out))
        else:
            S.emit(eng, lambda e: e.tensor_scalar(out=out, in0=in0, scalar1=s1, scalar2=s2, op0=op0, op1=op1), reads, writes, cost=ecost(eng, out))

    def stt(out, in0, scalar, in1, op0, op1, reads, writes):
        S.emit("dve", lambda e: e.scalar_tensor_tensor(out=out, in0=in0, scalar=scalar, in1=in1, op0=op0, op1=op1), reads, writes, cost=ecost("dve", out))

    def cp(out, in_, reads, writes, eng="dve"):
        if eng == "act":
            act(out, in_, AF.Copy, reads, writes)
        else:
            S.emit(eng, lambda e: e.tensor_copy(out=out, in_=in_), reads, writes, cost=ecost(eng, out))

    def recip(out, in_, reads, writes):
        S.emit("dve", lambda e: e.reciprocal(out=out, in_=in_), reads, writes)

    def mset(ap, val, writes, eng="pool"):
        S.emit(eng, lambda e: e.memset(ap, val), (), writes)

    def mm(out, lhsT, rhs, start, stop, reads, writes):
        S.emit("pe", lambda e: e.matmul(out, lhsT=lhsT, rhs=rhs, start=start, stop=stop), reads, writes, cost=0.05 + fsz(rhs) / 2000.0)

    def dma(eng, out, in_, reads, writes):
        S.emit(eng, lambda e: e.dma_start(out=out, in_=in_), reads, writes, dma=True)

    identb = sb("identb", [128, 128], BF16); r_ident = R("ident")
    maskb = sb("maskb", [128, 128], BF16); r_mask = R("mask")
    jb = sb("jb", [128, 128], BF16); r_j = R("j")
    epst = sb("epst", [128, 1]); r_eps = R("eps")
    onesT = sb("onesT", [128, 8]); r_ones = R("ones")
    hvt = sb("hvt", [128, 1]); r_hv = R("hv")
    gmix = sb("gmix_s", [128, 8]); r_gmix = R("gmix")
    gffn = sb("gffn_s", [128, 8]); r_gffn = R("gffn")
    gfin = sb("gfin_s", [128, D]); r_gfin = R("gfin")
    gnrep = sb("gnrep", [128, 4, 128]); r_gn = R("gn")
    cw = sb("cw_s", [128, NFT, 3]); r_cw = R("cw")
    cb = sb("cb_s", [128, NFT]); r_cb = R("cb")
    lbl = sb("lbl_s", [128, 2, 4]); r_lbl = R("lbl")
    lb = sb("lb_s", [128, 4]); oml = sb("oml_s", [128, 4]); r_lb = R("lb")
    ET = sb("ET", [128, 5, 8, 128], BF16); r_ET = R("ET")

    dma("pool", identb[:], ident_d.ap(), (), [r_ident])
    dma("pool", maskb[:], mask_d.ap(), (), [r_mask])
    dma("pool", jb[:], jmat_d.ap(), (), [r_j])
    mset(epst[:], EPS, [r_eps])
    nhalf = sb("nhalf", [128, 8]); r_nh = R("nhalf")
    mset(nhalf[:], -0.5, [r_nh])
    mset(onesT[:], 1.0, [r_ones])
    dma("sp", hvt[:], hv_d.ap(), (), [r_hv])
    dma("sp", gmix[:], gmix_d.ap(), (), [r_gmix])
    dma("sp", gffn[:], gffn_d.ap(), (), [r_gffn])
    dma("sp", gfin[:], AP(gfin_d, 0, [[0, 128], [1, D]]), (), [r_gfin])
    dma("sp", gnrep[:], AP(gn_d, 0, [[0, 128], [0, 4], [1, 128]]), (), [r_gn])
    dma("sp", cw[:], cw_d.ap(), (), [r_cw])
    dma("sp", cb[:], cb_d.ap(), (), [r_cb])
    dma("sp", lbl[:], lbl_d.ap(), (), [r_lbl])

    if init_stop <= 1:
        S.finalize(st)
        with nc.Block() as block:
            S.run(block)
        st.close()
        return nc
    ps_att = st.enter_context(nc.psum_tensor("ps_att", [128, 1536], F32))
    r_att = [R("att0", True), R("att1", True)]
    r_attB = R("attB", True)
    NGEN = 2
    ps_gen = [st.enter_context(nc.psum_tensor("ps_g%d" % i, [128, 512], F32)) for i in range(NGEN)]
    r_gen = [R("g%d" % i, True) for i in range(NGEN)]
    ps_o = st.enter_context(nc.psum_tensor("ps_o", [128, 512], F32))
    r_o = R("ps_o", True)
    ps_tr = [st.enter_context(nc.psum_tensor("ps_t%d" % i, [128, 1024], BF16)) for i in range(2)]
    r_tr = [R("t%d" % i, True) for i in range(2)]
    cnt = {"g": 0, "t": 0, "a": 0, "p": 0}

    gpool_full = [(ps_gen[i], r_gen[i]) for i in range(NGEN)]
    gpool_front = gpool_full + [(ps_o, r_o)]
    gpool_back = [(ps_att[:, 0:512], r_att[0]), (ps_att[:, 512:1024], r_att[1]), (ps_att[:, 1024:1536], r_attB)]
    gstate = {"pool": gpool_full, "tr": (0, 1)}

    def gbank():
        pool = gstate["pool"]
        i = cnt["g"] % len(pool)
        cnt["g"] += 1
        return pool[i]

    def tbank():
        sel = gstate["tr"]
        i = sel[cnt["t"] % len(sel)]
        cnt["t"] += 1
        return ps_tr[i], r_tr[i]

    lbt = sb("lbt", [128, 4])
    tt(lbt[:], lbl[:, 1, :], lbl[:, 0, :], ALU.subtract, [r_lbl], [r_lb])
    act(lbt[:], lbt[:], AF.Exp, [r_lb], [r_lb])
    ts(lbt[:], lbt[:], 1.0, None, ALU.add, None, [r_lb], [r_lb])
    recip(lb[:], lbt[:], [r_lb], [r_lb])
    ts(oml[:], lb[:], -1.0, 1.0, ALU.mult, ALU.add, [r_lb], [r_lb])
    omlh = sb("omlh_s", [128, 4]); lbh = sb("lbh_s", [128, 4])
    ts(omlh[:], oml[:], 0.5, None, ALU.mult, None, [r_lb], [r_lb])
    tt(lbh[:], lb[:], omlh[:], ALU.add, [r_lb], [r_lb])

    if init_stop <= 2:
        S.finalize(st)
        with nc.Block() as block:
            S.run(block)
        st.close()
        return nc
    if init_stop <= 3:
        S.finalize(st)
        with nc.Block() as block:
            S.run(block)
        st.close()
        return nc
    r_wb = {}
    conv_jobs = []
    for name, src, dst, rows in (("in", w_in_d, wb_in, D), ("a", w_a_d, wb_a, 512), ("b", w_b_d, wb_b, 512),
                                 ("o", w_o_d, wb_o, D), ("g", w_g_d, wb_g, D), ("u", w_u_d, wb_u, D),
                                 ("d", w_d_d, wb_d, DFF)):
        r_wb[name] = R("wb_" + name)
        for r0 in range(0, rows, 128):
            conv_jobs.append((name, dst.ap()[r0:r0 + 128, :], src.ap()[r0:r0 + 128, :]))

    def emit_conv(n):
        for _ in range(n):
            if conv_jobs:
                name, d_, s_ = conv_jobs.pop(0)
                dma("pool", d_, s_, (), [r_wb[name]])

    wslot = [sb("wslot%d" % i, [128, SLOT_E], BF16) for i in range(NSLOT)]
    r_wslot = [R("wslot%d" % i) for i in range(NSLOT)]
    W1v = []
    for i, c0 in enumerate((2048, 2560, 512, 1024)):
        v_ = wslot[i][:, :].rearrange("p (k n) -> p k n", k=8)
        dma("pool", v_, w_in_d.ap()[:, c0:c0 + 512].rearrange("(k p) n -> p k n", p=128), (), [r_wslot[i]])
        W1v.append(v_)

    def wgroups(mode):
        g = []
        for i in (3, 4, 5, 6, 0, 1, 2, 7, 8, 9, 10):
            g.append(("in%d" % i, "in", wb_in.ap()[:, 512 * i:512 * i + 512].rearrange("(k p) n -> p k n", p=128), 8, 512))
        g.append(("a", "a", wb_a.ap().rearrange("(k p) n -> p k n", p=128), 4, 1024))
        g.append(("b", "b", wb_b.ap().rearrange("(k p) n -> p k n", p=128), 4, 1024))
        for n in range(2):
            g.append(("o%d" % n, "o", wb_o.ap()[:, 512 * n:512 * n + 512].rearrange("(k p) n -> p k n", p=128), 8, 512))
        for gi in range(6):
            nc_ = 512 if gi < 5 else 256
            g.append(("g%d" % gi, "g", wb_g.ap()[:, 512 * gi:512 * gi + nc_].rearrange("(k p) n -> p k n", p=128), 8, nc_))
            if mode != "pre":
                g.append(("u%d" % gi, "u", wb_u.ap()[:, 512 * gi:512 * gi + nc_].rearrange("(k p) n -> p k n", p=128), 8, nc_))
        if mode != "pre":
            for n in range(2):
                for gk in range(3):
                    nk = 8 if gk < 2 else 6
                    g.append(("d%d_%d" % (n, gk), "d",
                              wb_d.ap()[1024 * gk:1024 * gk + 128 * nk, 512 * n:512 * n + 512].rearrange("(k p) n -> p k n", p=128), nk, 512))
        return g

    wq = []
    tiles_plan = ([("pre", 0)] if do_pre else []) + [("main", m) for m in range(n_main)] + ([("sample", 0)] if do_sample else [])
    for mode, _ in tiles_plan:
        wq.extend(wgroups(mode))
    wstate = {"issued": 0, "consumed": 0, "slot": {}}

    def w_issue(slot):
        i = wstate["issued"]
        if i >= len(wq):
            return
        key, wname, src, nk, ncol = wq[i]
        view = wslot[slot][:, 0:nk * ncol].rearrange("p (k n) -> p k n", k=nk)
        dma("sp", view, src, [r_wb[wname]], [r_wslot[slot]])
        wstate["slot"][i] = slot
        wstate["issued"] += 1

    def w_next(key):
        i = wstate["consumed"]
        assert wq[i][0] == key, (wq[i][0], key)
        slot = wstate["slot"][i]
        _, _, _, nk, ncol = wq[i]
        view = wslot[slot][:, 0:nk * ncol].rearrange("p (k n) -> p k n", k=nk)
        return view, r_wslot[slot], slot

    def w_done(slot):
        wstate["consumed"] += 1
        w_issue(slot)

    if init_stop <= 4:
        S.finalize(st)
        with nc.Block() as block:
            S.run(block)
        st.close()
        return nc
    X = [sb("X%d" % i, [128, 2, D]) for i in range(2)]
    r_X = [[R("X%d_%d" % (i, j)) for j in range(2)] for i in range(2)]
    XN = sb("XN", [128, 2, D], BF16); r_XN = [R("XN0"), R("XN1")]
    ss = sb("ss", [128, 8]); r_ss = R("ss")
    hTs = [sb("hTa", [128, 8, T], BF16), sb("hTb", [128, 8, T], BF16), sb("h2T", [128, 8, T], BF16)]
    r_hTs = [[R("hTa0"), R("hTa1")], [R("hTb0"), R("hTb1")], [R("h2T0"), R("h2T1")]]
    qT = sb("qT", [128, 4, T], BF16); r_qT = R("qT")
    kTr = sb("kTr", [128, 4, 6, 128], BF16); r_kT = [R("kT%d" % i) for i in range(6)]
    Vr = sb("Vr", [128, 6, 8, 65], BF16); r_V = [R("V%d" % i) for i in range(6)]
    Pb = [sb("Pb%d" % i, [128, 5, 128], BF16) for i in range(3)]; r_Pb = [R("Pb%d" % i) for i in range(3)]
    oa = sb("oa", [128, 2, 512], BF16); r_oa = [R("oa0"), R("oa1")]
    oaT = sb("oaT", [128, 4, T], BF16); r_oaT = [R("oaT0"), R("oaT1")]
    rec = sb("rec", [128, 8]); r_rec = R("rec")
    sg = sb("sg", [128, 4, T]); r_sg = [R("sg%d" % i) for i in range(4)]
    siluq = sb("siluq", [128, 4, T]); r_sq = [R("siluq%d" % i) for i in range(4)]
    gF = sb("gF", [128, 4 * T]); r_gF = R("gF")
    gL = sb("gL", [128, 4 * T]); r_gL = R("gL")
    gB = sb("gB", [128, 4 * T]); r_gB = R("gB")
    gK = sb("gK", [128, 4 * T]); r_gK = R("gK")
    gE = sb("gE", [128, 4 * T]); r_gE = R("gE")
    dec = sb("dec", [128, 3, 4, 4]); r_dec = R("dec")
    Zq = sb("Zq", [128, 4, 4, 128], BF16); r_Zq = [R("Zq%d" % i) for i in range(4)]
    ktT = sb("ktT", [128, 4, T], BF16); r_ktT = [R("ktT%d" % i) for i in range(4)]
    khT = sb("khT", [128, 4, T], BF16); r_khT = [R("khT%d" % i) for i in range(4)]
    khtm = sb("khtm", [128, 2, 512], BF16); r_khtm = [R("khtm0"), R("khtm1")]
    vh = sb("vh", [128, 2, 512], BF16); r_vh = [R("vh0"), R("vh1")]
    sgb = sb("sgb", [128, 2, 512]); r_sgb = [R("sgb0"), R("sgb1")]
    Sst = sb("Sst", [128, 4, 128]); r_S = R("S")
    Sp = sb("Sp", [128, 2, 4, 128], BF16); r_Sp = [R("Sp0"), R("Sp1")]
    AT = sb("AT", [128, 4, 128], BF16); r_AT = R("AT")
    sqb = sb("sqb", [128, 512]); r_sqb = R("sqb")
    ssq = sb("ssq", [128, 4]); r_ssq = R("ssq")
    t1 = sb("t1", [128, 512]); r_t1 = R("t1")
    gG = sb("gG", [128, 512]); r_gG = R("gG")
    ob = sb("ob", [128, 512], BF16); r_ob = R("ob")
    obT = sb("obT", [128, 4, T], BF16); r_obT = [R("obT0"), R("obT1")]
    zs = sb("zs", [128, 16, T], BF16); r_zs = [R("zs%d" % i) for i in range(16)]
    mT = sb("mT", [128, 8, T], BF16); r_mT = [R("mT%d" % i) for i in range(8)]
    aT = [sb("aT%d" % i, [128, T + 2]) for i in range(2)]; r_aT = [R("aT0"), R("aT1")]
    c1 = [sb("c1_%d" % i, [128, T]) for i in range(2)]; r_c1 = [R("c1_0"), R("c1_1")]
    c2 = [sb("c2_%d" % i, [128, T]) for i in range(2)]; r_c2 = [R("c2_0"), R("c2_1")]
    m1, r_m1, m2, r_m2 = c1[0], r_c1[0], c2[0], r_c2[0]
    gT = sb("gT", [128, NFT, T], BF16); r_gT = [R("gT%d" % i) for i in range(NFT)]
    cprev = sb("cprev_s", [128, NFT, 2]); r_cprev = R("cprev")

    ext_ = siluq[0:8].rearrange("p h t -> p (h t)")[:, 0:768]; r_ext = r_sq; r_extd = R("extd")
    dma("sp", ext_[:, 64:256], rel_d.ap(), (), r_ext)
    act(ext_[:, 0:64], ext_[:, 64:65].to_broadcast([8, 64]), AF.Identity, r_ext, r_ext)
    act(ext_[:, 256:768], ext_[:, 255:256].to_broadcast([8, 512]), AF.Identity, r_ext, r_ext)
    act(ext_[:, :], ext_[:, :], AF.Exp, r_ext, r_ext)
    dma("sp", ext_d.ap(), ext_[:, :], r_ext, [r_extd])
    hk = sg[:].rearrange("p h t -> p (h t)").rearrange("p (a b) -> p a b", a=8); r_hk = r_sg
    hkb = ktT[:].rearrange("p h t -> p (h t)").rearrange("p (a b) -> p a b", a=8); r_hkb = r_ktT
    for kt in range(5):
        dma("sp", hk, AP(ext_d, 512 - 128 * kt, [[1, 128], [768, 8], [1, 128]]), [r_extd], r_hk)
        cp(hkb, hk, r_hk, r_hkb)
        for n in range(2):
            pb, rb = gbank()
            mm(pb[:, :], jb[:], hkb[:, 4 * n:4 * n + 4, :].rearrange("p h q -> p (h q)"), True, True, [r_j] + r_hkb, [rb])
            cp(ET[:, kt, 4 * n:4 * n + 4, :].rearrange("p h q -> p (h q)"), pb[:, :], [rb], [r_ET])
    mset(ET[0:64, 0, :, 64:128], 0.0, [r_ET])
    mset(ET[64:128, 4, :, 0:64], 0.0, [r_ET])

    kst = gT[:, 0:8, :].rearrange("p a b -> p (a b)").bitcast(F32).rearrange("p (h t) -> p h t", h=4)
    vst = gT[:, 8:16, :].rearrange("p a b -> p (a b)").bitcast(F32).rearrange("p (j c) -> p j c", j=2)
    rl_kst = r_gT[0:8]
    rl_vst = [r_gT[8:12], r_gT[12:16]]
    sg2 = gT[:, 0:8, :].rearrange("p a b -> p (a b)").bitcast(F32).rearrange("p (h t) -> p h t", h=4)
    r_sg2 = [[r_gT[2 * i], r_gT[2 * i + 1]] for i in range(4)]
    vh2 = gT[:, 8:12, :].rearrange("p a b -> p (a b)").rearrange("p (j c) -> p j c", j=2)
    r_vh2 = [[r_gT[8], r_gT[9]], [r_gT[10], r_gT[11]]]
    vh3 = gT[:, 12:16, :].rearrange("p a b -> p (a b)").rearrange("p (j c) -> p j c", j=2)
    r_vh3 = [[r_gT[12], r_gT[13]], [r_gT[14], r_gT[15]]]
    ktT2 = gT[:, 16:20, :]
    r_ktT2 = r_gT[16:20]
    dec2 = sb("dec2", [128, 3, 4, 4]); r_dec2 = R("dec2")
    sgs = [(sg, [[r] for r in r_sg]), (sg2, r_sg2)]
    vhs = [(vh, [[r] for r in r_vh]), (vh2, r_vh2), (vh3, r_vh3)]
    kts = [(ktT, r_ktT, dec, r_dec), (ktT2, r_ktT2, dec2, r_dec2)]
    mset(Zq[:], 0.0, r_Zq)
    mset(Sst[:], 0.0, [r_S])
    mset(cprev[:], 0.0, [r_cprev])

    if init_stop <= 5:
        S.finalize(st)
        with nc.Block() as block:
            S.run(block)
        st.close()
        return nc
    def load_x(xsrc, ntok, xb):
        nsub = (ntok + 127) // 128
        for j in range(nsub):
            nt = min(128, ntok - 128 * j)
            dma("sp", X[xb][:nt, j, :], xsrc[j * 128:j * 128 + nt, :], (), [r_X[xb][j]])

    def norm_T(src_tile, rsrc, gcol, rg, col0, hsel, ntok):
        nsub = (ntok + 127) // 128
        dst, rdst = hTs[hsel], r_hTs[hsel]
        for j in range(nsub):
            nt = min(128, ntok - 128 * j)
            act(XN[:nt, j, :], src_tile[:nt, j, :], AF.Square, [rsrc[j]], [r_XN[j], r_ss], accum_out=ss[:nt, col0 + j:col0 + j + 1])
            ts(ss[:nt, col0 + j:col0 + j + 1], ss[:nt, col0 + j:col0 + j + 1], 1.0 / D, EPS, ALU.mult, ALU.add, [r_ss], [r_ss], eng="pool")
            tt(ss[:nt, col0 + j:col0 + j + 1], ss[:nt, col0 + j:col0 + j + 1], nhalf[:nt, 0:1], ALU.pow, [r_ss, r_nh], [r_ss], eng="pool")
            act(XN[:nt, j, :], src_tile[:nt, j, :], AF.Copy, [rsrc[j], r_ss], [r_XN[j]], scale=ss[:nt, col0 + j:col0 + j + 1])
            pt, rt = tbank()
            for kc in range(8):
                S.emit("pe", lambda e, kc=kc, j=j, nt=nt, pt=pt: e.transpose(out=pt[:, kc * 128:kc * 128 + nt], in_=XN[:nt, j, kc * 128:(kc + 1) * 128],
                                                                              identity=identb[:nt, :nt]), [r_XN[j], r_ident], [rt])
            tt(dst[:, :, j * 128:j * 128 + nt], pt[:, :].rearrange("p (k t) -> p k t", k=8)[:, :, 0:nt],
               gcol[:, 0:8].unsqueeze(2).to_broadcast([128, 8, nt]), ALU.mult, [rt, rg], [rdst[j]])

    def macro_tile(mode, ntok, xb, gt0, hsel=0, normed=False, kv=False, out_row=None, final_kv=None, after_norm=None, prenorm=None,
                   part="all", bsel=0, vsel=0, ksel=0):
        sample = mode == "sample"
        nsub = (ntok + 127) // 128
        nts = [min(128, ntok - 128 * j) for j in range(nsub)]
        C = 16 if sample else (ntok if mode == "scan" else 64)
        nch = ntok // C
        ri = C - 1 if mode == "scan" else C // 2 - 1
        cps = 1 if sample else 2
        Xb = X[xb]
        rX = r_X[xb]
        hT = hTs[hsel]
        r_hT = r_hTs[hsel]
        rhT_all = r_hT[:nsub]
        cur["hT"] = hT
        cur["r_hT"] = r_hT
        sg, rl_sg = sgs[bsel]
        vh, rl_vh = vhs[vsel]
        ktT, r_ktT, dec, r_dec = kts[ksel]
        if mode == "scan":
            gstate["pool"] = gpool_front if part == "front" else gpool_back
            gstate["tr"] = (0,) if part == "front" else (1,)
        else:
            gstate["pool"] = gpool_front + gpool_back
            gstate["tr"] = (0, 1)

        if part in ("all", "front"):
            if not normed:
                norm_T(Xb, rX, gmix, r_gmix, 0, hsel, ntok)
            if after_norm is not None:
                after_norm()

        def proj_fm(wv, rw, ct, evac):
            pb, rb = gbank()
            for kc in range(8):
                mm(pb[:, 0:ntok], wv[:, kc, ct * 128:(ct + 1) * 128], hT[:, kc, 0:ntok], kc == 0, kc == 7, [rw] + rhT_all, [rb])
            evac(pb, rb)

        def proj_tm(wv, rw, j, evac, ncol=512):
            pb, rb = gbank()
            nt = nts[j]
            for kc in range(8):
                mm(pb[:nt, 0:ncol], hT[:, kc, j * 128:j * 128 + nt], wv[:, kc, 0:ncol], kc == 0, kc == 7, [rw, r_hT[j]], [rb])
            evac(pb, rb, j, nt)

        def slot_of(j):
            return (gt0 + j) % 6 if not sample else 4

        def hgrn_gates_all():
            n4 = 4 * ntok
            nc4 = 4 * nch
            fl = lambda b: b[:, 0:n4]
            v3 = lambda b: b[:, 0:n4].rearrange("p (h t) -> p h t", h=4)
            ch = lambda b: b[:, 0:n4].rearrange("p (c t) -> p c t", t=C)
            hc = lambda b: b[:, 0:n4].rearrange("p (h c t) -> p h c t", h=4, t=C)
            rsg = [r for l_ in rl_sg for r in l_]
            sgf = sg.rearrange("p h t -> p (h t)")
            tt(v3(gF), v3(sgf), omlh[:, 0:4].unsqueeze(2).to_broadcast([128, 4, ntok]), ALU.mult, rsg + [r_lb], [r_gF])
            tt(v3(gF), v3(gF), lbh[:, 0:4].unsqueeze(2).to_broadcast([128, 4, ntok]), ALU.add, [r_gF, r_lb], [r_gF])
            act(fl(gL), fl(gF), AF.Ln, [r_gF], [r_gL])
            S.emit("dve", lambda e: e.tensor_tensor_scan(out=fl(gB), data0=fl(gL), data1=fl(gL), initial=0.0, op0=ALU.add, op1=ALU.min),
                   [r_gL], [r_gB], cost=0.1 + 2 * n4 / 900.0)
            ts(fl(gK), fl(gF), -1.0, 1.0, ALU.mult, ALU.add, [r_gF], [r_gK], eng="pool")
            tt(ch(gL), ch(gB), ch(gB)[:, :, ri:ri + 1].to_broadcast([128, nc4, C]), ALU.subtract, [r_gB, r_gL], [r_gL])
            act(fl(gE), fl(gL), AF.Exp, [r_gL], [r_gE], scale=-1.0)
            tt(ktT[:, :, 0:ntok], v3(gK), v3(gE), ALU.mult, [r_gK, r_gE], r_ktT, eng="pool")
            dsl = lambda i: dec[:, i, :, 0:nch]
            if mode != "scan":
                act(fl(gB), fl(gL), AF.Exp, [r_gL, r_gB], [r_gB])
                cp(dsl(2), hc(gB)[:, :, :, C - 1], [r_gB], [r_dec])
            tt(dsl(0), hc(gE)[:, :, :, 0], hc(gF)[:, :, :, 0], ALU.mult, [r_gE, r_gF], [r_dec])
            if mode == "scan":
                return
            tt(dsl(1), dsl(0), dsl(2), ALU.mult, [r_dec], [r_dec])
            k4 = lambda b: b[:, :, 0:ntok].rearrange("p h (c t) -> p h c t", t=C)
            tt(k4(khT), k4(ktT), dsl(2).unsqueeze(3).to_broadcast([128, 4, nch, C]), ALU.mult, r_ktT + [r_dec], r_khT)
            if mode != "scan":
                sqf = siluq.rearrange("p h t -> p (h t)")
                if sample:
                    tt(Zq[:, :, 0, 0:ntok], v3(sqf), v3(gB), ALU.mult, r_sq + [r_gB], r_Zq)
                else:
                    base = Zq[:, 0, 0, 0:64]
                    zout = AP(base.tensor, base.offset, [list(base.ap[0]), [512 // nsub, 4 * nsub], [192, 2], [1, 64]])
                    tt(zout, fl(sqf).rearrange("p (a c t) -> p a c t", c=2, t=64), fl(gB).rearrange("p (a c t) -> p a c t", c=2, t=64),
                       ALU.mult, r_sq + [r_gB], r_Zq)

        def khat_transpose(j):
            nt = nts[j]
            pt, rt = tbank()
            ksrc, rks = (ktT, r_ktT) if mode == "scan" else (khT, r_khT)
            for hb in range(4):
                S.emit("pe", lambda e, hb=hb, pt=pt: e.transpose(out=pt[:nt, hb * 128:(hb + 1) * 128], in_=ksrc[:, hb, j * 128:j * 128 + nt],
                                                                  identity=identb[:, :]), [rks[hb], r_ident], [rt])
            cp(khtm[:nt, j, :], pt[:nt, 0:512], [rt], [r_khtm[j]], eng="act")

        def s_update(j, ci):
            p = ci % cps
            rows = slice(p * C, p * C + C)
            pb, rb = gbank()
            for hb in range(4):
                mm(pb[:, hb * 128:(hb + 1) * 128], khtm[rows, j, hb * 128:(hb + 1) * 128], vh[rows, j, hb * 128:(hb + 1) * 128],
                   True, True, [r_khtm[j]] + rl_vh[j], [rb])
            tt(Sst[:], Sst[:], dec[:, 1, :, ci:ci + 1].to_broadcast([128, 4, 128]), ALU.mult, [r_S, r_dec], [r_S])
            tt(Sst[:].rearrange("p h v -> p (h v)"), Sst[:].rearrange("p h v -> p (h v)"), pb[:, :], ALU.add, [r_S, rb], [r_S])

        if mode == "scan":
            wv_f, wv_i = W1v[0], W1v[1]
            if part == "front":
                for hb in range(4):
                    proj_fm(wv_f, r_wslot[0], hb, lambda pb, rb, hb=hb: act(sg.rearrange("p h t -> p (h t)")[:, hb * ntok:(hb + 1) * ntok], pb[:, 0:ntok], AF.Tanh, [rb], rl_sg[hb], scale=0.5))
                for j in range(nsub):
                    proj_tm(wv_i, r_wslot[1], j, lambda pb, rb, j, nt: cp(vh[:nt, j, :], pb[:nt, :], [rb], rl_vh[j], eng="act"))
                if kv:
                    kv_proj_only()
            if part == "gates":
                hgrn_gates_all()
            if part == "upd":
                for j in range(nsub):
                    khat_transpose(j)
                pb, rb = gbank()
                for hb in range(4):
                    for j in range(nsub):
                        mm(pb[:, hb * 128:(hb + 1) * 128], khtm[:, j, hb * 128:(hb + 1) * 128], vh[:, j, hb * 128:(hb + 1) * 128],
                           j == 0, j == nsub - 1, [r_khtm[j]] + rl_vh[j], [rb])
                tt(Sst[:], Sst[:], dec[:, 0, :, 0:1].to_broadcast([128, 4, 128]), ALU.mult, [r_S, r_dec], [r_S])
                tt(Sst[:].rearrange("p h v -> p (h v)"), Sst[:].rearrange("p h v -> p (h v)"), pb[:, :], ALU.add, [r_S, rb], [r_S])
            return

        def ev_q(ct):
            return lambda pb, rb: act(qT[:, ct, 0:ntok], pb[:, 0:ntok], AF.Copy, [rb], [r_qT], scale=0.125)

        def ev_k(ct):
            def f(pb, rb):
                for j in range(nsub):
                    cp(kTr[:, ct, slot_of(j), 0:nts[j]], pb[:, j * 128:j * 128 + nts[j]], [rb], [r_kT[slot_of(j)]], eng="act")
                if final_kv is not None:
                    cp(kst[:, ct, 0:ntok], pb[:, 0:ntok], [rb], rl_kst)
            return f

        def ev_v(pb, rb, j, nt):
            s_ = slot_of(j)
            cp(Vr[:nt, s_, :, 0:64], pb[:nt, :].rearrange("p (h d) -> p h d", h=8), [rb], [r_V[s_]], eng="act")
            if mode == "main" or sample:
                cp(Vr[:nt, s_, :, 64:65], onesT[:nt, 0:8].unsqueeze(2), [r_ones], [r_V[s_]], eng="pool")
            else:
                cp(Vr[:nt, s_, :, 64:65], hvt[:nt, 0:1].unsqueeze(2).to_broadcast([nt, 8, 1]), [r_hv], [r_V[s_]], eng="pool")
            if final_kv is not None:
                cp(vst[:nt, j, :], pb[:nt, :], [rb], rl_vst[j])
                dma("act", final_kv[1].ap()[final_kv[2] + j * 128:final_kv[2] + j * 128 + nt, :], vst[:nt, j, :], rl_vst[j], ())

        wv, rw, sl = w_next("in3")
        for hb in range(4):
            proj_fm(wv, rw, hb, lambda pb, rb, hb=hb: act(siluq.rearrange("p h t -> p (h t)")[:, hb * ntok:(hb + 1) * ntok], pb[:, 0:ntok], AF.Silu, [rb], [r_sq[hb]]))
        w_done(sl)
        wv, rw, sl = w_next("in4")
        for hb in range(4):
            proj_fm(wv, rw, hb, lambda pb, rb, hb=hb: act(sg.rearrange("p h t -> p (h t)")[:, hb * ntok:(hb + 1) * ntok], pb[:, 0:ntok], AF.Tanh, [rb], rl_sg[hb], scale=0.5))
        w_done(sl)
        wv, rw, sl = w_next("in5")
        for j in range(nsub):
            proj_tm(wv, rw, j, lambda pb, rb, j, nt: cp(vh[:nt, j, :], pb[:nt, :], [rb], rl_vh[j], eng="act"))
        w_done(sl)
        wv, rw, sl = w_next("in6")
        for j in range(nsub):
            proj_tm(wv, rw, j, lambda pb, rb, j, nt: act(sgb[:nt, j, :], pb[:nt, :], AF.Silu, [rb], [r_sgb[j]]))
        w_done(sl)
        wv, rw, sl = w_next("in0")
        for ct in range(4):
            proj_fm(wv, rw, ct, ev_q(ct))
        w_done(sl)
        wv, rw, sl = w_next("in1")
        for ct in range(4):
            proj_fm(wv, rw, ct, ev_k(ct))
        w_done(sl)
        if final_kv is not None:
            for ct in range(4):
                dma("act", final_kv[0].ap()[ct, :, final_kv[2]:final_kv[2] + ntok], kst[:, ct, 0:ntok], rl_kst, ())
        wv, rw, sl = w_next("in2")
        for j in range(nsub):
            proj_tm(wv, rw, j, ev_v)
        w_done(sl)

        att_steps, z_steps, h_steps = [], [], []

        zst = {}

        def z_step(zi):
            gi, ct = divmod(zi, 4)
            if ct == 0:
                zst["w"] = w_next("in%d" % (7 + gi))
            wv_, rw_, sl_ = zst["w"]
            proj_fm(wv_, rw_, ct, lambda pb, rb: act(zs[:, zi, 0:ntok], pb[:, 0:ntok], AF.Tanh, [rb], [r_zs[zi]], scale=0.5))
            if ct == 3:
                w_done(sl_)
        for zi in range(16):
            z_steps.append(lambda zi=zi: z_step(zi))

        def make_att(j):
            nq = nts[j]
            if sample:
                ktiles = [(0, 128), (1, 128), (2, 128), (3, 128), (4, 16)]
            else:
                ktiles = [((gt0 + j - 4 + kt) % 6, 128) for kt in range(5)]
            pend = []

            def pv(h, pslot):
                for kt, (s_, nk) in enumerate(ktiles):
                    mm(ps_o[:nq, (h % 4) * 65:(h % 4) * 65 + 65], Pb[pslot][:nk, kt, 0:nq], Vr[:nk, s_, h, :], kt == 0, kt == 4,
                       [r_Pb[pslot], r_V[s_]], [r_o])

            def normalize(half):
                o3 = ps_o[:nq, 0:260].rearrange("p (h d) -> p h d", h=4)
                ts(rec[:nq, half * 4:half * 4 + 4].unsqueeze(2), o3[:, :, 64:65], 1e-30, None, ALU.max, None, [r_o], [r_rec])
                recip(rec[:nq, half * 4:half * 4 + 4], rec[:nq, half * 4:half * 4 + 4], [r_rec], [r_rec])
                tt(oa[:nq, j, half * 256:half * 256 + 256].rearrange("p (h d) -> p h d", h=4), o3[:, :, 0:64],
                   rec[:nq, half * 4:half * 4 + 4].unsqueeze(2).to_broadcast([nq, 4, 64]), ALU.mult, [r_o, r_rec], [r_oa[j]])

            def head(h):
                hp, r0 = h // 2, (h % 2) * 64
                ai = cnt["a"] % 2
                cnt["a"] += 1
                offA = ai * 512
                offB = 1024 + ai * 128
                for kt in (4, 0, 1, 2, 3):
                    s_, nk = ktiles[kt]
                    o_ = offB if kt == 4 else offA + kt * 128
                    mm(ps_att[:nk, o_:o_ + nq], kTr[r0:r0 + 64, hp, s_, 0:nk], qT[r0:r0 + 64, hp, j * 128:j * 128 + nq],
                       True, True, [r_kT[s_], r_qT], [r_attB if kt == 4 else r_att[ai]])
                pi = cnt["p"] % 3
                cnt["p"] += 1
                sattA = ps_att[:, offA:offA + 512].rearrange("p (k q) -> p k q", k=4)
                sattB = ps_att[:, offB:offB + 128]
                if sample:
                    act(Pb[pi][:16, 4, 0:nq], sattB[:16, 0:nq], AF.Exp, [r_attB], [r_Pb[pi]])
                    act(Pb[pi][:, 0:4, 0:nq], sattA[:, :, 0:nq], AF.Exp, [r_att[ai]], [r_Pb[pi]])
                    tt(Pb[pi][:, 0:4, 0:nq], Pb[pi][:, 0:4, 0:nq], ET[:, 0:4, h, 0:nq], ALU.mult, [r_Pb[pi], r_ET], [r_Pb[pi]], eng="pool")
                    tt(Pb[pi][:16, 4, 0:nq], Pb[pi][:16, 4, 0:nq], ET[:16, 4, h, 0:nq], ALU.mult, [r_Pb[pi], r_ET], [r_Pb[pi]], eng="pool")
                else:
                    act(Pb[pi][:, 4, :], sattB, AF.Exp, [r_attB], [r_Pb[pi]])
                    act(Pb[pi][:, 0:4, :], sattA, AF.Exp, [r_att[ai]], [r_Pb[pi]])
                    tt(Pb[pi][:, :, :], Pb[pi][:, :, :], ET[:, :, h, :], ALU.mult, [r_Pb[pi], r_ET], [r_Pb[pi]], eng="pool")
                if pend:
                    ph, ppi = pend.pop(0)
                    pv(ph, ppi)
                    if ph == 3:
                        normalize(0)
                pend.append((h, pi))

            def tail():
                ph, ppi = pend.pop(0)
                pv(ph, ppi)
                normalize(1)
                pt, rt = tbank()
                for kc in range(4):
                    S.emit("pe", lambda e, kc=kc, pt=pt: e.transpose(out=pt[:, kc * 128:kc * 128 + nq], in_=oa[:nq, j, kc * 128:(kc + 1) * 128],
                                                                      identity=identb[:nq, :nq]), [r_oa[j], r_ident], [rt])
                cp(oaT[:, :, j * 128:j * 128 + nq], pt[:, 0:512].rearrange("p (k t) -> p k t", k=4)[:, :, 0:nq], [rt], [r_oaT[j]], eng="act")
            for h in range(8):
                att_steps.append(lambda h=h: head(h))
            att_steps.append(tail)
        for j in range(nsub):
            make_att(j)

        def make_h(j):
            nt = nts[j]

            def h_at():
                pb, rb = gbank()
                for hb in range(4):
                    if sample:
                        qrhs = Zq[:, hb, 0, 0:nt]
                    else:
                        base = Zq[:, hb, 2 * j, 0:64]
                        qrhs = AP(base.tensor, base.offset, [list(base.ap[0]), [192, 2], [1, 64]])
                    mm(pb[:nt, hb * 128:hb * 128 + nt], ktT[:, hb, j * 128:j * 128 + nt], qrhs, True, True, [r_ktT[hb], r_Zq[hb]], [rb])
                tt(AT[:nt, :, 0:nt], pb[:nt, :].rearrange("p (h t) -> p h t", h=4)[:, :, 0:nt],
                   maskb[:nt, 0:nt].unsqueeze(1).to_broadcast([nt, 4, nt]), ALU.mult, [rb, r_mask], [r_AT])

            def h_chunk(p):
                ci = j * cps + p
                tt(Sp[:, p], Sst[:], dec[:, 0, :, ci:ci + 1].to_broadcast([128, 4, 128]), ALU.mult, [r_S, r_dec], [r_Sp[p]])
                s_update(j, ci)

            def h_out():
                ob_, rob = gbank()
                for hb in range(4):
                    for p in range(cps):
                        zl = Zq[:, hb, 0, 0:nt] if sample else Zq[:, hb, 2 * j + p, :]
                        mm(ob_[:nt, hb * 128:(hb + 1) * 128], zl, Sp[:, p, hb, :], p == 0, False, [r_Zq[hb], r_Sp[p]], [rob])
                    mm(ob_[:nt, hb * 128:(hb + 1) * 128], AT[:nt, hb, 0:nt], vh[:nt, j, hb * 128:(hb + 1) * 128], False, True, [r_AT] + rl_vh[j], [rob])
                act(sqb[:nt, :], ob_[:nt, :], AF.Square, [rob], [r_sqb])
                S.emit("dve", lambda e: e.tensor_reduce(out=ssq[:nt, 0:4], in_=sqb[:nt, :].rearrange("p (h v) -> p h v", h=4), axis=AX.X, op=ALU.add),
                       [r_sqb], [r_ssq])
                ts(ssq[:nt, :], ssq[:nt, :], 1.0 / 128, EPS, ALU.mult, ALU.add, [r_ssq], [r_ssq], eng="pool")
                tt(ssq[:nt, :], ssq[:nt, :], nhalf[:nt, 0:4], ALU.pow, [r_ssq, r_nh], [r_ssq], eng="pool")
                tt(t1[:nt, :].rearrange("p (h v) -> p h v", h=4), ob_[:nt, :].rearrange("p (h v) -> p h v", h=4),
                   ssq[:nt, 0:4].unsqueeze(2).to_broadcast([nt, 4, 128]), ALU.mult, [rob, r_ssq], [r_t1])
                tt(gG[:nt, :], sgb[:nt, j, :], gnrep[:nt].rearrange("p h v -> p (h v)"), ALU.mult, [r_sgb[j], r_gn], [r_gG], eng="pool")
                tt(ob[:nt, :], t1[:nt, :], gG[:nt, :], ALU.mult, [r_t1, r_gG], [r_ob])

            def h_tr():
                pt, rt = tbank()
                for hb in range(4):
                    S.emit("pe", lambda e, hb=hb, pt=pt: e.transpose(out=pt[:, hb * 128:hb * 128 + nt], in_=ob[:nt, hb * 128:(hb + 1) * 128],
                                                                      identity=identb[:nt, :nt]), [r_ob, r_ident], [rt])
                cp(obT[:, :, j * 128:j * 128 + nt], pt[:, 0:512].rearrange("p (k t) -> p k t", k=4)[:, :, 0:nt], [rt], [r_obT[j]], eng="act")
            h_steps.append(lambda: khat_transpose(j))
            h_steps.append(h_at)
            for p in range(cps):
                h_steps.append(lambda p=p: h_chunk(p))
            h_steps.append(h_out)
            h_steps.append(h_tr)
        for j in range(nsub):
            make_h(j)

        def run_steps(steps, pool, tr, pre=None):
            def f():
                gstate["pool"] = pool
                gstate["tr"] = tr
                if pre is not None:
                    pre()
                for st_ in steps:
                    st_()
            return f
        att_ops = S.capture(run_steps(att_steps, [], (0,)))
        z_ops = S.capture(run_steps(z_steps, gpool_full[0:1], (0,)))
        h_ops = S.capture(run_steps(h_steps, gpool_full[1:2], (1,), pre=hgrn_gates_all))
        S.emit_merged(att_ops, z_ops, h_ops)
        gstate["tr"] = (0, 1)

        gstate["pool"] = gpool_front + gpool_back
        wva, rwa, sla = w_next("a")
        wstate["consumed"] += 1
        wvb, rwb, slb = w_next("b")
        wstate["consumed"] -= 1
        for ct in range(8):
            pb, rb = gbank()
            for kc in range(4):
                mm(pb[:, 0:ntok], wva[:, kc, ct * 128:(ct + 1) * 128], oaT[:, kc, 0:ntok], kc == 0, kc == 3, [rwa] + r_oaT[:nsub], [rb])
            for kc in range(4):
                mm(pb[:, 256:256 + ntok], wvb[:, kc, ct * 128:(ct + 1) * 128], obT[:, kc, 0:ntok], kc == 0, kc == 3, [rwb] + r_obT[:nsub], [rb])
            stt(m1[:, 0:ntok], zs[:, ct, 0:ntok], 1.0, pb[:, 0:ntok], ALU.add, ALU.mult, [rb, r_zs[ct]], [r_m1])
            stt(m2[:, 0:ntok], zs[:, 8 + ct, 0:ntok], 1.0, pb[:, 256:256 + ntok], ALU.add, ALU.mult, [rb, r_zs[8 + ct]], [r_m2])
            tt(mT[:, ct, 0:ntok], m1[:, 0:ntok], m2[:, 0:ntok], ALU.add, [r_m1, r_m2], [r_mT[ct]], eng="pool")
        w_done(sla)
        w_done(slb)

        for n in range(2):
            wv, rw, sl = w_next("o%d" % n)
            for j in range(nsub):
                nt = nts[j]
                pb, rb = gbank()
                for kc in range(8):
                    mm(pb[:nt, :], mT[:, kc, j * 128:j * 128 + nt], wv[:, kc, :], kc == 0, kc == 7, [rw, r_mT[kc]], [rb])
                stt(Xb[:nt, j, n * 512:(n + 1) * 512], pb[:nt, :], 0.5, Xb[:nt, j, n * 512:(n + 1) * 512], ALU.mult, ALU.add, [rX[j], rb], [rX[j]])
            w_done(sl)
        norm_T(Xb, rX, gffn, r_gffn, 2, 2, ntok)
        hT = hTs[2]
        r_hT = r_hTs[2]
        rhT_all = r_hT[:nsub]

        ffn_pend = []
        for gi in range(6):
            ntile = 4 if gi < 5 else 2
            if gi == 2 and prenorm is not None:
                prenorm()
            wvg, rwg, slg = w_next("g%d" % gi)
            if mode != "pre":
                wstate["consumed"] += 1
                wvu, rwu, slu = w_next("u%d" % gi)
                wstate["consumed"] -= 1
            for ct in range(ntile):
                ft = gi * 4 + ct
                pb, rb = gbank()
                if mode == "pre":
                    for kc in range(8):
                        mm(pb[:, 0:2], wvg[:, kc, ct * 128:(ct + 1) * 128], hT[:, kc, ntok - 2:ntok], kc == 0, kc == 7, [rwg] + rhT_all, [rb])
                    cp(cprev[:, ft, :], pb[:, 0:2], [rb], [r_cprev])
                    continue
                for kc in range(8):
                    mm(pb[:, 0:ntok], wvg[:, kc, ct * 128:(ct + 1) * 128], hT[:, kc, 0:ntok], kc == 0, kc == 7, [rwg] + rhT_all, [rb])
                for kc in range(8):
                    mm(pb[:, 256:256 + ntok], wvu[:, kc, ct * 128:(ct + 1) * 128], hT[:, kc, 0:ntok], kc == 0, kc == 7, [rwu] + rhT_all, [rb])
                bi = ft % 2
                a_ = aT[bi]
                cp(a_[:, 0:2], cprev[:, ft, :], [r_cprev], [r_aT[bi]], eng="pool")
                act(a_[:, 2:2 + ntok], pb[:, 0:ntok], AF.Copy, [rb], [r_aT[bi]])
                cp(cprev[:, ft, :], a_[:, ntok:ntok + 2], [r_aT[bi]], [r_cprev], eng="pool")
                act(c1[bi][:, 0:ntok], pb[:, 0:ntok], AF.Identity, [rb, r_cw, r_cb], [r_c1[bi]], scale=cw[:, ft, 2:3], bias=cb[:, ft:ft + 1])
                stt(c2[bi][:, 0:ntok], a_[:, 1:1 + ntok], cw[:, ft, 1:2], c1[bi][:, 0:ntok], ALU.mult, ALU.add, [r_aT[bi], r_c1[bi], r_cw], [r_c2[bi]])
                stt(c1[bi][:, 0:ntok], a_[:, 0:ntok], cw[:, ft, 0:1], c2[bi][:, 0:ntok], ALU.mult, ALU.add, [r_aT[bi], r_c2[bi], r_cw], [r_c1[bi]])
                def fin(bi=bi, ft=ft, pb=pb, rb=rb):
                    act(c2[bi][:, 0:ntok], c1[bi][:, 0:ntok], AF.Gelu_apprx_tanh, [r_c1[bi]], [r_c2[bi]])
                    tt(gT[:, ft, 0:ntok], c2[bi][:, 0:ntok], pb[:, 256:256 + ntok], ALU.mult, [r_c2[bi], rb], [r_gT[ft]])
                if ffn_pend:
                    ffn_pend.pop(0)()
                ffn_pend.append(fin)
            w_done(slg)
            if mode != "pre":
                w_done(slu)
        if mode == "pre":
            return
        while ffn_pend:
            ffn_pend.pop(0)()

        for n in range(2):
            banks = [gbank() for _ in range(nsub)]
            for gk in range(3):
                wv, rw, sl = w_next("d%d_%d" % (n, gk))
                nk = 8 if gk < 2 else 6
                for kl in range(nk):
                    kc = gk * 8 + kl
                    for j in range(nsub):
                        nt = nts[j]
                        mm(banks[j][0][:nt, :], gT[:, kc, j * 128:j * 128 + nt], wv[:, kl, :], kc == 0, kc == NFT - 1, [rw, r_gT[kc]], [banks[j][1]])
                w_done(sl)
            for j in range(nsub):
                nt = nts[j]
                tt(Xb[:nt, j, n * 512:(n + 1) * 512], Xb[:nt, j, n * 512:(n + 1) * 512], banks[j][0][:nt, :], ALU.add, [rX[j], banks[j][1]], [rX[j]])
        for j in range(nsub):
            nt = nts[j]
            act(XN[:nt, j, :], Xb[:nt, j, :], AF.Square, [rX[j]], [r_XN[j], r_ss], accum_out=ss[:nt, 4 + j:5 + j])
            ts(ss[:nt, 4 + j:5 + j], ss[:nt, 4 + j:5 + j], 1.0 / D, EPS, ALU.mult, ALU.add, [r_ss], [r_ss], eng="pool")
            tt(ss[:nt, 4 + j:5 + j], ss[:nt, 4 + j:5 + j], nhalf[:nt, 0:1], ALU.pow, [r_ss, r_nh], [r_ss], eng="pool")
            stt(Xb[:nt, j, :], Xb[:nt, j, :], ss[:nt, 4 + j:5 + j], gfin[:nt, :], ALU.mult, ALU.mult, [rX[j], r_ss, r_gfin], [rX[j]])
            dma("act", out_row[j * 128:j * 128 + nt, :], Xb[:nt, j, :], [rX[j]], ())

    cur = {}

    def kv_proj_only():
        ntok, gt0 = cur["ntok"], cur["gt0"]
        for ct in range(4):
            pb, rb = gbank()
            for kc in range(8):
                mm(pb[:, 0:ntok], W1v[2][:, kc, ct * 128:(ct + 1) * 128], cur["hT"][:, kc, 0:ntok], kc == 0, kc == 7, [r_wslot[2]] + cur["r_hT"], [rb])
            for j in range(2):
                s_ = (gt0 + j) % 6
                cp(kTr[:, ct, s_, :], pb[:, j * 128:(j + 1) * 128], [rb], [r_kT[s_]], eng="act")
        for j in range(2):
            s_ = (gt0 + j) % 6
            pb, rb = gbank()
            for kc in range(8):
                mm(pb[:, :], cur["hT"][:, kc, j * 128:(j + 1) * 128], W1v[3][:, kc, :], kc == 0, kc == 7, [r_wslot[3], cur["r_hT"][j]], [rb])
            cp(Vr[:, s_, :, 0:64], pb[:, :].rearrange("p (h d) -> p h d", h=8), [rb], [r_V[s_]], eng="act")
            cp(Vr[:, s_, :, 64:65], hvt[:, 0:1].unsqueeze(2).to_broadcast([128, 8, 1]), [r_hv], [r_V[s_]], eng="pool")


    xa = xin.ap()
    nscan = n_scan
    nmain = n_main
    plan = []
    for m in range(nscan):
        plan.append(("scan", xa[m * T:(m + 1) * T, :], T, dict(gt0=-5 + 2 * (m - (nscan - 2)) + 6, kv=m >= nscan - 2)))
    if do_pre:
        plan.append(("pre", xa[NPRE:NPRE + PRE_T, :], PRE_T, dict(gt0=5)))
    for m in range(nmain):
        fk = (okT_d, ov_d, (m - (nmain - 2)) * T) if (m >= nmain - 2 and not NOFK) else None
        r0 = NPRE + PRE_T + m * T
        plan.append(("main", xa[r0:r0 + T, :], T, dict(gt0=2 * m + 6, out_row=y_d.ap()[m * T:(m + 1) * T, :], final_kv=fk)))
    if do_sample:
        plan.append(("sample", xs_d.ap(), 16, dict(gt0=0, out_row=ys_d.ap(), final_kv=(okTs_d, ovs_d, 0))))
    if plan:
        load_x(plan[0][1], plan[0][2], 0)
    if not do_conv:
        conv_jobs.clear()
    for i, (mode, xsrc, ntok, kw) in enumerate(plan):
        xb = i % 2
        nxt = None
        pren = None
        if i + 1 < len(plan):
            nxt = (lambda p=plan[i + 1], b=(i + 1) % 2: load_x(p[1], p[2], b))
            if mode == "main":
                pren = (lambda p=plan[i + 1], b=(i + 1) % 2: norm_T(X[b], r_X[b], gmix, r_gmix, 0, b, p[2]))
        normed = i > 0 and plan[i - 1][0] == "main"
        if mode == "scan":
            if i > 0:
                continue

            def stage(k, part):
                md, xs_, nt_, kw_ = plan[k]
                if part == "front":
                    cur["ntok"] = nt_
                    cur["gt0"] = kw_["gt0"]
                    emit_conv(2)
                    nx = (lambda p=plan[k + 1], b=(k + 1) % 2: load_x(p[1], p[2], b)) if k + 1 < len(plan) else None
                else:
                    nx = None
                macro_tile("scan", nt_, k % 2, hsel=k % 2, after_norm=nx, part=part, bsel=k % 2, vsel=k % 3, ksel=k % 2, **kw_)
            for w in range(-2, nscan):
                lists = []
                if 0 <= w:
                    lists.append(S.capture(lambda: stage(w, "upd")))
                if 0 <= w + 1 < nscan:
                    lists.append(S.capture(lambda: stage(w + 1, "gates")))
                if w + 2 < nscan:
                    lists.append(S.capture(lambda: stage(w + 2, "front")))
                S.emit_merged(*lists)
            continue
        if i == nscan:
            emit_conv(len(conv_jobs))
            for s_ in range(NSLOT):
                w_issue(s_)
        if mode == "sample":
            dma("act", oS_d.ap().rearrange("h k v -> k h v"), Sst[:], [r_S], ())
            dma("act", oconv_d.ap(), cprev[:], [r_cprev], ())
            dma("pool", kTr[:, :, 0:4, :], ckT_d.ap().rearrange("p c (t k) -> p c t k", t=4), (), r_kT[0:4])
            for t_ in range(4):
                dma("pool", Vr[:, t_, :, 0:64], cv_d.ap()[:, t_ * 128:(t_ + 1) * 128, :].rearrange("h p d -> p h d"), (), [r_V[t_]])
                cp(Vr[:, t_, :, 64:65], onesT[:, 0:8].unsqueeze(2), [r_ones], [r_V[t_]], eng="pool")
            dma("sp", Sst[:], s0_d.ap().rearrange("h k v -> k h v"), (), [r_S])
            dma("sp", cprev[:], cprev_d.ap(), (), [r_cprev])
        macro_tile(mode, ntok, xb, hsel=xb, normed=normed, after_norm=nxt, prenorm=pren, **kw)
    emit_conv(len(conv_jobs))
    if not do_sample:
        dma("act", oS_d.ap().rearrange("h k v -> k h v"), Sst[:], [r_S], ())
        dma("act", oconv_d.ap(), cprev[:], [r_cprev], ())
    else:
        dma("act", oSs_d.ap().rearrange("h k v -> k h v"), Sst[:], [r_S], ())
        dma("act", oconvs_d.ap(), cprev[:], [r_cprev], ())
    for nm, getter in dumps:
        ap_, res_ = getter(locals())
        d_ = nc.dram_tensor("dbg_" + nm, list(ap_.shape), ap_.dtype if hasattr(ap_, "dtype") else F32, kind="ExternalOutput")
        dma("pool", d_.ap(), ap_, res_, ())

    S.finalize(st)
    with nc.Block() as block:
        S.run(block)
    st.close()
    return nc


_NC_CACHE = {}


def kernel(x_prompt, x_sample, cache_attn_k, cache_attn_v, state_hgrn, state_ffn_conv,
           norm_mix_g, w_in, rel_bias, hgrn_lb_logits, hgrn_norm_g, w_branch_a, w_branch_b, w_out,
           norm_ffn_g, w_ffn_gate, w_ffn_up, ffn_conv_w, ffn_conv_b, w_ffn_down, norm_final_g):
    f32 = np.float32
    A = lambda a: np.ascontiguousarray(np.asarray(a, dtype=f32))
    x_prompt = A(x_prompt)
    if "nc" not in _NC_CACHE:
        _NC_CACHE["nc"] = build_program()
    nc = _NC_CACHE["nc"]
    s_idx = np.arange(128)
    mask = ((s_idx[:, None] // 64 == s_idx[None, :] // 64) & (s_idx[:, None] <= s_idx[None, :])).astype(f32)
    shared = {
        "w_in": A(w_in[0]), "w_a": A(w_branch_a[0]), "w_b": A(w_branch_b[0]), "w_o": A(w_out[0]),
        "w_g": A(w_ffn_gate[0]), "w_u": A(w_ffn_up[0]), "w_d": A(w_ffn_down[0]),
        "gmix": A(np.asarray(norm_mix_g[0]).reshape(8, 128).T), "gffn": A(np.asarray(norm_ffn_g[0]).reshape(8, 128).T),
        "gfin": A(norm_final_g), "rel": A(rel_bias[0]),
        "lbl": A(np.asarray(hgrn_lb_logits).reshape(2, 4, 128).transpose(2, 0, 1)),
        "gn": A(hgrn_norm_g[0]),
        "cw": A(np.asarray(ffn_conv_w[0]).reshape(3, NFT, 128).transpose(2, 1, 0)),
        "cb": A(np.asarray(ffn_conv_b[0]).reshape(NFT, 128).T),
        "ident": np.eye(128, dtype=f32), "mask": mask, "jmat": np.ascontiguousarray(np.eye(128, dtype=f32)[::-1]),
    }
    in_maps = []
    for c in range(8):
        b, j = divmod(c, 4)
        s = j * SEG
        lo = s - PRE_T - NPRE
        xin = np.zeros((NTOK_IN, D), f32)
        a0 = max(lo, 0)
        xin[a0 - lo:] = x_prompt[b, a0:s + SEG]
        m = dict(shared)
        m["xin"] = xin
        m["hv"] = np.full((128, 1), 1.0 if j > 0 else 0.0, f32)
        m["xs"] = A(x_sample[c])
        m["ckT"] = A(np.asarray(cache_attn_k[0, c]).transpose(0, 2, 1).reshape(4, 128, 512).transpose(1, 0, 2))
        m["cv"] = A(cache_attn_v[0, c])
        m["s0"] = A(state_hgrn[0, c])
        m["cprev"] = A(np.asarray(state_ffn_conv[0, c]).reshape(2, NFT, 128).transpose(2, 1, 0))
        in_maps.append(m)
    res = run_bass_kernel_spmd(nc, in_maps, core_ids=list(range(8)))
    R_ = res.results
    B = 2
    y_prompt = np.stack([np.concatenate([R_[b * 4 + j]["y"] for j in range(4)], axis=0) for b in range(B)])
    y_sample = np.stack([R_[c]["ys"] for c in range(8)])

    def kT_to_rows(a):
        n = a.shape[-1]
        return a.reshape(8, 64, n).transpose(0, 2, 1)

    def v_to_rows(a):
        n = a.shape[0]
        return a.reshape(n, 8, 64).transpose(1, 0, 2)

    def conv_rows(a):
        return a.transpose(2, 1, 0).reshape(2, DFF)

    last = [3, 7]
    new_k_p = np.stack([kT_to_rows(R_[c]["okT"]) for c in last])[None]
    new_v_p = np.stack([v_to_rows(R_[c]["ov"]) for c in last])[None]
    hg_p = np.stack([R_[c]["oS"] for c in last])[None]
    cv_p = np.stack([conv_rows(R_[c]["oconv"]) for c in last])[None]
    new_k_s = np.stack([kT_to_rows(R_[c]["okTs"]) for c in range(8)])[None]
    new_v_s = np.stack([v_to_rows(R_[c]["ovs"]) for c in range(8)])[None]
    hg_s = np.stack([R_[c]["oSs"] for c in range(8)])[None]
    cv_s = np.stack([conv_rows(R_[c]["oconvs"]) for c in range(8)])[None]
    outs = (y_prompt, y_sample, new_k_p, new_v_p, hg_p, cv_p, new_k_s, new_v_s, hg_s, cv_s)
    return tuple(np.ascontiguousarray(o, dtype=f32) for o in outs)
```

```python
import numpy as np
from contextlib import ExitStack
import concourse.bass as bass
import concourse.mybir as mybir
from concourse.bass import AP
from concourse.bass_utils import run_bass_kernel_spmd

F32 = mybir.dt.float32
BF16 = mybir.dt.bfloat16
AF = mybir.ActivationFunctionType
ALU = mybir.AluOpType
AX = mybir.AxisListType

D = 1024
DFF = 2816
NFT = 22
SEG = 4096
NPRE = 12288
PRE_T = 128
NTOK_IN = NPRE + PRE_T + SEG
T = 256
EPS = 1e-6
NSLOT = 6
STOP = 99
STOPMODE = 'main'
NOFK = False
SLOT_E = 4096


class Res:
    __slots__ = ("name", "w", "r", "excl")

    def __init__(self, name, excl=False):
        self.name = name
        self.w = None
        self.r = {}
        self.excl = excl


class Op:
    __slots__ = ("eng", "fn", "deps", "is_dma", "needs_inc", "sem", "val", "idx")


class Sched:
    ENGS = ("pe", "act", "dve", "pool", "sp")

    def __init__(self, nc, n_dma_sems=8):
        self.nc = nc
        self.ops = []
        self.n_dma_sems = n_dma_sems
        self.cap = None

    def capture(self, fn):
        assert self.cap is None
        self.cap = []
        try:
            fn()
            return self.cap
        finally:
            self.cap = None

    DUR = {"pe": 0.15, "act": 0.75, "dve": 0.85, "pool": 1.1, "sp": 0.3}

    def emit_merged(self, *lists):
        eng_free = {}
        ready = {}
        rdone = {}
        LAT = 0.25

        def start_of(op):
            eng, fn, reads, writes, dma, cost = op
            t = eng_free.get(eng, 0.0)
            for r in reads:
                t = max(t, ready.get(id(r), 0.0) + LAT)
            for w in writes:
                t = max(t, ready.get(id(w), 0.0) + LAT, rdone.get(id(w), 0.0) + LAT)
            return t

        def commit(op, t):
            eng, fn, reads, writes, dma, cost = op
            d = 2.5 if dma else (cost if cost is not None else self.DUR[eng])
            eng_free[eng] = t + (0.1 if dma else d)
            for r in reads:
                rdone[id(r)] = max(rdone.get(id(r), 0.0), t + d)
            for w in writes:
                ready[id(w)] = t + d
            self.emit(eng, fn, reads, writes, dma)

        pos = [0] * len(lists)
        while True:
            best, bt = -1, None
            for k, l in enumerate(lists):
                if pos[k] < len(l):
                    t = start_of(l[pos[k]])
                    if bt is None or t < bt:
                        best, bt = k, t
            if best < 0:
                break
            commit(lists[best][pos[best]], bt)
            pos[best] += 1

    def emit(self, eng, fn, reads=(), writes=(), dma=False, cost=None):
        if self.cap is not None:
            self.cap.append((eng, fn, tuple(reads), tuple(writes), dma, cost))
            return None
        op = Op()
        op.eng = eng
        op.fn = fn
        op.is_dma = dma
        op.needs_inc = dma
        op.sem = None
        op.val = 0
        op.idx = len(self.ops)
        deps = {}
        xr = [r for r in reads if r.excl]
        if xr:
            reads = [r for r in reads if not r.excl]
            writes = list(writes) + [r for r in xr if r not in writes]
        for r in reads:
            if r.w is not None:
                deps[r.w.idx] = r.w
        for w in writes:
            if w.w is not None:
                deps[w.w.idx] = w.w
            for o in w.r.values():
                deps[o.idx] = o
        op.deps = list(deps.values())
        for r in reads:
            key = (eng, op.idx) if dma else (eng, -1)
            r.r[key] = op
        for w in writes:
            w.w = op
            w.r = {}
        self.ops.append(op)
        return op

    def finalize(self, stack):
        nc = self.nc
        for op in self.ops:
            for d in op.deps:
                if d.is_dma or d.eng != op.eng or op.eng != "pe" or op.is_dma:
                    d.needs_inc = True
        csem = {e: stack.enter_context(nc.semaphore("cs_" + e)) for e in ("pe", "act", "dve", "pool")}
        dsem = {e: [stack.enter_context(nc.semaphore("ds_%s%d" % (e, i))) for i in range(self.n_dma_sems)]
                for e in ("sp", "pool", "act")}
        ccount = {e: 0 for e in csem}
        dstate = {e: [None] * self.n_dma_sems for e in dsem}
        duse = {e: [0] * self.n_dma_sems for e in dsem}
        drr = {e: 0 for e in dsem}
        waited = {e: {} for e in self.ENGS}
        streams = {e: [] for e in self.ENGS}
        for op in self.ops:
            waits = []
            e = op.eng
            extra = []
            if op.is_dma:
                k = drr[e] % self.n_dma_sems
                drr[e] += 1
                prev = dstate[e][k]
                if prev is not None:
                    extra.append(prev)
                duse[e][k] += 1
                op.sem = dsem[e][k]
                op.val = 16 * duse[e][k]
                dstate[e][k] = op
            elif op.needs_inc:
                ccount[e] += 1
                op.sem = csem[e]
                op.val = ccount[e]
            for d in op.deps + extra:
                if (not d.is_dma) and d.eng == e and e == "pe" and not op.is_dma:
                    continue
                key = id(d.sem)
                if waited[e].get(key, 0) >= d.val:
                    continue
                waited[e][key] = d.val
                waits.append((d.sem, d.val))
            streams[e].append((waits, op))
        fin = []
        for e in dsem:
            for k in range(self.n_dma_sems):
                if duse[e][k] and waited["sp"].get(id(dsem[e][k]), 0) < 16 * duse[e][k]:
                    fin.append((dsem[e][k], 16 * duse[e][k]))
        self.streams = streams
        self.fin = fin

    def run(self, block):
        streams = self.streams
        fin = self.fin

        def body(name):
            def f(eng):
                for waits, op in streams[name]:
                    for s, v in waits:
                        eng.wait_ge(s, v)
                    inst = op.fn(eng)
                    if op.sem is not None:
                        inst.then_inc(op.sem, 16 if op.is_dma else 1)
                if name == "sp":
                    for s, v in fin:
                        eng.wait_ge(s, v)
            return f
        block.tensor(body("pe"))
        block.scalar(body("act"))
        block.vector(body("dve"))
        block.gpsimd(body("pool"))
        block.sync(body("sp"))


def build_program(n_scan=NPRE // T, n_main=SEG // T, do_pre=True, do_sample=True, dumps=(), do_conv=True, init_stop=99):
    NPRE = n_scan * T
    NTOK_IN = NPRE + PRE_T + n_main * T
    nc = bass.Bass("TRN2", target_bir_lowering=False)
    st = ExitStack()
    S = Sched(nc)

    def din(name, shape):
        return nc.dram_tensor(name, list(shape), F32, kind="ExternalInput")

    def dout(name, shape):
        return nc.dram_tensor(name, list(shape), F32, kind="ExternalOutput")

    xin = din("xin", [NTOK_IN, D])
    hv_d = din("hv", [128, 1])
    xs_d = din("xs", [16, D])
    ckT_d = din("ckT", [128, 4, 512])
    cv_d = din("cv", [8, 512, 64])
    s0_d = din("s0", [4, 128, 128])
    cprev_d = din("cprev", [128, NFT, 2])
    w_in_d = din("w_in", [D, 5632])
    w_a_d = din("w_a", [512, D])
    w_b_d = din("w_b", [512, D])
    w_o_d = din("w_o", [D, D])
    w_g_d = din("w_g", [D, DFF])
    w_u_d = din("w_u", [D, DFF])
    w_d_d = din("w_d", [DFF, D])
    gmix_d = din("gmix", [128, 8])
    gffn_d = din("gffn", [128, 8])
    gfin_d = din("gfin", [D])
    rel_d = din("rel", [8, 192])
    lbl_d = din("lbl", [128, 2, 4])
    gn_d = din("gn", [128])
    cw_d = din("cw", [128, NFT, 3])
    cb_d = din("cb", [128, NFT])
    ident_d = din("ident", [128, 128])
    mask_d = din("mask", [128, 128])
    jmat_d = din("jmat", [128, 128])

    y_d = dout("y", [SEG, D])
    ys_d = dout("ys", [16, D])
    okT_d = dout("okT", [4, 128, 512])
    ov_d = dout("ov", [512, 512])
    oS_d = dout("oS", [4, 128, 128])
    oconv_d = dout("oconv", [128, NFT, 2])
    okTs_d = dout("okTs", [4, 128, 16])
    ovs_d = dout("ovs", [16, 512])
    oSs_d = dout("oSs", [4, 128, 128])
    oconvs_d = dout("oconvs", [128, NFT, 2])

    def scratch(name, shape, dt=BF16):
        return nc.dram_tensor(name, list(shape), dt, kind="Internal")

    wb_in = scratch("wb_in", [D, 5632])
    wb_a = scratch("wb_a", [512, D])
    wb_b = scratch("wb_b", [512, D])
    wb_o = scratch("wb_o", [D, D])
    wb_g = scratch("wb_g", [D, DFF])
    wb_u = scratch("wb_u", [D, DFF])
    wb_d = scratch("wb_d", [DFF, D])
    ext_d = scratch("ext_d", [8, 768], F32)

    def sb(name, shape, dt=F32):
        return st.enter_context(nc.sbuf_tensor(name, list(shape), dt))

    def R(name, excl=False):
        return Res(name, excl)

    def fsz(ap):
        n = 1
        for d_ in list(ap.shape)[1:]:
            n *= int(d_)
        return n

    def ecost(eng, ap):
        n = fsz(ap)
        return {"act": 0.25 + n / 1000.0, "dve": 0.1 + n / 850.0, "pool": 0.2 + n / 480.0}[eng]

    def act(out, in_, func, reads, writes, **kw):
        S.emit("act", lambda e: e.activation(out=out, in_=in_, func=func, **kw), reads, writes, cost=ecost("act", out))

    def tt(out, in0, in1, op, reads, writes, eng="dve"):
        S.emit(eng, lambda e: e.tensor_tensor(out=out, in0=in0, in1=in1, op=op), reads, writes, cost=ecost(eng, out))

    def ts(out, in0, s1, s2, op0, op1, reads, writes, eng="dve"):
        if op1 is None:
            S.emit(eng, lambda e: e.tensor_scalar(out=out, in0=in0, scalar1=s1, scalar2=None, op0=op0), reads, writes, cost=ecost(eng, out))
        else:
            S.emit(eng, lambda e: e.tensor_scalar(out=out, in0=in0, scalar1=s1, scalar2=s2, op0=op0, op1=op1), reads, writes, cost=ecost(eng, out))

    def stt(out, in0, scalar, in1, op0, op1, reads, writes):
        S.emit("dve", lambda e: e.scalar_tensor_tensor(out=out, in0=in0, scalar=scalar, in1=in1, op0=op0, op1=op1), reads, writes, cost=ecost("dve", out))

    def cp(out, in_, reads, writes, eng="dve"):
        if eng == "act":
            act(out, in_, AF.Copy, reads, writes)
        else:
            S.emit(eng, lambda e: e.tensor_copy(out=out, in_=in_), reads, writes, cost=ecost(eng, out))

    def recip(out, in_, reads, writes):
        S.emit("dve", lambda e: e.reciprocal(out=out, in_=in_), reads, writes)

    def mset(ap, val, writes, eng="pool"):
        S.emit(eng, lambda e: e.memset(ap, val), (), writes)

    def mm(out, lhsT, rhs, start, stop, reads, writes):
        S.emit("pe", lambda e: e.matmul(out, lhsT=lhsT, rhs=rhs, start=start, stop=stop), reads, writes, cost=0.05 + fsz(rhs) / 2000.0)

    def dma(eng, out, in_, reads, writes):
        S.emit(eng, lambda e: e.dma_start(out=out, in_=in_), reads, writes, dma=True)

    identb = sb("identb", [128, 128], BF16); r_ident = R("ident")
    maskb = sb("maskb", [128, 128], BF16); r_mask = R("mask")
    jb = sb("jb", [128, 128], BF16); r_j = R("j")
    epst = sb("epst", [128, 1]); r_eps = R("eps")
    onesT = sb("onesT", [128, 8]); r_ones = R("ones")
    hvt = sb("hvt", [128, 1]); r_hv = R("hv")
    gmix = sb("gmix_s", [128, 8]); r_gmix = R("gmix")
    gffn = sb("gffn_s", [128, 8]); r_gffn = R("gffn")
    gfin = sb("gfin_s", [128, D]); r_gfin = R("gfin")
    gnrep = sb("gnrep", [128, 4, 128]); r_gn = R("gn")
    cw = sb("cw_s", [128, NFT, 3]); r_cw = R("cw")
    cb = sb("cb_s", [128, NFT]); r_cb = R("cb")
    lbl = sb("lbl_s", [128, 2, 4]); r_lbl = R("lbl")
    lb = sb("lb_s", [128, 4]); oml = sb("oml_s", [128, 4]); r_lb = R("lb")
    ET = sb("ET", [128, 5, 8, 128], BF16); r_ET = R("ET")

    dma("pool", identb[:], ident_d.ap(), (), [r_ident])
    dma("pool", maskb[:], mask_d.ap(), (), [r_mask])
    dma("pool", jb[:], jmat_d.ap(), (), [r_j])
    mset(epst[:], EPS, [r_eps])
    nhalf = sb("nhalf", [128, 8]); r_nh = R("nhalf")
    mset(nhalf[:], -0.5, [r_nh])
    mset(onesT[:], 1.0, [r_ones])
    dma("sp", hvt[:], hv_d.ap(), (), [r_hv])
    dma("sp", gmix[:], gmix_d.ap(), (), [r_gmix])
    dma("sp", gffn[:], gffn_d.ap(), (), [r_gffn])
    dma("sp", gfin[:], AP(gfin_d, 0, [[0, 128], [1, D]]), (), [r_gfin])
    dma("sp", gnrep[:], AP(gn_d, 0, [[0, 128], [0, 4], [1, 128]]), (), [r_gn])
    dma("sp", cw[:], cw_d.ap(), (), [r_cw])
    dma("sp", cb[:], cb_d.ap(), (), [r_cb])
    dma("sp", lbl[:], lbl_d.ap(), (), [r_lbl])

    if init_stop <= 1:
        S.finalize(st)
        with nc.Block() as block:
            S.run(block)
        st.close()
        return nc
    ps_att = st.enter_context(nc.psum_tensor("ps_att", [128, 1536], F32))
    r_att = [R("att0", True), R("att1", True)]
    r_attB = R("attB", True)
    NGEN = 2
    ps_gen = [st.enter_context(nc.psum_tensor("ps_g%d" % i, [128, 512], F32)) for i in range(NGEN)]
    r_gen = [R("g%d" % i, True) for i in range(NGEN)]
    ps_o = st.enter_context(nc.psum_tensor("ps_o", [128, 512], F32))
    r_o = R("ps_o", True)
    ps_tr = [st.enter_context(nc.psum_tensor("ps_t%d" % i, [128, 1024], BF16)) for i in range(2)]
    r_tr = [R("t%d" % i, True) for i in range(2)]
    cnt = {"g": 0, "t": 0, "a": 0, "p": 0}

    gpool_full = [(ps_gen[i], r_gen[i]) for i in range(NGEN)]
    gpool_front = gpool_full + [(ps_o, r_o)]
    gpool_back = [(ps_att[:, 0:512], r_att[0]), (ps_att[:, 512:1024], r_att[1]), (ps_att[:, 1024:1536], r_attB)]
    gstate = {"pool": gpool_full, "tr": (0, 1)}

    def gbank():
        pool = gstate["pool"]
        i = cnt["g"] % len(pool)
        cnt["g"] += 1
        return pool[i]

    def tbank():
        sel = gstate["tr"]
        i = sel[cnt["t"] % len(sel)]
        cnt["t"] += 1
        return ps_tr[i], r_tr[i]

    lbt = sb("lbt", [128, 4])
    tt(lbt[:], lbl[:, 1, :], lbl[:, 0, :], ALU.subtract, [r_lbl], [r_lb])
    act(lbt[:], lbt[:], AF.Exp, [r_lb], [r_lb])
    ts(lbt[:], lbt[:], 1.0, None, ALU.add, None, [r_lb], [r_lb])
    recip(lb[:], lbt[:], [r_lb], [r_lb])
    ts(oml[:], lb[:], -1.0, 1.0, ALU.mult, ALU.add, [r_lb], [r_lb])
    omlh = sb("omlh_s", [128, 4]); lbh = sb("lbh_s", [128, 4])
    ts(omlh[:], oml[:], 0.5, None, ALU.mult, None, [r_lb], [r_lb])
    tt(lbh[:], lb[:], omlh[:], ALU.add, [r_lb], [r_lb])

    if init_stop <= 2:
        S.finalize(st)
        with nc.Block() as block:
            S.run(block)
        st.close()
        return nc
    if init_stop <= 3:
        S.finalize(st)
        with nc.Block() as block:
            S.run(block)
        st.close()
        return nc
    r_wb = {}
    conv_jobs = []
    for name, src, dst, rows in (("in", w_in_d, wb_in, D), ("a", w_a_d, wb_a, 512), ("b", w_b_d, wb_b, 512),
                                 ("o", w_o_d, wb_o, D), ("g", w_g_d, wb_g, D), ("u", w_u_d, wb_u, D),
                                 ("d", w_d_d, wb_d, DFF)):
        r_wb[name] = R("wb_" + name)
        for r0 in range(0, rows, 128):
            conv_jobs.append((name, dst.ap()[r0:r0 + 128, :], src.ap()[r0:r0 + 128, :]))

    def emit_conv(n):
        for _ in range(n):
            if conv_jobs:
                name, d_, s_ = conv_jobs.pop(0)
                dma("pool", d_, s_, (), [r_wb[name]])

    wslot = [sb("wslot%d" % i, [128, SLOT_E], BF16) for i in range(NSLOT)]
    r_wslot = [R("wslot%d" % i) for i in range(NSLOT)]
    W1v = []
    for i, c0 in enumerate((2048, 2560, 512, 1024)):
        v_ = wslot[i][:, :].rearrange("p (k n) -> p k n", k=8)
        dma("pool", v_, w_in_d.ap()[:, c0:c0 + 512].rearrange("(k p) n -> p k n", p=128), (), [r_wslot[i]])
        W1v.append(v_)

    def wgroups(mode):
        g = []
        for i in (3, 4, 5, 6, 0, 1, 2, 7, 8, 9, 10):
            g.append(("in%d" % i, "in", wb_in.ap()[:, 512 * i:512 * i + 512].rearrange("(k p) n -> p k n", p=128), 8, 512))
        g.append(("a", "a", wb_a.ap().rearrange("(k p) n -> p k n", p=128), 4, 1024))
        g.append(("b", "b", wb_b.ap().rearrange("(k p) n -> p k n", p=128), 4, 1024))
        for n in range(2):
            g.append(("o%d" % n, "o", wb_o.ap()[:, 512 * n:512 * n + 512].rearrange("(k p) n -> p k n", p=128), 8, 512))
        for gi in range(6):
            nc_ = 512 if gi < 5 else 256
            g.append(("g%d" % gi, "g", wb_g.ap()[:, 512 * gi:512 * gi + nc_].rearrange("(k p) n -> p k n", p=128), 8, nc_))
            if mode != "pre":
                g.append(("u%d" % gi, "u", wb_u.ap()[:, 512 * gi:512 * gi + nc_].rearrange("(k p) n -> p k n", p=128), 8, nc_))
        if mode != "pre":
            for n in range(2):
                for gk in range(3):
                    nk = 8 if gk < 2 else 6
                    g.append(("d%d_%d" % (n, gk), "d",
                              wb_d.ap()[1024 * gk:1024 * gk + 128 * nk, 512 * n:512 * n + 512].rearrange("(k p) n -> p k n", p=128), nk, 512))
        return g

    wq = []
    tiles_plan = ([("pre", 0)] if do_pre else []) + [("main", m) for m in range(n_main)] + ([("sample", 0)] if do_sample else [])
    for mode, _ in tiles_plan:
        wq.extend(wgroups(mode))
    wstate = {"issued": 0, "consumed": 0, "slot": {}}

    def w_issue(slot):
        i = wstate["issued"]
        if i >= len(wq):
            return
        key, wname, src, nk, ncol = wq[i]
        view = wslot[slot][:, 0:nk * ncol].rearrange("p (k n) -> p k n", k=nk)
        dma("sp", view, src, [r_wb[wname]], [r_wslot[slot]])
        wstate["slot"][i] = slot
        wstate["issued"] += 1

    def w_next(key):
        i = wstate["consumed"]
        assert wq[i][0] == key, (wq[i][0], key)
        slot = wstate["slot"][i]
        _, _, _, nk, ncol = wq[i]
        view = wslot[slot][:, 0:nk * ncol].rearrange("p (k n) -> p k n", k=nk)
        return view, r_wslot[slot], slot

    def w_done(slot):
        wstate["consumed"] += 1
        w_issue(slot)

    if init_stop <= 4:
        S.finalize(st)
        with nc.Block() as block:
            S.run(block)
        st.close()
        return nc
    X = [sb("X%d" % i, [128, 2, D]) for i in range(2)]
    r_X = [[R("X%d_%d" % (i, j)) for j in range(2)] for i in range(2)]
    XN = sb("XN", [128, 2, D], BF16); r_XN = [R("XN0"), R("XN1")]
    ss = sb("ss", [128, 8]); r_ss = R("ss")
    hTs = [sb("hTa", [128, 8, T], BF16), sb("hTb", [128, 8, T], BF16), sb("h2T", [128, 8, T], BF16)]
    r_hTs = [[R("hTa0"), R("hTa1")], [R("hTb0"), R("hTb1")], [R("h2T0"), R("h2T1")]]
    qT = sb("qT", [128, 4, T], BF16); r_qT = R("qT")
    kTr = sb("kTr", [128, 4, 6, 128], BF16); r_kT = [R("kT%d" % i) for i in range(6)]
    Vr = sb("Vr", [128, 6, 8, 65], BF16); r_V = [R("V%d" % i) for i in range(6)]
    Pb = [sb("Pb%d" % i, [128, 5, 128], BF16) for i in range(3)]; r_Pb = [R("Pb%d" % i) for i in range(3)]
    oa = sb("oa", [128, 2, 512], BF16); r_oa = [R("oa0"), R("oa1")]
    oaT = sb("oaT", [128, 4, T], BF16); r_oaT = [R("oaT0"), R("oaT1")]
    rec = sb("rec", [128, 8]); r_rec = R("rec")
    sg = sb("sg", [128, 4, T]); r_sg = [R("sg%d" % i) for i in range(4)]
    siluq = sb("siluq", [128, 4, T]); r_sq = [R("siluq%d" % i) for i in range(4)]
    gF = sb("gF", [128, 4 * T]); r_gF = R("gF")
    gL = sb("gL", [128, 4 * T]); r_gL = R("gL")
    gB = sb("gB", [128, 4 * T]); r_gB = R("gB")
    gK = sb("gK", [128, 4 * T]); r_gK = R("gK")
    gE = sb("gE", [128, 4 * T]); r_gE = R("gE")
    dec = sb("dec", [128, 3, 4, 4]); r_dec = R("dec")
    Zq = sb("Zq", [128, 4, 4, 128], BF16); r_Zq = [R("Zq%d" % i) for i in range(4)]
    ktT = sb("ktT", [128, 4, T], BF16); r_ktT = [R("ktT%d" % i) for i in range(4)]
    khT = sb("khT", [128, 4, T], BF16); r_khT = [R("khT%d" % i) for i in range(4)]
    khtm = sb("khtm", [128, 2, 512], BF16); r_khtm = [R("khtm0"), R("khtm1")]
    vh = sb("vh", [128, 2, 512], BF16); r_vh = [R("vh0"), R("vh1")]
    sgb = sb("sgb", [128, 2, 512]); r_sgb = [R("sgb0"), R("sgb1")]
    Sst = sb("Sst", [128, 4, 128]); r_S = R("S")
    Sp = sb("Sp", [128, 2, 4, 128], BF16); r_Sp = [R("Sp0"), R("Sp1")]
    AT = sb("AT", [128, 4, 128], BF16); r_AT = R("AT")
    sqb = sb("sqb", [128, 512]); r_sqb = R("sqb")
    ssq = sb("ssq", [128, 4]); r_ssq = R("ssq")
    t1 = sb("t1", [128, 512]); r_t1 = R("t1")
    gG = sb("gG", [128, 512]); r_gG = R("gG")
    ob = sb("ob", [128, 512], BF16); r_ob = R("ob")
    obT = sb("obT", [128, 4, T], BF16); r_obT = [R("obT0"), R("obT1")]
    zs = sb("zs", [128, 16, T], BF16); r_zs = [R("zs%d" % i) for i in range(16)]
    mT = sb("mT", [128, 8, T], BF16); r_mT = [R("mT%d" % i) for i in range(8)]
    aT = [sb("aT%d" % i, [128, T + 2]) for i in range(2)]; r_aT = [R("aT0"), R("aT1")]
    c1 = [sb("c1_%d" % i, [128, T]) for i in range(2)]; r_c1 = [R("c1_0"), R("c1_1")]
    c2 = [sb("c2_%d" % i, [128, T]) for i in range(2)]; r_c2 = [R("c2_0"), R("c2_1")]
    m1, r_m1, m2, r_m2 = c1[0], r_c1[0], c2[0], r_c2[0]
    gT = sb("gT", [128, NFT, T], BF16); r_gT = [R("gT%d" % i) for i in range(NFT)]
    cprev = sb("cprev_s", [128, NFT, 2]); r_cprev = R("cprev")

    ext_ = siluq[0:8].rearrange("p h t -> p (h t)")[:, 0:768]; r_ext = r_sq; r_extd = R("extd")
    dma("sp", ext_[:, 64:256], rel_d.ap(), (), r_ext)
    act(ext_[:, 0:64], ext_[:, 64:65].to_broadcast([8, 64]), AF.Identity, r_ext, r_ext)
    act(ext_[:, 256:768], ext_[:, 255:256].to_broadcast([8, 512]), AF.Identity, r_ext, r_ext)
    act(ext_[:, :], ext_[:, :], AF.Exp, r_ext, r_ext)
    dma("sp", ext_d.ap(), ext_[:, :], r_ext, [r_extd])
    hk = sg[:].rearrange("p h t -> p (h t)").rearrange("p (a b) -> p a b", a=8); r_hk = r_sg
    hkb = ktT[:].rearrange("p h t -> p (h t)").rearrange("p (a b) -> p a b", a=8); r_hkb = r_ktT
    for kt in range(5):
        dma("sp", hk, AP(ext_d, 512 - 128 * kt, [[1, 128], [768, 8], [1, 128]]), [r_extd], r_hk)
        cp(hkb, hk, r_hk, r_hkb)
        for n in range(2):
            pb, rb = gbank()
            mm(pb[:, :], jb[:], hkb[:, 4 * n:4 * n + 4, :].rearrange("p h q -> p (h q)"), True, True, [r_j] + r_hkb, [rb])
            cp(ET[:, kt, 4 * n:4 * n + 4, :].rearrange("p h q -> p (h q)"), pb[:, :], [rb], [r_ET])
    mset(ET[0:64, 0, :, 64:128], 0.0, [r_ET])
    mset(ET[64:128, 4, :, 0:64], 0.0, [r_ET])

    kst = gT[:, 0:8, :].rearrange("p a b -> p (a b)").bitcast(F32).rearrange("p (h t) -> p h t", h=4)
    vst = gT[:, 8:16, :].rearrange("p a b -> p (a b)").bitcast(F32).rearrange("p (j c) -> p j c", j=2)
    rl_kst = r_gT[0:8]
    rl_vst = [r_gT[8:12], r_gT[12:16]]
    sg2 = gT[:, 0:8, :].rearrange("p a b -> p (a b)").bitcast(F32).rearrange("p (h t) -> p h t", h=4)
    r_sg2 = [[r_gT[2 * i], r_gT[2 * i + 1]] for i in range(4)]
    vh2 = gT[:, 8:12, :].rearrange("p a b -> p (a b)").rearrange("p (j c) -> p j c", j=2)
    r_vh2 = [[r_gT[8], r_gT[9]], [r_gT[10], r_gT[11]]]
    vh3 = gT[:, 12:16, :].rearrange("p a b -> p (a b)").rearrange("p (j c) -> p j c", j=2)
    r_vh3 = [[r_gT[12], r_gT[13]], [r_gT[14], r_gT[15]]]
    ktT2 = gT[:, 16:20, :]
    r_ktT2 = r_gT[16:20]
    dec2 = sb("dec2", [128, 3, 4, 4]); r_dec2 = R("dec2")
    sgs = [(sg, [[r] for r in r_sg]), (sg2, r_sg2)]
    vhs = [(vh, [[r] for r in r_vh]), (vh2, r_vh2), (vh3, r_vh3)]
    kts = [(ktT, r_ktT, dec, r_dec), (ktT2, r_ktT2, dec2, r_dec2)]
    mset(Zq[:], 0.0, r_Zq)
    mset(Sst[:], 0.0, [r_S])
    mset(cprev[:], 0.0, [r_cprev])

    if init_stop <= 5:
        S.finalize(st)
        with nc.Block() as block:
            S.run(block)
        st.close()
        return nc
    def load_x(xsrc, ntok, xb):
        nsub = (ntok + 127) // 128
        for j in range(nsub):
            nt = min(128, ntok - 128 * j)
            dma("sp", X[xb][:nt, j, :], xsrc[j * 128:j * 128 + nt, :], (), [r_X[xb][j]])

    def norm_T(src_tile, rsrc, gcol, rg, col0, hsel, ntok):
        nsub = (ntok + 127) // 128
        dst, rdst = hTs[hsel], r_hTs[hsel]
        nts_ = [min(128, ntok - 128 * j) for j in range(nsub)]
        for j in range(nsub):
            nt = nts_[j]
            act(XN[:nt, j, :], src_tile[:nt, j, :], AF.Square, [rsrc[j]], [r_XN[j], r_ss], accum_out=ss[:nt, col0 + j:col0 + j + 1])
        for j in range(nsub):
            nt = nts_[j]
            ts(ss[:nt, col0 + j:col0 + j + 1], ss[:nt, col0 + j:col0 + j + 1], 1.0 / D, EPS, ALU.mult, ALU.add, [r_ss], [r_ss], eng="pool")
            tt(ss[:nt, col0 + j:col0 + j + 1], ss[:nt, col0 + j:col0 + j + 1], nhalf[:nt, 0:1], ALU.pow, [r_ss, r_nh], [r_ss], eng="pool")
        for j in range(nsub):
            nt = nts_[j]
            act(XN[:nt, j, :], src_tile[:nt, j, :], AF.Copy, [rsrc[j], r_ss], [r_XN[j]], scale=ss[:nt, col0 + j:col0 + j + 1])
            pt, rt = tbank()
            for kc in range(8):
                S.emit("pe", lambda e, kc=kc, j=j, nt=nt, pt=pt: e.transpose(out=pt[:, kc * 128:kc * 128 + nt], in_=XN[:nt, j, kc * 128:(kc + 1) * 128],
                                                                              identity=identb[:nt, :nt]), [r_XN[j], r_ident], [rt], cost=0.1)
            tt(dst[:, :, j * 128:j * 128 + nt], pt[:, :].rearrange("p (k t) -> p k t", k=8)[:, :, 0:nt],
               gcol[:, 0:8].unsqueeze(2).to_broadcast([128, 8, nt]), ALU.mult, [rt, rg], [rdst[j]])

    def macro_tile(mode, ntok, xb, gt0, hsel=0, normed=False, kv=False, out_row=None, final_kv=None, after_norm=None, prenorm=None,
                   part="all", bsel=0, vsel=0, ksel=0):
        sample = mode == "sample"
        nsub = (ntok + 127) // 128
        nts = [min(128, ntok - 128 * j) for j in range(nsub)]
        C = 16 if sample else (ntok if mode == "scan" else 64)
        nch = ntok // C
        ri = C - 1 if mode == "scan" else C // 2 - 1
        cps = 1 if sample else 2
        Xb = X[xb]
        rX = r_X[xb]
        hT = hTs[hsel]
        r_hT = r_hTs[hsel]
        rhT_all = r_hT[:nsub]
        cur["hT"] = hT
        cur["r_hT"] = r_hT
        sg, rl_sg = sgs[bsel]
        vh, rl_vh = vhs[vsel]
        ktT, r_ktT, dec, r_dec = kts[ksel]
        if mode == "scan":
            gstate["pool"] = gpool_front if part in ("f1", "f2") else gpool_back
            gstate["tr"] = (0,) if part in ("f1", "f2") else (1,)
        else:
            gstate["pool"] = gpool_front + gpool_back
            gstate["tr"] = (0, 1)

        if part in ("all", "f1"):
            if not normed:
                norm_T(Xb, rX, gmix, r_gmix, 0, hsel, ntok)
            if after_norm is not None:
                after_norm()

        def proj_fm(wv, rw, ct, evac):
            pb, rb = gbank()
            for kc in range(8):
                mm(pb[:, 0:ntok], wv[:, kc, ct * 128:(ct + 1) * 128], hT[:, kc, 0:ntok], kc == 0, kc == 7, [rw] + rhT_all, [rb])
            evac(pb, rb)

        def proj_tm(wv, rw, j, evac, ncol=512):
            pb, rb = gbank()
            nt = nts[j]
            for kc in range(8):
                mm(pb[:nt, 0:ncol], hT[:, kc, j * 128:j * 128 + nt], wv[:, kc, 0:ncol], kc == 0, kc == 7, [rw, r_hT[j]], [rb])
            evac(pb, rb, j, nt)

        def slot_of(j):
            return (gt0 + j) % 6 if not sample else 4

        def hgrn_gates_all():
            n4 = 4 * ntok
            nc4 = 4 * nch
            fl = lambda b: b[:, 0:n4]
            v3 = lambda b: b[:, 0:n4].rearrange("p (h t) -> p h t", h=4)
            ch = lambda b: b[:, 0:n4].rearrange("p (c t) -> p c t", t=C)
            hc = lambda b: b[:, 0:n4].rearrange("p (h c t) -> p h c t", h=4, t=C)
            rsg = [r for l_ in rl_sg for r in l_]
            sgf = sg.rearrange("p h t -> p (h t)")
            tt(v3(gF), v3(sgf), omlh[:, 0:4].unsqueeze(2).to_broadcast([128, 4, ntok]), ALU.mult, rsg + [r_lb], [r_gF])
            tt(v3(gF), v3(gF), lbh[:, 0:4].unsqueeze(2).to_broadcast([128, 4, ntok]), ALU.add, [r_gF, r_lb], [r_gF])
            act(fl(gL), fl(gF), AF.Ln, [r_gF], [r_gL])
            S.emit("dve", lambda e: e.tensor_tensor_scan(out=fl(gB), data0=fl(gL), data1=fl(gL), initial=0.0, op0=ALU.add, op1=ALU.min),
                   [r_gL], [r_gB], cost=0.1 + 2 * n4 / 900.0)
            ts(fl(gK), fl(gF), -1.0, 1.0, ALU.mult, ALU.add, [r_gF], [r_gK], eng="pool")
            tt(ch(gL), ch(gB), ch(gB)[:, :, ri:ri + 1].to_broadcast([128, nc4, C]), ALU.subtract, [r_gB, r_gL], [r_gL])
            act(fl(gE), fl(gL), AF.Exp, [r_gL], [r_gE], scale=-1.0)
            tt(ktT[:, :, 0:ntok], v3(gK), v3(gE), ALU.mult, [r_gK, r_gE], r_ktT, eng="pool")
            dsl = lambda i: dec[:, i, :, 0:nch]
            if mode != "scan":
                act(fl(gB), fl(gL), AF.Exp, [r_gL, r_gB], [r_gB])
                cp(dsl(2), hc(gB)[:, :, :, C - 1], [r_gB], [r_dec])
            tt(dsl(0), hc(gE)[:, :, :, 0], hc(gF)[:, :, :, 0], ALU.mult, [r_gE, r_gF], [r_dec])
            if mode == "scan":
                return
            tt(dsl(1), dsl(0), dsl(2), ALU.mult, [r_dec], [r_dec])
            k4 = lambda b: b[:, :, 0:ntok].rearrange("p h (c t) -> p h c t", t=C)
            tt(k4(khT), k4(ktT), dsl(2).unsqueeze(3).to_broadcast([128, 4, nch, C]), ALU.mult, r_ktT + [r_dec], r_khT)
            if mode != "scan":
                sqf = siluq.rearrange("p h t -> p (h t)")
                if sample:
                    tt(Zq[:, :, 0, 0:ntok], v3(sqf), v3(gB), ALU.mult, r_sq + [r_gB], r_Zq)
                else:
                    base = Zq[:, 0, 0, 0:64]
                    zout = AP(base.tensor, base.offset, [list(base.ap[0]), [512 // nsub, 4 * nsub], [192, 2], [1, 64]])
                    tt(zout, fl(sqf).rearrange("p (a c t) -> p a c t", c=2, t=64), fl(gB).rearrange("p (a c t) -> p a c t", c=2, t=64),
                       ALU.mult, r_sq + [r_gB], r_Zq)

        def khat_transpose(j):
            nt = nts[j]
            pt, rt = tbank()
            ksrc, rks = (ktT, r_ktT) if mode == "scan" else (khT, r_khT)
            for hb in range(4):
                S.emit("pe", lambda e, hb=hb, pt=pt: e.transpose(out=pt[:nt, hb * 128:(hb + 1) * 128], in_=ksrc[:, hb, j * 128:j * 128 + nt],
                                                                  identity=identb[:, :]), [rks[hb], r_ident], [rt])
            cp(khtm[:nt, j, :], pt[:nt, 0:512], [rt], [r_khtm[j]], eng="act")

        def s_update(j, ci):
            p = ci % cps
            rows = slice(p * C, p * C + C)
            pb, rb = gbank()
            for hb in range(4):
                mm(pb[:, hb * 128:(hb + 1) * 128], khtm[rows, j, hb * 128:(hb + 1) * 128], vh[rows, j, hb * 128:(hb + 1) * 128],
                   True, True, [r_khtm[j]] + rl_vh[j], [rb])
            tt(Sst[:], Sst[:], dec[:, 1, :, ci:ci + 1].to_broadcast([128, 4, 128]), ALU.mult, [r_S, r_dec], [r_S])
            tt(Sst[:].rearrange("p h v -> p (h v)"), Sst[:].rearrange("p h v -> p (h v)"), pb[:, :], ALU.add, [r_S, rb], [r_S])

        if mode == "scan":
            wv_f, wv_i = W1v[0], W1v[1]
            if part == "f2":
                for hb in range(4):
                    proj_fm(wv_f, r_wslot[0], hb, lambda pb, rb, hb=hb: act(sg.rearrange("p h t -> p (h t)")[:, hb * ntok:(hb + 1) * ntok], pb[:, 0:ntok], AF.Tanh, [rb], rl_sg[hb], scale=0.5))
                for j in range(nsub):
                    proj_tm(wv_i, r_wslot[1], j, lambda pb, rb, j, nt: cp(vh[:nt, j, :], pb[:nt, :], [rb], rl_vh[j], eng="act"))
                if kv:
                    kv_proj_only()
            if part == "gates":
                hgrn_gates_all()
            if part == "upd":
                for j in range(nsub):
                    khat_transpose(j)
                pb, rb = gbank()
                for hb in range(4):
                    for j in range(nsub):
                        mm(pb[:, hb * 128:(hb + 1) * 128], khtm[:, j, hb * 128:(hb + 1) * 128], vh[:, j, hb * 128:(hb + 1) * 128],
                           j == 0, j == nsub - 1, [r_khtm[j]] + rl_vh[j], [rb])
                tt(Sst[:], Sst[:], dec[:, 0, :, 0:1].to_broadcast([128, 4, 128]), ALU.mult, [r_S, r_dec], [r_S])
                tt(Sst[:].rearrange("p h v -> p (h v)"), Sst[:].rearrange("p h v -> p (h v)"), pb[:, :], ALU.add, [r_S, rb], [r_S])
            return

        def ev_q(ct):
            return lambda pb, rb: act(qT[:, ct, 0:ntok], pb[:, 0:ntok], AF.Copy, [rb], [r_qT], scale=0.125)

        def ev_k(ct):
            def f(pb, rb):
                for j in range(nsub):
                    cp(kTr[:, ct, slot_of(j), 0:nts[j]], pb[:, j * 128:j * 128 + nts[j]], [rb], [r_kT[slot_of(j)]], eng="act")
                if final_kv is not None:
                    cp(kst[:, ct, 0:ntok], pb[:, 0:ntok], [rb], rl_kst)
            return f

        def ev_v(pb, rb, j, nt):
            s_ = slot_of(j)
            cp(Vr[:nt, s_, :, 0:64], pb[:nt, :].rearrange("p (h d) -> p h d", h=8), [rb], [r_V[s_]], eng="act")
            if mode == "main" or sample:
                cp(Vr[:nt, s_, :, 64:65], onesT[:nt, 0:8].unsqueeze(2), [r_ones], [r_V[s_]], eng="pool")
            else:
                cp(Vr[:nt, s_, :, 64:65], hvt[:nt, 0:1].unsqueeze(2).to_broadcast([nt, 8, 1]), [r_hv], [r_V[s_]], eng="pool")
            if final_kv is not None:
                cp(vst[:nt, j, :], pb[:nt, :], [rb], rl_vst[j])
                dma("act", final_kv[1].ap()[final_kv[2] + j * 128:final_kv[2] + j * 128 + nt, :], vst[:nt, j, :], rl_vst[j], ())

        wv, rw, sl = w_next("in3")
        for hb in range(4):
            proj_fm(wv, rw, hb, lambda pb, rb, hb=hb: act(siluq.rearrange("p h t -> p (h t)")[:, hb * ntok:(hb + 1) * ntok], pb[:, 0:ntok], AF.Silu, [rb], [r_sq[hb]]))
        w_done(sl)
        wv, rw, sl = w_next("in4")
        for hb in range(4):
            proj_fm(wv, rw, hb, lambda pb, rb, hb=hb: act(sg.rearrange("p h t -> p (h t)")[:, hb * ntok:(hb + 1) * ntok], pb[:, 0:ntok], AF.Tanh, [rb], rl_sg[hb], scale=0.5))
        w_done(sl)
        wv, rw, sl = w_next("in5")
        for j in range(nsub):
            proj_tm(wv, rw, j, lambda pb, rb, j, nt: cp(vh[:nt, j, :], pb[:nt, :], [rb], rl_vh[j], eng="act"))
        w_done(sl)
        wv, rw, sl = w_next("in6")
        for j in range(nsub):
            proj_tm(wv, rw, j, lambda pb, rb, j, nt: act(sgb[:nt, j, :], pb[:nt, :], AF.Silu, [rb], [r_sgb[j]]))
        w_done(sl)
        wv, rw, sl = w_next("in0")
        for ct in range(4):
            proj_fm(wv, rw, ct, ev_q(ct))
        w_done(sl)
        wv, rw, sl = w_next("in1")
        for ct in range(4):
            proj_fm(wv, rw, ct, ev_k(ct))
        w_done(sl)
        if final_kv is not None:
            for ct in range(4):
                dma("act", final_kv[0].ap()[ct, :, final_kv[2]:final_kv[2] + ntok], kst[:, ct, 0:ntok], rl_kst, ())
        wv, rw, sl = w_next("in2")
        for j in range(nsub):
            proj_tm(wv, rw, j, ev_v)
        w_done(sl)

        att_steps, z_steps, h_steps = [], [], []

        zst = {}

        def z_step(zi):
            gi, ct = divmod(zi, 4)
            if ct == 0:
                zst["w"] = w_next("in%d" % (7 + gi))
            wv_, rw_, sl_ = zst["w"]
            proj_fm(wv_, rw_, ct, lambda pb, rb: act(zs[:, zi, 0:ntok], pb[:, 0:ntok], AF.Tanh, [rb], [r_zs[zi]], scale=0.5))
            if ct == 3:
                w_done(sl_)
        for zi in range(16):
            z_steps.append(lambda zi=zi: z_step(zi))

        def make_att(j):
            nq = nts[j]
            if sample:
                ktiles = [(0, 128), (1, 128), (2, 128), (3, 128), (4, 16)]
            else:
                ktiles = [((gt0 + j - 4 + kt) % 6, 128) for kt in range(5)]
            pend = []

            def pv(h, pslot):
                for kt, (s_, nk) in enumerate(ktiles):
                    mm(ps_o[:nq, (h % 4) * 65:(h % 4) * 65 + 65], Pb[pslot][:nk, kt, 0:nq], Vr[:nk, s_, h, :], kt == 0, kt == 4,
                       [r_Pb[pslot], r_V[s_]], [r_o])

            def normalize(half):
                o3 = ps_o[:nq, 0:260].rearrange("p (h d) -> p h d", h=4)
                ts(rec[:nq, half * 4:half * 4 + 4].unsqueeze(2), o3[:, :, 64:65], 1e-30, None, ALU.max, None, [r_o], [r_rec])
                recip(rec[:nq, half * 4:half * 4 + 4], rec[:nq, half * 4:half * 4 + 4], [r_rec], [r_rec])
                tt(oa[:nq, j, half * 256:half * 256 + 256].rearrange("p (h d) -> p h d", h=4), o3[:, :, 0:64],
                   rec[:nq, half * 4:half * 4 + 4].unsqueeze(2).to_broadcast([nq, 4, 64]), ALU.mult, [r_o, r_rec], [r_oa[j]])

            def head(h):
                hp, r0 = h // 2, (h % 2) * 64
                ai = cnt["a"] % 2
                cnt["a"] += 1
                offA = ai * 512
                offB = 1024 + ai * 128
                for kt in (4, 0, 1, 2, 3):
                    s_, nk = ktiles[kt]
                    o_ = offB if kt == 4 else offA + kt * 128
                    mm(ps_att[:nk, o_:o_ + nq], kTr[r0:r0 + 64, hp, s_, 0:nk], qT[r0:r0 + 64, hp, j * 128:j * 128 + nq],
                       True, True, [r_kT[s_], r_qT], [r_attB if kt == 4 else r_att[ai]])
                pi = cnt["p"] % 3
                cnt["p"] += 1
                sattA = ps_att[:, offA:offA + 512].rearrange("p (k q) -> p k q", k=4)
                sattB = ps_att[:, offB:offB + 128]
                if sample:
                    act(Pb[pi][:16, 4, 0:nq], sattB[:16, 0:nq], AF.Exp, [r_attB], [r_Pb[pi]])
                    act(Pb[pi][:, 0:4, 0:nq], sattA[:, :, 0:nq], AF.Exp, [r_att[ai]], [r_Pb[pi]])
                    tt(Pb[pi][:, 0:4, 0:nq], Pb[pi][:, 0:4, 0:nq], ET[:, 0:4, h, 0:nq], ALU.mult, [r_Pb[pi], r_ET], [r_Pb[pi]], eng="pool")
                    tt(Pb[pi][:16, 4, 0:nq], Pb[pi][:16, 4, 0:nq], ET[:16, 4, h, 0:nq], ALU.mult, [r_Pb[pi], r_ET], [r_Pb[pi]], eng="pool")
                else:
                    act(Pb[pi][:, 4, :], sattB, AF.Exp, [r_attB], [r_Pb[pi]])
                    act(Pb[pi][:, 0:4, :], sattA, AF.Exp, [r_att[ai]], [r_Pb[pi]])
                    tt(Pb[pi][:, :, :], Pb[pi][:, :, :], ET[:, :, h, :], ALU.mult, [r_Pb[pi], r_ET], [r_Pb[pi]], eng="pool")
                if pend:
                    ph, ppi = pend.pop(0)
                    pv(ph, ppi)
                    if ph == 3:
                        normalize(0)
                pend.append((h, pi))

            def tail():
                ph, ppi = pend.pop(0)
                pv(ph, ppi)
                normalize(1)
                pt, rt = tbank()
                for kc in range(4):
                    S.emit("pe", lambda e, kc=kc, pt=pt: e.transpose(out=pt[:, kc * 128:kc * 128 + nq], in_=oa[:nq, j, kc * 128:(kc + 1) * 128],
                                                                      identity=identb[:nq, :nq]), [r_oa[j], r_ident], [rt])
                cp(oaT[:, :, j * 128:j * 128 + nq], pt[:, 0:512].rearrange("p (k t) -> p k t", k=4)[:, :, 0:nq], [rt], [r_oaT[j]], eng="act")
            for h in range(8):
                att_steps.append(lambda h=h: head(h))
            att_steps.append(tail)
        for j in range(nsub):
            make_att(j)

        def make_h(j):
            nt = nts[j]

            def h_at():
                pb, rb = gbank()
                for hb in range(4):
                    if sample:
                        qrhs = Zq[:, hb, 0, 0:nt]
                    else:
                        base = Zq[:, hb, 2 * j, 0:64]
                        qrhs = AP(base.tensor, base.offset, [list(base.ap[0]), [192, 2], [1, 64]])
                    mm(pb[:nt, hb * 128:hb * 128 + nt], ktT[:, hb, j * 128:j * 128 + nt], qrhs, True, True, [r_ktT[hb], r_Zq[hb]], [rb])
                tt(AT[:nt, :, 0:nt], pb[:nt, :].rearrange("p (h t) -> p h t", h=4)[:, :, 0:nt],
                   maskb[:nt, 0:nt].unsqueeze(1).to_broadcast([nt, 4, nt]), ALU.mult, [rb, r_mask], [r_AT])

            def h_chunk(p):
                ci = j * cps + p
                tt(Sp[:, p], Sst[:], dec[:, 0, :, ci:ci + 1].to_broadcast([128, 4, 128]), ALU.mult, [r_S, r_dec], [r_Sp[p]])
                s_update(j, ci)

            def h_out():
                ob_, rob = gbank()
                for hb in range(4):
                    for p in range(cps):
                        zl = Zq[:, hb, 0, 0:nt] if sample else Zq[:, hb, 2 * j + p, :]
                        mm(ob_[:nt, hb * 128:(hb + 1) * 128], zl, Sp[:, p, hb, :], p == 0, False, [r_Zq[hb], r_Sp[p]], [rob])
                    mm(ob_[:nt, hb * 128:(hb + 1) * 128], AT[:nt, hb, 0:nt], vh[:nt, j, hb * 128:(hb + 1) * 128], False, True, [r_AT] + rl_vh[j], [rob])
                act(sqb[:nt, :], ob_[:nt, :], AF.Square, [rob], [r_sqb])
                S.emit("dve", lambda e: e.tensor_reduce(out=ssq[:nt, 0:4], in_=sqb[:nt, :].rearrange("p (h v) -> p h v", h=4), axis=AX.X, op=ALU.add),
                       [r_sqb], [r_ssq])
                ts(ssq[:nt, :], ssq[:nt, :], 1.0 / 128, EPS, ALU.mult, ALU.add, [r_ssq], [r_ssq], eng="pool")
                tt(ssq[:nt, :], ssq[:nt, :], nhalf[:nt, 0:4], ALU.pow, [r_ssq, r_nh], [r_ssq], eng="pool")
                tt(t1[:nt, :].rearrange("p (h v) -> p h v", h=4), ob_[:nt, :].rearrange("p (h v) -> p h v", h=4),
                   ssq[:nt, 0:4].unsqueeze(2).to_broadcast([nt, 4, 128]), ALU.mult, [rob, r_ssq], [r_t1])
                tt(gG[:nt, :], sgb[:nt, j, :], gnrep[:nt].rearrange("p h v -> p (h v)"), ALU.mult, [r_sgb[j], r_gn], [r_gG], eng="pool")
                tt(ob[:nt, :], t1[:nt, :], gG[:nt, :], ALU.mult, [r_t1, r_gG], [r_ob])

            def h_tr():
                pt, rt = tbank()
                for hb in range(4):
                    S.emit("pe", lambda e, hb=hb, pt=pt: e.transpose(out=pt[:, hb * 128:hb * 128 + nt], in_=ob[:nt, hb * 128:(hb + 1) * 128],
                                                                      identity=identb[:nt, :nt]), [r_ob, r_ident], [rt])
                cp(obT[:, :, j * 128:j * 128 + nt], pt[:, 0:512].rearrange("p (k t) -> p k t", k=4)[:, :, 0:nt], [rt], [r_obT[j]], eng="act")
            h_steps.append(lambda: khat_transpose(j))
            h_steps.append(h_at)
            for p in range(cps):
                h_steps.append(lambda p=p: h_chunk(p))
            h_steps.append(h_out)
            h_steps.append(h_tr)
        for j in range(nsub):
            make_h(j)

        def run_steps(steps, pool, tr, pre=None):
            def f():
                gstate["pool"] = pool
                gstate["tr"] = tr
                if pre is not None:
                    pre()
                for st_ in steps:
                    st_()
            return f
        att_ops = S.capture(run_steps(att_steps, [], (0,)))
        z_ops = S.capture(run_steps(z_steps, gpool_full[0:1], (0,)))
        h_ops = S.capture(run_steps(h_steps, gpool_full[1:2], (1,), pre=hgrn_gates_all))
        S.emit_merged(att_ops, z_ops, h_ops)
        gstate["tr"] = (0, 1)

        gstate["pool"] = gpool_front + gpool_back
        wva, rwa, sla = w_next("a")
        wstate["consumed"] += 1
        wvb, rwb, slb = w_next("b")
        wstate["consumed"] -= 1
        for ct in range(8):
            pb, rb = gbank()
            for kc in range(4):
                mm(pb[:, 0:ntok], wva[:, kc, ct * 128:(ct + 1) * 128], oaT[:, kc, 0:ntok], kc == 0, kc == 3, [rwa] + r_oaT[:nsub], [rb])
            for kc in range(4):
                mm(pb[:, 256:256 + ntok], wvb[:, kc, ct * 128:(ct + 1) * 128], obT[:, kc, 0:ntok], kc == 0, kc == 3, [rwb] + r_obT[:nsub], [rb])
            stt(m1[:, 0:ntok], zs[:, ct, 0:ntok], 1.0, pb[:, 0:ntok], ALU.add, ALU.mult, [rb, r_zs[ct]], [r_m1])
            stt(m2[:, 0:ntok], zs[:, 8 + ct, 0:ntok], 1.0, pb[:, 256:256 + ntok], ALU.add, ALU.mult, [rb, r_zs[8 + ct]], [r_m2])
            tt(mT[:, ct, 0:ntok], m1[:, 0:ntok], m2[:, 0:ntok], ALU.add, [r_m1, r_m2], [r_mT[ct]], eng="pool")
        w_done(sla)
        w_done(slb)

        for n in range(2):
            wv, rw, sl = w_next("o%d" % n)
            for j in range(nsub):
                nt = nts[j]
                pb, rb = gbank()
                for kc in range(8):
                    mm(pb[:nt, :], mT[:, kc, j * 128:j * 128 + nt], wv[:, kc, :], kc == 0, kc == 7, [rw, r_mT[kc]], [rb])
                stt(Xb[:nt, j, n * 512:(n + 1) * 512], pb[:nt, :], 0.5, Xb[:nt, j, n * 512:(n + 1) * 512], ALU.mult, ALU.add, [rX[j], rb], [rX[j]])
            w_done(sl)
        norm_T(Xb, rX, gffn, r_gffn, 2, 2, ntok)
        hT = hTs[2]
        r_hT = r_hTs[2]
        rhT_all = r_hT[:nsub]

        ffn_pend = []
        for gi in range(6):
            ntile = 4 if gi < 5 else 2
            if gi == 2 and prenorm is not None:
                prenorm()
            wvg, rwg, slg = w_next("g%d" % gi)
            if mode != "pre":
                wstate["consumed"] += 1
                wvu, rwu, slu = w_next("u%d" % gi)
                wstate["consumed"] -= 1
            for ct in range(ntile):
                ft = gi * 4 + ct
                pb, rb = gbank()
                if mode == "pre":
                    for kc in range(8):
                        mm(pb[:, 0:2], wvg[:, kc, ct * 128:(ct + 1) * 128], hT[:, kc, ntok - 2:ntok], kc == 0, kc == 7, [rwg] + rhT_all, [rb])
                    cp(cprev[:, ft, :], pb[:, 0:2], [rb], [r_cprev])
                    continue
                for kc in range(8):
                    mm(pb[:, 0:ntok], wvg[:, kc, ct * 128:(ct + 1) * 128], hT[:, kc, 0:ntok], kc == 0, kc == 7, [rwg] + rhT_all, [rb])
                for kc in range(8):
                    mm(pb[:, 256:256 + ntok], wvu[:, kc, ct * 128:(ct + 1) * 128], hT[:, kc, 0:ntok], kc == 0, kc == 7, [rwu] + rhT_all, [rb])
                bi = ft % 2
                a_ = aT[bi]
                cp(a_[:, 0:2], cprev[:, ft, :], [r_cprev], [r_aT[bi]], eng="pool")
                act(a_[:, 2:2 + ntok], pb[:, 0:ntok], AF.Copy, [rb], [r_aT[bi]])
                cp(cprev[:, ft, :], a_[:, ntok:ntok + 2], [r_aT[bi]], [r_cprev], eng="pool")
                act(c1[bi][:, 0:ntok], pb[:, 0:ntok], AF.Identity, [rb, r_cw, r_cb], [r_c1[bi]], scale=cw[:, ft, 2:3], bias=cb[:, ft:ft + 1])
                stt(c2[bi][:, 0:ntok], a_[:, 1:1 + ntok], cw[:, ft, 1:2], c1[bi][:, 0:ntok], ALU.mult, ALU.add, [r_aT[bi], r_c1[bi], r_cw], [r_c2[bi]])
                stt(c1[bi][:, 0:ntok], a_[:, 0:ntok], cw[:, ft, 0:1], c2[bi][:, 0:ntok], ALU.mult, ALU.add, [r_aT[bi], r_c2[bi], r_cw], [r_c1[bi]])
                def fin(bi=bi, ft=ft, pb=pb, rb=rb):
                    act(c2[bi][:, 0:ntok], c1[bi][:, 0:ntok], AF.Gelu_apprx_tanh, [r_c1[bi]], [r_c2[bi]])
                    tt(gT[:, ft, 0:ntok], c2[bi][:, 0:ntok], pb[:, 256:256 + ntok], ALU.mult, [r_c2[bi], rb], [r_gT[ft]])
                if ffn_pend:
                    ffn_pend.pop(0)()
                ffn_pend.append(fin)
            w_done(slg)
            if mode != "pre":
                w_done(slu)
        if mode == "pre":
            return
        while ffn_pend:
            ffn_pend.pop(0)()

        for n in range(2):
            banks = [gbank() for _ in range(nsub)]
            for gk in range(3):
                wv, rw, sl = w_next("d%d_%d" % (n, gk))
                nk = 8 if gk < 2 else 6
                for kl in range(nk):
                    kc = gk * 8 + kl
                    for j in range(nsub):
                        nt = nts[j]
                        mm(banks[j][0][:nt, :], gT[:, kc, j * 128:j * 128 + nt], wv[:, kl, :], kc == 0, kc == NFT - 1, [rw, r_gT[kc]], [banks[j][1]])
                w_done(sl)
            for j in range(nsub):
                nt = nts[j]
                tt(Xb[:nt, j, n * 512:(n + 1) * 512], Xb[:nt, j, n * 512:(n + 1) * 512], banks[j][0][:nt, :], ALU.add, [rX[j], banks[j][1]], [rX[j]])
        for j in range(nsub):
            nt = nts[j]
            act(XN[:nt, j, :], Xb[:nt, j, :], AF.Square, [rX[j]], [r_XN[j], r_ss], accum_out=ss[:nt, 4 + j:5 + j])
            ts(ss[:nt, 4 + j:5 + j], ss[:nt, 4 + j:5 + j], 1.0 / D, EPS, ALU.mult, ALU.add, [r_ss], [r_ss], eng="pool")
            tt(ss[:nt, 4 + j:5 + j], ss[:nt, 4 + j:5 + j], nhalf[:nt, 0:1], ALU.pow, [r_ss, r_nh], [r_ss], eng="pool")
            stt(Xb[:nt, j, :], Xb[:nt, j, :], ss[:nt, 4 + j:5 + j], gfin[:nt, :], ALU.mult, ALU.mult, [rX[j], r_ss, r_gfin], [rX[j]])
            dma("act", out_row[j * 128:j * 128 + nt, :], Xb[:nt, j, :], [rX[j]], ())

    cur = {}

    def kv_proj_only():
        ntok, gt0 = cur["ntok"], cur["gt0"]
        for ct in range(4):
            pb, rb = gbank()
            for kc in range(8):
                mm(pb[:, 0:ntok], W1v[2][:, kc, ct * 128:(ct + 1) * 128], cur["hT"][:, kc, 0:ntok], kc == 0, kc == 7, [r_wslot[2]] + cur["r_hT"], [rb])
            for j in range(2):
                s_ = (gt0 + j) % 6
                cp(kTr[:, ct, s_, :], pb[:, j * 128:(j + 1) * 128], [rb], [r_kT[s_]], eng="act")
        for j in range(2):
            s_ = (gt0 + j) % 6
            pb, rb = gbank()
            for kc in range(8):
                mm(pb[:, :], cur["hT"][:, kc, j * 128:(j + 1) * 128], W1v[3][:, kc, :], kc == 0, kc == 7, [r_wslot[3], cur["r_hT"][j]], [rb])
            cp(Vr[:, s_, :, 0:64], pb[:, :].rearrange("p (h d) -> p h d", h=8), [rb], [r_V[s_]], eng="act")
            cp(Vr[:, s_, :, 64:65], hvt[:, 0:1].unsqueeze(2).to_broadcast([128, 8, 1]), [r_hv], [r_V[s_]], eng="pool")


    xa = xin.ap()
    nscan = n_scan
    nmain = n_main
    plan = []
    for m in range(nscan):
        plan.append(("scan", xa[m * T:(m + 1) * T, :], T, dict(gt0=-5 + 2 * (m - (nscan - 2)) + 6, kv=m >= nscan - 2)))
    if do_pre:
        plan.append(("pre", xa[NPRE:NPRE + PRE_T, :], PRE_T, dict(gt0=5)))
    for m in range(nmain):
        fk = (okT_d, ov_d, (m - (nmain - 2)) * T) if (m >= nmain - 2 and not NOFK) else None
        r0 = NPRE + PRE_T + m * T
        plan.append(("main", xa[r0:r0 + T, :], T, dict(gt0=2 * m + 6, out_row=y_d.ap()[m * T:(m + 1) * T, :], final_kv=fk)))
    if do_sample:
        plan.append(("sample", xs_d.ap(), 16, dict(gt0=0, out_row=ys_d.ap(), final_kv=(okTs_d, ovs_d, 0))))
    if plan:
        load_x(plan[0][1], plan[0][2], 0)
    if not do_conv:
        conv_jobs.clear()
    for i, (mode, xsrc, ntok, kw) in enumerate(plan):
        xb = i % 2
        nxt = None
        pren = None
        if i + 1 < len(plan):
            nxt = (lambda p=plan[i + 1], b=(i + 1) % 2: load_x(p[1], p[2], b))
            if mode == "main":
                pren = (lambda p=plan[i + 1], b=(i + 1) % 2: norm_T(X[b], r_X[b], gmix, r_gmix, 0, b, p[2]))
        normed = i > 0 and plan[i - 1][0] == "main"
        if mode == "scan":
            if i > 0:
                continue

            def stage(k, part):
                md, xs_, nt_, kw_ = plan[k]
                nx = None
                if part == "f1":
                    emit_conv(2)
                    nx = (lambda p=plan[k + 1], b=(k + 1) % 2: load_x(p[1], p[2], b)) if k + 1 < len(plan) else None
                if part == "f2":
                    cur["ntok"] = nt_
                    cur["gt0"] = kw_["gt0"]
                macro_tile("scan", nt_, k % 2, hsel=k % 2, after_norm=nx, part=part, bsel=k % 2, vsel=k % 3, ksel=k % 2, **kw_)
            for w in range(-3, nscan):
                lists = []
                for off, part in ((0, "upd"), (1, "gates"), (2, "f2"), (3, "f1")):
                    k = w + off
                    if 0 <= k < nscan:
                        lists.append(S.capture(lambda k=k, part=part: stage(k, part)))
                S.emit_merged(*lists)
            continue
        if i == nscan:
            emit_conv(len(conv_jobs))
            for s_ in range(NSLOT):
                w_issue(s_)
        if mode == "sample":
            dma("act", oS_d.ap().rearrange("h k v -> k h v"), Sst[:], [r_S], ())
            dma("act", oconv_d.ap(), cprev[:], [r_cprev], ())
            dma("pool", kTr[:, :, 0:4, :], ckT_d.ap().rearrange("p c (t k) -> p c t k", t=4), (), r_kT[0:4])
            for t_ in range(4):
                dma("pool", Vr[:, t_, :, 0:64], cv_d.ap()[:, t_ * 128:(t_ + 1) * 128, :].rearrange("h p d -> p h d"), (), [r_V[t_]])
                cp(Vr[:, t_, :, 64:65], onesT[:, 0:8].unsqueeze(2), [r_ones], [r_V[t_]], eng="pool")
            dma("sp", Sst[:], s0_d.ap().rearrange("h k v -> k h v"), (), [r_S])
            dma("sp", cprev[:], cprev_d.ap(), (), [r_cprev])
        macro_tile(mode, ntok, xb, hsel=xb, normed=normed, after_norm=nxt, prenorm=pren, **kw)
    emit_conv(len(conv_jobs))
    if not do_sample:
        dma("act", oS_d.ap().rearrange("h k v -> k h v"), Sst[:], [r_S], ())
        dma("act", oconv_d.ap(), cprev[:], [r_cprev], ())
    else:
        dma("act", oSs_d.ap().rearrange("h k v -> k h v"), Sst[:], [r_S], ())
        dma("act", oconvs_d.ap(), cprev[:], [r_cprev], ())
    for nm, getter in dumps:
        ap_, res_ = getter(locals())
        d_ = nc.dram_tensor("dbg_" + nm, list(ap_.shape), ap_.dtype if hasattr(ap_, "dtype") else F32, kind="ExternalOutput")
        dma("pool", d_.ap(), ap_, res_, ())

    S.finalize(st)
    with nc.Block() as block:
        S.run(block)
    st.close()
    return nc


_NC_CACHE = {}


def kernel(x_prompt, x_sample, cache_attn_k, cache_attn_v, state_hgrn, state_ffn_conv,
           norm_mix_g, w_in, rel_bias, hgrn_lb_logits, hgrn_norm_g, w_branch_a, w_branch_b, w_out,
           norm_ffn_g, w_ffn_gate, w_ffn_up, ffn_conv_w, ffn_conv_b, w_ffn_down, norm_final_g):
    f32 = np.float32
    A = lambda a: np.ascontiguousarray(np.asarray(a, dtype=f32))
    x_prompt = A(x_prompt)
    if "nc" not in _NC_CACHE:
        _NC_CACHE["nc"] = build_program()
    nc = _NC_CACHE["nc"]
    s_idx = np.arange(128)
    mask = ((s_idx[:, None] // 64 == s_idx[None, :] // 64) & (s_idx[:, None] <= s_idx[None, :])).astype(f32)
    shared = {
        "w_in": A(w_in[0]), "w_a": A(w_branch_a[0]), "w_b": A(w_branch_b[0]), "w_o": A(w_out[0]),
        "w_g": A(w_ffn_gate[0]), "w_u": A(w_ffn_up[0]), "w_d": A(w_ffn_down[0]),
        "gmix": A(np.asarray(norm_mix_g[0]).reshape(8, 128).T), "gffn": A(np.asarray(norm_ffn_g[0]).reshape(8, 128).T),
        "gfin": A(norm_final_g), "rel": A(rel_bias[0]),
        "lbl": A(np.asarray(hgrn_lb_logits).reshape(2, 4, 128).transpose(2, 0, 1)),
        "gn": A(hgrn_norm_g[0]),
        "cw": A(np.asarray(ffn_conv_w[0]).reshape(3, NFT, 128).transpose(2, 1, 0)),
        "cb": A(np.asarray(ffn_conv_b[0]).reshape(NFT, 128).T),
        "ident": np.eye(128, dtype=f32), "mask": mask, "jmat": np.ascontiguousarray(np.eye(128, dtype=f32)[::-1]),
    }
    in_maps = []
    for c in range(8):
        b, j = divmod(c, 4)
        s = j * SEG
        lo = s - PRE_T - NPRE
        xin = np.zeros((NTOK_IN, D), f32)
        a0 = max(lo, 0)
        xin[a0 - lo:] = x_prompt[b, a0:s + SEG]
        m = dict(shared)
        m["xin"] = xin
        m["hv"] = np.full((128, 1), 1.0 if j > 0 else 0.0, f32)
        m["xs"] = A(x_sample[c])
        m["ckT"] = A(np.asarray(cache_attn_k[0, c]).transpose(0, 2, 1).reshape(4, 128, 512).transpose(1, 0, 2))
        m["cv"] = A(cache_attn_v[0, c])
        m["s0"] = A(state_hgrn[0, c])
        m["cprev"] = A(np.asarray(state_ffn_conv[0, c]).reshape(2, NFT, 128).transpose(2, 1, 0))
        in_maps.append(m)
    res = run_bass_kernel_spmd(nc, in_maps, core_ids=list(range(8)))
    R_ = res.results
    B = 2
    y_prompt = np.stack([np.concatenate([R_[b * 4 + j]["y"] for j in range(4)], axis=0) for b in range(B)])
    y_sample = np.stack([R_[c]["ys"] for c in range(8)])

    def kT_to_rows(a):
        n = a.shape[-1]
        return a.reshape(8, 64, n).transpose(0, 2, 1)

    def v_to_rows(a):
        n = a.shape[0]
        return a.reshape(n, 8, 64).transpose(1, 0, 2)

    def conv_rows(a):
        return a.transpose(2, 1, 0).reshape(2, DFF)

    last = [3, 7]
    new_k_p = np.stack([kT_to_rows(R_[c]["okT"]) for c in last])[None]
    new_v_p = np.stack([v_to_rows(R_[c]["ov"]) for c in last])[None]
    hg_p = np.stack([R_[c]["oS"] for c in last])[None]
    cv_p = np.stack([conv_rows(R_[c]["oconv"]) for c in last])[None]
    new_k_s = np.stack([kT_to_rows(R_[c]["okTs"]) for c in range(8)])[None]
    new_v_s = np.stack([v_to_rows(R_[c]["ovs"]) for c in range(8)])[None]
    hg_s = np.stack([R_[c]["oSs"] for c in range(8)])[None]
    cv_s = np.stack([conv_rows(R_[c]["oconvs"]) for c in range(8)])[None]
    outs = (y_prompt, y_sample, new_k_p, new_v_p, hg_p, cv_p, new_k_s, new_v_s, hg_s, cv_s)
    return tuple(np.ascontiguousarray(o, dtype=f32) for o in outs)
```

```python
import numpy as np
from contextlib import ExitStack
import concourse.bass as bass
import concourse.mybir as mybir
from concourse.bass import AP
from concourse.bass_utils import run_bass_kernel_spmd

F32 = mybir.dt.float32
BF16 = mybir.dt.bfloat16
AF = mybir.ActivationFunctionType
ALU = mybir.AluOpType
AX = mybir.AxisListType

D = 1024
DFF = 2816
NFT = 22
SEG = 4096
NPRE = 12288
PRE_T = 128
NTOK_IN = NPRE + PRE_T + SEG
T = 256
EPS = 1e-6
NSLOT = 6
STOP = 99
STOPMODE = 'main'
NOFK = False
SLOT_E = 4096


class Res:
    __slots__ = ("name", "w", "r", "excl")

    def __init__(self, name, excl=False):
        self.name = name
        self.w = None
        self.r = {}
        self.excl = excl


class Op:
    __slots__ = ("eng", "fn", "deps", "is_dma", "needs_inc", "sem", "val", "idx")


class Sched:
    ENGS = ("pe", "act", "dve", "pool", "sp")

    def __init__(self, nc, n_dma_sems=8):
        self.nc = nc
        self.ops = []
        self.n_dma_sems = n_dma_sems
        self.cap = None

    def capture(self, fn):
        assert self.cap is None
        self.cap = []
        try:
            fn()
            return self.cap
        finally:
            self.cap = None

    DUR = {"pe": 0.15, "act": 0.75, "dve": 0.85, "pool": 1.1, "sp": 0.3}

    def emit_merged(self, *lists):
        eng_free = {}
        ready = {}
        rdone = {}
        LAT = 0.25

        def start_of(op):
            eng, fn, reads, writes, dma, cost = op
            t = eng_free.get(eng, 0.0)
            for r in reads:
                t = max(t, ready.get(id(r), 0.0) + LAT)
            for w in writes:
                t = max(t, ready.get(id(w), 0.0) + LAT, rdone.get(id(w), 0.0) + LAT)
            return t

        def commit(op, t):
            eng, fn, reads, writes, dma, cost = op
            d = 2.5 if dma else (cost if cost is not None else self.DUR[eng])
            eng_free[eng] = t + (0.1 if dma else d)
            for r in reads:
                rdone[id(r)] = max(rdone.get(id(r), 0.0), t + d)
            for w in writes:
                ready[id(w)] = t + d
            self.emit(eng, fn, reads, writes, dma)

        pos = [0] * len(lists)
        while True:
            best, bt = -1, None
            for k, l in enumerate(lists):
                if pos[k] < len(l):
                    t = start_of(l[pos[k]])
                    if bt is None or t < bt:
                        best, bt = k, t
            if best < 0:
                break
            commit(lists[best][pos[best]], bt)
            pos[best] += 1

    def emit(self, eng, fn, reads=(), writes=(), dma=False, cost=None):
        if self.cap is not None:
            self.cap.append((eng, fn, tuple(reads), tuple(writes), dma, cost))
            return None
        op = Op()
        op.eng = eng
        op.fn = fn
        op.is_dma = dma
        op.needs_inc = dma
        op.sem = None
        op.val = 0
        op.idx = len(self.ops)
        deps = {}
        xr = [r for r in reads if r.excl]
        if xr:
            reads = [r for r in reads if not r.excl]
            writes = list(writes) + [r for r in xr if r not in writes]
        for r in reads:
            if r.w is not None:
                deps[r.w.idx] = r.w
        for w in writes:
            if w.w is not None:
                deps[w.w.idx] = w.w
            for o in w.r.values():
                deps[o.idx] = o
        op.deps = list(deps.values())
        for r in reads:
            key = (eng, op.idx) if dma else (eng, -1)
            r.r[key] = op
        for w in writes:
            w.w = op
            w.r = {}
        self.ops.append(op)
        return op

    def finalize(self, stack):
        nc = self.nc
        for op in self.ops:
            for d in op.deps:
                if d.is_dma or d.eng != op.eng or op.eng != "pe" or op.is_dma:
                    d.needs_inc = True
        csem = {e: stack.enter_context(nc.semaphore("cs_" + e)) for e in ("pe", "act", "dve", "pool")}
        dsem = {e: [stack.enter_context(nc.semaphore("ds_%s%d" % (e, i))) for i in range(self.n_dma_sems)]
                for e in ("sp", "pool", "act")}
        ccount = {e: 0 for e in csem}
        dstate = {e: [None] * self.n_dma_sems for e in dsem}
        duse = {e: [0] * self.n_dma_sems for e in dsem}
        drr = {e: 0 for e in dsem}
        waited = {e: {} for e in self.ENGS}
        streams = {e: [] for e in self.ENGS}
        for op in self.ops:
            waits = []
            e = op.eng
            extra = []
            if op.is_dma:
                k = drr[e] % self.n_dma_sems
                drr[e] += 1
                prev = dstate[e][k]
                if prev is not None:
                    extra.append(prev)
                duse[e][k] += 1
                op.sem = dsem[e][k]
                op.val = 16 * duse[e][k]
                dstate[e][k] = op
            elif op.needs_inc:
                ccount[e] += 1
                op.sem = csem[e]
                op.val = ccount[e]
            for d in op.deps + extra:
                if (not d.is_dma) and d.eng == e and e == "pe" and not op.is_dma:
                    continue
                key = id(d.sem)
                if waited[e].get(key, 0) >= d.val:
                    continue
                waited[e][key] = d.val
                waits.append((d.sem, d.val))
            streams[e].append((waits, op))
        fin = []
        for e in dsem:
            for k in range(self.n_dma_sems):
                if duse[e][k] and waited["sp"].get(id(dsem[e][k]), 0) < 16 * duse[e][k]:
                    fin.append((dsem[e][k], 16 * duse[e][k]))
        self.streams = streams
        self.fin = fin

    def run(self, block):
        streams = self.streams
        fin = self.fin

        def body(name):
            def f(eng):
                for waits, op in streams[name]:
                    for s, v in waits:
                        eng.wait_ge(s, v)
                    inst = op.fn(eng)
                    if op.sem is not None:
                        inst.then_inc(op.sem, 16 if op.is_dma else 1)
                if name == "sp":
                    for s, v in fin:
                        eng.wait_ge(s, v)
            return f
        block.tensor(body("pe"))
        block.scalar(body("act"))
        block.vector(body("dve"))
        block.gpsimd(body("pool"))
        block.sync(body("sp"))


def build_program(n_scan=NPRE // T, n_main=SEG // T, do_pre=True, do_sample=True, dumps=(), do_conv=True, init_stop=99):
    NPRE = n_scan * T
    NTOK_IN = NPRE + PRE_T + n_main * T
    nc = bass.Bass("TRN2", target_bir_lowering=False)
    st = ExitStack()
    S = Sched(nc)

    def din(name, shape):
        return nc.dram_tensor(name, list(shape), F32, kind="ExternalInput")

    def dout(name, shape):
        return nc.dram_tensor(name, list(shape), F32, kind="ExternalOutput")

    xin = din("xin", [NTOK_IN, D])
    hv_d = din("hv", [128, 1])
    xs_d = din("xs", [16, D])
    ckT_d = din("ckT", [128, 4, 512])
    cv_d = din("cv", [8, 512, 64])
    s0_d = din("s0", [4, 128, 128])
    cprev_d = din("cprev", [128, NFT, 2])
    w_in_d = din("w_in", [D, 5632])
    w_a_d = din("w_a", [512, D])
    w_b_d = din("w_b", [512, D])
    w_o_d = din("w_o", [D, D])
    w_g_d = din("w_g", [D, DFF])
    w_u_d = din("w_u", [D, DFF])
    w_d_d = din("w_d", [DFF, D])
    gmix_d = din("gmix", [128, 8])
    gffn_d = din("gffn", [128, 8])
    gfin_d = din("gfin", [D])
    rel_d = din("rel", [8, 192])
    lbl_d = din("lbl", [128, 2, 4])
    gn_d = din("gn", [128])
    cw_d = din("cw", [128, NFT, 3])
    cb_d = din("cb", [128, NFT])
    ident_d = din("ident", [128, 128])
    mask_d = din("mask", [128, 128])
    jmat_d = din("jmat", [128, 128])

    y_d = dout("y", [SEG, D])
    ys_d = dout("ys", [16, D])
    okT_d = dout("okT", [4, 128, 512])
    ov_d = dout("ov", [512, 512])
    oS_d = dout("oS", [4, 128, 128])
    oconv_d = dout("oconv", [128, NFT, 2])
    okTs_d = dout("okTs", [4, 128, 16])
    ovs_d = dout("ovs", [16, 512])
    oSs_d = dout("oSs", [4, 128, 128])
    oconvs_d = dout("oconvs", [128, NFT, 2])

    def scratch(name, shape, dt=BF16):
        return nc.dram_tensor(name, list(shape), dt, kind="Internal")

    wb_in = scratch("wb_in", [D, 5632])
    wb_a = scratch("wb_a", [512, D])
    wb_b = scratch("wb_b", [512, D])
    wb_o = scratch("wb_o", [D, D])
    wb_g = scratch("wb_g", [D, DFF])
    wb_u = scratch("wb_u", [D, DFF])
    wb_d = scratch("wb_d", [DFF, D])
    ext_d = scratch("ext_d", [8, 768], F32)

    def sb(name, shape, dt=F32):
        return st.enter_context(nc.sbuf_tensor(name, list(shape), dt))

    def R(name, excl=False):
        return Res(name, excl)

    def fsz(ap):
        n = 1
        for d_ in list(ap.shape)[1:]:
            n *= int(d_)
        return n

    def ecost(eng, ap):
        n = fsz(ap)
        return {"act": 0.25 + n / 1000.0, "dve": 0.1 + n / 850.0, "pool": 0.2 + n / 480.0}[eng]

    def act(out, in_, func, reads, writes, **kw):
        S.emit("act", lambda e: e.activation(out=out, in_=in_, func=func, **kw), reads, writes, cost=ecost("act", out))

    def tt(out, in0, in1, op, reads, writes, eng="dve"):
        S.emit(eng, lambda e: e.tensor_tensor(out=out, in0=in0, in1=in1, op=op), reads, writes, cost=ecost(eng, out))

    def ts(out, in0, s1, s2, op0, op1, reads, writes, eng="dve"):
        if op1 is None:
            S.emit(eng, lambda e: e.tensor_scalar(out=out, in0=in0, scalar1=s1, scalar2=None, op0=op0), reads, writes, cost=ecost(eng, out))
        else:
            S.emit(eng, lambda e: e.tensor_scalar(out=out, in0=in0, scalar1=s1, scalar2=s2, op0=op0, op1=op1), reads, writes, cost=ecost(eng, out))

    def stt(out, in0, scalar, in1, op0, op1, reads, writes):
        S.emit("dve", lambda e: e.scalar_tensor_tensor(out=out, in0=in0, scalar=scalar, in1=in1, op0=op0, op1=op1), reads, writes, cost=ecost("dve", out))

    def cp(out, in_, reads, writes, eng="dve"):
        if eng == "act":
            act(out, in_, AF.Copy, reads, writes)
        else:
            S.emit(eng, lambda e: e.tensor_copy(out=out, in_=in_), reads, writes, cost=ecost(eng, out))

    def recip(out, in_, reads, writes):
        S.emit("dve", lambda e: e.reciprocal(out=out, in_=in_), reads, writes)

    def mset(ap, val, writes, eng="pool"):
        S.emit(eng, lambda e: e.memset(ap, val), (), writes)

    def mm(out, lhsT, rhs, start, stop, reads, writes):
        S.emit("pe", lambda e: e.matmul(out, lhsT=lhsT, rhs=rhs, start=start, stop=stop), reads, writes, cost=0.05 + fsz(rhs) / 2000.0)

    def dma(eng, out, in_, reads, writes):
        S.emit(eng, lambda e: e.dma_start(out=out, in_=in_), reads, writes, dma=True)

    identb = sb("identb", [128, 128], BF16); r_ident = R("ident")
    maskb = sb("maskb", [128, 128], BF16); r_mask = R("mask")
    jb = sb("jb", [128, 128], BF16); r_j = R("j")
    epst = sb("epst", [128, 1]); r_eps = R("eps")
    onesT = sb("onesT", [128, 8]); r_ones = R("ones")
    hvt = sb("hvt", [128, 1]); r_hv = R("hv")
    gmix = sb("gmix_s", [128, 8]); r_gmix = R("gmix")
    gffn = sb("gffn_s", [128, 8]); r_gffn = R("gffn")
    gfin = sb("gfin_s", [128, D]); r_gfin = R("gfin")
    gnrep = sb("gnrep", [128, 4, 128]); r_gn = R("gn")
    cw = sb("cw_s", [128, NFT, 3]); r_cw = R("cw")
    cb = sb("cb_s", [128, NFT]); r_cb = R("cb")
    lbl = sb("lbl_s", [128, 2, 4]); r_lbl = R("lbl")
    lb = sb("lb_s", [128, 4]); oml = sb("oml_s", [128, 4]); r_lb = R("lb")
    ET = sb("ET", [128, 5, 8, 128], BF16); r_ET = R("ET")

    dma("pool", identb[:], ident_d.ap(), (), [r_ident])
    dma("pool", maskb[:], mask_d.ap(), (), [r_mask])
    dma("pool", jb[:], jmat_d.ap(), (), [r_j])
    mset(epst[:], EPS, [r_eps])
    nhalf = sb("nhalf", [128, 8]); r_nh = R("nhalf")
    mset(nhalf[:], -0.5, [r_nh])
    mset(onesT[:], 1.0, [r_ones])
    dma("sp", hvt[:], hv_d.ap(), (), [r_hv])
    dma("sp", gmix[:], gmix_d.ap(), (), [r_gmix])
    dma("sp", gffn[:], gffn_d.ap(), (), [r_gffn])
    dma("sp", gfin[:], AP(gfin_d, 0, [[0, 128], [1, D]]), (), [r_gfin])
    dma("sp", gnrep[:], AP(gn_d, 0, [[0, 128], [0, 4], [1, 128]]), (), [r_gn])
    dma("sp", cw[:], cw_d.ap(), (), [r_cw])
    dma("sp", cb[:], cb_d.ap(), (), [r_cb])
    dma("sp", lbl[:], lbl_d.ap(), (), [r_lbl])

    if init_stop <= 1:
        S.finalize(st)
        with nc.Block() as block:
            S.run(block)
        st.close()
        return nc
    ps_att = st.enter_context(nc.psum_tensor("ps_att", [128, 1536], F32))
    r_att = [R("att0", True), R("att1", True)]
    r_attB = R("attB", True)
    NGEN = 2
    ps_gen = [st.enter_context(nc.psum_tensor("ps_g%d" % i, [128, 512], F32)) for i in range(NGEN)]
    r_gen = [R("g%d" % i, True) for i in range(NGEN)]
    ps_o = st.enter_context(nc.psum_tensor("ps_o", [128, 512], F32))
    r_o = R("ps_o", True)
    ps_tr = [st.enter_context(nc.psum_tensor("ps_t%d" % i, [128, 1024], BF16)) for i in range(2)]
    r_tr = [R("t%d" % i, True) for i in range(2)]
    cnt = {"g": 0, "t": 0, "a": 0, "p": 0}

    gpool_full = [(ps_gen[i], r_gen[i]) for i in range(NGEN)]
    gpool_front = gpool_full + [(ps_o, r_o)]
    gpool_back = [(ps_att[:, 0:512], r_att[0]), (ps_att[:, 512:1024], r_att[1]), (ps_att[:, 1024:1536], r_attB)]
    gstate = {"pool": gpool_full, "tr": (0, 1)}

    def gbank():
        pool = gstate["pool"]
        i = cnt["g"] % len(pool)
        cnt["g"] += 1
        return pool[i]

    def tbank():
        sel = gstate["tr"]
        i = sel[cnt["t"] % len(sel)]
        cnt["t"] += 1
        return ps_tr[i], r_tr[i]

    lbt = sb("lbt", [128, 4])
    tt(lbt[:], lbl[:, 1, :], lbl[:, 0, :], ALU.subtract, [r_lbl], [r_lb])
    act(lbt[:], lbt[:], AF.Exp, [r_lb], [r_lb])
    ts(lbt[:], lbt[:], 1.0, None, ALU.add, None, [r_lb], [r_lb])
    recip(lb[:], lbt[:], [r_lb], [r_lb])
    ts(oml[:], lb[:], -1.0, 1.0, ALU.mult, ALU.add, [r_lb], [r_lb])
    omlh = sb("omlh_s", [128, 4]); lbh = sb("lbh_s", [128, 4])
    ts(omlh[:], oml[:], 0.5, None, ALU.mult, None, [r_lb], [r_lb])
    tt(lbh[:], lb[:], omlh[:], ALU.add, [r_lb], [r_lb])

    if init_stop <= 2:
        S.finalize(st)
        with nc.Block() as block:
            S.run(block)
        st.close()
        return nc
    if init_stop <= 3:
        S.finalize(st)
        with nc.Block() as block:
            S.run(block)
        st.close()
        return nc
    r_wb = {}
    conv_jobs = []
    for name, src, dst, rows in (("in", w_in_d, wb_in, D), ("a", w_a_d, wb_a, 512), ("b", w_b_d, wb_b, 512),
                                 ("o", w_o_d, wb_o, D), ("g", w_g_d, wb_g, D), ("u", w_u_d, wb_u, D),
                                 ("d", w_d_d, wb_d, DFF)):
        r_wb[name] = R("wb_" + name)
        for r0 in range(0, rows, 128):
            conv_jobs.append((name, dst.ap()[r0:r0 + 128, :], src.ap()[r0:r0 + 128, :]))

    def emit_conv(n):
        for _ in range(n):
            if conv_jobs:
                name, d_, s_ = conv_jobs.pop(0)
                dma("pool", d_, s_, (), [r_wb[name]])

    wslot = [sb("wslot%d" % i, [128, SLOT_E], BF16) for i in range(NSLOT)]
    r_wslot = [R("wslot%d" % i) for i in range(NSLOT)]
    W1v = []
    for i, c0 in enumerate((2048, 2560, 512, 1024)):
        v_ = wslot[i][:, :].rearrange("p (k n) -> p k n", k=8)
        dma("pool", v_, w_in_d.ap()[:, c0:c0 + 512].rearrange("(k p) n -> p k n", p=128), (), [r_wslot[i]])
        W1v.append(v_)

    def wgroups(mode):
        g = []
        for i in (3, 4, 5, 6, 0, 1, 2, 7, 8, 9, 10):
            g.append(("in%d" % i, "in", wb_in.ap()[:, 512 * i:512 * i + 512].rearrange("(k p) n -> p k n", p=128), 8, 512))
        g.append(("a", "a", wb_a.ap().rearrange("(k p) n -> p k n", p=128), 4, 1024))
        g.append(("b", "b", wb_b.ap().rearrange("(k p) n -> p k n", p=128), 4, 1024))
        for n in range(2):
            g.append(("o%d" % n, "o", wb_o.ap()[:, 512 * n:512 * n + 512].rearrange("(k p) n -> p k n", p=128), 8, 512))
        for gi in range(6):
            nc_ = 512 if gi < 5 else 256
            g.append(("g%d" % gi, "g", wb_g.ap()[:, 512 * gi:512 * gi + nc_].rearrange("(k p) n -> p k n", p=128), 8, nc_))
            if mode != "pre":
                g.append(("u%d" % gi, "u", wb_u.ap()[:, 512 * gi:512 * gi + nc_].rearrange("(k p) n -> p k n", p=128), 8, nc_))
        if mode != "pre":
            for n in range(2):
                for gk in range(3):
                    nk = 8 if gk < 2 else 6
                    g.append(("d%d_%d" % (n, gk), "d",
                              wb_d.ap()[1024 * gk:1024 * gk + 128 * nk, 512 * n:512 * n + 512].rearrange("(k p) n -> p k n", p=128), nk, 512))
        return g

    wq = []
    tiles_plan = ([("pre", 0)] if do_pre else []) + [("main", m) for m in range(n_main)] + ([("sample", 0)] if do_sample else [])
    for mode, _ in tiles_plan:
        wq.extend(wgroups(mode))
    wstate = {"issued": 0, "consumed": 0, "slot": {}}

    def w_issue(slot):
        i = wstate["issued"]
        if i >= len(wq):
            return
        key, wname, src, nk, ncol = wq[i]
        view = wslot[slot][:, 0:nk * ncol].rearrange("p (k n) -> p k n", k=nk)
        dma("sp", view, src, [r_wb[wname]], [r_wslot[slot]])
        wstate["slot"][i] = slot
        wstate["issued"] += 1

    def w_next(key):
        i = wstate["consumed"]
        assert wq[i][0] == key, (wq[i][0], key)
        slot = wstate["slot"][i]
        _, _, _, nk, ncol = wq[i]
        view = wslot[slot][:, 0:nk * ncol].rearrange("p (k n) -> p k n", k=nk)
        return view, r_wslot[slot], slot

    def w_done(slot):
        wstate["consumed"] += 1
        w_issue(slot)

    if init_stop <= 4:
        S.finalize(st)
        with nc.Block() as block:
            S.run(block)
        st.close()
        return nc
    X = [sb("X%d" % i, [128, 2, D]) for i in range(2)]
    r_X = [[R("X%d_%d" % (i, j)) for j in range(2)] for i in range(2)]
    XN = sb("XN", [128, 2, D], BF16); r_XN = [R("XN0"), R("XN1")]
    ss = sb("ss", [128, 8]); r_ss = R("ss")
    hTs = [sb("hTa", [128, 8, T], BF16), sb("hTb", [128, 8, T], BF16), sb("h2T", [128, 8, T], BF16)]
    r_hTs = [[R("hTa0"), R("hTa1")], [R("hTb0"), R("hTb1")], [R("h2T0"), R("h2T1")]]
    qT = sb("qT", [128, 4, T], BF16); r_qT = R("qT")
    kTr = sb("kTr", [128, 4, 6, 128], BF16); r_kT = [R("kT%d" % i) for i in range(6)]
    Vr = sb("Vr", [128, 6, 8, 65], BF16); r_V = [R("V%d" % i) for i in range(6)]
    Pb = [sb("Pb%d" % i, [128, 5, 128], BF16) for i in range(3)]; r_Pb = [R("Pb%d" % i) for i in range(3)]
    oa = sb("oa", [128, 2, 512], BF16); r_oa = [R("oa0"), R("oa1")]
    oaT = sb("oaT", [128, 4, T], BF16); r_oaT = [R("oaT0"), R("oaT1")]
    rec = sb("rec", [128, 8]); r_rec = R("rec")
    sg = sb("sg", [128, 4, T]); r_sg = [R("sg%d" % i) for i in range(4)]
    siluq = sb("siluq", [128, 4, T]); r_sq = [R("siluq%d" % i) for i in range(4)]
    gF = sb("gF", [128, 4 * T]); r_gF = R("gF")
    gL = sb("gL", [128, 4 * T]); r_gL = R("gL")
    gB = sb("gB", [128, 4 * T]); r_gB = R("gB")
    gK = sb("gK", [128, 4 * T]); r_gK = R("gK")
    gE = sb("gE", [128, 4 * T]); r_gE = R("gE")
    dec = sb("dec", [128, 3, 4, 4]); r_dec = R("dec")
    Zq = sb("Zq", [128, 4, 4, 128], BF16); r_Zq = [R("Zq%d" % i) for i in range(4)]
    ktT = sb("ktT", [128, 4, T], BF16); r_ktT = [R("ktT%d" % i) for i in range(4)]
    khT = sb("khT", [128, 4, T], BF16); r_khT = [R("khT%d" % i) for i in range(4)]
    khtm = sb("khtm", [128, 2, 512], BF16); r_khtm = [R("khtm0"), R("khtm1")]
    vh = sb("vh", [128, 2, 512], BF16); r_vh = [R("vh0"), R("vh1")]
    sgb = sb("sgb", [128, 2, 512]); r_sgb = [R("sgb0"), R("sgb1")]
    Sst = sb("Sst", [128, 4, 128]); r_S = R("S")
    Sp = sb("Sp", [128, 2, 4, 128], BF16); r_Sp = [R("Sp0"), R("Sp1")]
    AT = sb("AT", [128, 4, 128], BF16); r_AT = R("AT")
    sqb = sb("sqb", [128, 512]); r_sqb = R("sqb")
    ssq = sb("ssq", [128, 4]); r_ssq = R("ssq")
    t1 = sb("t1", [128, 512]); r_t1 = R("t1")
    gG = sb("gG", [128, 512]); r_gG = R("gG")
    ob = sb("ob", [128, 512], BF16); r_ob = R("ob")
    obT = sb("obT", [128, 4, T], BF16); r_obT = [R("obT0"), R("obT1")]
    zs = sb("zs", [128, 16, T], BF16); r_zs = [R("zs%d" % i) for i in range(16)]
    mT = sb("mT", [128, 8, T], BF16); r_mT = [R("mT%d" % i) for i in range(8)]
    aT = [sb("aT%d" % i, [128, T + 2]) for i in range(2)]; r_aT = [R("aT0"), R("aT1")]
    c1 = [sb("c1_%d" % i, [128, T]) for i in range(2)]; r_c1 = [R("c1_0"), R("c1_1")]
    c2 = [sb("c2_%d" % i, [128, T]) for i in range(2)]; r_c2 = [R("c2_0"), R("c2_1")]
    m1, r_m1, m2, r_m2 = c1[0], r_c1[0], c2[0], r_c2[0]
    gT = sb("gT", [128, NFT, T], BF16); r_gT = [R("gT%d" % i) for i in range(NFT)]
    cprev = sb("cprev_s", [128, NFT, 2]); r_cprev = R("cprev")

    ext_ = siluq[0:8].rearrange("p h t -> p (h t)")[:, 0:768]; r_ext = r_sq; r_extd = R("extd")
    dma("sp", ext_[:, 64:256], rel_d.ap(), (), r_ext)
    act(ext_[:, 0:64], ext_[:, 64:65].to_broadcast([8, 64]), AF.Identity, r_ext, r_ext)
    act(ext_[:, 256:768], ext_[:, 255:256].to_broadcast([8, 512]), AF.Identity, r_ext, r_ext)
    act(ext_[:, :], ext_[:, :], AF.Exp, r_ext, r_ext)
    dma("sp", ext_d.ap(), ext_[:, :], r_ext, [r_extd])
    hk = sg[:].rearrange("p h t -> p (h t)").rearrange("p (a b) -> p a b", a=8); r_hk = r_sg
    hkb = ktT[:].rearrange("p h t -> p (h t)").rearrange("p (a b) -> p a b", a=8); r_hkb = r_ktT
    for kt in range(5):
        dma("sp", hk, AP(ext_d, 512 - 128 * kt, [[1, 128], [768, 8], [1, 128]]), [r_extd], r_hk)
        cp(hkb, hk, r_hk, r_hkb)
        for n in range(2):
            pb, rb = gbank()
            mm(pb[:, :], jb[:], hkb[:, 4 * n:4 * n + 4, :].rearrange("p h q -> p (h q)"), True, True, [r_j] + r_hkb, [rb])
            cp(ET[:, kt, 4 * n:4 * n + 4, :].rearrange("p h q -> p (h q)"), pb[:, :], [rb], [r_ET])
    mset(ET[0:64, 0, :, 64:128], 0.0, [r_ET])
    mset(ET[64:128, 4, :, 0:64], 0.0, [r_ET])

    kst = gT[:, 0:8, :].rearrange("p a b -> p (a b)").bitcast(F32).rearrange("p (h t) -> p h t", h=4)
    vst = gT[:, 8:16, :].rearrange("p a b -> p (a b)").bitcast(F32).rearrange("p (j c) -> p j c", j=2)
    rl_kst = r_gT[0:8]
    rl_vst = [r_gT[8:12], r_gT[12:16]]
    sg2 = gT[:, 0:8, :].rearrange("p a b -> p (a b)").bitcast(F32).rearrange("p (h t) -> p h t", h=4)
    r_sg2 = [[r_gT[2 * i], r_gT[2 * i + 1]] for i in range(4)]
    vh2 = gT[:, 8:12, :].rearrange("p a b -> p (a b)").rearrange("p (j c) -> p j c", j=2)
    r_vh2 = [[r_gT[8], r_gT[9]], [r_gT[10], r_gT[11]]]
    vh3 = gT[:, 12:16, :].rearrange("p a b -> p (a b)").rearrange("p (j c) -> p j c", j=2)
    r_vh3 = [[r_gT[12], r_gT[13]], [r_gT[14], r_gT[15]]]
    ktT2 = gT[:, 16:20, :]
    r_ktT2 = r_gT[16:20]
    dec2 = sb("dec2", [128, 3, 4, 4]); r_dec2 = R("dec2")
    sgs = [(sg, [[r] for r in r_sg]), (sg2, r_sg2)]
    vhs = [(vh, [[r] for r in r_vh]), (vh2, r_vh2), (vh3, r_vh3)]
    kts = [(ktT, r_ktT, dec, r_dec), (ktT2, r_ktT2, dec2, r_dec2)]
    mset(Zq[:], 0.0, r_Zq)
    mset(Sst[:], 0.0, [r_S])
    mset(cprev[:], 0.0, [r_cprev])

    if init_stop <= 5:
        S.finalize(st)
        with nc.Block() as block:
            S.run(block)
        st.close()
        return nc
    def load_x(xsrc, ntok, xb):
        nsub = (ntok + 127) // 128
        for j in range(nsub):
            nt = min(128, ntok - 128 * j)
            dma("sp", X[xb][:nt, j, :], xsrc[j * 128:j * 128 + nt, :], (), [r_X[xb][j]])

    def norm_T(src_tile, rsrc, gcol, rg, col0, hsel, ntok):
        nsub = (ntok + 127) // 128
        dst, rdst = hTs[hsel], r_hTs[hsel]
        nts_ = [min(128, ntok - 128 * j) for j in range(nsub)]
        for j in range(nsub):
            nt = nts_[j]
            act(XN[:nt, j, :], src_tile[:nt, j, :], AF.Square, [rsrc[j]], [r_XN[j], r_ss], accum_out=ss[:nt, col0 + j:col0 + j + 1])
        for j in range(nsub):
            nt = nts_[j]
            ts(ss[:nt, col0 + j:col0 + j + 1], ss[:nt, col0 + j:col0 + j + 1], 1.0 / D, EPS, ALU.mult, ALU.add, [r_ss], [r_ss], eng="pool")
            tt(ss[:nt, col0 + j:col0 + j + 1], ss[:nt, col0 + j:col0 + j + 1], nhalf[:nt, 0:1], ALU.pow, [r_ss, r_nh], [r_ss], eng="pool")
        for j in range(nsub):
            nt = nts_[j]
            act(XN[:nt, j, :], src_tile[:nt, j, :], AF.Copy, [rsrc[j], r_ss], [r_XN[j]], scale=ss[:nt, col0 + j:col0 + j + 1])
            pt, rt = tbank()
            for kc in range(8):
                S.emit("pe", lambda e, kc=kc, j=j, nt=nt, pt=pt: e.transpose(out=pt[:, kc * 128:kc * 128 + nt], in_=XN[:nt, j, kc * 128:(kc + 1) * 128],
                                                                              identity=identb[:nt, :nt]), [r_XN[j], r_ident], [rt], cost=0.1)
            tt(dst[:, :, j * 128:j * 128 + nt], pt[:, :].rearrange("p (k t) -> p k t", k=8)[:, :, 0:nt],
               gcol[:, 0:8].unsqueeze(2).to_broadcast([128, 8, nt]), ALU.mult, [rt, rg], [rdst[j]])

    def macro_tile(mode, ntok, xb, gt0, hsel=0, normed=False, kv=False, out_row=None, final_kv=None, after_norm=None, prenorm=None,
                   part="all", bsel=0, vsel=0, ksel=0, deferred=None):
        sample = mode == "sample"
        nsub = (ntok + 127) // 128
        nts = [min(128, ntok - 128 * j) for j in range(nsub)]
        C = 16 if sample else (ntok if mode == "scan" else 64)
        nch = ntok // C
        ri = C - 1 if mode == "scan" else C // 2 - 1
        cps = 1 if sample else 2
        Xb = X[xb]
        rX = r_X[xb]
        hT = hTs[hsel]
        r_hT = r_hTs[hsel]
        rhT_all = r_hT[:nsub]
        cur["hT"] = hT
        cur["r_hT"] = r_hT
        sg, rl_sg = sgs[bsel]
        vh, rl_vh = vhs[vsel]
        ktT, r_ktT, dec, r_dec = kts[ksel]
        if mode == "scan":
            gstate["pool"] = gpool_front if part in ("f1", "f2") else gpool_back
            gstate["tr"] = (0,) if part in ("f1", "f2") else (1,)
        else:
            gstate["pool"] = gpool_front + gpool_back
            gstate["tr"] = (0, 1)

        if part in ("all", "f1"):
            if not normed:
                norm_T(Xb, rX, gmix, r_gmix, 0, hsel, ntok)
            if after_norm is not None and (deferred is None or not deferred):
                after_norm()
                after_norm = None

        def proj_fm(wv, rw, ct, evac):
            pb, rb = gbank()
            for kc in range(8):
                mm(pb[:, 0:ntok], wv[:, kc, ct * 128:(ct + 1) * 128], hT[:, kc, 0:ntok], kc == 0, kc == 7, [rw] + rhT_all, [rb])
            evac(pb, rb)

        def proj_tm(wv, rw, j, evac, ncol=512):
            pb, rb = gbank()
            nt = nts[j]
            for kc in range(8):
                mm(pb[:nt, 0:ncol], hT[:, kc, j * 128:j * 128 + nt], wv[:, kc, 0:ncol], kc == 0, kc == 7, [rw, r_hT[j]], [rb])
            evac(pb, rb, j, nt)

        def slot_of(j):
            return (gt0 + j) % 6 if not sample else 4

        def hgrn_gates_all():
            n4 = 4 * ntok
            nc4 = 4 * nch
            fl = lambda b: b[:, 0:n4]
            v3 = lambda b: b[:, 0:n4].rearrange("p (h t) -> p h t", h=4)
            ch = lambda b: b[:, 0:n4].rearrange("p (c t) -> p c t", t=C)
            hc = lambda b: b[:, 0:n4].rearrange("p (h c t) -> p h c t", h=4, t=C)
            rsg = [r for l_ in rl_sg for r in l_]
            sgf = sg.rearrange("p h t -> p (h t)")
            tt(v3(gF), v3(sgf), omlh[:, 0:4].unsqueeze(2).to_broadcast([128, 4, ntok]), ALU.mult, rsg + [r_lb], [r_gF])
            tt(v3(gF), v3(gF), lbh[:, 0:4].unsqueeze(2).to_broadcast([128, 4, ntok]), ALU.add, [r_gF, r_lb], [r_gF])
            act(fl(gL), fl(gF), AF.Ln, [r_gF], [r_gL])
            S.emit("dve", lambda e: e.tensor_tensor_scan(out=fl(gB), data0=fl(gL), data1=fl(gL), initial=0.0, op0=ALU.add, op1=ALU.min),
                   [r_gL], [r_gB], cost=0.1 + 2 * n4 / 900.0)
            ts(fl(gK), fl(gF), -1.0, 1.0, ALU.mult, ALU.add, [r_gF], [r_gK], eng="pool")
            tt(ch(gL), ch(gB), ch(gB)[:, :, ri:ri + 1].to_broadcast([128, nc4, C]), ALU.subtract, [r_gB, r_gL], [r_gL])
            act(fl(gE), fl(gL), AF.Exp, [r_gL], [r_gE], scale=-1.0)
            tt(ktT[:, :, 0:ntok], v3(gK), v3(gE), ALU.mult, [r_gK, r_gE], r_ktT, eng="pool")
            dsl = lambda i: dec[:, i, :, 0:nch]
            if mode != "scan":
                act(fl(gB), fl(gL), AF.Exp, [r_gL, r_gB], [r_gB])
                cp(dsl(2), hc(gB)[:, :, :, C - 1], [r_gB], [r_dec])
            tt(dsl(0), hc(gE)[:, :, :, 0], hc(gF)[:, :, :, 0], ALU.mult, [r_gE, r_gF], [r_dec])
            if mode == "scan":
                return
            tt(dsl(1), dsl(0), dsl(2), ALU.mult, [r_dec], [r_dec])
            k4 = lambda b: b[:, :, 0:ntok].rearrange("p h (c t) -> p h c t", t=C)
            tt(k4(khT), k4(ktT), dsl(2).unsqueeze(3).to_broadcast([128, 4, nch, C]), ALU.mult, r_ktT + [r_dec], r_khT)
            if mode != "scan":
                sqf = siluq.rearrange("p h t -> p (h t)")
                if sample:
                    tt(Zq[:, :, 0, 0:ntok], v3(sqf), v3(gB), ALU.mult, r_sq + [r_gB], r_Zq)
                else:
                    base = Zq[:, 0, 0, 0:64]
                    zout = AP(base.tensor, base.offset, [list(base.ap[0]), [512 // nsub, 4 * nsub], [192, 2], [1, 64]])
                    tt(zout, fl(sqf).rearrange("p (a c t) -> p a c t", c=2, t=64), fl(gB).rearrange("p (a c t) -> p a c t", c=2, t=64),
                       ALU.mult, r_sq + [r_gB], r_Zq)

        def khat_transpose(j):
            nt = nts[j]
            pt, rt = tbank()
            ksrc, rks = (ktT, r_ktT) if mode == "scan" else (khT, r_khT)
            for hb in range(4):
                S.emit("pe", lambda e, hb=hb, pt=pt: e.transpose(out=pt[:nt, hb * 128:(hb + 1) * 128], in_=ksrc[:, hb, j * 128:j * 128 + nt],
                                                                  identity=identb[:, :]), [rks[hb], r_ident], [rt])
            cp(khtm[:nt, j, :], pt[:nt, 0:512], [rt], [r_khtm[j]], eng="dve" if mode == "scan" else "act")

        def s_update(j, ci):
            p = ci % cps
            rows = slice(p * C, p * C + C)
            pb, rb = gbank()
            for hb in range(4):
                mm(pb[:, hb * 128:(hb + 1) * 128], khtm[rows, j, hb * 128:(hb + 1) * 128], vh[rows, j, hb * 128:(hb + 1) * 128],
                   True, True, [r_khtm[j]] + rl_vh[j], [rb])
            tt(Sst[:], Sst[:], dec[:, 1, :, ci:ci + 1].to_broadcast([128, 4, 128]), ALU.mult, [r_S, r_dec], [r_S])
            tt(Sst[:].rearrange("p h v -> p (h v)"), Sst[:].rearrange("p h v -> p (h v)"), pb[:, :], ALU.add, [r_S, rb], [r_S])

        if mode == "scan":
            wv_f, wv_i = W1v[0], W1v[1]
            if part == "f2":
                for hb in range(4):
                    proj_fm(wv_f, r_wslot[0], hb, lambda pb, rb, hb=hb: act(sg.rearrange("p h t -> p (h t)")[:, hb * ntok:(hb + 1) * ntok], pb[:, 0:ntok], AF.Tanh, [rb], rl_sg[hb], scale=0.5))
                for j in range(nsub):
                    proj_tm(wv_i, r_wslot[1], j, lambda pb, rb, j, nt: cp(vh[:nt, j, :], pb[:nt, :], [rb], rl_vh[j], eng="dve"))
                if kv:
                    kv_proj_only()
            if part == "gates":
                hgrn_gates_all()
            if part == "upd":
                for j in range(nsub):
                    khat_transpose(j)
                pb, rb = gbank()
                for hb in range(4):
                    for j in range(nsub):
                        mm(pb[:, hb * 128:(hb + 1) * 128], khtm[:, j, hb * 128:(hb + 1) * 128], vh[:, j, hb * 128:(hb + 1) * 128],
                           j == 0, j == nsub - 1, [r_khtm[j]] + rl_vh[j], [rb])
                tt(Sst[:], Sst[:], dec[:, 0, :, 0:1].to_broadcast([128, 4, 128]), ALU.mult, [r_S, r_dec], [r_S])
                tt(Sst[:].rearrange("p h v -> p (h v)"), Sst[:].rearrange("p h v -> p (h v)"), pb[:, :], ALU.add, [r_S, rb], [r_S])
            return

        def ev_q(ct):
            return lambda pb, rb: act(qT[:, ct, 0:ntok], pb[:, 0:ntok], AF.Copy, [rb], [r_qT], scale=0.125)

        def ev_k(ct):
            def f(pb, rb):
                for j in range(nsub):
                    cp(kTr[:, ct, slot_of(j), 0:nts[j]], pb[:, j * 128:j * 128 + nts[j]], [rb], [r_kT[slot_of(j)]], eng="act")
                if final_kv is not None:
                    cp(kst[:, ct, 0:ntok], pb[:, 0:ntok], [rb], rl_kst)
            return f

        def ev_v(pb, rb, j, nt):
            s_ = slot_of(j)
            cp(Vr[:nt, s_, :, 0:64], pb[:nt, :].rearrange("p (h d) -> p h d", h=8), [rb], [r_V[s_]], eng="act")
            if mode == "main" or sample:
                cp(Vr[:nt, s_, :, 64:65], onesT[:nt, 0:8].unsqueeze(2), [r_ones], [r_V[s_]], eng="pool")
            else:
                cp(Vr[:nt, s_, :, 64:65], hvt[:nt, 0:1].unsqueeze(2).to_broadcast([nt, 8, 1]), [r_hv], [r_V[s_]], eng="pool")
            if final_kv is not None:
                cp(vst[:nt, j, :], pb[:nt, :], [rb], rl_vst[j])
                dma("act", final_kv[1].ap()[final_kv[2] + j * 128:final_kv[2] + j * 128 + nt, :], vst[:nt, j, :], rl_vst[j], ())

        wv, rw, sl = w_next("in3")
        for hb in range(4):
            proj_fm(wv, rw, hb, lambda pb, rb, hb=hb: act(siluq.rearrange("p h t -> p (h t)")[:, hb * ntok:(hb + 1) * ntok], pb[:, 0:ntok], AF.Silu, [rb], [r_sq[hb]]))
        w_done(sl)
        wv, rw, sl = w_next("in4")
        for hb in range(4):
            proj_fm(wv, rw, hb, lambda pb, rb, hb=hb: act(sg.rearrange("p h t -> p (h t)")[:, hb * ntok:(hb + 1) * ntok], pb[:, 0:ntok], AF.Tanh, [rb], rl_sg[hb], scale=0.5))
        w_done(sl)
        wv, rw, sl = w_next("in5")
        for j in range(nsub):
            proj_tm(wv, rw, j, lambda pb, rb, j, nt: cp(vh[:nt, j, :], pb[:nt, :], [rb], rl_vh[j], eng="act"))
        w_done(sl)
        wv, rw, sl = w_next("in6")
        for j in range(nsub):
            proj_tm(wv, rw, j, lambda pb, rb, j, nt: act(sgb[:nt, j, :], pb[:nt, :], AF.Silu, [rb], [r_sgb[j]]))
        w_done(sl)
        wv, rw, sl = w_next("in0")
        for ct in range(4):
            proj_fm(wv, rw, ct, ev_q(ct))
        w_done(sl)
        wv, rw, sl = w_next("in1")
        for ct in range(4):
            proj_fm(wv, rw, ct, ev_k(ct))
        w_done(sl)
        if final_kv is not None:
            for ct in range(4):
                dma("act", final_kv[0].ap()[ct, :, final_kv[2]:final_kv[2] + ntok], kst[:, ct, 0:ntok], rl_kst, ())
        wv, rw, sl = w_next("in2")
        for j in range(nsub):
            proj_tm(wv, rw, j, ev_v)
        w_done(sl)

        if deferred:
            deferred.pop(0)()
        if after_norm is not None:
            after_norm()
            after_norm = None

        att_steps, z_steps, h_steps = [], [], []

        zst = {}

        def z_step(zi):
            gi, ct = divmod(zi, 4)
            if ct == 0:
                zst["w"] = w_next("in%d" % (7 + gi))
            wv_, rw_, sl_ = zst["w"]
            proj_fm(wv_, rw_, ct, lambda pb, rb: act(zs[:, zi, 0:ntok], pb[:, 0:ntok], AF.Tanh, [rb], [r_zs[zi]], scale=0.5))
            if ct == 3:
                w_done(sl_)
        for zi in range(16):
            z_steps.append(lambda zi=zi: z_step(zi))

        def make_att(j):
            nq = nts[j]
            if sample:
                ktiles = [(0, 128), (1, 128), (2, 128), (3, 128), (4, 16)]
            else:
                ktiles = [((gt0 + j - 4 + kt) % 6, 128) for kt in range(5)]
            pend = []

            def pv(h, pslot):
                for kt, (s_, nk) in enumerate(ktiles):
                    mm(ps_o[:nq, (h % 4) * 65:(h % 4) * 65 + 65], Pb[pslot][:nk, kt, 0:nq], Vr[:nk, s_, h, :], kt == 0, kt == 4,
                       [r_Pb[pslot], r_V[s_]], [r_o])

            def normalize(half):
                o3 = ps_o[:nq, 0:260].rearrange("p (h d) -> p h d", h=4)
                ts(rec[:nq, half * 4:half * 4 + 4].unsqueeze(2), o3[:, :, 64:65], 1e-30, None, ALU.max, None, [r_o], [r_rec])
                recip(rec[:nq, half * 4:half * 4 + 4], rec[:nq, half * 4:half * 4 + 4], [r_rec], [r_rec])
                tt(oa[:nq, j, half * 256:half * 256 + 256].rearrange("p (h d) -> p h d", h=4), o3[:, :, 0:64],
                   rec[:nq, half * 4:half * 4 + 4].unsqueeze(2).to_broadcast([nq, 4, 64]), ALU.mult, [r_o, r_rec], [r_oa[j]])

            def head(h):
                hp, r0 = h // 2, (h % 2) * 64
                ai = cnt["a"] % 2
                cnt["a"] += 1
                offA = ai * 512
                offB = 1024 + ai * 128
                for kt in (4, 0, 1, 2, 3):
                    s_, nk = ktiles[kt]
                    o_ = offB if kt == 4 else offA + kt * 128
                    mm(ps_att[:nk, o_:o_ + nq], kTr[r0:r0 + 64, hp, s_, 0:nk], qT[r0:r0 + 64, hp, j * 128:j * 128 + nq],
                       True, True, [r_kT[s_], r_qT], [r_attB if kt == 4 else r_att[ai]])
                pi = cnt["p"] % 3
                cnt["p"] += 1
                sattA = ps_att[:, offA:offA + 512].rearrange("p (k q) -> p k q", k=4)
                sattB = ps_att[:, offB:offB + 128]
                if sample:
                    act(Pb[pi][:16, 4, 0:nq], sattB[:16, 0:nq], AF.Exp, [r_attB], [r_Pb[pi]])
                    act(Pb[pi][:, 0:4, 0:nq], sattA[:, :, 0:nq], AF.Exp, [r_att[ai]], [r_Pb[pi]])
                    tt(Pb[pi][:, 0:4, 0:nq], Pb[pi][:, 0:4, 0:nq], ET[:, 0:4, h, 0:nq], ALU.mult, [r_Pb[pi], r_ET], [r_Pb[pi]], eng="pool")
                    tt(Pb[pi][:16, 4, 0:nq], Pb[pi][:16, 4, 0:nq], ET[:16, 4, h, 0:nq], ALU.mult, [r_Pb[pi], r_ET], [r_Pb[pi]], eng="pool")
                else:
                    act(Pb[pi][:, 4, :], sattB, AF.Exp, [r_attB], [r_Pb[pi]])
                    act(Pb[pi][:, 0:4, :], sattA, AF.Exp, [r_att[ai]], [r_Pb[pi]])
                    tt(Pb[pi][:, :, :], Pb[pi][:, :, :], ET[:, :, h, :], ALU.mult, [r_Pb[pi], r_ET], [r_Pb[pi]], eng="pool")
                if pend:
                    ph, ppi = pend.pop(0)
                    pv(ph, ppi)
                    if ph == 3:
                        normalize(0)
                pend.append((h, pi))

            def tail():
                ph, ppi = pend.pop(0)
                pv(ph, ppi)
                normalize(1)
                pt, rt = tbank()
                for kc in range(4):
                    S.emit("pe", lambda e, kc=kc, pt=pt: e.transpose(out=pt[:, kc * 128:kc * 128 + nq], in_=oa[:nq, j, kc * 128:(kc + 1) * 128],
                                                                      identity=identb[:nq, :nq]), [r_oa[j], r_ident], [rt])
                cp(oaT[:, :, j * 128:j * 128 + nq], pt[:, 0:512].rearrange("p (k t) -> p k t", k=4)[:, :, 0:nq], [rt], [r_oaT[j]], eng="act")
            for h in range(8):
                att_steps.append(lambda h=h: head(h))
            att_steps.append(tail)
        for j in range(nsub):
            make_att(j)

        def make_h(j):
            nt = nts[j]

            def h_at():
                pb, rb = gbank()
                for hb in range(4):
                    if sample:
                        qrhs = Zq[:, hb, 0, 0:nt]
                    else:
                        base = Zq[:, hb, 2 * j, 0:64]
                        qrhs = AP(base.tensor, base.offset, [list(base.ap[0]), [192, 2], [1, 64]])
                    mm(pb[:nt, hb * 128:hb * 128 + nt], ktT[:, hb, j * 128:j * 128 + nt], qrhs, True, True, [r_ktT[hb], r_Zq[hb]], [rb])
                tt(AT[:nt, :, 0:nt], pb[:nt, :].rearrange("p (h t) -> p h t", h=4)[:, :, 0:nt],
                   maskb[:nt, 0:nt].unsqueeze(1).to_broadcast([nt, 4, nt]), ALU.mult, [rb, r_mask], [r_AT])

            def h_chunk(p):
                ci = j * cps + p
                tt(Sp[:, p], Sst[:], dec[:, 0, :, ci:ci + 1].to_broadcast([128, 4, 128]), ALU.mult, [r_S, r_dec], [r_Sp[p]])
                s_update(j, ci)

            def h_out():
                ob_, rob = gbank()
                for hb in range(4):
                    for p in range(cps):
                        zl = Zq[:, hb, 0, 0:nt] if sample else Zq[:, hb, 2 * j + p, :]
                        mm(ob_[:nt, hb * 128:(hb + 1) * 128], zl, Sp[:, p, hb, :], p == 0, False, [r_Zq[hb], r_Sp[p]], [rob])
                    mm(ob_[:nt, hb * 128:(hb + 1) * 128], AT[:nt, hb, 0:nt], vh[:nt, j, hb * 128:(hb + 1) * 128], False, True, [r_AT] + rl_vh[j], [rob])
                act(sqb[:nt, :], ob_[:nt, :], AF.Square, [rob], [r_sqb])
                S.emit("dve", lambda e: e.tensor_reduce(out=ssq[:nt, 0:4], in_=sqb[:nt, :].rearrange("p (h v) -> p h v", h=4), axis=AX.X, op=ALU.add),
                       [r_sqb], [r_ssq])
                ts(ssq[:nt, :], ssq[:nt, :], 1.0 / 128, EPS, ALU.mult, ALU.add, [r_ssq], [r_ssq], eng="pool")
                tt(ssq[:nt, :], ssq[:nt, :], nhalf[:nt, 0:4], ALU.pow, [r_ssq, r_nh], [r_ssq], eng="pool")
                tt(t1[:nt, :].rearrange("p (h v) -> p h v", h=4), ob_[:nt, :].rearrange("p (h v) -> p h v", h=4),
                   ssq[:nt, 0:4].unsqueeze(2).to_broadcast([nt, 4, 128]), ALU.mult, [rob, r_ssq], [r_t1])
                tt(gG[:nt, :], sgb[:nt, j, :], gnrep[:nt].rearrange("p h v -> p (h v)"), ALU.mult, [r_sgb[j], r_gn], [r_gG], eng="pool")
                tt(ob[:nt, :], t1[:nt, :], gG[:nt, :], ALU.mult, [r_t1, r_gG], [r_ob])

            def h_tr():
                pt, rt = tbank()
                for hb in range(4):
                    S.emit("pe", lambda e, hb=hb, pt=pt: e.transpose(out=pt[:, hb * 128:hb * 128 + nt], in_=ob[:nt, hb * 128:(hb + 1) * 128],
                                                                      identity=identb[:nt, :nt]), [r_ob, r_ident], [rt])
                cp(obT[:, :, j * 128:j * 128 + nt], pt[:, 0:512].rearrange("p (k t) -> p k t", k=4)[:, :, 0:nt], [rt], [r_obT[j]], eng="act")
            h_steps.append(lambda: khat_transpose(j))
            h_steps.append(h_at)
            for p in range(cps):
                h_steps.append(lambda p=p: h_chunk(p))
            h_steps.append(h_out)
            h_steps.append(h_tr)
        for j in range(nsub):
            make_h(j)

        def run_steps(steps, pool, tr, pre=None):
            def f():
                gstate["pool"] = pool
                gstate["tr"] = tr
                if pre is not None:
                    pre()
                for st_ in steps:
                    st_()
            return f
        att_ops = S.capture(run_steps(att_steps, [], (0,)))
        z_ops = S.capture(run_steps(z_steps, gpool_full[0:1], (0,)))
        h_ops = S.capture(run_steps(h_steps, gpool_full[1:2], (1,), pre=hgrn_gates_all))
        S.emit_merged(att_ops, z_ops, h_ops)
        gstate["tr"] = (0, 1)

        gstate["pool"] = gpool_front + gpool_back
        wva, rwa, sla = w_next("a")
        wstate["consumed"] += 1
        wvb, rwb, slb = w_next("b")
        wstate["consumed"] -= 1
        for ct in range(8):
            pb, rb = gbank()
            for kc in range(4):
                mm(pb[:, 0:ntok], wva[:, kc, ct * 128:(ct + 1) * 128], oaT[:, kc, 0:ntok], kc == 0, kc == 3, [rwa] + r_oaT[:nsub], [rb])
            for kc in range(4):
                mm(pb[:, 256:256 + ntok], wvb[:, kc, ct * 128:(ct + 1) * 128], obT[:, kc, 0:ntok], kc == 0, kc == 3, [rwb] + r_obT[:nsub], [rb])
            stt(m1[:, 0:ntok], zs[:, ct, 0:ntok], 1.0, pb[:, 0:ntok], ALU.add, ALU.mult, [rb, r_zs[ct]], [r_m1])
            stt(m2[:, 0:ntok], zs[:, 8 + ct, 0:ntok], 1.0, pb[:, 256:256 + ntok], ALU.add, ALU.mult, [rb, r_zs[8 + ct]], [r_m2])
            tt(mT[:, ct, 0:ntok], m1[:, 0:ntok], m2[:, 0:ntok], ALU.add, [r_m1, r_m2], [r_mT[ct]], eng="pool")
        w_done(sla)
        w_done(slb)

        for n in range(2):
            wv, rw, sl = w_next("o%d" % n)
            for j in range(nsub):
                nt = nts[j]
                pb, rb = gbank()
                for kc in range(8):
                    mm(pb[:nt, :], mT[:, kc, j * 128:j * 128 + nt], wv[:, kc, :], kc == 0, kc == 7, [rw, r_mT[kc]], [rb])
                stt(Xb[:nt, j, n * 512:(n + 1) * 512], pb[:nt, :], 0.5, Xb[:nt, j, n * 512:(n + 1) * 512], ALU.mult, ALU.add, [rX[j], rb], [rX[j]])
            w_done(sl)
        norm_T(Xb, rX, gffn, r_gffn, 2, 2, ntok)
        hT = hTs[2]
        r_hT = r_hTs[2]
        rhT_all = r_hT[:nsub]

        ffn_pend = []
        for gi in range(6):
            ntile = 4 if gi < 5 else 2
            if gi == 2 and prenorm is not None:
                prenorm()
            wvg, rwg, slg = w_next("g%d" % gi)
            if mode != "pre":
                wstate["consumed"] += 1
                wvu, rwu, slu = w_next("u%d" % gi)
                wstate["consumed"] -= 1
            for ct in range(ntile):
                ft = gi * 4 + ct
                pb, rb = gbank()
                if mode == "pre":
                    for kc in range(8):
                        mm(pb[:, 0:2], wvg[:, kc, ct * 128:(ct + 1) * 128], hT[:, kc, ntok - 2:ntok], kc == 0, kc == 7, [rwg] + rhT_all, [rb])
                    cp(cprev[:, ft, :], pb[:, 0:2], [rb], [r_cprev])
                    continue
                for kc in range(8):
                    mm(pb[:, 0:ntok], wvg[:, kc, ct * 128:(ct + 1) * 128], hT[:, kc, 0:ntok], kc == 0, kc == 7, [rwg] + rhT_all, [rb])
                for kc in range(8):
                    mm(pb[:, 256:256 + ntok], wvu[:, kc, ct * 128:(ct + 1) * 128], hT[:, kc, 0:ntok], kc == 0, kc == 7, [rwu] + rhT_all, [rb])
                bi = ft % 2
                a_ = aT[bi]
                cp(a_[:, 0:2], cprev[:, ft, :], [r_cprev], [r_aT[bi]], eng="pool")
                act(a_[:, 2:2 + ntok], pb[:, 0:ntok], AF.Copy, [rb], [r_aT[bi]])
                cp(cprev[:, ft, :], a_[:, ntok:ntok + 2], [r_aT[bi]], [r_cprev], eng="pool")
                act(c1[bi][:, 0:ntok], pb[:, 0:ntok], AF.Identity, [rb, r_cw, r_cb], [r_c1[bi]], scale=cw[:, ft, 2:3], bias=cb[:, ft:ft + 1])
                stt(c2[bi][:, 0:ntok], a_[:, 1:1 + ntok], cw[:, ft, 1:2], c1[bi][:, 0:ntok], ALU.mult, ALU.add, [r_aT[bi], r_c1[bi], r_cw], [r_c2[bi]])
                stt(c1[bi][:, 0:ntok], a_[:, 0:ntok], cw[:, ft, 0:1], c2[bi][:, 0:ntok], ALU.mult, ALU.add, [r_aT[bi], r_c2[bi], r_cw], [r_c1[bi]])
                def fin(bi=bi, ft=ft, pb=pb, rb=rb):
                    act(c2[bi][:, 0:ntok], c1[bi][:, 0:ntok], AF.Gelu_apprx_tanh, [r_c1[bi]], [r_c2[bi]])
                    tt(gT[:, ft, 0:ntok], c2[bi][:, 0:ntok], pb[:, 256:256 + ntok], ALU.mult, [r_c2[bi], rb], [r_gT[ft]])
                if ffn_pend:
                    ffn_pend.pop(0)()
                ffn_pend.append(fin)
            w_done(slg)
            if mode != "pre":
                w_done(slu)
        if mode == "pre":
            return
        while ffn_pend:
            ffn_pend.pop(0)()

        for n in range(2):
            banks = [gbank() for _ in range(nsub)]
            for gk in range(3):
                wv, rw, sl = w_next("d%d_%d" % (n, gk))
                nk = 8 if gk < 2 else 6
                for kl in range(nk):
                    kc = gk * 8 + kl
                    for j in range(nsub):
                        nt = nts[j]
                        mm(banks[j][0][:nt, :], gT[:, kc, j * 128:j * 128 + nt], wv[:, kl, :], kc == 0, kc == NFT - 1, [rw, r_gT[kc]], [banks[j][1]])
                w_done(sl)
            for j in range(nsub):
                nt = nts[j]
                tt(Xb[:nt, j, n * 512:(n + 1) * 512], Xb[:nt, j, n * 512:(n + 1) * 512], banks[j][0][:nt, :], ALU.add, [rX[j], banks[j][1]], [rX[j]])
        def final_norm():
            for j in range(nsub):
                nt = nts[j]
                act(XN[:nt, j, :], Xb[:nt, j, :], AF.Square, [rX[j]], [r_XN[j], r_ss], accum_out=ss[:nt, 4 + j:5 + j])
                ts(ss[:nt, 4 + j:5 + j], ss[:nt, 4 + j:5 + j], 1.0 / D, EPS, ALU.mult, ALU.add, [r_ss], [r_ss], eng="pool")
                tt(ss[:nt, 4 + j:5 + j], ss[:nt, 4 + j:5 + j], nhalf[:nt, 0:1], ALU.pow, [r_ss, r_nh], [r_ss], eng="pool")
                stt(Xb[:nt, j, :], Xb[:nt, j, :], ss[:nt, 4 + j:5 + j], gfin[:nt, :], ALU.mult, ALU.mult, [rX[j], r_ss, r_gfin], [rX[j]])
                dma("act", out_row[j * 128:j * 128 + nt, :], Xb[:nt, j, :], [rX[j]], ())
        if deferred is not None and mode == "main":
            deferred.append(final_norm)
        else:
            final_norm()

    cur = {}

    def kv_proj_only():
        ntok, gt0 = cur["ntok"], cur["gt0"]
        for ct in range(4):
            pb, rb = gbank()
            for kc in range(8):
                mm(pb[:, 0:ntok], W1v[2][:, kc, ct * 128:(ct + 1) * 128], cur["hT"][:, kc, 0:ntok], kc == 0, kc == 7, [r_wslot[2]] + cur["r_hT"], [rb])
            for j in range(2):
                s_ = (gt0 + j) % 6
                cp(kTr[:, ct, s_, :], pb[:, j * 128:(j + 1) * 128], [rb], [r_kT[s_]], eng="act")
        for j in range(2):
            s_ = (gt0 + j) % 6
            pb, rb = gbank()
            for kc in range(8):
                mm(pb[:, :], cur["hT"][:, kc, j * 128:(j + 1) * 128], W1v[3][:, kc, :], kc == 0, kc == 7, [r_wslot[3], cur["r_hT"][j]], [rb])
            cp(Vr[:, s_, :, 0:64], pb[:, :].rearrange("p (h d) -> p h d", h=8), [rb], [r_V[s_]], eng="act")
            cp(Vr[:, s_, :, 64:65], hvt[:, 0:1].unsqueeze(2).to_broadcast([128, 8, 1]), [r_hv], [r_V[s_]], eng="pool")


    xa = xin.ap()
    nscan = n_scan
    nmain = n_main
    plan = []
    for m in range(nscan):
        plan.append(("scan", xa[m * T:(m + 1) * T, :], T, dict(gt0=-5 + 2 * (m - (nscan - 2)) + 6, kv=m >= nscan - 2)))
    if do_pre:
        plan.append(("pre", xa[NPRE:NPRE + PRE_T, :], PRE_T, dict(gt0=5)))
    for m in range(nmain):
        fk = (okT_d, ov_d, (m - (nmain - 2)) * T) if (m >= nmain - 2 and not NOFK) else None
        r0 = NPRE + PRE_T + m * T
        plan.append(("main", xa[r0:r0 + T, :], T, dict(gt0=2 * m + 6, out_row=y_d.ap()[m * T:(m + 1) * T, :], final_kv=fk)))
    if do_sample:
        plan.append(("sample", xs_d.ap(), 16, dict(gt0=0, out_row=ys_d.ap(), final_kv=(okTs_d, ovs_d, 0))))
    if plan:
        load_x(plan[0][1], plan[0][2], 0)
    if not do_conv:
        conv_jobs.clear()
    dfr = []
    for i, (mode, xsrc, ntok, kw) in enumerate(plan):
        xb = i % 2
        nxt = None
        pren = None
        if i + 1 < len(plan):
            nxt = (lambda p=plan[i + 1], b=(i + 1) % 2: load_x(p[1], p[2], b))
            if mode == "main":
                pren = (lambda p=plan[i + 1], b=(i + 1) % 2: norm_T(X[b], r_X[b], gmix, r_gmix, 0, b, p[2]))
        normed = i > 0 and plan[i - 1][0] == "main"
        if mode == "scan":
            if i > 0:
                continue

            def stage(k, part):
                md, xs_, nt_, kw_ = plan[k]
                nx = None
                if part == "f1":
                    emit_conv(2)
                    nx = (lambda p=plan[k + 1], b=(k + 1) % 2: load_x(p[1], p[2], b)) if k + 1 < len(plan) else None
                if part == "f2":
                    cur["ntok"] = nt_
                    cur["gt0"] = kw_["gt0"]
                macro_tile("scan", nt_, k % 2, hsel=k % 2, after_norm=nx, part=part, bsel=k % 2, vsel=k % 3, ksel=k % 2, **kw_)
            for w in range(-3, nscan):
                lists = []
                for off, part in ((0, "upd"), (1, "gates"), (2, "f2"), (3, "f1")):
                    k = w + off
                    if 0 <= k < nscan:
                        lists.append(S.capture(lambda k=k, part=part: stage(k, part)))
                S.emit_merged(*lists)
            continue
        if i == nscan:
            emit_conv(len(conv_jobs))
            for s_ in range(NSLOT):
                w_issue(s_)
        if mode == "sample":
            dma("act", oS_d.ap().rearrange("h k v -> k h v"), Sst[:], [r_S], ())
            dma("act", oconv_d.ap(), cprev[:], [r_cprev], ())
            dma("pool", kTr[:, :, 0:4, :], ckT_d.ap().rearrange("p c (t k) -> p c t k", t=4), (), r_kT[0:4])
            for t_ in range(4):
                dma("pool", Vr[:, t_, :, 0:64], cv_d.ap()[:, t_ * 128:(t_ + 1) * 128, :].rearrange("h p d -> p h d"), (), [r_V[t_]])
                cp(Vr[:, t_, :, 64:65], onesT[:, 0:8].unsqueeze(2), [r_ones], [r_V[t_]], eng="pool")
            dma("sp", Sst[:], s0_d.ap().rearrange("h k v -> k h v"), (), [r_S])
            dma("sp", cprev[:], cprev_d.ap(), (), [r_cprev])
        macro_tile(mode, ntok, xb, hsel=xb, normed=normed, after_norm=nxt, prenorm=pren, deferred=dfr, **kw)
    while dfr:
        dfr.pop(0)()
    emit_conv(len(conv_jobs))
    if not do_sample:
        dma("act", oS_d.ap().rearrange("h k v -> k h v"), Sst[:], [r_S], ())
        dma("act", oconv_d.ap(), cprev[:], [r_cprev], ())
    else:
        dma("act", oSs_d.ap().rearrange("h k v -> k h v"), Sst[:], [r_S], ())
        dma("act", oconvs_d.ap(), cprev[:], [r_cprev], ())
    for nm, getter in dumps:
        ap_, res_ = getter(locals())
        d_ = nc.dram_tensor("dbg_" + nm, list(ap_.shape), ap_.dtype if hasattr(ap_, "dtype") else F32, kind="ExternalOutput")
        dma("pool", d_.ap(), ap_, res_, ())

    S.finalize(st)
    with nc.Block() as block:
        S.run(block)
    st.close()
    return nc


_NC_CACHE = {}


def kernel(x_prompt, x_sample, cache_attn_k, cache_attn_v, state_hgrn, state_ffn_conv,
           norm_mix_g, w_in, rel_bias, hgrn_lb_logits, hgrn_norm_g, w_branch_a, w_branch_b, w_out,
           norm_ffn_g, w_ffn_gate, w_ffn_up, ffn_conv_w, ffn_conv_b, w_ffn_down, norm_final_g):
    f32 = np.float32
    A = lambda a: np.ascontiguousarray(np.asarray(a, dtype=f32))
    x_prompt = A(x_prompt)
    if "nc" not in _NC_CACHE:
        _NC_CACHE["nc"] = build_program()
    nc = _NC_CACHE["nc"]
    s_idx = np.arange(128)
    mask = ((s_idx[:, None] // 64 == s_idx[None, :] // 64) & (s_idx[:, None] <= s_idx[None, :])).astype(f32)
    shared = {
        "w_in": A(w_in[0]), "w_a": A(w_branch_a[0]), "w_b": A(w_branch_b[0]), "w_o": A(w_out[0]),
        "w_g": A(w_ffn_gate[0]), "w_u": A(w_ffn_up[0]), "w_d": A(w_ffn_down[0]),
        "gmix": A(np.asarray(norm_mix_g[0]).reshape(8, 128).T), "gffn": A(np.asarray(norm_ffn_g[0]).reshape(8, 128).T),
        "gfin": A(norm_final_g), "rel": A(rel_bias[0]),
        "lbl": A(np.asarray(hgrn_lb_logits).reshape(2, 4, 128).transpose(2, 0, 1)),
        "gn": A(hgrn_norm_g[0]),
        "cw": A(np.asarray(ffn_conv_w[0]).reshape(3, NFT, 128).transpose(2, 1, 0)),
        "cb": A(np.asarray(ffn_conv_b[0]).reshape(NFT, 128).T),
        "ident": np.eye(128, dtype=f32), "mask": mask, "jmat": np.ascontiguousarray(np.eye(128, dtype=f32)[::-1]),
    }
    in_maps = []
    for c in range(8):
        b, j = divmod(c, 4)
        s = j * SEG
        lo = s - PRE_T - NPRE
        xin = np.zeros((NTOK_IN, D), f32)
        a0 = max(lo, 0)
        xin[a0 - lo:] = x_prompt[b, a0:s + SEG]
        m = dict(shared)
        m["xin"] = xin
        m["hv"] = np.full((128, 1), 1.0 if j > 0 else 0.0, f32)
        m["xs"] = A(x_sample[c])
        m["ckT"] = A(np.asarray(cache_attn_k[0, c]).transpose(0, 2, 1).reshape(4, 128, 512).transpose(1, 0, 2))
        m["cv"] = A(cache_attn_v[0, c])
        m["s0"] = A(state_hgrn[0, c])
        m["cprev"] = A(np.asarray(state_ffn_conv[0, c]).reshape(2, NFT, 128).transpose(2, 1, 0))
        in_maps.append(m)
    res = run_bass_kernel_spmd(nc, in_maps, core_ids=list(range(8)))
    R_ = res.results
    B = 2
    y_prompt = np.stack([np.concatenate([R_[b * 4 + j]["y"] for j in range(4)], axis=0) for b in range(B)])
    y_sample = np.stack([R_[c]["ys"] for c in range(8)])

    def kT_to_rows(a):
        n = a.shape[-1]
        return a.reshape(8, 64, n).transpose(0, 2, 1)

    def v_to_rows(a):
        n = a.shape[0]
        return a.reshape(n, 8, 64).transpose(1, 0, 2)

    def conv_rows(a):
        return a.transpose(2, 1, 0).reshape(2, DFF)

    last = [3, 7]
    new_k_p = np.stack([kT_to_rows(R_[c]["okT"]) for c in last])[None]
    new_v_p = np.stack([v_to_rows(R_[c]["ov"]) for c in last])[None]
    hg_p = np.stack([R_[c]["oS"] for c in last])[None]
    cv_p = np.stack([conv_rows(R_[c]["oconv"]) for c in last])[None]
    new_k_s = np.stack([kT_to_rows(R_[c]["okTs"]) for c in range(8)])[None]
    new_v_s = np.stack([v_to_rows(R_[c]["ovs"]) for c in range(8)])[None]
    hg_s = np.stack([R_[c]["oSs"] for c in range(8)])[None]
    cv_s = np.stack([conv_rows(R_[c]["oconvs"]) for c in range(8)])[None]
    outs = (y_prompt, y_sample, new_k_p, new_v_p, hg_p, cv_p, new_k_s, new_v_s, hg_s, cv_s)
    return tuple(np.ascontiguousarray(o, dtype=f32) for o in outs)
```

```python
import numpy as np
from contextlib import ExitStack
import concourse.bass as bass
import concourse.mybir as mybir
from concourse.bass import AP
from concourse.bass_utils import run_bass_kernel_spmd

F32 = mybir.dt.float32
BF16 = mybir.dt.bfloat16
AF = mybir.ActivationFunctionType
ALU = mybir.AluOpType
AX = mybir.AxisListType

D = 1024
DFF = 2816
NFT = 22
SEG = 4096
NPRE = 12288
PRE_T = 128
NTOK_IN = NPRE + PRE_T + SEG
T = 256
EPS = 1e-6
NSLOT = 6
STOP = 99
STOPMODE = 'main'
NOFK = False
SLOT_E = 4096


class Res:
    __slots__ = ("name", "w", "r", "excl")

    def __init__(self, name, excl=False):
        self.name = name
        self.w = None
        self.r = {}
        self.excl = excl


class Op:
    __slots__ = ("eng", "fn", "deps", "is_dma", "needs_inc", "sem", "val", "idx")


class Sched:
    ENGS = ("pe", "act", "dve", "pool", "sp")

    def __init__(self, nc, n_dma_sems=8):
        self.nc = nc
        self.ops = []
        self.n_dma_sems = n_dma_sems
        self.cap = None

    def capture(self, fn):
        assert self.cap is None
        self.cap = []
        try:
            fn()
            return self.cap
        finally:
            self.cap = None

    DUR = {"pe": 0.15, "act": 0.75, "dve": 0.85, "pool": 1.1, "sp": 0.3}

    def emit_merged(self, *lists):
        eng_free = {}
        ready = {}
        rdone = {}
        LAT = 0.25

        def start_of(op):
            eng, fn, reads, writes, dma, cost = op
            t = eng_free.get(eng, 0.0)
            for r in reads:
                t = max(t, ready.get(id(r), 0.0) + LAT)
            for w in writes:
                t = max(t, ready.get(id(w), 0.0) + LAT, rdone.get(id(w), 0.0) + LAT)
            return t

        def commit(op, t):
            eng, fn, reads, writes, dma, cost = op
            d = 2.5 if dma else (cost if cost is not None else self.DUR[eng])
            eng_free[eng] = t + (0.1 if dma else d)
            for r in reads:
                rdone[id(r)] = max(rdone.get(id(r), 0.0), t + d)
            for w in writes:
                ready[id(w)] = t + d
            self.emit(eng, fn, reads, writes, dma)

        pos = [0] * len(lists)
        while True:
            best, bt = -1, None
            for k, l in enumerate(lists):
                if pos[k] < len(l):
                    t = start_of(l[pos[k]])
                    if bt is None or t < bt:
                        best, bt = k, t
            if best < 0:
                break
            commit(lists[best][pos[best]], bt)
            pos[best] += 1

    def emit(self, eng, fn, reads=(), writes=(), dma=False, cost=None):
        if self.cap is not None:
            self.cap.append((eng, fn, tuple(reads), tuple(writes), dma, cost))
            return None
        op = Op()
        op.eng = eng
        op.fn = fn
        op.is_dma = dma
        op.needs_inc = dma
        op.sem = None
        op.val = 0
        op.idx = len(self.ops)
        deps = {}
        xr = [r for r in reads if r.excl]
        if xr:
            reads = [r for r in reads if not r.excl]
            writes = list(writes) + [r for r in xr if r not in writes]
        for r in reads:
            if r.w is not None:
                deps[r.w.idx] = r.w
        for w in writes:
            if w.w is not None:
                deps[w.w.idx] = w.w
            for o in w.r.values():
                deps[o.idx] = o
        op.deps = list(deps.values())
        for r in reads:
            key = (eng, op.idx) if dma else (eng, -1)
            r.r[key] = op
        for w in writes:
            w.w = op
            w.r = {}
        self.ops.append(op)
        return op

    def finalize(self, stack):
        nc = self.nc
        for op in self.ops:
            for d in op.deps:
                if d.is_dma or d.eng != op.eng or op.eng != "pe" or op.is_dma:
                    d.needs_inc = True
        csem = {e: stack.enter_context(nc.semaphore("cs_" + e)) for e in ("pe", "act", "dve", "pool")}
        dsem = {e: [stack.enter_context(nc.semaphore("ds_%s%d" % (e, i))) for i in range(self.n_dma_sems)]
                for e in ("sp", "pool", "act")}
        ccount = {e: 0 for e in csem}
        dstate = {e: [None] * self.n_dma_sems for e in dsem}
        duse = {e: [0] * self.n_dma_sems for e in dsem}
        drr = {e: 0 for e in dsem}
        waited = {e: {} for e in self.ENGS}
        streams = {e: [] for e in self.ENGS}
        for op in self.ops:
            waits = []
            e = op.eng
            extra = []
            if op.is_dma:
                k = drr[e] % self.n_dma_sems
                drr[e] += 1
                prev = dstate[e][k]
                if prev is not None:
                    extra.append(prev)
                duse[e][k] += 1
                op.sem = dsem[e][k]
                op.val = 16 * duse[e][k]
                dstate[e][k] = op
            elif op.needs_inc:
                ccount[e] += 1
                op.sem = csem[e]
                op.val = ccount[e]
            for d in op.deps + extra:
                if (not d.is_dma) and d.eng == e and e == "pe" and not op.is_dma:
                    continue
                key = id(d.sem)
                if waited[e].get(key, 0) >= d.val:
                    continue
                waited[e][key] = d.val
                waits.append((d.sem, d.val))
            streams[e].append((waits, op))
        fin = []
        for e in dsem:
            for k in range(self.n_dma_sems):
                if duse[e][k] and waited["sp"].get(id(dsem[e][k]), 0) < 16 * duse[e][k]:
                    fin.append((dsem[e][k], 16 * duse[e][k]))
        self.streams = streams
        self.fin = fin

    def run(self, block):
        streams = self.streams
        fin = self.fin

        def body(name):
            def f(eng):
                for waits, op in streams[name]:
                    for s, v in waits:
                        eng.wait_ge(s, v)
                    inst = op.fn(eng)
                    if op.sem is not None:
                        inst.then_inc(op.sem, 16 if op.is_dma else 1)
                if name == "sp":
                    for s, v in fin:
                        eng.wait_ge(s, v)
            return f
        block.tensor(body("pe"))
        block.scalar(body("act"))
        block.vector(body("dve"))
        block.gpsimd(body("pool"))
        block.sync(body("sp"))


def build_program(n_scan=NPRE // T, n_main=SEG // T, do_pre=True, do_sample=True, dumps=(), do_conv=True, init_stop=99):
    NPRE = n_scan * T
    NTOK_IN = NPRE + PRE_T + n_main * T
    nc = bass.Bass("TRN2", target_bir_lowering=False)
    st = ExitStack()
    S = Sched(nc)

    def din(name, shape):
        return nc.dram_tensor(name, list(shape), F32, kind="ExternalInput")

    def dout(name, shape):
        return nc.dram_tensor(name, list(shape), F32, kind="ExternalOutput")

    xin = din("xin", [NTOK_IN, D])
    hv_d = din("hv", [128, 1])
    xs_d = din("xs", [16, D])
    ckT_d = din("ckT", [128, 4, 512])
    cv_d = din("cv", [8, 512, 64])
    s0_d = din("s0", [4, 128, 128])
    cprev_d = din("cprev", [128, NFT, 2])
    w_in_d = din("w_in", [D, 5632])
    w_a_d = din("w_a", [512, D])
    w_b_d = din("w_b", [512, D])
    w_o_d = din("w_o", [D, D])
    w_g_d = din("w_g", [D, DFF])
    w_u_d = din("w_u", [D, DFF])
    w_d_d = din("w_d", [DFF, D])
    gmix_d = din("gmix", [128, 8])
    gffn_d = din("gffn", [128, 8])
    gfin_d = din("gfin", [D])
    rel_d = din("rel", [8, 192])
    lbl_d = din("lbl", [128, 2, 4])
    gn_d = din("gn", [128])
    cw_d = din("cw", [128, NFT, 3])
    cb_d = din("cb", [128, NFT])
    ident_d = din("ident", [128, 128])
    mask_d = din("mask", [128, 128])
    jmat_d = din("jmat", [128, 128])

    y_d = dout("y", [SEG, D])
    ys_d = dout("ys", [16, D])
    okT_d = dout("okT", [4, 128, 512])
    ov_d = dout("ov", [512, 512])
    oS_d = dout("oS", [4, 128, 128])
    oconv_d = dout("oconv", [128, NFT, 2])
    okTs_d = dout("okTs", [4, 128, 16])
    ovs_d = dout("ovs", [16, 512])
    oSs_d = dout("oSs", [4, 128, 128])
    oconvs_d = dout("oconvs", [128, NFT, 2])

    def scratch(name, shape, dt=BF16):
        return nc.dram_tensor(name, list(shape), dt, kind="Internal")

    wb_in = scratch("wb_in", [D, 5632])
    wb_a = scratch("wb_a", [512, D])
    wb_b = scratch("wb_b", [512, D])
    wb_o = scratch("wb_o", [D, D])
    wb_g = scratch("wb_g", [D, DFF])
    wb_u = scratch("wb_u", [D, DFF])
    wb_d = scratch("wb_d", [DFF, D])
    ext_d = scratch("ext_d", [8, 768], F32)

    def sb(name, shape, dt=F32):
        return st.enter_context(nc.sbuf_tensor(name, list(shape), dt))

    def R(name, excl=False):
        return Res(name, excl)

    def fsz(ap):
        n = 1
        for d_ in list(ap.shape)[1:]:
            n *= int(d_)
        return n

    def ecost(eng, ap):
        n = fsz(ap)
        return {"act": 0.25 + n / 1000.0, "dve": 0.1 + n / 850.0, "pool": 0.2 + n / 480.0}[eng]

    def act(out, in_, func, reads, writes, **kw):
        S.emit("act", lambda e: e.activation(out=out, in_=in_, func=func, **kw), reads, writes, cost=ecost("act", out))

    def tt(out, in0, in1, op, reads, writes, eng="dve"):
        S.emit(eng, lambda e: e.tensor_tensor(out=out, in0=in0, in1=in1, op=op), reads, writes, cost=ecost(eng, out))

    def ts(out, in0, s1, s2, op0, op1, reads, writes, eng="dve"):
        if op1 is None:
            S.emit(eng, lambda e: e.tensor_scalar(out=out, in0=in0, scalar1=s1, scalar2=None, op0=op0), reads, writes, cost=ecost(eng, out))
        else:
            S.emit(eng, lambda e: e.tensor_scalar(out=out, in0=in0, scalar1=s1, scalar2=s2, op0=op0, op1=op1), reads, writes, cost=ecost(eng, out))

    def stt(out, in0, scalar, in1, op0, op1, reads, writes):
        S.emit("dve", lambda e: e.scalar_tensor_tensor(out=out, in0=in0, scalar=scalar, in1=in1, op0=op0, op1=op1), reads, writes, cost=ecost("dve", out))

    def cp(out, in_, reads, writes, eng="dve"):
        if eng == "act":
            act(out, in_, AF.Copy, reads, writes)
        else:
            S.emit(eng, lambda e: e.tensor_copy(out=out, in_=in_), reads, writes, cost=ecost(eng, out))

    def recip(out, in_, reads, writes):
        S.emit("dve", lambda e: e.reciprocal(out=out, in_=in_), reads, writes)

    def mset(ap, val, writes, eng="pool"):
        S.emit(eng, lambda e: e.memset(ap, val), (), writes)

    def mm(out, lhsT, rhs, start, stop, reads, writes):
        S.emit("pe", lambda e: e.matmul(out, lhsT=lhsT, rhs=rhs, start=start, stop=stop), reads, writes, cost=0.05 + fsz(rhs) / 2000.0)

    def dma(eng, out, in_, reads, writes):
        S.emit(eng, lambda e: e.dma_start(out=out, in_=in_), reads, writes, dma=True)

    identb = sb("identb", [128, 128], BF16); r_ident = R("ident")
    maskb = sb("maskb", [128, 128], BF16); r_mask = R("mask")
    jb = sb("jb", [128, 128], BF16); r_j = R("j")
    epst = sb("epst", [128, 1]); r_eps = R("eps")
    onesT = sb("onesT", [128, 8]); r_ones = R("ones")
    hvt = sb("hvt", [128, 1]); r_hv = R("hv")
    gmix = sb("gmix_s", [128, 8]); r_gmix = R("gmix")
    gffn = sb("gffn_s", [128, 8]); r_gffn = R("gffn")
    gfin = sb("gfin_s", [128, D]); r_gfin = R("gfin")
    gnrep = sb("gnrep", [128, 4, 128]); r_gn = R("gn")
    cw = sb("cw_s", [128, NFT, 3]); r_cw = R("cw")
    cb = sb("cb_s", [128, NFT]); r_cb = R("cb")
    lbl = sb("lbl_s", [128, 2, 4]); r_lbl = R("lbl")
    lb = sb("lb_s", [128, 4]); oml = sb("oml_s", [128, 4]); r_lb = R("lb")
    ET = sb("ET", [128, 5, 8, 128], BF16); r_ET = R("ET")

    dma("pool", identb[:], ident_d.ap(), (), [r_ident])
    dma("pool", maskb[:], mask_d.ap(), (), [r_mask])
    dma("pool", jb[:], jmat_d.ap(), (), [r_j])
    mset(epst[:], EPS, [r_eps])
    nhalf = sb("nhalf", [128, 8]); r_nh = R("nhalf")
    mset(nhalf[:], -0.5, [r_nh])
    mset(onesT[:], 1.0, [r_ones])
    dma("sp", hvt[:], hv_d.ap(), (), [r_hv])
    dma("sp", gmix[:], gmix_d.ap(), (), [r_gmix])
    dma("sp", gffn[:], gffn_d.ap(), (), [r_gffn])
    dma("sp", gfin[:], AP(gfin_d, 0, [[0, 128], [1, D]]), (), [r_gfin])
    dma("sp", gnrep[:], AP(gn_d, 0, [[0, 128], [0, 4], [1, 128]]), (), [r_gn])
    dma("sp", cw[:], cw_d.ap(), (), [r_cw])
    dma("sp", cb[:], cb_d.ap(), (), [r_cb])
    dma("sp", lbl[:], lbl_d.ap(), (), [r_lbl])

    if init_stop <= 1:
        S.finalize(st)
        with nc.Block() as block:
            S.run(block)
        st.close()
        return nc
    ps_att = st.enter_context(nc.psum_tensor("ps_att", [128, 1536], F32))
    r_att = [R("att0", True), R("att1", True)]
    r_attB = R("attB", True)
    NGEN = 2
    ps_gen = [st.enter_context(nc.psum_tensor("ps_g%d" % i, [128, 512], F32)) for i in range(NGEN)]
    r_gen = [R("g%d" % i, True) for i in range(NGEN)]
    ps_o = st.enter_context(nc.psum_tensor("ps_o", [128, 512], F32))
    r_o = R("ps_o", True)
    ps_tr = [st.enter_context(nc.psum_tensor("ps_t%d" % i, [128, 1024], BF16)) for i in range(2)]
    r_tr = [R("t%d" % i, True) for i in range(2)]
    cnt = {"g": 0, "t": 0, "a": 0, "p": 0}

    gpool_full = [(ps_gen[i], r_gen[i]) for i in range(NGEN)]
    gpool_front = gpool_full + [(ps_o, r_o)]
    gpool_back = [(ps_att[:, 0:512], r_att[0]), (ps_att[:, 512:1024], r_att[1]), (ps_att[:, 1024:1536], r_attB)]
    gstate = {"pool": gpool_full, "tr": (0, 1)}

    def gbank():
        pool = gstate["pool"]
        i = cnt["g"] % len(pool)
        cnt["g"] += 1
        return pool[i]

    def tbank():
        sel = gstate["tr"]
        i = sel[cnt["t"] % len(sel)]
        cnt["t"] += 1
        return ps_tr[i], r_tr[i]

    lbt = sb("lbt", [128, 4])
    tt(lbt[:], lbl[:, 1, :], lbl[:, 0, :], ALU.subtract, [r_lbl], [r_lb])
    act(lbt[:], lbt[:], AF.Exp, [r_lb], [r_lb])
    ts(lbt[:], lbt[:], 1.0, None, ALU.add, None, [r_lb], [r_lb])
    recip(lb[:], lbt[:], [r_lb], [r_lb])
    ts(oml[:], lb[:], -1.0, 1.0, ALU.mult, ALU.add, [r_lb], [r_lb])
    omlh = sb("omlh_s", [128, 4]); lbh = sb("lbh_s", [128, 4])
    ts(omlh[:], oml[:], 0.5, None, ALU.mult, None, [r_lb], [r_lb])
    tt(lbh[:], lb[:], omlh[:], ALU.add, [r_lb], [r_lb])

    if init_stop <= 2:
        S.finalize(st)
        with nc.Block() as block:
            S.run(block)
        st.close()
        return nc
    if init_stop <= 3:
        S.finalize(st)
        with nc.Block() as block:
            S.run(block)
        st.close()
        return nc
    r_wb = {}
    conv_jobs = []
    for name, src, dst, rows in (("in", w_in_d, wb_in, D), ("a", w_a_d, wb_a, 512), ("b", w_b_d, wb_b, 512),
                                 ("o", w_o_d, wb_o, D), ("g", w_g_d, wb_g, D), ("u", w_u_d, wb_u, D),
                                 ("d", w_d_d, wb_d, DFF)):
        r_wb[name] = R("wb_" + name)
        for r0 in range(0, rows, 128):
            conv_jobs.append((name, dst.ap()[r0:r0 + 128, :], src.ap()[r0:r0 + 128, :]))

    def emit_conv(n):
        for _ in range(n):
            if conv_jobs:
                name, d_, s_ = conv_jobs.pop(0)
                dma("pool", d_, s_, (), [r_wb[name]])

    wslot = [sb("wslot%d" % i, [128, SLOT_E], BF16) for i in range(NSLOT)]
    r_wslot = [R("wslot%d" % i) for i in range(NSLOT)]
    W1v = []
    for i, c0 in enumerate((2048, 2560, 512, 1024)):
        v_ = wslot[i][:, :].rearrange("p (k n) -> p k n", k=8)
        dma("pool", v_, w_in_d.ap()[:, c0:c0 + 512].rearrange("(k p) n -> p k n", p=128), (), [r_wslot[i]])
        W1v.append(v_)

    def wgroups(mode):
        g = []
        for i in (3, 4, 5, 6, 0, 1, 2, 7, 8, 9, 10):
            g.append(("in%d" % i, "in", wb_in.ap()[:, 512 * i:512 * i + 512].rearrange("(k p) n -> p k n", p=128), 8, 512))
        g.append(("a", "a", wb_a.ap().rearrange("(k p) n -> p k n", p=128), 4, 1024))
        g.append(("b", "b", wb_b.ap().rearrange("(k p) n -> p k n", p=128), 4, 1024))
        for n in range(2):
            g.append(("o%d" % n, "o", wb_o.ap()[:, 512 * n:512 * n + 512].rearrange("(k p) n -> p k n", p=128), 8, 512))
        for gi in range(6):
            nc_ = 512 if gi < 5 else 256
            g.append(("g%d" % gi, "g", wb_g.ap()[:, 512 * gi:512 * gi + nc_].rearrange("(k p) n -> p k n", p=128), 8, nc_))
            if mode != "pre":
                g.append(("u%d" % gi, "u", wb_u.ap()[:, 512 * gi:512 * gi + nc_].rearrange("(k p) n -> p k n", p=128), 8, nc_))
        if mode != "pre":
            for n in range(2):
                for gk in range(3):
                    nk = 8 if gk < 2 else 6
                    g.append(("d%d_%d" % (n, gk), "d",
                              wb_d.ap()[1024 * gk:1024 * gk + 128 * nk, 512 * n:512 * n + 512].rearrange("(k p) n -> p k n", p=128), nk, 512))
        return g

    wq = []
    tiles_plan = ([("pre", 0)] if do_pre else []) + [("main", m) for m in range(n_main)] + ([("sample", 0)] if do_sample else [])
    for mode, _ in tiles_plan:
        wq.extend(wgroups(mode))
    wstate = {"issued": 0, "consumed": 0, "slot": {}}

    def w_issue(slot):
        i = wstate["issued"]
        if i >= len(wq):
            return
        key, wname, src, nk, ncol = wq[i]
        view = wslot[slot][:, 0:nk * ncol].rearrange("p (k n) -> p k n", k=nk)
        dma("sp", view, src, [r_wb[wname]], [r_wslot[slot]])
        wstate["slot"][i] = slot
        wstate["issued"] += 1

    def w_next(key):
        i = wstate["consumed"]
        assert wq[i][0] == key, (wq[i][0], key)
        slot = wstate["slot"][i]
        _, _, _, nk, ncol = wq[i]
        view = wslot[slot][:, 0:nk * ncol].rearrange("p (k n) -> p k n", k=nk)
        return view, r_wslot[slot], slot

    def w_done(slot):
        wstate["consumed"] += 1
        w_issue(slot)

    if init_stop <= 4:
        S.finalize(st)
        with nc.Block() as block:
            S.run(block)
        st.close()
        return nc
    X = [sb("X%d" % i, [128, 2, D]) for i in range(2)]
    r_X = [[R("X%d_%d" % (i, j)) for j in range(2)] for i in range(2)]
    XN = sb("XN", [128, 2, D], BF16); r_XN = [R("XN0"), R("XN1")]
    ss = sb("ss", [128, 8]); r_ss = R("ss")
    hTs = [sb("hTa", [128, 8, T], BF16), sb("hTb", [128, 8, T], BF16), sb("h2T", [128, 8, T], BF16)]
    r_hTs = [[R("hTa0"), R("hTa1")], [R("hTb0"), R("hTb1")], [R("h2T0"), R("h2T1")]]
    qT = sb("qT", [128, 4, T], BF16); r_qT = R("qT")
    kTr = sb("kTr", [128, 4, 6, 128], BF16); r_kT = [R("kT%d" % i) for i in range(6)]
    Vr = sb("Vr", [128, 6, 8, 65], BF16); r_V = [R("V%d" % i) for i in range(6)]
    Pb = [sb("Pb%d" % i, [128, 5, 128], BF16) for i in range(3)]; r_Pb = [R("Pb%d" % i) for i in range(3)]
    oa = sb("oa", [128, 2, 512], BF16); r_oa = [R("oa0"), R("oa1")]
    oaT = sb("oaT", [128, 4, T], BF16); r_oaT = [R("oaT0"), R("oaT1")]
    rec = sb("rec", [128, 8]); r_rec = R("rec")
    sg = sb("sg", [128, 4, T]); r_sg = [R("sg%d" % i) for i in range(4)]
    siluq = sb("siluq", [128, 4, T]); r_sq = [R("siluq%d" % i) for i in range(4)]
    gF = sb("gF", [128, 4 * T]); r_gF = R("gF")
    gL = sb("gL", [128, 4 * T]); r_gL = R("gL")
    gB = sb("gB", [128, 4 * T]); r_gB = R("gB")
    gK = sb("gK", [128, 4 * T]); r_gK = R("gK")
    gE = sb("gE", [128, 4 * T]); r_gE = R("gE")
    dec = sb("dec", [128, 3, 4, 4]); r_dec = R("dec")
    Zq = sb("Zq", [128, 4, 4, 128], BF16); r_Zq = [R("Zq%d" % i) for i in range(4)]
    ktT = sb("ktT", [128, 4, T], BF16); r_ktT = [R("ktT%d" % i) for i in range(4)]
    khT = sb("khT", [128, 4, T], BF16); r_khT = [R("khT%d" % i) for i in range(4)]
    khtm = sb("khtm", [128, 2, 512], BF16); r_khtm = [R("khtm0"), R("khtm1")]
    vh = sb("vh", [128, 2, 512], BF16); r_vh = [R("vh0"), R("vh1")]
    sgb = sb("sgb", [128, 2, 512]); r_sgb = [R("sgb0"), R("sgb1")]
    Sst = sb("Sst", [128, 4, 128]); r_S = R("S")
    Sp = sb("Sp", [128, 2, 4, 128], BF16); r_Sp = [R("Sp0"), R("Sp1")]
    AT = sb("AT", [128, 4, 128], BF16); r_AT = R("AT")
    sqb = sb("sqb", [128, 512]); r_sqb = R("sqb")
    ssq = sb("ssq", [128, 4]); r_ssq = R("ssq")
    t1 = sb("t1", [128, 512]); r_t1 = R("t1")
    gG = sb("gG", [128, 512]); r_gG = R("gG")
    ob = sb("ob", [128, 512], BF16); r_ob = R("ob")
    obT = sb("obT", [128, 4, T], BF16); r_obT = [R("obT0"), R("obT1")]
    zs = sb("zs", [128, 16, T], BF16); r_zs = [R("zs%d" % i) for i in range(16)]
    mT = sb("mT", [128, 8, T], BF16); r_mT = [R("mT%d" % i) for i in range(8)]
    aT = [sb("aT%d" % i, [128, T + 2]) for i in range(2)]; r_aT = [R("aT0"), R("aT1")]
    c1 = [sb("c1_%d" % i, [128, T]) for i in range(2)]; r_c1 = [R("c1_0"), R("c1_1")]
    c2 = [sb("c2_%d" % i, [128, T]) for i in range(2)]; r_c2 = [R("c2_0"), R("c2_1")]
    m1, r_m1, m2, r_m2 = c1[0], r_c1[0], c2[0], r_c2[0]
    gT = sb("gT", [128, NFT, T], BF16); r_gT = [R("gT%d" % i) for i in range(NFT)]
    cprev = sb("cprev_s", [128, NFT, 2]); r_cprev = R("cprev")

    ext_ = siluq[0:8].rearrange("p h t -> p (h t)")[:, 0:768]; r_ext = r_sq; r_extd = R("extd")
    dma("sp", ext_[:, 64:256], rel_d.ap(), (), r_ext)
    act(ext_[:, 0:64], ext_[:, 64:65].to_broadcast([8, 64]), AF.Identity, r_ext, r_ext)
    act(ext_[:, 256:768], ext_[:, 255:256].to_broadcast([8, 512]), AF.Identity, r_ext, r_ext)
    act(ext_[:, :], ext_[:, :], AF.Exp, r_ext, r_ext)
    dma("sp", ext_d.ap(), ext_[:, :], r_ext, [r_extd])
    hk = sg[:].rearrange("p h t -> p (h t)").rearrange("p (a b) -> p a b", a=8); r_hk = r_sg
    hkb = ktT[:].rearrange("p h t -> p (h t)").rearrange("p (a b) -> p a b", a=8); r_hkb = r_ktT
    for kt in range(5):
        dma("sp", hk, AP(ext_d, 512 - 128 * kt, [[1, 128], [768, 8], [1, 128]]), [r_extd], r_hk)
        cp(hkb, hk, r_hk, r_hkb)
        for n in range(2):
            pb, rb = gbank()
            mm(pb[:, :], jb[:], hkb[:, 4 * n:4 * n + 4, :].rearrange("p h q -> p (h q)"), True, True, [r_j] + r_hkb, [rb])
            cp(ET[:, kt, 4 * n:4 * n + 4, :].rearrange("p h q -> p (h q)"), pb[:, :], [rb], [r_ET])
    mset(ET[0:64, 0, :, 64:128], 0.0, [r_ET])
    mset(ET[64:128, 4, :, 0:64], 0.0, [r_ET])

    kst = gT[:, 0:8, :].rearrange("p a b -> p (a b)").bitcast(F32).rearrange("p (h t) -> p h t", h=4)
    vst = gT[:, 8:16, :].rearrange("p a b -> p (a b)").bitcast(F32).rearrange("p (j c) -> p j c", j=2)
    rl_kst = r_gT[0:8]
    rl_vst = [r_gT[8:12], r_gT[12:16]]
    sg2 = gT[:, 0:8, :].rearrange("p a b -> p (a b)").bitcast(F32).rearrange("p (h t) -> p h t", h=4)
    r_sg2 = [[r_gT[2 * i], r_gT[2 * i + 1]] for i in range(4)]
    vh2 = gT[:, 8:12, :].rearrange("p a b -> p (a b)").rearrange("p (j c) -> p j c", j=2)
    r_vh2 = [[r_gT[8], r_gT[9]], [r_gT[10], r_gT[11]]]
    vh3 = gT[:, 12:16, :].rearrange("p a b -> p (a b)").rearrange("p (j c) -> p j c", j=2)
    r_vh3 = [[r_gT[12], r_gT[13]], [r_gT[14], r_gT[15]]]
    ktT2 = gT[:, 16:20, :]
    r_ktT2 = r_gT[16:20]
    dec2 = sb("dec2", [128, 3, 4, 4]); r_dec2 = R("dec2")
    sgs = [(sg, [[r] for r in r_sg]), (sg2, r_sg2)]
    vhs = [(vh, [[r] for r in r_vh]), (vh2, r_vh2), (vh3, r_vh3)]
    kts = [(ktT, r_ktT, dec, r_dec), (ktT2, r_ktT2, dec2, r_dec2)]
    mset(Zq[:], 0.0, r_Zq)
    mset(Sst[:], 0.0, [r_S])
    mset(cprev[:], 0.0, [r_cprev])

    if init_stop <= 5:
        S.finalize(st)
        with nc.Block() as block:
            S.run(block)
        st.close()
        return nc
    def load_x(xsrc, ntok, xb):
        nsub = (ntok + 127) // 128
        for j in range(nsub):
            nt = min(128, ntok - 128 * j)
            dma("sp", X[xb][:nt, j, :], xsrc[j * 128:j * 128 + nt, :], (), [r_X[xb][j]])

    def norm_T(src_tile, rsrc, gcol, rg, col0, hsel, ntok):
        nsub = (ntok + 127) // 128
        dst, rdst = hTs[hsel], r_hTs[hsel]
        nts_ = [min(128, ntok - 128 * j) for j in range(nsub)]
        for j in range(nsub):
            nt = nts_[j]
            act(XN[:nt, j, :], src_tile[:nt, j, :], AF.Square, [rsrc[j]], [r_XN[j], r_ss], accum_out=ss[:nt, col0 + j:col0 + j + 1])
        for j in range(nsub):
            nt = nts_[j]
            ts(ss[:nt, col0 + j:col0 + j + 1], ss[:nt, col0 + j:col0 + j + 1], 1.0 / D, EPS, ALU.mult, ALU.add, [r_ss], [r_ss], eng="pool")
            tt(ss[:nt, col0 + j:col0 + j + 1], ss[:nt, col0 + j:col0 + j + 1], nhalf[:nt, 0:1], ALU.pow, [r_ss, r_nh], [r_ss], eng="pool")
        for j in range(nsub):
            nt = nts_[j]
            act(XN[:nt, j, :], src_tile[:nt, j, :], AF.Copy, [rsrc[j], r_ss], [r_XN[j]], scale=ss[:nt, col0 + j:col0 + j + 1])
            pt, rt = tbank()
            for kc in range(8):
                S.emit("pe", lambda e, kc=kc, j=j, nt=nt, pt=pt: e.transpose(out=pt[:, kc * 128:kc * 128 + nt], in_=XN[:nt, j, kc * 128:(kc + 1) * 128],
                                                                              identity=identb[:nt, :nt]), [r_XN[j], r_ident], [rt], cost=0.1)
            tt(dst[:, :, j * 128:j * 128 + nt], pt[:, :].rearrange("p (k t) -> p k t", k=8)[:, :, 0:nt],
               gcol[:, 0:8].unsqueeze(2).to_broadcast([128, 8, nt]), ALU.mult, [rt, rg], [rdst[j]])

    def macro_tile(mode, ntok, xb, gt0, hsel=0, normed=False, kv=False, out_row=None, final_kv=None, after_norm=None, prenorm=None,
                   part="all", bsel=0, vsel=0, ksel=0, deferred=None):
        sample = mode == "sample"
        nsub = (ntok + 127) // 128
        nts = [min(128, ntok - 128 * j) for j in range(nsub)]
        C = 16 if sample else (ntok if mode == "scan" else 64)
        nch = ntok // C
        ri = C - 1 if mode == "scan" else C // 2 - 1
        cps = 1 if sample else 2
        Xb = X[xb]
        rX = r_X[xb]
        hT = hTs[hsel]
        r_hT = r_hTs[hsel]
        rhT_all = r_hT[:nsub]
        cur["hT"] = hT
        cur["r_hT"] = r_hT
        sg, rl_sg = sgs[bsel]
        vh, rl_vh = vhs[vsel]
        ktT, r_ktT, dec, r_dec = kts[ksel]
        if mode == "scan":
            gstate["pool"] = gpool_front if part in ("f1", "f2") else gpool_back
            gstate["tr"] = (0,) if part in ("f1", "f2") else (1,)
        else:
            gstate["pool"] = gpool_front + gpool_back
            gstate["tr"] = (0, 1)

        if part in ("all", "f1"):
            if not normed:
                norm_T(Xb, rX, gmix, r_gmix, 0, hsel, ntok)
            if after_norm is not None and (deferred is None or not deferred):
                after_norm()
                after_norm = None

        def proj_fm(wv, rw, ct, evac):
            pb, rb = gbank()
            for kc in range(8):
                mm(pb[:, 0:ntok], wv[:, kc, ct * 128:(ct + 1) * 128], hT[:, kc, 0:ntok], kc == 0, kc == 7, [rw] + rhT_all, [rb])
            evac(pb, rb)

        def proj_tm(wv, rw, j, evac, ncol=512):
            pb, rb = gbank()
            nt = nts[j]
            for kc in range(8):
                mm(pb[:nt, 0:ncol], hT[:, kc, j * 128:j * 128 + nt], wv[:, kc, 0:ncol], kc == 0, kc == 7, [rw, r_hT[j]], [rb])
            evac(pb, rb, j, nt)

        def slot_of(j):
            return (gt0 + j) % 6 if not sample else 4

        def hgrn_gates_all():
            n4 = 4 * ntok
            nc4 = 4 * nch
            fl = lambda b: b[:, 0:n4]
            v3 = lambda b: b[:, 0:n4].rearrange("p (h t) -> p h t", h=4)
            ch = lambda b: b[:, 0:n4].rearrange("p (c t) -> p c t", t=C)
            hc = lambda b: b[:, 0:n4].rearrange("p (h c t) -> p h c t", h=4, t=C)
            rsg = [r for l_ in rl_sg for r in l_]
            sgf = sg.rearrange("p h t -> p (h t)")
            tt(v3(gF), v3(sgf), omlh[:, 0:4].unsqueeze(2).to_broadcast([128, 4, ntok]), ALU.mult, rsg + [r_lb], [r_gF])
            tt(v3(gF), v3(gF), lbh[:, 0:4].unsqueeze(2).to_broadcast([128, 4, ntok]), ALU.add, [r_gF, r_lb], [r_gF])
            act(fl(gL), fl(gF), AF.Ln, [r_gF], [r_gL])
            S.emit("dve", lambda e: e.tensor_tensor_scan(out=fl(gB), data0=fl(gL), data1=fl(gL), initial=0.0, op0=ALU.add, op1=ALU.min),
                   [r_gL], [r_gB], cost=0.1 + 2 * n4 / 900.0)
            ts(fl(gK), fl(gF), -1.0, 1.0, ALU.mult, ALU.add, [r_gF], [r_gK], eng="pool")
            tt(ch(gL), ch(gB), ch(gB)[:, :, ri:ri + 1].to_broadcast([128, nc4, C]), ALU.subtract, [r_gB, r_gL], [r_gL])
            act(fl(gE), fl(gL), AF.Exp, [r_gL], [r_gE], scale=-1.0)
            tt(ktT[:, :, 0:ntok], v3(gK), v3(gE), ALU.mult, [r_gK, r_gE], r_ktT, eng="pool")
            dsl = lambda i: dec[:, i, :, 0:nch]
            if mode != "scan":
                act(fl(gB), fl(gL), AF.Exp, [r_gL, r_gB], [r_gB])
                cp(dsl(2), hc(gB)[:, :, :, C - 1], [r_gB], [r_dec])
            tt(dsl(0), hc(gE)[:, :, :, 0], hc(gF)[:, :, :, 0], ALU.mult, [r_gE, r_gF], [r_dec])
            if mode == "scan":
                return
            tt(dsl(1), dsl(0), dsl(2), ALU.mult, [r_dec], [r_dec])
            k4 = lambda b: b[:, :, 0:ntok].rearrange("p h (c t) -> p h c t", t=C)
            tt(k4(khT), k4(ktT), dsl(2).unsqueeze(3).to_broadcast([128, 4, nch, C]), ALU.mult, r_ktT + [r_dec], r_khT)
            if mode != "scan":
                sqf = siluq.rearrange("p h t -> p (h t)")
                if sample:
                    tt(Zq[:, :, 0, 0:ntok], v3(sqf), v3(gB), ALU.mult, r_sq + [r_gB], r_Zq)
                else:
                    base = Zq[:, 0, 0, 0:64]
                    zout = AP(base.tensor, base.offset, [list(base.ap[0]), [512 // nsub, 4 * nsub], [192, 2], [1, 64]])
                    tt(zout, fl(sqf).rearrange("p (a c t) -> p a c t", c=2, t=64), fl(gB).rearrange("p (a c t) -> p a c t", c=2, t=64),
                       ALU.mult, r_sq + [r_gB], r_Zq)

        def khat_transpose(j):
            nt = nts[j]
            pt, rt = tbank()
            ksrc, rks = (ktT, r_ktT) if mode == "scan" else (khT, r_khT)
            for hb in range(4):
                S.emit("pe", lambda e, hb=hb, pt=pt: e.transpose(out=pt[:nt, hb * 128:(hb + 1) * 128], in_=ksrc[:, hb, j * 128:j * 128 + nt],
                                                                  identity=identb[:, :]), [rks[hb], r_ident], [rt])
            cp(khtm[:nt, j, :], pt[:nt, 0:512], [rt], [r_khtm[j]], eng="dve" if mode == "scan" else "act")

        def s_update(j, ci):
            p = ci % cps
            rows = slice(p * C, p * C + C)
            pb, rb = gbank()
            for hb in range(4):
                mm(pb[:, hb * 128:(hb + 1) * 128], khtm[rows, j, hb * 128:(hb + 1) * 128], vh[rows, j, hb * 128:(hb + 1) * 128],
                   True, True, [r_khtm[j]] + rl_vh[j], [rb])
            tt(Sst[:], Sst[:], dec[:, 1, :, ci:ci + 1].to_broadcast([128, 4, 128]), ALU.mult, [r_S, r_dec], [r_S])
            tt(Sst[:].rearrange("p h v -> p (h v)"), Sst[:].rearrange("p h v -> p (h v)"), pb[:, :], ALU.add, [r_S, rb], [r_S])

        if mode == "scan":
            wv_f, wv_i = W1v[0], W1v[1]
            if part == "f2":
                for hb in range(4):
                    proj_fm(wv_f, r_wslot[0], hb, lambda pb, rb, hb=hb: act(sg.rearrange("p h t -> p (h t)")[:, hb * ntok:(hb + 1) * ntok], pb[:, 0:ntok], AF.Tanh, [rb], rl_sg[hb], scale=0.5))
                for j in range(nsub):
                    proj_tm(wv_i, r_wslot[1], j, lambda pb, rb, j, nt: cp(vh[:nt, j, :], pb[:nt, :], [rb], rl_vh[j], eng="dve"))
                if kv:
                    kv_proj_only()
            if part == "gates":
                hgrn_gates_all()
            if part == "upd":
                for j in range(nsub):
                    khat_transpose(j)
                pb, rb = gbank()
                for hb in range(4):
                    for j in range(nsub):
                        mm(pb[:, hb * 128:(hb + 1) * 128], khtm[:, j, hb * 128:(hb + 1) * 128], vh[:, j, hb * 128:(hb + 1) * 128],
                           j == 0, j == nsub - 1, [r_khtm[j]] + rl_vh[j], [rb])
                tt(Sst[:], Sst[:], dec[:, 0, :, 0:1].to_broadcast([128, 4, 128]), ALU.mult, [r_S, r_dec], [r_S])
                tt(Sst[:].rearrange("p h v -> p (h v)"), Sst[:].rearrange("p h v -> p (h v)"), pb[:, :], ALU.add, [r_S, rb], [r_S])
            return

        def ev_q(ct):
            return lambda pb, rb: act(qT[:, ct, 0:ntok], pb[:, 0:ntok], AF.Copy, [rb], [r_qT], scale=0.125)

        def ev_k(ct):
            def f(pb, rb):
                for j in range(nsub):
                    cp(kTr[:, ct, slot_of(j), 0:nts[j]], pb[:, j * 128:j * 128 + nts[j]], [rb], [r_kT[slot_of(j)]], eng="act")
                if final_kv is not None:
                    cp(kst[:, ct, 0:ntok], pb[:, 0:ntok], [rb], rl_kst)
            return f

        def ev_v(pb, rb, j, nt):
            s_ = slot_of(j)
            cp(Vr[:nt, s_, :, 0:64], pb[:nt, :].rearrange("p (h d) -> p h d", h=8), [rb], [r_V[s_]], eng="act")
            if mode == "main" or sample:
                cp(Vr[:nt, s_, :, 64:65], onesT[:nt, 0:8].unsqueeze(2), [r_ones], [r_V[s_]], eng="pool")
            else:
                cp(Vr[:nt, s_, :, 64:65], hvt[:nt, 0:1].unsqueeze(2).to_broadcast([nt, 8, 1]), [r_hv], [r_V[s_]], eng="pool")
            if final_kv is not None:
                cp(vst[:nt, j, :], pb[:nt, :], [rb], rl_vst[j])
                dma("act", final_kv[1].ap()[final_kv[2] + j * 128:final_kv[2] + j * 128 + nt, :], vst[:nt, j, :], rl_vst[j], ())

        wv, rw, sl = w_next("in3")
        for hb in range(4):
            proj_fm(wv, rw, hb, lambda pb, rb, hb=hb: act(siluq.rearrange("p h t -> p (h t)")[:, hb * ntok:(hb + 1) * ntok], pb[:, 0:ntok], AF.Silu, [rb], [r_sq[hb]]))
        w_done(sl)
        wv, rw, sl = w_next("in4")
        for hb in range(4):
            proj_fm(wv, rw, hb, lambda pb, rb, hb=hb: act(sg.rearrange("p h t -> p (h t)")[:, hb * ntok:(hb + 1) * ntok], pb[:, 0:ntok], AF.Tanh, [rb], rl_sg[hb], scale=0.5))
        w_done(sl)
        wv, rw, sl = w_next("in5")
        for j in range(nsub):
            proj_tm(wv, rw, j, lambda pb, rb, j, nt: cp(vh[:nt, j, :], pb[:nt, :], [rb], rl_vh[j], eng="act"))
        w_done(sl)
        wv, rw, sl = w_next("in6")
        for j in range(nsub):
            proj_tm(wv, rw, j, lambda pb, rb, j, nt: act(sgb[:nt, j, :], pb[:nt, :], AF.Silu, [rb], [r_sgb[j]]))
        w_done(sl)
        wv, rw, sl = w_next("in0")
        for ct in range(4):
            proj_fm(wv, rw, ct, ev_q(ct))
        w_done(sl)
        wv, rw, sl = w_next("in1")
        for ct in range(4):
            proj_fm(wv, rw, ct, ev_k(ct))
        w_done(sl)
        if final_kv is not None:
            for ct in range(4):
                dma("act", final_kv[0].ap()[ct, :, final_kv[2]:final_kv[2] + ntok], kst[:, ct, 0:ntok], rl_kst, ())
        wv, rw, sl = w_next("in2")
        for j in range(nsub):
            proj_tm(wv, rw, j, ev_v)
        w_done(sl)

        d_ops = S.capture(deferred.pop(0)) if deferred else []

        att_steps, z_steps, h_steps = [], [], []

        zst = {}

        def z_step(zi):
            gi, ct = divmod(zi, 4)
            if ct == 0:
                zst["w"] = w_next("in%d" % (7 + gi))
            wv_, rw_, sl_ = zst["w"]
            proj_fm(wv_, rw_, ct, lambda pb, rb: act(zs[:, zi, 0:ntok], pb[:, 0:ntok], AF.Tanh, [rb], [r_zs[zi]], scale=0.5))
            if ct == 3:
                w_done(sl_)
        for zi in range(16):
            z_steps.append(lambda zi=zi: z_step(zi))

        def make_att(j):
            nq = nts[j]
            if sample:
                ktiles = [(0, 128), (1, 128), (2, 128), (3, 128), (4, 16)]
            else:
                ktiles = [((gt0 + j - 4 + kt) % 6, 128) for kt in range(5)]
            pend = []

            def pv(h, pslot):
                for kt, (s_, nk) in enumerate(ktiles):
                    mm(ps_o[:nq, (h % 4) * 65:(h % 4) * 65 + 65], Pb[pslot][:nk, kt, 0:nq], Vr[:nk, s_, h, :], kt == 0, kt == 4,
                       [r_Pb[pslot], r_V[s_]], [r_o])

            def normalize(half):
                o3 = ps_o[:nq, 0:260].rearrange("p (h d) -> p h d", h=4)
                ts(rec[:nq, half * 4:half * 4 + 4].unsqueeze(2), o3[:, :, 64:65], 1e-30, None, ALU.max, None, [r_o], [r_rec])
                recip(rec[:nq, half * 4:half * 4 + 4], rec[:nq, half * 4:half * 4 + 4], [r_rec], [r_rec])
                tt(oa[:nq, j, half * 256:half * 256 + 256].rearrange("p (h d) -> p h d", h=4), o3[:, :, 0:64],
                   rec[:nq, half * 4:half * 4 + 4].unsqueeze(2).to_broadcast([nq, 4, 64]), ALU.mult, [r_o, r_rec], [r_oa[j]])

            def head(h):
                hp, r0 = h // 2, (h % 2) * 64
                ai = cnt["a"] % 2
                cnt["a"] += 1
                offA = ai * 512
                offB = 1024 + ai * 128
                for kt in (4, 0, 1, 2, 3):
                    s_, nk = ktiles[kt]
                    o_ = offB if kt == 4 else offA + kt * 128
                    mm(ps_att[:nk, o_:o_ + nq], kTr[r0:r0 + 64, hp, s_, 0:nk], qT[r0:r0 + 64, hp, j * 128:j * 128 + nq],
                       True, True, [r_kT[s_], r_qT], [r_attB if kt == 4 else r_att[ai]])
                pi = cnt["p"] % 3
                cnt["p"] += 1
                sattA = ps_att[:, offA:offA + 512].rearrange("p (k q) -> p k q", k=4)
                sattB = ps_att[:, offB:offB + 128]
                if sample:
                    act(Pb[pi][:16, 4, 0:nq], sattB[:16, 0:nq], AF.Exp, [r_attB], [r_Pb[pi]])
                    act(Pb[pi][:, 0:4, 0:nq], sattA[:, :, 0:nq], AF.Exp, [r_att[ai]], [r_Pb[pi]])
                    tt(Pb[pi][:, 0:4, 0:nq], Pb[pi][:, 0:4, 0:nq], ET[:, 0:4, h, 0:nq], ALU.mult, [r_Pb[pi], r_ET], [r_Pb[pi]], eng="pool")
                    tt(Pb[pi][:16, 4, 0:nq], Pb[pi][:16, 4, 0:nq], ET[:16, 4, h, 0:nq], ALU.mult, [r_Pb[pi], r_ET], [r_Pb[pi]], eng="pool")
                else:
                    act(Pb[pi][:, 4, :], sattB, AF.Exp, [r_attB], [r_Pb[pi]])
                    act(Pb[pi][:, 0:4, :], sattA, AF.Exp, [r_att[ai]], [r_Pb[pi]])
                    tt(Pb[pi][:, :, :], Pb[pi][:, :, :], ET[:, :, h, :], ALU.mult, [r_Pb[pi], r_ET], [r_Pb[pi]], eng="pool")
                if pend:
                    ph, ppi = pend.pop(0)
                    pv(ph, ppi)
                    if ph == 3:
                        normalize(0)
                pend.append((h, pi))

            def tail():
                ph, ppi = pend.pop(0)
                pv(ph, ppi)
                normalize(1)
                pt, rt = tbank()
                for kc in range(4):
                    S.emit("pe", lambda e, kc=kc, pt=pt: e.transpose(out=pt[:, kc * 128:kc * 128 + nq], in_=oa[:nq, j, kc * 128:(kc + 1) * 128],
                                                                      identity=identb[:nq, :nq]), [r_oa[j], r_ident], [rt])
                cp(oaT[:, :, j * 128:j * 128 + nq], pt[:, 0:512].rearrange("p (k t) -> p k t", k=4)[:, :, 0:nq], [rt], [r_oaT[j]], eng="act")
            for h in range(8):
                att_steps.append(lambda h=h: head(h))
            att_steps.append(tail)
        for j in range(nsub):
            make_att(j)

        def make_h(j):
            nt = nts[j]

            def h_at():
                pb, rb = gbank()
                for hb in range(4):
                    if sample:
                        qrhs = Zq[:, hb, 0, 0:nt]
                    else:
                        base = Zq[:, hb, 2 * j, 0:64]
                        qrhs = AP(base.tensor, base.offset, [list(base.ap[0]), [192, 2], [1, 64]])
                    mm(pb[:nt, hb * 128:hb * 128 + nt], ktT[:, hb, j * 128:j * 128 + nt], qrhs, True, True, [r_ktT[hb], r_Zq[hb]], [rb])
                tt(AT[:nt, :, 0:nt], pb[:nt, :].rearrange("p (h t) -> p h t", h=4)[:, :, 0:nt],
                   maskb[:nt, 0:nt].unsqueeze(1).to_broadcast([nt, 4, nt]), ALU.mult, [rb, r_mask], [r_AT])

            def h_chunk(p):
                ci = j * cps + p
                tt(Sp[:, p], Sst[:], dec[:, 0, :, ci:ci + 1].to_broadcast([128, 4, 128]), ALU.mult, [r_S, r_dec], [r_Sp[p]])
                s_update(j, ci)

            def h_out():
                ob_, rob = gbank()
                for hb in range(4):
                    for p in range(cps):
                        zl = Zq[:, hb, 0, 0:nt] if sample else Zq[:, hb, 2 * j + p, :]
                        mm(ob_[:nt, hb * 128:(hb + 1) * 128], zl, Sp[:, p, hb, :], p == 0, False, [r_Zq[hb], r_Sp[p]], [rob])
                    mm(ob_[:nt, hb * 128:(hb + 1) * 128], AT[:nt, hb, 0:nt], vh[:nt, j, hb * 128:(hb + 1) * 128], False, True, [r_AT] + rl_vh[j], [rob])
                act(sqb[:nt, :], ob_[:nt, :], AF.Square, [rob], [r_sqb])
                S.emit("dve", lambda e: e.tensor_reduce(out=ssq[:nt, 0:4], in_=sqb[:nt, :].rearrange("p (h v) -> p h v", h=4), axis=AX.X, op=ALU.add),
                       [r_sqb], [r_ssq])
                ts(ssq[:nt, :], ssq[:nt, :], 1.0 / 128, EPS, ALU.mult, ALU.add, [r_ssq], [r_ssq], eng="pool")
                tt(ssq[:nt, :], ssq[:nt, :], nhalf[:nt, 0:4], ALU.pow, [r_ssq, r_nh], [r_ssq], eng="pool")
                tt(t1[:nt, :].rearrange("p (h v) -> p h v", h=4), ob_[:nt, :].rearrange("p (h v) -> p h v", h=4),
                   ssq[:nt, 0:4].unsqueeze(2).to_broadcast([nt, 4, 128]), ALU.mult, [rob, r_ssq], [r_t1])
                tt(gG[:nt, :], sgb[:nt, j, :], gnrep[:nt].rearrange("p h v -> p (h v)"), ALU.mult, [r_sgb[j], r_gn], [r_gG], eng="pool")
                tt(ob[:nt, :], t1[:nt, :], gG[:nt, :], ALU.mult, [r_t1, r_gG], [r_ob])

            def h_tr():
                pt, rt = tbank()
                for hb in range(4):
                    S.emit("pe", lambda e, hb=hb, pt=pt: e.transpose(out=pt[:, hb * 128:hb * 128 + nt], in_=ob[:nt, hb * 128:(hb + 1) * 128],
                                                                      identity=identb[:nt, :nt]), [r_ob, r_ident], [rt])
                cp(obT[:, :, j * 128:j * 128 + nt], pt[:, 0:512].rearrange("p (k t) -> p k t", k=4)[:, :, 0:nt], [rt], [r_obT[j]], eng="act")
            h_steps.append(lambda: khat_transpose(j))
            h_steps.append(h_at)
            for p in range(cps):
                h_steps.append(lambda p=p: h_chunk(p))
            h_steps.append(h_out)
            h_steps.append(h_tr)
        for j in range(nsub):
            make_h(j)

        def run_steps(steps, pool, tr, pre=None):
            def f():
                gstate["pool"] = pool
                gstate["tr"] = tr
                if pre is not None:
                    pre()
                for st_ in steps:
                    st_()
            return f
        att_ops = S.capture(run_steps(att_steps, [], (0,)))
        z_ops = S.capture(run_steps(z_steps, gpool_full[0:1], (0,)))
        h_ops = S.capture(run_steps(h_steps, gpool_full[1:2], (1,), pre=hgrn_gates_all))
        S.emit_merged(att_ops, z_ops, h_ops, d_ops)
        gstate["tr"] = (0, 1)
        if after_norm is not None:
            after_norm()
            after_norm = None

        gstate["pool"] = gpool_front + gpool_back
        wva, rwa, sla = w_next("a")
        wstate["consumed"] += 1
        wvb, rwb, slb = w_next("b")
        wstate["consumed"] -= 1
        for ct in range(8):
            pb, rb = gbank()
            for kc in range(4):
                mm(pb[:, 0:ntok], wva[:, kc, ct * 128:(ct + 1) * 128], oaT[:, kc, 0:ntok], kc == 0, kc == 3, [rwa] + r_oaT[:nsub], [rb])
            for kc in range(4):
                mm(pb[:, 256:256 + ntok], wvb[:, kc, ct * 128:(ct + 1) * 128], obT[:, kc, 0:ntok], kc == 0, kc == 3, [rwb] + r_obT[:nsub], [rb])
            stt(m1[:, 0:ntok], zs[:, ct, 0:ntok], 1.0, pb[:, 0:ntok], ALU.add, ALU.mult, [rb, r_zs[ct]], [r_m1])
            stt(m2[:, 0:ntok], zs[:, 8 + ct, 0:ntok], 1.0, pb[:, 256:256 + ntok], ALU.add, ALU.mult, [rb, r_zs[8 + ct]], [r_m2])
            tt(mT[:, ct, 0:ntok], m1[:, 0:ntok], m2[:, 0:ntok], ALU.add, [r_m1, r_m2], [r_mT[ct]], eng="pool")
        w_done(sla)
        w_done(slb)

        for n in range(2):
            wv, rw, sl = w_next("o%d" % n)
            for j in range(nsub):
                nt = nts[j]
                pb, rb = gbank()
                for kc in range(8):
                    mm(pb[:nt, :], mT[:, kc, j * 128:j * 128 + nt], wv[:, kc, :], kc == 0, kc == 7, [rw, r_mT[kc]], [rb])
                stt(Xb[:nt, j, n * 512:(n + 1) * 512], pb[:nt, :], 0.5, Xb[:nt, j, n * 512:(n + 1) * 512], ALU.mult, ALU.add, [rX[j], rb], [rX[j]])
            w_done(sl)
        norm_T(Xb, rX, gffn, r_gffn, 2, 2, ntok)
        hT = hTs[2]
        r_hT = r_hTs[2]
        rhT_all = r_hT[:nsub]

        ffn_pend = []
        for gi in range(6):
            ntile = 4 if gi < 5 else 2
            if gi == 2 and prenorm is not None:
                prenorm()
            wvg, rwg, slg = w_next("g%d" % gi)
            if mode != "pre":
                wstate["consumed"] += 1
                wvu, rwu, slu = w_next("u%d" % gi)
                wstate["consumed"] -= 1
            for ct in range(ntile):
                ft = gi * 4 + ct
                pb, rb = gbank()
                if mode == "pre":
                    for kc in range(8):
                        mm(pb[:, 0:2], wvg[:, kc, ct * 128:(ct + 1) * 128], hT[:, kc, ntok - 2:ntok], kc == 0, kc == 7, [rwg] + rhT_all, [rb])
                    cp(cprev[:, ft, :], pb[:, 0:2], [rb], [r_cprev])
                    continue
                for kc in range(8):
                    mm(pb[:, 0:ntok], wvg[:, kc, ct * 128:(ct + 1) * 128], hT[:, kc, 0:ntok], kc == 0, kc == 7, [rwg] + rhT_all, [rb])
                for kc in range(8):
                    mm(pb[:, 256:256 + ntok], wvu[:, kc, ct * 128:(ct + 1) * 128], hT[:, kc, 0:ntok], kc == 0, kc == 7, [rwu] + rhT_all, [rb])
                bi = ft % 2
                a_ = aT[bi]
                cp(a_[:, 0:2], cprev[:, ft, :], [r_cprev], [r_aT[bi]], eng="pool")
                act(a_[:, 2:2 + ntok], pb[:, 0:ntok], AF.Copy, [rb], [r_aT[bi]])
                cp(cprev[:, ft, :], a_[:, ntok:ntok + 2], [r_aT[bi]], [r_cprev], eng="pool")
                act(c1[bi][:, 0:ntok], pb[:, 0:ntok], AF.Identity, [rb, r_cw, r_cb], [r_c1[bi]], scale=cw[:, ft, 2:3], bias=cb[:, ft:ft + 1])
                stt(c2[bi][:, 0:ntok], a_[:, 1:1 + ntok], cw[:, ft, 1:2], c1[bi][:, 0:ntok], ALU.mult, ALU.add, [r_aT[bi], r_c1[bi], r_cw], [r_c2[bi]])
                stt(c1[bi][:, 0:ntok], a_[:, 0:ntok], cw[:, ft, 0:1], c2[bi][:, 0:ntok], ALU.mult, ALU.add, [r_aT[bi], r_c2[bi], r_cw], [r_c1[bi]])
                def fin(bi=bi, ft=ft, pb=pb, rb=rb):
                    act(c2[bi][:, 0:ntok], c1[bi][:, 0:ntok], AF.Gelu_apprx_tanh, [r_c1[bi]], [r_c2[bi]])
                    tt(gT[:, ft, 0:ntok], c2[bi][:, 0:ntok], pb[:, 256:256 + ntok], ALU.mult, [r_c2[bi], rb], [r_gT[ft]])
                if ffn_pend:
                    ffn_pend.pop(0)()
                ffn_pend.append(fin)
            w_done(slg)
            if mode != "pre":
                w_done(slu)
        if mode == "pre":
            return
        while ffn_pend:
            ffn_pend.pop(0)()

        for n in range(2):
            banks = [gbank() for _ in range(nsub)]
            for gk in range(3):
                wv, rw, sl = w_next("d%d_%d" % (n, gk))
                nk = 8 if gk < 2 else 6
                for kl in range(nk):
                    kc = gk * 8 + kl
                    for j in range(nsub):
                        nt = nts[j]
                        mm(banks[j][0][:nt, :], gT[:, kc, j * 128:j * 128 + nt], wv[:, kl, :], kc == 0, kc == NFT - 1, [rw, r_gT[kc]], [banks[j][1]])
                w_done(sl)
            for j in range(nsub):
                nt = nts[j]
                tt(Xb[:nt, j, n * 512:(n + 1) * 512], Xb[:nt, j, n * 512:(n + 1) * 512], banks[j][0][:nt, :], ALU.add, [rX[j], banks[j][1]], [rX[j]])
        def final_norm():
            for j in range(nsub):
                nt = nts[j]
                act(XN[:nt, j, :], Xb[:nt, j, :], AF.Square, [rX[j]], [r_XN[j], r_ss], accum_out=ss[:nt, 4 + j:5 + j])
                ts(ss[:nt, 4 + j:5 + j], ss[:nt, 4 + j:5 + j], 1.0 / D, EPS, ALU.mult, ALU.add, [r_ss], [r_ss], eng="pool")
                tt(ss[:nt, 4 + j:5 + j], ss[:nt, 4 + j:5 + j], nhalf[:nt, 0:1], ALU.pow, [r_ss, r_nh], [r_ss], eng="pool")
                stt(Xb[:nt, j, :], Xb[:nt, j, :], ss[:nt, 4 + j:5 + j], gfin[:nt, :], ALU.mult, ALU.mult, [rX[j], r_ss, r_gfin], [rX[j]])
                dma("act", out_row[j * 128:j * 128 + nt, :], Xb[:nt, j, :], [rX[j]], ())
        if deferred is not None and mode == "main":
            deferred.append(final_norm)
        else:
            final_norm()

    cur = {}

    def kv_proj_only():
        ntok, gt0 = cur["ntok"], cur["gt0"]
        for ct in range(4):
            pb, rb = gbank()
            for kc in range(8):
                mm(pb[:, 0:ntok], W1v[2][:, kc, ct * 128:(ct + 1) * 128], cur["hT"][:, kc, 0:ntok], kc == 0, kc == 7, [r_wslot[2]] + cur["r_hT"], [rb])
            for j in range(2):
                s_ = (gt0 + j) % 6
                cp(kTr[:, ct, s_, :], pb[:, j * 128:(j + 1) * 128], [rb], [r_kT[s_]], eng="act")
        for j in range(2):
            s_ = (gt0 + j) % 6
            pb, rb = gbank()
            for kc in range(8):
                mm(pb[:, :], cur["hT"][:, kc, j * 128:(j + 1) * 128], W1v[3][:, kc, :], kc == 0, kc == 7, [r_wslot[3], cur["r_hT"][j]], [rb])
            cp(Vr[:, s_, :, 0:64], pb[:, :].rearrange("p (h d) -> p h d", h=8), [rb], [r_V[s_]], eng="act")
            cp(Vr[:, s_, :, 64:65], hvt[:, 0:1].unsqueeze(2).to_broadcast([128, 8, 1]), [r_hv], [r_V[s_]], eng="pool")


    xa = xin.ap()
    nscan = n_scan
    nmain = n_main
    plan = []
    for m in range(nscan):
        plan.append(("scan", xa[m * T:(m + 1) * T, :], T, dict(gt0=-5 + 2 * (m - (nscan - 2)) + 6, kv=m >= nscan - 2)))
    if do_pre:
        plan.append(("pre", xa[NPRE:NPRE + PRE_T, :], PRE_T, dict(gt0=5)))
    for m in range(nmain):
        fk = (okT_d, ov_d, (m - (nmain - 2)) * T) if (m >= nmain - 2 and not NOFK) else None
        r0 = NPRE + PRE_T + m * T
        plan.append(("main", xa[r0:r0 + T, :], T, dict(gt0=2 * m + 6, out_row=y_d.ap()[m * T:(m + 1) * T, :], final_kv=fk)))
    if do_sample:
        plan.append(("sample", xs_d.ap(), 16, dict(gt0=0, out_row=ys_d.ap(), final_kv=(okTs_d, ovs_d, 0))))
    if plan:
        load_x(plan[0][1], plan[0][2], 0)
    if not do_conv:
        conv_jobs.clear()
    dfr = []
    for i, (mode, xsrc, ntok, kw) in enumerate(plan):
        xb = i % 2
        nxt = None
        pren = None
        if i + 1 < len(plan):
            nxt = (lambda p=plan[i + 1], b=(i + 1) % 2: load_x(p[1], p[2], b))
            if mode == "main":
                pren = (lambda p=plan[i + 1], b=(i + 1) % 2: norm_T(X[b], r_X[b], gmix, r_gmix, 0, b, p[2]))
        normed = i > 0 and plan[i - 1][0] == "main"
        if mode == "scan":
            if i > 0:
                continue

            def stage(k, part):
                md, xs_, nt_, kw_ = plan[k]
                nx = None
                if part == "f1":
                    emit_conv(2)
                    nx = (lambda p=plan[k + 1], b=(k + 1) % 2: load_x(p[1], p[2], b)) if k + 1 < len(plan) else None
                if part == "f2":
                    cur["ntok"] = nt_
                    cur["gt0"] = kw_["gt0"]
                macro_tile("scan", nt_, k % 2, hsel=k % 2, after_norm=nx, part=part, bsel=k % 2, vsel=k % 3, ksel=k % 2, **kw_)
            for w in range(-3, nscan):
                lists = []
                for off, part in ((0, "upd"), (1, "gates"), (2, "f2"), (3, "f1")):
                    k = w + off
                    if 0 <= k < nscan:
                        lists.append(S.capture(lambda k=k, part=part: stage(k, part)))
                S.emit_merged(*lists)
            continue
        if i == nscan:
            emit_conv(len(conv_jobs))
            for s_ in range(NSLOT):
                w_issue(s_)
        if mode == "sample":
            dma("act", oS_d.ap().rearrange("h k v -> k h v"), Sst[:], [r_S], ())
            dma("act", oconv_d.ap(), cprev[:], [r_cprev], ())
            dma("pool", kTr[:, :, 0:4, :], ckT_d.ap().rearrange("p c (t k) -> p c t k", t=4), (), r_kT[0:4])
            for t_ in range(4):
                dma("pool", Vr[:, t_, :, 0:64], cv_d.ap()[:, t_ * 128:(t_ + 1) * 128, :].rearrange("h p d -> p h d"), (), [r_V[t_]])
                cp(Vr[:, t_, :, 64:65], onesT[:, 0:8].unsqueeze(2), [r_ones], [r_V[t_]], eng="pool")
            dma("sp", Sst[:], s0_d.ap().rearrange("h k v -> k h v"), (), [r_S])
            dma("sp", cprev[:], cprev_d.ap(), (), [r_cprev])
        macro_tile(mode, ntok, xb, hsel=xb, normed=normed, after_norm=nxt, prenorm=pren, deferred=dfr, **kw)
    while dfr:
        dfr.pop(0)()
    emit_conv(len(conv_jobs))
    if not do_sample:
        dma("act", oS_d.ap().rearrange("h k v -> k h v"), Sst[:], [r_S], ())
        dma("act", oconv_d.ap(), cprev[:], [r_cprev], ())
    else:
        dma("act", oSs_d.ap().rearrange("h k v -> k h v"), Sst[:], [r_S], ())
        dma("act", oconvs_d.ap(), cprev[:], [r_cprev], ())
    for nm, getter in dumps:
        ap_, res_ = getter(locals())
        d_ = nc.dram_tensor("dbg_" + nm, list(ap_.shape), ap_.dtype if hasattr(ap_, "dtype") else F32, kind="ExternalOutput")
        dma("pool", d_.ap(), ap_, res_, ())

    S.finalize(st)
    with nc.Block() as block:
        S.run(block)
    st.close()
    return nc


_NC_CACHE = {}


def kernel(x_prompt, x_sample, cache_attn_k, cache_attn_v, state_hgrn, state_ffn_conv,
           norm_mix_g, w_in, rel_bias, hgrn_lb_logits, hgrn_norm_g, w_branch_a, w_branch_b, w_out,
           norm_ffn_g, w_ffn_gate, w_ffn_up, ffn_conv_w, ffn_conv_b, w_ffn_down, norm_final_g):
    f32 = np.float32
    A = lambda a: np.ascontiguousarray(np.asarray(a, dtype=f32))
    x_prompt = A(x_prompt)
    if "nc" not in _NC_CACHE:
        _NC_CACHE["nc"] = build_program()
    nc = _NC_CACHE["nc"]
    s_idx = np.arange(128)
    mask = ((s_idx[:, None] // 64 == s_idx[None, :] // 64) & (s_idx[:, None] <= s_idx[None, :])).astype(f32)
    shared = {
        "w_in": A(w_in[0]), "w_a": A(w_branch_a[0]), "w_b": A(w_branch_b[0]), "w_o": A(w_out[0]),
        "w_g": A(w_ffn_gate[0]), "w_u": A(w_ffn_up[0]), "w_d": A(w_ffn_down[0]),
        "gmix": A(np.asarray(norm_mix_g[0]).reshape(8, 128).T), "gffn": A(np.asarray(norm_ffn_g[0]).reshape(8, 128).T),
        "gfin": A(norm_final_g), "rel": A(rel_bias[0]),
        "lbl": A(np.asarray(hgrn_lb_logits).reshape(2, 4, 128).transpose(2, 0, 1)),
        "gn": A(hgrn_norm_g[0]),
        "cw": A(np.asarray(ffn_conv_w[0]).reshape(3, NFT, 128).transpose(2, 1, 0)),
        "cb": A(np.asarray(ffn_conv_b[0]).reshape(NFT, 128).T),
        "ident": np.eye(128, dtype=f32), "mask": mask, "jmat": np.ascontiguousarray(np.eye(128, dtype=f32)[::-1]),
    }
    in_maps = []
    for c in range(8):
        b, j = divmod(c, 4)
        s = j * SEG
        lo = s - PRE_T - NPRE
        xin = np.zeros((NTOK_IN, D), f32)
        a0 = max(lo, 0)
        xin[a0 - lo:] = x_prompt[b, a0:s + SEG]
        m = dict(shared)
        m["xin"] = xin
        m["hv"] = np.full((128, 1), 1.0 if j > 0 else 0.0, f32)
        m["xs"] = A(x_sample[c])
        m["ckT"] = A(np.asarray(cache_attn_k[0, c]).transpose(0, 2, 1).reshape(4, 128, 512).transpose(1, 0, 2))
        m["cv"] = A(cache_attn_v[0, c])
        m["s0"] = A(state_hgrn[0, c])
        m["cprev"] = A(np.asarray(state_ffn_conv[0, c]).reshape(2, NFT, 128).transpose(2, 1, 0))
        in_maps.append(m)
    res = run_bass_kernel_spmd(nc, in_maps, core_ids=list(range(8)))
    R_ = res.results
    B = 2
    y_prompt = np.stack([np.concatenate([R_[b * 4 + j]["y"] for j in range(4)], axis=0) for b in range(B)])
    y_sample = np.stack([R_[c]["ys"] for c in range(8)])

    def kT_to_rows(a):
        n = a.shape[-1]
        return a.reshape(8, 64, n).transpose(0, 2, 1)

    def v_to_rows(a):
        n = a.shape[0]
        return a.reshape(n, 8, 64).transpose(1, 0, 2)

    def conv_rows(a):
        return a.transpose(2, 1, 0).reshape(2, DFF)

    last = [3, 7]
    new_k_p = np.stack([kT_to_rows(R_[c]["okT"]) for c in last])[None]
    new_v_p = np.stack([v_to_rows(R_[c]["ov"]) for c in last])[None]
    hg_p = np.stack([R_[c]["oS"] for c in last])[None]
    cv_p = np.stack([conv_rows(R_[c]["oconv"]) for c in last])[None]
    new_k_s = np.stack([kT_to_rows(R_[c]["okTs"]) for c in range(8)])[None]
    new_v_s = np.stack([v_to_rows(R_[c]["ovs"]) for c in range(8)])[None]
    hg_s = np.stack([R_[c]["oSs"] for c in range(8)])[None]
    cv_s = np.stack([conv_rows(R_[c]["oconvs"]) for c in range(8)])[None]
    outs = (y_prompt, y_sample, new_k_p, new_v_p, hg_p, cv_p, new_k_s, new_v_s, hg_s, cv_s)
    return tuple(np.ascontiguousarray(o, dtype=f32) for o in outs)
```

```python
import numpy as np
from contextlib import ExitStack
import concourse.bass as bass
import concourse.mybir as mybir
from concourse.bass import AP
from concourse.bass_utils import run_bass_kernel_spmd

F32 = mybir.dt.float32
BF16 = mybir.dt.bfloat16
AF = mybir.ActivationFunctionType
ALU = mybir.AluOpType
AX = mybir.AxisListType

D = 1024
DFF = 2816
NFT = 22
SEG = 4096
NPRE = 12288
PRE_T = 128
NTOK_IN = NPRE + PRE_T + SEG
T = 256
EPS = 1e-6
NSLOT = 6
STOP = 99
STOPMODE = 'main'
NOFK = False
SLOT_E = 4096


class Res:
    __slots__ = ("name", "w", "r", "excl")

    def __init__(self, name, excl=False):
        self.name = name
        self.w = None
        self.r = {}
        self.excl = excl


class Op:
    __slots__ = ("eng", "fn", "deps", "is_dma", "needs_inc", "sem", "val", "idx")


class Sched:
    ENGS = ("pe", "act", "dve", "pool", "sp")

    def __init__(self, nc, n_dma_sems=8):
        self.nc = nc
        self.ops = []
        self.n_dma_sems = n_dma_sems
        self.cap = None

    def capture(self, fn):
        assert self.cap is None
        self.cap = []
        try:
            fn()
            return self.cap
        finally:
            self.cap = None

    DUR = {"pe": 0.15, "act": 0.75, "dve": 0.85, "pool": 1.1, "sp": 0.3}

    def emit_merged(self, *lists):
        eng_free = {}
        ready = {}
        rdone = {}
        LAT = 0.25

        def start_of(op):
            eng, fn, reads, writes, dma, cost = op
            t = eng_free.get(eng, 0.0)
            for r in reads:
                t = max(t, ready.get(id(r), 0.0) + LAT)
            for w in writes:
                t = max(t, ready.get(id(w), 0.0) + LAT, rdone.get(id(w), 0.0) + LAT)
            return t

        def commit(op, t):
            eng, fn, reads, writes, dma, cost = op
            d = 2.5 if dma else (cost if cost is not None else self.DUR[eng])
            eng_free[eng] = t + (0.1 if dma else d)
            for r in reads:
                rdone[id(r)] = max(rdone.get(id(r), 0.0), t + d)
            for w in writes:
                ready[id(w)] = t + d
            self.emit(eng, fn, reads, writes, dma)

        pos = [0] * len(lists)
        while True:
            best, bt = -1, None
            for k, l in enumerate(lists):
                if pos[k] < len(l):
                    t = start_of(l[pos[k]])
                    if bt is None or t < bt:
                        best, bt = k, t
            if best < 0:
                break
            commit(lists[best][pos[best]], bt)
            pos[best] += 1

    def emit(self, eng, fn, reads=(), writes=(), dma=False, cost=None):
        if self.cap is not None:
            self.cap.append((eng, fn, tuple(reads), tuple(writes), dma, cost))
            return None
        op = Op()
        op.eng = eng
        op.fn = fn
        op.is_dma = dma
        op.needs_inc = dma
        op.sem = None
        op.val = 0
        op.idx = len(self.ops)
        deps = {}
        xr = [r for r in reads if r.excl]
        if xr:
            reads = [r for r in reads if not r.excl]
            writes = list(writes) + [r for r in xr if r not in writes]
        for r in reads:
            if r.w is not None:
                deps[r.w.idx] = r.w
        for w in writes:
            if w.w is not None:
                deps[w.w.idx] = w.w
            for o in w.r.values():
                deps[o.idx] = o
        op.deps = list(deps.values())
        for r in reads:
            key = (eng, op.idx) if dma else (eng, -1)
            r.r[key] = op
        for w in writes:
            w.w = op
            w.r = {}
        self.ops.append(op)
        return op

    def finalize(self, stack):
        nc = self.nc
        for op in self.ops:
            for d in op.deps:
                if d.is_dma or d.eng != op.eng or op.eng != "pe" or op.is_dma:
                    d.needs_inc = True
        csem = {e: stack.enter_context(nc.semaphore("cs_" + e)) for e in ("pe", "act", "dve", "pool")}
        dsem = {e: [stack.enter_context(nc.semaphore("ds_%s%d" % (e, i))) for i in range(self.n_dma_sems)]
                for e in ("sp", "pool", "act")}
        ccount = {e: 0 for e in csem}
        dstate = {e: [None] * self.n_dma_sems for e in dsem}
        duse = {e: [0] * self.n_dma_sems for e in dsem}
        drr = {e: 0 for e in dsem}
        waited = {e: {} for e in self.ENGS}
        streams = {e: [] for e in self.ENGS}
        for op in self.ops:
            waits = []
            e = op.eng
            extra = []
            if op.is_dma:
                k = drr[e] % self.n_dma_sems
                drr[e] += 1
                prev = dstate[e][k]
                if prev is not None:
                    extra.append(prev)
                duse[e][k] += 1
                op.sem = dsem[e][k]
                op.val = 16 * duse[e][k]
                dstate[e][k] = op
            elif op.needs_inc:
                ccount[e] += 1
                op.sem = csem[e]
                op.val = ccount[e]
            for d in op.deps + extra:
                if (not d.is_dma) and d.eng == e and e == "pe" and not op.is_dma:
                    continue
                key = id(d.sem)
                if waited[e].get(key, 0) >= d.val:
                    continue
                waited[e][key] = d.val
                waits.append((d.sem, d.val))
            streams[e].append((waits, op))
        fin = []
        for e in dsem:
            for k in range(self.n_dma_sems):
                if duse[e][k] and waited["sp"].get(id(dsem[e][k]), 0) < 16 * duse[e][k]:
                    fin.append((dsem[e][k], 16 * duse[e][k]))
        self.streams = streams
        self.fin = fin

    def run(self, block):
        streams = self.streams
        fin = self.fin

        def body(name):
            def f(eng):
                for waits, op in streams[name]:
                    for s, v in waits:
                        eng.wait_ge(s, v)
                    inst = op.fn(eng)
                    if op.sem is not None:
                        inst.then_inc(op.sem, 16 if op.is_dma else 1)
                if name == "sp":
                    for s, v in fin:
                        eng.wait_ge(s, v)
            return f
        block.tensor(body("pe"))
        block.scalar(body("act"))
        block.vector(body("dve"))
        block.gpsimd(body("pool"))
        block.sync(body("sp"))


def build_program(n_scan=NPRE // T, n_main=SEG // T, do_pre=True, do_sample=True, dumps=(), do_conv=True, init_stop=99):
    NPRE = n_scan * T
    NTOK_IN = NPRE + PRE_T + n_main * T
    nc = bass.Bass("TRN2", target_bir_lowering=False)
    st = ExitStack()
    S = Sched(nc)

    def din(name, shape):
        return nc.dram_tensor(name, list(shape), F32, kind="ExternalInput")

    def dout(name, shape):
        return nc.dram_tensor(name, list(shape), F32, kind="ExternalOutput")

    xin = din("xin", [NTOK_IN, D])
    hv_d = din("hv", [128, 1])
    xs_d = din("xs", [16, D])
    ckT_d = din("ckT", [128, 4, 512])
    cv_d = din("cv", [8, 512, 64])
    s0_d = din("s0", [4, 128, 128])
    cprev_d = din("cprev", [128, NFT, 2])
    w_in_d = din("w_in", [D, 5632])
    w_a_d = din("w_a", [512, D])
    w_b_d = din("w_b", [512, D])
    w_o_d = din("w_o", [D, D])
    w_g_d = din("w_g", [D, DFF])
    w_u_d = din("w_u", [D, DFF])
    w_d_d = din("w_d", [DFF, D])
    gmix_d = din("gmix", [128, 8])
    gffn_d = din("gffn", [128, 8])
    gfin_d = din("gfin", [D])
    rel_d = din("rel", [8, 192])
    lbl_d = din("lbl", [128, 2, 4])
    gn_d = din("gn", [128])
    cw_d = din("cw", [128, NFT, 3])
    cb_d = din("cb", [128, NFT])
    ident_d = din("ident", [128, 128])
    mask_d = din("mask", [128, 128])
    jmat_d = din("jmat", [128, 128])

    y_d = dout("y", [SEG, D])
    ys_d = dout("ys", [16, D])
    okT_d = dout("okT", [4, 128, 512])
    ov_d = dout("ov", [512, 512])
    oS_d = dout("oS", [4, 128, 128])
    oconv_d = dout("oconv", [128, NFT, 2])
    okTs_d = dout("okTs", [4, 128, 16])
    ovs_d = dout("ovs", [16, 512])
    oSs_d = dout("oSs", [4, 128, 128])
    oconvs_d = dout("oconvs", [128, NFT, 2])

    def scratch(name, shape, dt=BF16):
        return nc.dram_tensor(name, list(shape), dt, kind="Internal")

    wb_in = scratch("wb_in", [D, 5632])
    wb_a = scratch("wb_a", [512, D])
    wb_b = scratch("wb_b", [512, D])
    wb_o = scratch("wb_o", [D, D])
    wb_g = scratch("wb_g", [D, DFF])
    wb_u = scratch("wb_u", [D, DFF])
    wb_d = scratch("wb_d", [DFF, D])
    ext_d = scratch("ext_d", [8, 768], F32)

    def sb(name, shape, dt=F32):
        return st.enter_context(nc.sbuf_tensor(name, list(shape), dt))

    def R(name, excl=False):
        return Res(name, excl)

    def fsz(ap):
        n = 1
        for d_ in list(ap.shape)[1:]:
            n *= int(d_)
        return n

    def ecost(eng, ap):
        n = fsz(ap)
        return {"act": 0.25 + n / 1000.0, "dve": 0.1 + n / 850.0, "pool": 0.2 + n / 480.0}[eng]

    def act(out, in_, func, reads, writes, **kw):
        S.emit("act", lambda e: e.activation(out=out, in_=in_, func=func, **kw), reads, writes, cost=ecost("act", out))

    def tt(out, in0, in1, op, reads, writes, eng="dve"):
        S.emit(eng, lambda e: e.tensor_tensor(out=out, in0=in0, in1=in1, op=op), reads, writes, cost=ecost(eng, out))

    def ts(out, in0, s1, s2, op0, op1, reads, writes, eng="dve"):
        if op1 is None:
            S.emit(eng, lambda e: e.tensor_scalar(out=out, in0=in0, scalar1=s1, scalar2=None, op0=op0), reads, writes, cost=ecost(eng, out))
        else:
            S.emit(eng, lambda e: e.tensor_scalar(out=out, in0=in0, scalar1=s1, scalar2=s2, op0=op0, op1=op1), reads, writes, cost=ecost(eng, out))

    def stt(out, in0, scalar, in1, op0, op1, reads, writes):
        S.emit("dve", lambda e: e.scalar_tensor_tensor(out=out, in0=in0, scalar=scalar, in1=in1, op0=op0, op1=op1), reads, writes, cost=ecost("dve", out))

    def cp(out, in_, reads, writes, eng="dve"):
        if eng == "act":
            act(out, in_, AF.Copy, reads, writes)
        else:
            S.emit(eng, lambda e: e.tensor_copy(out=out, in_=in_), reads, writes, cost=ecost(eng, out))

    def recip(out, in_, reads, writes):
        S.emit("dve", lambda e: e.reciprocal(out=out, in_=in_), reads, writes)

    def mset(ap, val, writes, eng="pool"):
        S.emit(eng, lambda e: e.memset(ap, val), (), writes)

    def mm(out, lhsT, rhs, start, stop, reads, writes):
        S.emit("pe", lambda e: e.matmul(out, lhsT=lhsT, rhs=rhs, start=start, stop=stop), reads, writes, cost=0.05 + fsz(rhs) / 2000.0)

    def dma(eng, out, in_, reads, writes):
        S.emit(eng, lambda e: e.dma_start(out=out, in_=in_), reads, writes, dma=True)

    identb = sb("identb", [128, 128], BF16); r_ident = R("ident")
    maskb = sb("maskb", [128, 128], BF16); r_mask = R("mask")
    jb = sb("jb", [128, 128], BF16); r_j = R("j")
    epst = sb("epst", [128, 1]); r_eps = R("eps")
    onesT = sb("onesT", [128, 8]); r_ones = R("ones")
    hvt = sb("hvt", [128, 1]); r_hv = R("hv")
    gmix = sb("gmix_s", [128, 8]); r_gmix = R("gmix")
    gffn = sb("gffn_s", [128, 8]); r_gffn = R("gffn")
    gfin = sb("gfin_s", [128, D]); r_gfin = R("gfin")
    gnrep = sb("gnrep", [128, 4, 128]); r_gn = R("gn")
    cw = sb("cw_s", [128, NFT, 3]); r_cw = R("cw")
    cb = sb("cb_s", [128, NFT]); r_cb = R("cb")
    lbl = sb("lbl_s", [128, 2, 4]); r_lbl = R("lbl")
    lb = sb("lb_s", [128, 4]); oml = sb("oml_s", [128, 4]); r_lb = R("lb")
    ET = sb("ET", [128, 5, 8, 128], BF16); r_ET = R("ET")

    dma("pool", identb[:], ident_d.ap(), (), [r_ident])
    dma("pool", maskb[:], mask_d.ap(), (), [r_mask])
    dma("pool", jb[:], jmat_d.ap(), (), [r_j])
    mset(epst[:], EPS, [r_eps])
    nhalf = sb("nhalf", [128, 8]); r_nh = R("nhalf")
    mset(nhalf[:], -0.5, [r_nh])
    mset(onesT[:], 1.0, [r_ones])
    dma("sp", hvt[:], hv_d.ap(), (), [r_hv])
    dma("sp", gmix[:], gmix_d.ap(), (), [r_gmix])
    dma("sp", gffn[:], gffn_d.ap(), (), [r_gffn])
    dma("sp", gfin[:], AP(gfin_d, 0, [[0, 128], [1, D]]), (), [r_gfin])
    dma("sp", gnrep[:], AP(gn_d, 0, [[0, 128], [0, 4], [1, 128]]), (), [r_gn])
    dma("sp", cw[:], cw_d.ap(), (), [r_cw])
    dma("sp", cb[:], cb_d.ap(), (), [r_cb])
    dma("sp", lbl[:], lbl_d.ap(), (), [r_lbl])

    if init_stop <= 1:
        S.finalize(st)
        with nc.Block() as block:
            S.run(block)
        st.close()
        return nc
    ps_att = st.enter_context(nc.psum_tensor("ps_att", [128, 1536], F32))
    r_att = [R("att0", True), R("att1", True)]
    r_attB = R("attB", True)
    NGEN = 2
    ps_gen = [st.enter_context(nc.psum_tensor("ps_g%d" % i, [128, 512], F32)) for i in range(NGEN)]
    r_gen = [R("g%d" % i, True) for i in range(NGEN)]
    ps_o = st.enter_context(nc.psum_tensor("ps_o", [128, 512], F32))
    r_o = R("ps_o", True)
    ps_tr = [st.enter_context(nc.psum_tensor("ps_t%d" % i, [128, 1024], BF16)) for i in range(2)]
    r_tr = [R("t%d" % i, True) for i in range(2)]
    cnt = {"g": 0, "t": 0, "a": 0, "p": 0}

    gpool_full = [(ps_gen[i], r_gen[i]) for i in range(NGEN)]
    gpool_front = gpool_full + [(ps_o, r_o)]
    gpool_back = [(ps_att[:, 0:512], r_att[0]), (ps_att[:, 512:1024], r_att[1]), (ps_att[:, 1024:1536], r_attB)]
    gstate = {"pool": gpool_full, "tr": (0, 1)}

    def gbank():
        pool = gstate["pool"]
        i = cnt["g"] % len(pool)
        cnt["g"] += 1
        return pool[i]

    def tbank():
        sel = gstate["tr"]
        i = sel[cnt["t"] % len(sel)]
        cnt["t"] += 1
        return ps_tr[i], r_tr[i]

    lbt = sb("lbt", [128, 4])
    tt(lbt[:], lbl[:, 1, :], lbl[:, 0, :], ALU.subtract, [r_lbl], [r_lb])
    act(lbt[:], lbt[:], AF.Exp, [r_lb], [r_lb])
    ts(lbt[:], lbt[:], 1.0, None, ALU.add, None, [r_lb], [r_lb])
    recip(lb[:], lbt[:], [r_lb], [r_lb])
    ts(oml[:], lb[:], -1.0, 1.0, ALU.mult, ALU.add, [r_lb], [r_lb])
    omlh = sb("omlh_s", [128, 4]); lbh = sb("lbh_s", [128, 4])
    ts(omlh[:], oml[:], 0.5, None, ALU.mult, None, [r_lb], [r_lb])
    tt(lbh[:], lb[:], omlh[:], ALU.add, [r_lb], [r_lb])

    if init_stop <= 2:
        S.finalize(st)
        with nc.Block() as block:
            S.run(block)
        st.close()
        return nc
    if init_stop <= 3:
        S.finalize(st)
        with nc.Block() as block:
            S.run(block)
        st.close()
        return nc
    r_wb = {}
    conv_jobs = []
    for name, src, dst, rows in (("in", w_in_d, wb_in, D), ("a", w_a_d, wb_a, 512), ("b", w_b_d, wb_b, 512),
                                 ("o", w_o_d, wb_o, D), ("g", w_g_d, wb_g, D), ("u", w_u_d, wb_u, D),
                                 ("d", w_d_d, wb_d, DFF)):
        r_wb[name] = R("wb_" + name)
        for r0 in range(0, rows, 128):
            conv_jobs.append((name, dst.ap()[r0:r0 + 128, :], src.ap()[r0:r0 + 128, :]))

    def emit_conv(n):
        for _ in range(n):
            if conv_jobs:
                name, d_, s_ = conv_jobs.pop(0)
                dma("pool", d_, s_, (), [r_wb[name]])

    wslot = [sb("wslot%d" % i, [128, SLOT_E], BF16) for i in range(NSLOT)]
    r_wslot = [R("wslot%d" % i) for i in range(NSLOT)]
    W1v = []
    for i, c0 in enumerate((2048, 2560, 512, 1024)):
        v_ = wslot[i][:, :].rearrange("p (k n) -> p k n", k=8)
        dma("pool", v_, w_in_d.ap()[:, c0:c0 + 512].rearrange("(k p) n -> p k n", p=128), (), [r_wslot[i]])
        W1v.append(v_)

    def wgroups(mode):
        g = []
        for i in (3, 4, 5, 6, 0, 1, 2, 7, 8, 9, 10):
            g.append(("in%d" % i, "in", wb_in.ap()[:, 512 * i:512 * i + 512].rearrange("(k p) n -> p k n", p=128), 8, 512))
        g.append(("a", "a", wb_a.ap().rearrange("(k p) n -> p k n", p=128), 4, 1024))
        g.append(("b", "b", wb_b.ap().rearrange("(k p) n -> p k n", p=128), 4, 1024))
        for n in range(2):
            g.append(("o%d" % n, "o", wb_o.ap()[:, 512 * n:512 * n + 512].rearrange("(k p) n -> p k n", p=128), 8, 512))
        for gi in range(6):
            nc_ = 512 if gi < 5 else 256
            g.append(("g%d" % gi, "g", wb_g.ap()[:, 512 * gi:512 * gi + nc_].rearrange("(k p) n -> p k n", p=128), 8, nc_))
            if mode != "pre":
                g.append(("u%d" % gi, "u", wb_u.ap()[:, 512 * gi:512 * gi + nc_].rearrange("(k p) n -> p k n", p=128), 8, nc_))
        if mode != "pre":
            for n in range(2):
                for gk in range(3):
                    nk = 8 if gk < 2 else 6
                    g.append(("d%d_%d" % (n, gk), "d",
                              wb_d.ap()[1024 * gk:1024 * gk + 128 * nk, 512 * n:512 * n + 512].rearrange("(k p) n -> p k n", p=128), nk, 512))
        return g

    wq = []
    tiles_plan = ([("pre", 0)] if do_pre else []) + [("main", m) for m in range(n_main)] + ([("sample", 0)] if do_sample else [])
    for mode, _ in tiles_plan:
        wq.extend(wgroups(mode))
    wstate = {"issued": 0, "consumed": 0, "slot": {}}

    def w_issue(slot):
        i = wstate["issued"]
        if i >= len(wq):
            return
        key, wname, src, nk, ncol = wq[i]
        view = wslot[slot][:, 0:nk * ncol].rearrange("p (k n) -> p k n", k=nk)
        dma("sp", view, src, [r_wb[wname]], [r_wslot[slot]])
        wstate["slot"][i] = slot
        wstate["issued"] += 1

    def w_next(key):
        i = wstate["consumed"]
        assert wq[i][0] == key, (wq[i][0], key)
        slot = wstate["slot"][i]
        _, _, _, nk, ncol = wq[i]
        view = wslot[slot][:, 0:nk * ncol].rearrange("p (k n) -> p k n", k=nk)
        return view, r_wslot[slot], slot

    def w_done(slot):
        wstate["consumed"] += 1
        w_issue(slot)

    if init_stop <= 4:
        S.finalize(st)
        with nc.Block() as block:
            S.run(block)
        st.close()
        return nc
    X = [sb("X%d" % i, [128, 2, D]) for i in range(2)]
    r_X = [[R("X%d_%d" % (i, j)) for j in range(2)] for i in range(2)]
    XN = sb("XN", [128, 2, D], BF16); r_XN = [R("XN0"), R("XN1")]
    ss = sb("ss", [128, 8]); r_ss = R("ss")
    hTs = [sb("hTa", [128, 8, T], BF16), sb("hTb", [128, 8, T], BF16), sb("h2T", [128, 8, T], BF16)]
    r_hTs = [[R("hTa0"), R("hTa1")], [R("hTb0"), R("hTb1")], [R("h2T0"), R("h2T1")]]
    qT = sb("qT", [128, 4, T], BF16); r_qT = R("qT")
    kTr = sb("kTr", [128, 4, 6, 128], BF16); r_kT = [R("kT%d" % i) for i in range(6)]
    Vr = sb("Vr", [128, 6, 8, 65], BF16); r_V = [R("V%d" % i) for i in range(6)]
    Pb = [sb("Pb%d" % i, [128, 5, 128], BF16) for i in range(3)]; r_Pb = [R("Pb%d" % i) for i in range(3)]
    oa = sb("oa", [128, 2, 512], BF16); r_oa = [R("oa0"), R("oa1")]
    oaT = sb("oaT", [128, 4, T], BF16); r_oaT = [R("oaT0"), R("oaT1")]
    rec = sb("rec", [128, 8]); r_rec = R("rec")
    sg = sb("sg", [128, 4, T]); r_sg = [R("sg%d" % i) for i in range(4)]
    siluq = sb("siluq", [128, 4, T]); r_sq = [R("siluq%d" % i) for i in range(4)]
    gF = sb("gF", [128, 4 * T]); r_gF = R("gF")
    gL = sb("gL", [128, 4 * T]); r_gL = R("gL")
    gB = sb("gB", [128, 4 * T]); r_gB = R("gB")
    gK = sb("gK", [128, 4 * T]); r_gK = R("gK")
    gE = sb("gE", [128, 4 * T]); r_gE = R("gE")
    dec = sb("dec", [128, 3, 4, 4]); r_dec = R("dec")
    Zq = sb("Zq", [128, 4, 4, 128], BF16); r_Zq = [R("Zq%d" % i) for i in range(4)]
    ktT = sb("ktT", [128, 4, T], BF16); r_ktT = [R("ktT%d" % i) for i in range(4)]
    khT = sb("khT", [128, 4, T], BF16); r_khT = [R("khT%d" % i) for i in range(4)]
    khtm = sb("khtm", [128, 2, 512], BF16); r_khtm = [R("khtm0"), R("khtm1")]
    vh = sb("vh", [128, 2, 512], BF16); r_vh = [R("vh0"), R("vh1")]
    sgb = sb("sgb", [128, 2, 512]); r_sgb = [R("sgb0"), R("sgb1")]
    Sst = sb("Sst", [128, 4, 128]); r_S = R("S")
    Sp = sb("Sp", [128, 2, 4, 128], BF16); r_Sp = [R("Sp0"), R("Sp1")]
    AT = sb("AT", [128, 4, 128], BF16); r_AT = R("AT")
    sqb = sb("sqb", [128, 512]); r_sqb = R("sqb")
    ssq = sb("ssq", [128, 4]); r_ssq = R("ssq")
    t1 = sb("t1", [128, 512]); r_t1 = R("t1")
    gG = sb("gG", [128, 512]); r_gG = R("gG")
    ob = sb("ob", [128, 512], BF16); r_ob = R("ob")
    obT = sb("obT", [128, 4, T], BF16); r_obT = [R("obT0"), R("obT1")]
    zs = sb("zs", [128, 16, T], BF16); r_zs = [R("zs%d" % i) for i in range(16)]
    mT = sb("mT", [128, 8, T], BF16); r_mT = [R("mT%d" % i) for i in range(8)]
    aT = [sb("aT%d" % i, [128, T + 2]) for i in range(2)]; r_aT = [R("aT0"), R("aT1")]
    c1 = [sb("c1_%d" % i, [128, T]) for i in range(2)]; r_c1 = [R("c1_0"), R("c1_1")]
    c2 = [sb("c2_%d" % i, [128, T]) for i in range(2)]; r_c2 = [R("c2_0"), R("c2_1")]
    m1, r_m1, m2, r_m2 = c1[0], r_c1[0], c2[0], r_c2[0]
    gT = sb("gT", [128, NFT, T], BF16); r_gT = [R("gT%d" % i) for i in range(NFT)]
    cprev = sb("cprev_s", [128, NFT, 2]); r_cprev = R("cprev")

    ext_ = siluq[0:8].rearrange("p h t -> p (h t)")[:, 0:768]; r_ext = r_sq; r_extd = R("extd")
    dma("sp", ext_[:, 64:256], rel_d.ap(), (), r_ext)
    act(ext_[:, 0:64], ext_[:, 64:65].to_broadcast([8, 64]), AF.Identity, r_ext, r_ext)
    act(ext_[:, 256:768], ext_[:, 255:256].to_broadcast([8, 512]), AF.Identity, r_ext, r_ext)
    act(ext_[:, :], ext_[:, :], AF.Exp, r_ext, r_ext)
    dma("sp", ext_d.ap(), ext_[:, :], r_ext, [r_extd])
    hk = sg[:].rearrange("p h t -> p (h t)").rearrange("p (a b) -> p a b", a=8); r_hk = r_sg
    hkb = ktT[:].rearrange("p h t -> p (h t)").rearrange("p (a b) -> p a b", a=8); r_hkb = r_ktT
    for kt in range(5):
        dma("sp", hk, AP(ext_d, 512 - 128 * kt, [[1, 128], [768, 8], [1, 128]]), [r_extd], r_hk)
        cp(hkb, hk, r_hk, r_hkb)
        for n in range(2):
            pb, rb = gbank()
            mm(pb[:, :], jb[:], hkb[:, 4 * n:4 * n + 4, :].rearrange("p h q -> p (h q)"), True, True, [r_j] + r_hkb, [rb])
            cp(ET[:, kt, 4 * n:4 * n + 4, :].rearrange("p h q -> p (h q)"), pb[:, :], [rb], [r_ET])
    mset(ET[0:64, 0, :, 64:128], 0.0, [r_ET])
    mset(ET[64:128, 4, :, 0:64], 0.0, [r_ET])

    kst = gT[:, 0:8, :].rearrange("p a b -> p (a b)").bitcast(F32).rearrange("p (h t) -> p h t", h=4)
    vst = gT[:, 8:16, :].rearrange("p a b -> p (a b)").bitcast(F32).rearrange("p (j c) -> p j c", j=2)
    rl_kst = r_gT[0:8]
    rl_vst = [r_gT[8:12], r_gT[12:16]]
    sg2 = gT[:, 0:8, :].rearrange("p a b -> p (a b)").bitcast(F32).rearrange("p (h t) -> p h t", h=4)
    r_sg2 = [[r_gT[2 * i], r_gT[2 * i + 1]] for i in range(4)]
    vh2 = gT[:, 8:12, :].rearrange("p a b -> p (a b)").rearrange("p (j c) -> p j c", j=2)
    r_vh2 = [[r_gT[8], r_gT[9]], [r_gT[10], r_gT[11]]]
    vh3 = gT[:, 12:16, :].rearrange("p a b -> p (a b)").rearrange("p (j c) -> p j c", j=2)
    r_vh3 = [[r_gT[12], r_gT[13]], [r_gT[14], r_gT[15]]]
    ktT2 = gT[:, 16:20, :]
    r_ktT2 = r_gT[16:20]
    dec2 = sb("dec2", [128, 3, 4, 4]); r_dec2 = R("dec2")
    sgs = [(sg, [[r] for r in r_sg]), (sg2, r_sg2)]
    vhs = [(vh, [[r] for r in r_vh]), (vh2, r_vh2), (vh3, r_vh3)]
    kts = [(ktT, r_ktT, dec, r_dec), (ktT2, r_ktT2, dec2, r_dec2)]
    mset(Zq[:], 0.0, r_Zq)
    mset(Sst[:], 0.0, [r_S])
    mset(cprev[:], 0.0, [r_cprev])

    if init_stop <= 5:
        S.finalize(st)
        with nc.Block() as block:
            S.run(block)
        st.close()
        return nc
    def load_x(xsrc, ntok, xb):
        nsub = (ntok + 127) // 128
        for j in range(nsub):
            nt = min(128, ntok - 128 * j)
            dma("sp", X[xb][:nt, j, :], xsrc[j * 128:j * 128 + nt, :], (), [r_X[xb][j]])

    def norm_T(src_tile, rsrc, gcol, rg, col0, hsel, ntok):
        nsub = (ntok + 127) // 128
        dst, rdst = hTs[hsel], r_hTs[hsel]
        nts_ = [min(128, ntok - 128 * j) for j in range(nsub)]
        for j in range(nsub):
            nt = nts_[j]
            act(XN[:nt, j, :], src_tile[:nt, j, :], AF.Square, [rsrc[j]], [r_XN[j], r_ss], accum_out=ss[:nt, col0 + j:col0 + j + 1])
        for j in range(nsub):
            nt = nts_[j]
            ts(ss[:nt, col0 + j:col0 + j + 1], ss[:nt, col0 + j:col0 + j + 1], 1.0 / D, EPS, ALU.mult, ALU.add, [r_ss], [r_ss], eng="pool")
            tt(ss[:nt, col0 + j:col0 + j + 1], ss[:nt, col0 + j:col0 + j + 1], nhalf[:nt, 0:1], ALU.pow, [r_ss, r_nh], [r_ss], eng="pool")
        for j in range(nsub):
            nt = nts_[j]
            act(XN[:nt, j, :], src_tile[:nt, j, :], AF.Copy, [rsrc[j], r_ss], [r_XN[j]], scale=ss[:nt, col0 + j:col0 + j + 1])
            pt, rt = tbank()
            for kc in range(8):
                S.emit("pe", lambda e, kc=kc, j=j, nt=nt, pt=pt: e.transpose(out=pt[:, kc * 128:kc * 128 + nt], in_=XN[:nt, j, kc * 128:(kc + 1) * 128],
                                                                              identity=identb[:nt, :nt]), [r_XN[j], r_ident], [rt], cost=0.1)
            tt(dst[:, :, j * 128:j * 128 + nt], pt[:, :].rearrange("p (k t) -> p k t", k=8)[:, :, 0:nt],
               gcol[:, 0:8].unsqueeze(2).to_broadcast([128, 8, nt]), ALU.mult, [rt, rg], [rdst[j]])

    def macro_tile(mode, ntok, xb, gt0, hsel=0, normed=False, kv=False, out_row=None, final_kv=None, after_norm=None, prenorm=None,
                   part="all", bsel=0, vsel=0, ksel=0, deferred=None):
        sample = mode == "sample"
        nsub = (ntok + 127) // 128
        nts = [min(128, ntok - 128 * j) for j in range(nsub)]
        C = 16 if sample else (ntok if mode == "scan" else 64)
        nch = ntok // C
        ri = C - 1 if mode == "scan" else C // 2 - 1
        cps = 1 if sample else 2
        Xb = X[xb]
        rX = r_X[xb]
        hT = hTs[hsel]
        r_hT = r_hTs[hsel]
        rhT_all = r_hT[:nsub]
        cur["hT"] = hT
        cur["r_hT"] = r_hT
        sg, rl_sg = sgs[bsel]
        vh, rl_vh = vhs[vsel]
        ktT, r_ktT, dec, r_dec = kts[ksel]
        if mode == "scan":
            gstate["pool"] = gpool_front if part in ("f1", "f2") else gpool_back
            gstate["tr"] = (0,) if part in ("f1", "f2") else (1,)
        else:
            gstate["pool"] = gpool_front + gpool_back
            gstate["tr"] = (0, 1)

        if part in ("all", "f1"):
            if not normed:
                norm_T(Xb, rX, gmix, r_gmix, 0, hsel, ntok)
            if after_norm is not None and (deferred is None or not deferred):
                after_norm()
                after_norm = None

        def proj_fm(wv, rw, ct, evac):
            pb, rb = gbank()
            for kc in range(8):
                mm(pb[:, 0:ntok], wv[:, kc, ct * 128:(ct + 1) * 128], hT[:, kc, 0:ntok], kc == 0, kc == 7, [rw] + rhT_all, [rb])
            evac(pb, rb)

        def proj_tm(wv, rw, j, evac, ncol=512):
            pb, rb = gbank()
            nt = nts[j]
            for kc in range(8):
                mm(pb[:nt, 0:ncol], hT[:, kc, j * 128:j * 128 + nt], wv[:, kc, 0:ncol], kc == 0, kc == 7, [rw, r_hT[j]], [rb])
            evac(pb, rb, j, nt)

        def slot_of(j):
            return (gt0 + j) % 6 if not sample else 4

        def hgrn_gates_all():
            n4 = 4 * ntok
            nc4 = 4 * nch
            fl = lambda b: b[:, 0:n4]
            v3 = lambda b: b[:, 0:n4].rearrange("p (h t) -> p h t", h=4)
            ch = lambda b: b[:, 0:n4].rearrange("p (c t) -> p c t", t=C)
            hc = lambda b: b[:, 0:n4].rearrange("p (h c t) -> p h c t", h=4, t=C)
            rsg = [r for l_ in rl_sg for r in l_]
            sgf = sg.rearrange("p h t -> p (h t)")
            tt(v3(gF), v3(sgf), omlh[:, 0:4].unsqueeze(2).to_broadcast([128, 4, ntok]), ALU.mult, rsg + [r_lb], [r_gF])
            tt(v3(gF), v3(gF), lbh[:, 0:4].unsqueeze(2).to_broadcast([128, 4, ntok]), ALU.add, [r_gF, r_lb], [r_gF])
            act(fl(gL), fl(gF), AF.Ln, [r_gF], [r_gL])
            S.emit("dve", lambda e: e.tensor_tensor_scan(out=fl(gB), data0=fl(gL), data1=fl(gL), initial=0.0, op0=ALU.add, op1=ALU.min),
                   [r_gL], [r_gB], cost=0.1 + 2 * n4 / 900.0)
            ts(fl(gK), fl(gF), -1.0, 1.0, ALU.mult, ALU.add, [r_gF], [r_gK], eng="pool")
            tt(ch(gL), ch(gB), ch(gB)[:, :, ri:ri + 1].to_broadcast([128, nc4, C]), ALU.subtract, [r_gB, r_gL], [r_gL])
            act(fl(gE), fl(gL), AF.Exp, [r_gL], [r_gE], scale=-1.0)
            tt(ktT[:, :, 0:ntok], v3(gK), v3(gE), ALU.mult, [r_gK, r_gE], r_ktT, eng="pool")
            dsl = lambda i: dec[:, i, :, 0:nch]
            if mode != "scan":
                act(fl(gB), fl(gL), AF.Exp, [r_gL, r_gB], [r_gB])
                cp(dsl(2), hc(gB)[:, :, :, C - 1], [r_gB], [r_dec])
            tt(dsl(0), hc(gE)[:, :, :, 0], hc(gF)[:, :, :, 0], ALU.mult, [r_gE, r_gF], [r_dec])
            if mode == "scan":
                return
            tt(dsl(1), dsl(0), dsl(2), ALU.mult, [r_dec], [r_dec])
            k4 = lambda b: b[:, :, 0:ntok].rearrange("p h (c t) -> p h c t", t=C)
            tt(k4(khT), k4(ktT), dsl(2).unsqueeze(3).to_broadcast([128, 4, nch, C]), ALU.mult, r_ktT + [r_dec], r_khT)
            if mode != "scan":
                sqf = siluq.rearrange("p h t -> p (h t)")
                if sample:
                    tt(Zq[:, :, 0, 0:ntok], v3(sqf), v3(gB), ALU.mult, r_sq + [r_gB], r_Zq)
                else:
                    base = Zq[:, 0, 0, 0:64]
                    zout = AP(base.tensor, base.offset, [list(base.ap[0]), [512 // nsub, 4 * nsub], [192, 2], [1, 64]])
                    tt(zout, fl(sqf).rearrange("p (a c t) -> p a c t", c=2, t=64), fl(gB).rearrange("p (a c t) -> p a c t", c=2, t=64),
                       ALU.mult, r_sq + [r_gB], r_Zq)

        def khat_transpose(j):
            nt = nts[j]
            pt, rt = tbank()
            ksrc, rks = (ktT, r_ktT) if mode == "scan" else (khT, r_khT)
            for hb in range(4):
                S.emit("pe", lambda e, hb=hb, pt=pt: e.transpose(out=pt[:nt, hb * 128:(hb + 1) * 128], in_=ksrc[:, hb, j * 128:j * 128 + nt],
                                                                  identity=identb[:, :]), [rks[hb], r_ident], [rt])
            cp(khtm[:nt, j, :], pt[:nt, 0:512], [rt], [r_khtm[j]], eng="dve" if mode == "scan" else "act")

        def s_update(j, ci):
            p = ci % cps
            rows = slice(p * C, p * C + C)
            pb, rb = gbank()
            for hb in range(4):
                mm(pb[:, hb * 128:(hb + 1) * 128], khtm[rows, j, hb * 128:(hb + 1) * 128], vh[rows, j, hb * 128:(hb + 1) * 128],
                   True, True, [r_khtm[j]] + rl_vh[j], [rb])
            tt(Sst[:], Sst[:], dec[:, 1, :, ci:ci + 1].to_broadcast([128, 4, 128]), ALU.mult, [r_S, r_dec], [r_S])
            tt(Sst[:].rearrange("p h v -> p (h v)"), Sst[:].rearrange("p h v -> p (h v)"), pb[:, :], ALU.add, [r_S, rb], [r_S])

        if mode == "scan":
            wv_f, wv_i = W1v[0], W1v[1]
            if part == "f2":
                for hb in range(4):
                    proj_fm(wv_f, r_wslot[0], hb, lambda pb, rb, hb=hb: act(sg.rearrange("p h t -> p (h t)")[:, hb * ntok:(hb + 1) * ntok], pb[:, 0:ntok], AF.Tanh, [rb], rl_sg[hb], scale=0.5))
                for j in range(nsub):
                    proj_tm(wv_i, r_wslot[1], j, lambda pb, rb, j, nt: cp(vh[:nt, j, :], pb[:nt, :], [rb], rl_vh[j], eng="dve"))
                if kv:
                    kv_proj_only()
            if part == "gates":
                hgrn_gates_all()
            if part == "upd":
                for j in range(nsub):
                    khat_transpose(j)
                pb, rb = gbank()
                for hb in range(4):
                    for j in range(nsub):
                        mm(pb[:, hb * 128:(hb + 1) * 128], khtm[:, j, hb * 128:(hb + 1) * 128], vh[:, j, hb * 128:(hb + 1) * 128],
                           j == 0, j == nsub - 1, [r_khtm[j]] + rl_vh[j], [rb])
                tt(Sst[:], Sst[:], dec[:, 0, :, 0:1].to_broadcast([128, 4, 128]), ALU.mult, [r_S, r_dec], [r_S])
                tt(Sst[:].rearrange("p h v -> p (h v)"), Sst[:].rearrange("p h v -> p (h v)"), pb[:, :], ALU.add, [r_S, rb], [r_S])
            return

        def ev_q(ct):
            return lambda pb, rb: act(qT[:, ct, 0:ntok], pb[:, 0:ntok], AF.Copy, [rb], [r_qT], scale=0.125)

        def ev_k(ct):
            def f(pb, rb):
                for j in range(nsub):
                    cp(kTr[:, ct, slot_of(j), 0:nts[j]], pb[:, j * 128:j * 128 + nts[j]], [rb], [r_kT[slot_of(j)]], eng="act")
                if final_kv is not None:
                    cp(kst[:, ct, 0:ntok], pb[:, 0:ntok], [rb], rl_kst)
            return f

        def ev_v(pb, rb, j, nt):
            s_ = slot_of(j)
            cp(Vr[:nt, s_, :, 0:64], pb[:nt, :].rearrange("p (h d) -> p h d", h=8), [rb], [r_V[s_]], eng="act")
            if mode == "main" or sample:
                cp(Vr[:nt, s_, :, 64:65], onesT[:nt, 0:8].unsqueeze(2), [r_ones], [r_V[s_]], eng="pool")
            else:
                cp(Vr[:nt, s_, :, 64:65], hvt[:nt, 0:1].unsqueeze(2).to_broadcast([nt, 8, 1]), [r_hv], [r_V[s_]], eng="pool")
            if final_kv is not None:
                cp(vst[:nt, j, :], pb[:nt, :], [rb], rl_vst[j])
                dma("act", final_kv[1].ap()[final_kv[2] + j * 128:final_kv[2] + j * 128 + nt, :], vst[:nt, j, :], rl_vst[j], ())

        wv, rw, sl = w_next("in3")
        for hb in range(4):
            proj_fm(wv, rw, hb, lambda pb, rb, hb=hb: act(siluq.rearrange("p h t -> p (h t)")[:, hb * ntok:(hb + 1) * ntok], pb[:, 0:ntok], AF.Silu, [rb], [r_sq[hb]]))
        w_done(sl)
        wv, rw, sl = w_next("in4")
        for hb in range(4):
            proj_fm(wv, rw, hb, lambda pb, rb, hb=hb: act(sg.rearrange("p h t -> p (h t)")[:, hb * ntok:(hb + 1) * ntok], pb[:, 0:ntok], AF.Tanh, [rb], rl_sg[hb], scale=0.5))
        w_done(sl)
        wv, rw, sl = w_next("in5")
        for j in range(nsub):
            proj_tm(wv, rw, j, lambda pb, rb, j, nt: cp(vh[:nt, j, :], pb[:nt, :], [rb], rl_vh[j], eng="act"))
        w_done(sl)
        wv, rw, sl = w_next("in6")
        for j in range(nsub):
            proj_tm(wv, rw, j, lambda pb, rb, j, nt: act(sgb[:nt, j, :], pb[:nt, :], AF.Silu, [rb], [r_sgb[j]]))
        w_done(sl)
        wv, rw, sl = w_next("in0")
        for ct in range(4):
            proj_fm(wv, rw, ct, ev_q(ct))
        w_done(sl)
        wv, rw, sl = w_next("in1")
        for ct in range(4):
            proj_fm(wv, rw, ct, ev_k(ct))
        w_done(sl)
        if final_kv is not None:
            for ct in range(4):
                dma("act", final_kv[0].ap()[ct, :, final_kv[2]:final_kv[2] + ntok], kst[:, ct, 0:ntok], rl_kst, ())
        wv, rw, sl = w_next("in2")
        for j in range(nsub):
            proj_tm(wv, rw, j, ev_v)
        w_done(sl)

        d_ops = S.capture(deferred.pop(0)) if deferred else []

        att_steps, z_steps, h_steps = [], [], []

        zst = {}

        def z_step(zi):
            gi, ct = divmod(zi, 4)
            if ct == 0:
                zst["w"] = w_next("in%d" % (7 + gi))
            wv_, rw_, sl_ = zst["w"]
            proj_fm(wv_, rw_, ct, lambda pb, rb: act(zs[:, zi, 0:ntok], pb[:, 0:ntok], AF.Tanh, [rb], [r_zs[zi]], scale=0.5))
            if ct == 3:
                w_done(sl_)
        for zi in range(16):
            z_steps.append(lambda zi=zi: z_step(zi))

        def make_att(j):
            nq = nts[j]
            if sample:
                ktiles = [(0, 128), (1, 128), (2, 128), (3, 128), (4, 16)]
            else:
                ktiles = [((gt0 + j - 4 + kt) % 6, 128) for kt in range(5)]
            pend = []

            def pv(h, pslot):
                for kt, (s_, nk) in enumerate(ktiles):
                    mm(ps_o[:nq, (h % 4) * 65:(h % 4) * 65 + 65], Pb[pslot][:nk, kt, 0:nq], Vr[:nk, s_, h, :], kt == 0, kt == 4,
                       [r_Pb[pslot], r_V[s_]], [r_o])

            def normalize(half):
                o3 = ps_o[:nq, 0:260].rearrange("p (h d) -> p h d", h=4)
                ts(rec[:nq, half * 4:half * 4 + 4].unsqueeze(2), o3[:, :, 64:65], 1e-30, None, ALU.max, None, [r_o], [r_rec])
                recip(rec[:nq, half * 4:half * 4 + 4], rec[:nq, half * 4:half * 4 + 4], [r_rec], [r_rec])
                tt(oa[:nq, j, half * 256:half * 256 + 256].rearrange("p (h d) -> p h d", h=4), o3[:, :, 0:64],
                   rec[:nq, half * 4:half * 4 + 4].unsqueeze(2).to_broadcast([nq, 4, 64]), ALU.mult, [r_o, r_rec], [r_oa[j]])

            def head(h):
                hp, r0 = h // 2, (h % 2) * 64
                ai = cnt["a"] % 2
                cnt["a"] += 1
                offA = ai * 512
                offB = 1024 + ai * 128
                for kt in (4, 0, 1, 2, 3):
                    s_, nk = ktiles[kt]
                    o_ = offB if kt == 4 else offA + kt * 128
                    mm(ps_att[:nk, o_:o_ + nq], kTr[r0:r0 + 64, hp, s_, 0:nk], qT[r0:r0 + 64, hp, j * 128:j * 128 + nq],
                       True, True, [r_kT[s_], r_qT], [r_attB if kt == 4 else r_att[ai]])
                pi = cnt["p"] % 3
                cnt["p"] += 1
                sattA = ps_att[:, offA:offA + 512].rearrange("p (k q) -> p k q", k=4)
                sattB = ps_att[:, offB:offB + 128]
                if sample:
                    act(Pb[pi][:16, 4, 0:nq], sattB[:16, 0:nq], AF.Exp, [r_attB], [r_Pb[pi]])
                    act(Pb[pi][:, 0:4, 0:nq], sattA[:, :, 0:nq], AF.Exp, [r_att[ai]], [r_Pb[pi]])
                    tt(Pb[pi][:, 0:4, 0:nq], Pb[pi][:, 0:4, 0:nq], ET[:, 0:4, h, 0:nq], ALU.mult, [r_Pb[pi], r_ET], [r_Pb[pi]], eng="pool")
                    tt(Pb[pi][:16, 4, 0:nq], Pb[pi][:16, 4, 0:nq], ET[:16, 4, h, 0:nq], ALU.mult, [r_Pb[pi], r_ET], [r_Pb[pi]], eng="pool")
                else:
                    act(Pb[pi][:, 4, :], sattB, AF.Exp, [r_attB], [r_Pb[pi]])
                    act(Pb[pi][:, 0:4, :], sattA, AF.Exp, [r_att[ai]], [r_Pb[pi]])
                    tt(Pb[pi][:, :, :], Pb[pi][:, :, :], ET[:, :, h, :], ALU.mult, [r_Pb[pi], r_ET], [r_Pb[pi]], eng="dve")
                if pend:
                    ph, ppi = pend.pop(0)
                    pv(ph, ppi)
                    if ph == 3:
                        normalize(0)
                pend.append((h, pi))

            def tail():
                ph, ppi = pend.pop(0)
                pv(ph, ppi)
                normalize(1)
                pt, rt = tbank()
                for kc in range(4):
                    S.emit("pe", lambda e, kc=kc, pt=pt: e.transpose(out=pt[:, kc * 128:kc * 128 + nq], in_=oa[:nq, j, kc * 128:(kc + 1) * 128],
                                                                      identity=identb[:nq, :nq]), [r_oa[j], r_ident], [rt])
                cp(oaT[:, :, j * 128:j * 128 + nq], pt[:, 0:512].rearrange("p (k t) -> p k t", k=4)[:, :, 0:nq], [rt], [r_oaT[j]], eng="act")
            for h in range(8):
                att_steps.append(lambda h=h: head(h))
            att_steps.append(tail)
        for j in range(nsub):
            make_att(j)

        def make_h(j):
            nt = nts[j]

            def h_at():
                pb, rb = gbank()
                for hb in range(4):
                    if sample:
                        qrhs = Zq[:, hb, 0, 0:nt]
                    else:
                        base = Zq[:, hb, 2 * j, 0:64]
                        qrhs = AP(base.tensor, base.offset, [list(base.ap[0]), [192, 2], [1, 64]])
                    mm(pb[:nt, hb * 128:hb * 128 + nt], ktT[:, hb, j * 128:j * 128 + nt], qrhs, True, True, [r_ktT[hb], r_Zq[hb]], [rb])
                tt(AT[:nt, :, 0:nt], pb[:nt, :].rearrange("p (h t) -> p h t", h=4)[:, :, 0:nt],
                   maskb[:nt, 0:nt].unsqueeze(1).to_broadcast([nt, 4, nt]), ALU.mult, [rb, r_mask], [r_AT])

            def h_chunk(p):
                ci = j * cps + p
                tt(Sp[:, p], Sst[:], dec[:, 0, :, ci:ci + 1].to_broadcast([128, 4, 128]), ALU.mult, [r_S, r_dec], [r_Sp[p]])
                s_update(j, ci)

            def h_out():
                ob_, rob = gbank()
                for hb in range(4):
                    for p in range(cps):
                        zl = Zq[:, hb, 0, 0:nt] if sample else Zq[:, hb, 2 * j + p, :]
                        mm(ob_[:nt, hb * 128:(hb + 1) * 128], zl, Sp[:, p, hb, :], p == 0, False, [r_Zq[hb], r_Sp[p]], [rob])
                    mm(ob_[:nt, hb * 128:(hb + 1) * 128], AT[:nt, hb, 0:nt], vh[:nt, j, hb * 128:(hb + 1) * 128], False, True, [r_AT] + rl_vh[j], [rob])
                act(sqb[:nt, :], ob_[:nt, :], AF.Square, [rob], [r_sqb])
                S.emit("dve", lambda e: e.tensor_reduce(out=ssq[:nt, 0:4], in_=sqb[:nt, :].rearrange("p (h v) -> p h v", h=4), axis=AX.X, op=ALU.add),
                       [r_sqb], [r_ssq])
                ts(ssq[:nt, :], ssq[:nt, :], 1.0 / 128, EPS, ALU.mult, ALU.add, [r_ssq], [r_ssq], eng="pool")
                tt(ssq[:nt, :], ssq[:nt, :], nhalf[:nt, 0:4], ALU.pow, [r_ssq, r_nh], [r_ssq], eng="pool")
                tt(t1[:nt, :].rearrange("p (h v) -> p h v", h=4), ob_[:nt, :].rearrange("p (h v) -> p h v", h=4),
                   ssq[:nt, 0:4].unsqueeze(2).to_broadcast([nt, 4, 128]), ALU.mult, [rob, r_ssq], [r_t1])
                tt(gG[:nt, :], sgb[:nt, j, :], gnrep[:nt].rearrange("p h v -> p (h v)"), ALU.mult, [r_sgb[j], r_gn], [r_gG], eng="pool")
                tt(ob[:nt, :], t1[:nt, :], gG[:nt, :], ALU.mult, [r_t1, r_gG], [r_ob])

            def h_tr():
                pt, rt = tbank()
                for hb in range(4):
                    S.emit("pe", lambda e, hb=hb, pt=pt: e.transpose(out=pt[:, hb * 128:hb * 128 + nt], in_=ob[:nt, hb * 128:(hb + 1) * 128],
                                                                      identity=identb[:nt, :nt]), [r_ob, r_ident], [rt])
                cp(obT[:, :, j * 128:j * 128 + nt], pt[:, 0:512].rearrange("p (k t) -> p k t", k=4)[:, :, 0:nt], [rt], [r_obT[j]], eng="act")
            h_steps.append(lambda: khat_transpose(j))
            h_steps.append(h_at)
            for p in range(cps):
                h_steps.append(lambda p=p: h_chunk(p))
            h_steps.append(h_out)
            h_steps.append(h_tr)
        for j in range(nsub):
            make_h(j)

        def run_steps(steps, pool, tr, pre=None):
            def f():
                gstate["pool"] = pool
                gstate["tr"] = tr
                if pre is not None:
                    pre()
                for st_ in steps:
                    st_()
            return f
        att_ops = S.capture(run_steps(att_steps, [], (0,)))
        z_ops = S.capture(run_steps(z_steps, gpool_full[0:1], (0,)))
        h_ops = S.capture(run_steps(h_steps, gpool_full[1:2], (1,), pre=hgrn_gates_all))
        S.emit_merged(att_ops, z_ops, h_ops, d_ops)
        gstate["tr"] = (0, 1)
        if after_norm is not None:
            after_norm()
            after_norm = None

        gstate["pool"] = gpool_front + gpool_back
        wva, rwa, sla = w_next("a")
        wstate["consumed"] += 1
        wvb, rwb, slb = w_next("b")
        wstate["consumed"] -= 1
        for ct in range(8):
            pb, rb = gbank()
            for kc in range(4):
                mm(pb[:, 0:ntok], wva[:, kc, ct * 128:(ct + 1) * 128], oaT[:, kc, 0:ntok], kc == 0, kc == 3, [rwa] + r_oaT[:nsub], [rb])
            for kc in range(4):
                mm(pb[:, 256:256 + ntok], wvb[:, kc, ct * 128:(ct + 1) * 128], obT[:, kc, 0:ntok], kc == 0, kc == 3, [rwb] + r_obT[:nsub], [rb])
            stt(m1[:, 0:ntok], zs[:, ct, 0:ntok], 1.0, pb[:, 0:ntok], ALU.add, ALU.mult, [rb, r_zs[ct]], [r_m1])
            stt(m2[:, 0:ntok], zs[:, 8 + ct, 0:ntok], 1.0, pb[:, 256:256 + ntok], ALU.add, ALU.mult, [rb, r_zs[8 + ct]], [r_m2])
            tt(mT[:, ct, 0:ntok], m1[:, 0:ntok], m2[:, 0:ntok], ALU.add, [r_m1, r_m2], [r_mT[ct]], eng="pool")
        w_done(sla)
        w_done(slb)

        for n in range(2):
            wv, rw, sl = w_next("o%d" % n)
            for j in range(nsub):
                nt = nts[j]
                pb, rb = gbank()
                for kc in range(8):
                    mm(pb[:nt, :], mT[:, kc, j * 128:j * 128 + nt], wv[:, kc, :], kc == 0, kc == 7, [rw, r_mT[kc]], [rb])
                stt(Xb[:nt, j, n * 512:(n + 1) * 512], pb[:nt, :], 0.5, Xb[:nt, j, n * 512:(n + 1) * 512], ALU.mult, ALU.add, [rX[j], rb], [rX[j]])
            w_done(sl)
        norm_T(Xb, rX, gffn, r_gffn, 2, 2, ntok)
        hT = hTs[2]
        r_hT = r_hTs[2]
        rhT_all = r_hT[:nsub]

        ffn_pend = []
        for gi in range(6):
            ntile = 4 if gi < 5 else 2
            if gi == 2 and prenorm is not None:
                prenorm()
            wvg, rwg, slg = w_next("g%d" % gi)
            if mode != "pre":
                wstate["consumed"] += 1
                wvu, rwu, slu = w_next("u%d" % gi)
                wstate["consumed"] -= 1
            for ct in range(ntile):
                ft = gi * 4 + ct
                pb, rb = gbank()
                if mode == "pre":
                    for kc in range(8):
                        mm(pb[:, 0:2], wvg[:, kc, ct * 128:(ct + 1) * 128], hT[:, kc, ntok - 2:ntok], kc == 0, kc == 7, [rwg] + rhT_all, [rb])
                    cp(cprev[:, ft, :], pb[:, 0:2], [rb], [r_cprev])
                    continue
                for kc in range(8):
                    mm(pb[:, 0:ntok], wvg[:, kc, ct * 128:(ct + 1) * 128], hT[:, kc, 0:ntok], kc == 0, kc == 7, [rwg] + rhT_all, [rb])
                for kc in range(8):
                    mm(pb[:, 256:256 + ntok], wvu[:, kc, ct * 128:(ct + 1) * 128], hT[:, kc, 0:ntok], kc == 0, kc == 7, [rwu] + rhT_all, [rb])
                bi = ft % 2
                a_ = aT[bi]
                cp(a_[:, 0:2], cprev[:, ft, :], [r_cprev], [r_aT[bi]], eng="pool")
                act(a_[:, 2:2 + ntok], pb[:, 0:ntok], AF.Copy, [rb], [r_aT[bi]])
                cp(cprev[:, ft, :], a_[:, ntok:ntok + 2], [r_aT[bi]], [r_cprev], eng="pool")
                act(c1[bi][:, 0:ntok], pb[:, 0:ntok], AF.Identity, [rb, r_cw, r_cb], [r_c1[bi]], scale=cw[:, ft, 2:3], bias=cb[:, ft:ft + 1])
                stt(c2[bi][:, 0:ntok], a_[:, 1:1 + ntok], cw[:, ft, 1:2], c1[bi][:, 0:ntok], ALU.mult, ALU.add, [r_aT[bi], r_c1[bi], r_cw], [r_c2[bi]])
                stt(c1[bi][:, 0:ntok], a_[:, 0:ntok], cw[:, ft, 0:1], c2[bi][:, 0:ntok], ALU.mult, ALU.add, [r_aT[bi], r_c2[bi], r_cw], [r_c1[bi]])
                def fin(bi=bi, ft=ft, pb=pb, rb=rb):
                    act(c2[bi][:, 0:ntok], c1[bi][:, 0:ntok], AF.Gelu_apprx_tanh, [r_c1[bi]], [r_c2[bi]])
                    tt(gT[:, ft, 0:ntok], c2[bi][:, 0:ntok], pb[:, 256:256 + ntok], ALU.mult, [r_c2[bi], rb], [r_gT[ft]])
                if ffn_pend:
                    ffn_pend.pop(0)()
                ffn_pend.append(fin)
            w_done(slg)
            if mode != "pre":
                w_done(slu)
        if mode == "pre":
            return
        while ffn_pend:
            ffn_pend.pop(0)()

        for n in range(2):
            banks = [gbank() for _ in range(nsub)]
            for gk in range(3):
                wv, rw, sl = w_next("d%d_%d" % (n, gk))
                nk = 8 if gk < 2 else 6
                for kl in range(nk):
                    kc = gk * 8 + kl
                    for j in range(nsub):
                        nt = nts[j]
                        mm(banks[j][0][:nt, :], gT[:, kc, j * 128:j * 128 + nt], wv[:, kl, :], kc == 0, kc == NFT - 1, [rw, r_gT[kc]], [banks[j][1]])
                w_done(sl)
            for j in range(nsub):
                nt = nts[j]
                tt(Xb[:nt, j, n * 512:(n + 1) * 512], Xb[:nt, j, n * 512:(n + 1) * 512], banks[j][0][:nt, :], ALU.add, [rX[j], banks[j][1]], [rX[j]])
        def final_norm():
            for j in range(nsub):
                nt = nts[j]
                act(XN[:nt, j, :], Xb[:nt, j, :], AF.Square, [rX[j]], [r_XN[j], r_ss], accum_out=ss[:nt, 4 + j:5 + j])
                ts(ss[:nt, 4 + j:5 + j], ss[:nt, 4 + j:5 + j], 1.0 / D, EPS, ALU.mult, ALU.add, [r_ss], [r_ss], eng="pool")
                tt(ss[:nt, 4 + j:5 + j], ss[:nt, 4 + j:5 + j], nhalf[:nt, 0:1], ALU.pow, [r_ss, r_nh], [r_ss], eng="pool")
                stt(Xb[:nt, j, :], Xb[:nt, j, :], ss[:nt, 4 + j:5 + j], gfin[:nt, :], ALU.mult, ALU.mult, [rX[j], r_ss, r_gfin], [rX[j]])
                dma("act", out_row[j * 128:j * 128 + nt, :], Xb[:nt, j, :], [rX[j]], ())
        if deferred is not None and mode == "main":
            deferred.append(final_norm)
        else:
            final_norm()

    cur = {}

    def kv_proj_only():
        ntok, gt0 = cur["ntok"], cur["gt0"]
        for ct in range(4):
            pb, rb = gbank()
            for kc in range(8):
                mm(pb[:, 0:ntok], W1v[2][:, kc, ct * 128:(ct + 1) * 128], cur["hT"][:, kc, 0:ntok], kc == 0, kc == 7, [r_wslot[2]] + cur["r_hT"], [rb])
            for j in range(2):
                s_ = (gt0 + j) % 6
                cp(kTr[:, ct, s_, :], pb[:, j * 128:(j + 1) * 128], [rb], [r_kT[s_]], eng="act")
        for j in range(2):
            s_ = (gt0 + j) % 6
            pb, rb = gbank()
            for kc in range(8):
                mm(pb[:, :], cur["hT"][:, kc, j * 128:(j + 1) * 128], W1v[3][:, kc, :], kc == 0, kc == 7, [r_wslot[3], cur["r_hT"][j]], [rb])
            cp(Vr[:, s_, :, 0:64], pb[:, :].rearrange("p (h d) -> p h d", h=8), [rb], [r_V[s_]], eng="act")
            cp(Vr[:, s_, :, 64:65], hvt[:, 0:1].unsqueeze(2).to_broadcast([128, 8, 1]), [r_hv], [r_V[s_]], eng="pool")


    xa = xin.ap()
    nscan = n_scan
    nmain = n_main
    plan = []
    for m in range(nscan):
        plan.append(("scan", xa[m * T:(m + 1) * T, :], T, dict(gt0=-5 + 2 * (m - (nscan - 2)) + 6, kv=m >= nscan - 2)))
    if do_pre:
        plan.append(("pre", xa[NPRE:NPRE + PRE_T, :], PRE_T, dict(gt0=5)))
    for m in range(nmain):
        fk = (okT_d, ov_d, (m - (nmain - 2)) * T) if (m >= nmain - 2 and not NOFK) else None
        r0 = NPRE + PRE_T + m * T
        plan.append(("main", xa[r0:r0 + T, :], T, dict(gt0=2 * m + 6, out_row=y_d.ap()[m * T:(m + 1) * T, :], final_kv=fk)))
    if do_sample:
        plan.append(("sample", xs_d.ap(), 16, dict(gt0=0, out_row=ys_d.ap(), final_kv=(okTs_d, ovs_d, 0))))
    if plan:
        load_x(plan[0][1], plan[0][2], 0)
    if not do_conv:
        conv_jobs.clear()
    dfr = []
    for i, (mode, xsrc, ntok, kw) in enumerate(plan):
        xb = i % 2
        nxt = None
        pren = None
        if i + 1 < len(plan):
            nxt = (lambda p=plan[i + 1], b=(i + 1) % 2: load_x(p[1], p[2], b))
            if mode == "main":
                pren = (lambda p=plan[i + 1], b=(i + 1) % 2: norm_T(X[b], r_X[b], gmix, r_gmix, 0, b, p[2]))
        normed = i > 0 and plan[i - 1][0] == "main"
        if mode == "scan":
            if i > 0:
                continue

            def stage(k, part):
                md, xs_, nt_, kw_ = plan[k]
                nx = None
                if part == "f1":
                    emit_conv(2)
                    nx = (lambda p=plan[k + 1], b=(k + 1) % 2: load_x(p[1], p[2], b)) if k + 1 < len(plan) else None
                if part == "f2":
                    cur["ntok"] = nt_
                    cur["gt0"] = kw_["gt0"]
                macro_tile("scan", nt_, k % 2, hsel=k % 2, after_norm=nx, part=part, bsel=k % 2, vsel=k % 3, ksel=k % 2, **kw_)
            for w in range(-3, nscan):
                lists = []
                for off, part in ((0, "upd"), (1, "gates"), (2, "f2"), (3, "f1")):
                    k = w + off
                    if 0 <= k < nscan:
                        lists.append(S.capture(lambda k=k, part=part: stage(k, part)))
                S.emit_merged(*lists)
            continue
        if i == nscan:
            emit_conv(len(conv_jobs))
            for s_ in range(NSLOT):
                w_issue(s_)
        if mode == "sample":
            dma("act", oS_d.ap().rearrange("h k v -> k h v"), Sst[:], [r_S], ())
            dma("act", oconv_d.ap(), cprev[:], [r_cprev], ())
            dma("pool", kTr[:, :, 0:4, :], ckT_d.ap().rearrange("p c (t k) -> p c t k", t=4), (), r_kT[0:4])
            for t_ in range(4):
                dma("pool", Vr[:, t_, :, 0:64], cv_d.ap()[:, t_ * 128:(t_ + 1) * 128, :].rearrange("h p d -> p h d"), (), [r_V[t_]])
                cp(Vr[:, t_, :, 64:65], onesT[:, 0:8].unsqueeze(2), [r_ones], [r_V[t_]], eng="pool")
            dma("sp", Sst[:], s0_d.ap().rearrange("h k v -> k h v"), (), [r_S])
            dma("sp", cprev[:], cprev_d.ap(), (), [r_cprev])
        macro_tile(mode, ntok, xb, hsel=xb, normed=normed, after_norm=nxt, prenorm=pren, deferred=dfr, **kw)
    while dfr:
        dfr.pop(0)()
    emit_conv(len(conv_jobs))
    if not do_sample:
        dma("act", oS_d.ap().rearrange("h k v -> k h v"), Sst[:], [r_S], ())
        dma("act", oconv_d.ap(), cprev[:], [r_cprev], ())
    else:
        dma("act", oSs_d.ap().rearrange("h k v -> k h v"), Sst[:], [r_S], ())
        dma("act", oconvs_d.ap(), cprev[:], [r_cprev], ())
    for nm, getter in dumps:
        ap_, res_ = getter(locals())
        d_ = nc.dram_tensor("dbg_" + nm, list(ap_.shape), ap_.dtype if hasattr(ap_, "dtype") else F32, kind="ExternalOutput")
        dma("pool", d_.ap(), ap_, res_, ())

    S.finalize(st)
    with nc.Block() as block:
        S.run(block)
    st.close()
    return nc


_NC_CACHE = {}


def kernel(x_prompt, x_sample, cache_attn_k, cache_attn_v, state_hgrn, state_ffn_conv,
           norm_mix_g, w_in, rel_bias, hgrn_lb_logits, hgrn_norm_g, w_branch_a, w_branch_b, w_out,
           norm_ffn_g, w_ffn_gate, w_ffn_up, ffn_conv_w, ffn_conv_b, w_ffn_down, norm_final_g):
    f32 = np.float32
    A = lambda a: np.ascontiguousarray(np.asarray(a, dtype=f32))
    x_prompt = A(x_prompt)
    if "nc" not in _NC_CACHE:
        _NC_CACHE["nc"] = build_program()
    nc = _NC_CACHE["nc"]
    s_idx = np.arange(128)
    mask = ((s_idx[:, None] // 64 == s_idx[None, :] // 64) & (s_idx[:, None] <= s_idx[None, :])).astype(f32)
    shared = {
        "w_in": A(w_in[0]), "w_a": A(w_branch_a[0]), "w_b": A(w_branch_b[0]), "w_o": A(w_out[0]),
        "w_g": A(w_ffn_gate[0]), "w_u": A(w_ffn_up[0]), "w_d": A(w_ffn_down[0]),
        "gmix": A(np.asarray(norm_mix_g[0]).reshape(8, 128).T), "gffn": A(np.asarray(norm_ffn_g[0]).reshape(8, 128).T),
        "gfin": A(norm_final_g), "rel": A(rel_bias[0]),
        "lbl": A(np.asarray(hgrn_lb_logits).reshape(2, 4, 128).transpose(2, 0, 1)),
        "gn": A(hgrn_norm_g[0]),
        "cw": A(np.asarray(ffn_conv_w[0]).reshape(3, NFT, 128).transpose(2, 1, 0)),
        "cb": A(np.asarray(ffn_conv_b[0]).reshape(NFT, 128).T),
        "ident": np.eye(128, dtype=f32), "mask": mask, "jmat": np.ascontiguousarray(np.eye(128, dtype=f32)[::-1]),
    }
    in_maps = []
    for c in range(8):
        b, j = divmod(c, 4)
        s = j * SEG
        lo = s - PRE_T - NPRE
        xin = np.zeros((NTOK_IN, D), f32)
        a0 = max(lo, 0)
        xin[a0 - lo:] = x_prompt[b, a0:s + SEG]
        m = dict(shared)
        m["xin"] = xin
        m["hv"] = np.full((128, 1), 1.0 if j > 0 else 0.0, f32)
        m["xs"] = A(x_sample[c])
        m["ckT"] = A(np.asarray(cache_attn_k[0, c]).transpose(0, 2, 1).reshape(4, 128, 512).transpose(1, 0, 2))
        m["cv"] = A(cache_attn_v[0, c])
        m["s0"] = A(state_hgrn[0, c])
        m["cprev"] = A(np.asarray(state_ffn_conv[0, c]).reshape(2, NFT, 128).transpose(2, 1, 0))
        in_maps.append(m)
    res = run_bass_kernel_spmd(nc, in_maps, core_ids=list(range(8)))
    R_ = res.results
    B = 2
    y_prompt = np.stack([np.concatenate([R_[b * 4 + j]["y"] for j in range(4)], axis=0) for b in range(B)])
    y_sample = np.stack([R_[c]["ys"] for c in range(8)])

    def kT_to_rows(a):
        n = a.shape[-1]
        return a.reshape(8, 64, n).transpose(0, 2, 1)

    def v_to_rows(a):
        n = a.shape[0]
        return a.reshape(n, 8, 64).transpose(1, 0, 2)

    def conv_rows(a):
        return a.transpose(2, 1, 0).reshape(2, DFF)

    last = [3, 7]
    new_k_p = np.stack([kT_to_rows(R_[c]["okT"]) for c in last])[None]
    new_v_p = np.stack([v_to_rows(R_[c]["ov"]) for c in last])[None]
    hg_p = np.stack([R_[c]["oS"] for c in last])[None]
    cv_p = np.stack([conv_rows(R_[c]["oconv"]) for c in last])[None]
    new_k_s = np.stack([kT_to_rows(R_[c]["okTs"]) for c in range(8)])[None]
    new_v_s = np.stack([v_to_rows(R_[c]["ovs"]) for c in range(8)])[None]
    hg_s = np.stack([R_[c]["oSs"] for c in range(8)])[None]
    cv_s = np.stack([conv_rows(R_[c]["oconvs"]) for c in range(8)])[None]
    outs = (y_prompt, y_sample, new_k_p, new_v_p, hg_p, cv_p, new_k_s, new_v_s, hg_s, cv_s)
    return tuple(np.ascontiguousarray(o, dtype=f32) for o in outs)
```

```python
import numpy as np
from contextlib import ExitStack
import concourse.bass as bass
import concourse.mybir as mybir
from concourse.bass import AP
from concourse.bass_utils import run_bass_kernel_spmd

F32 = mybir.dt.float32
BF16 = mybir.dt.bfloat16
AF = mybir.ActivationFunctionType
ALU = mybir.AluOpType
AX = mybir.AxisListType

D = 1024
DFF = 2816
NFT = 22
SEG = 4096
NPRE = 12288
PRE_T = 128
NTOK_IN = NPRE + PRE_T + SEG
T = 256
EPS = 1e-6
NSLOT = 6
STOP = 99
STOPMODE = 'main'
NOFK = False
SLOT_E = 4096


class Res:
    __slots__ = ("name", "w", "r", "excl")

    def __init__(self, name, excl=False):
        self.name = name
        self.w = None
        self.r = {}
        self.excl = excl


class Op:
    __slots__ = ("eng", "fn", "deps", "is_dma", "needs_inc", "sem", "val", "idx")


class Sched:
    ENGS = ("pe", "act", "dve", "pool", "sp")

    def __init__(self, nc, n_dma_sems=8):
        self.nc = nc
        self.ops = []
        self.n_dma_sems = n_dma_sems
        self.cap = None

    def capture(self, fn):
        assert self.cap is None
        self.cap = []
        try:
            fn()
            return self.cap
        finally:
            self.cap = None

    DUR = {"pe": 0.15, "act": 0.75, "dve": 0.85, "pool": 1.1, "sp": 0.3}

    def emit_merged(self, *lists):
        eng_free = {}
        ready = {}
        rdone = {}
        LAT = 0.25

        def start_of(op):
            eng, fn, reads, writes, dma, cost = op
            t = eng_free.get(eng, 0.0)
            for r in reads:
                t = max(t, ready.get(id(r), 0.0) + LAT)
            for w in writes:
                t = max(t, ready.get(id(w), 0.0) + LAT, rdone.get(id(w), 0.0) + LAT)
            return t

        def commit(op, t):
            eng, fn, reads, writes, dma, cost = op
            d = 2.5 if dma else (cost if cost is not None else self.DUR[eng])
            eng_free[eng] = t + (0.1 if dma else d)
            for r in reads:
                rdone[id(r)] = max(rdone.get(id(r), 0.0), t + d)
            for w in writes:
                ready[id(w)] = t + d
            self.emit(eng, fn, reads, writes, dma)

        pos = [0] * len(lists)
        while True:
            best, bt = -1, None
            for k, l in enumerate(lists):
                if pos[k] < len(l):
                    t = start_of(l[pos[k]])
                    if bt is None or t < bt:
                        best, bt = k, t
            if best < 0:
                break
            commit(lists[best][pos[best]], bt)
            pos[best] += 1

    def emit(self, eng, fn, reads=(), writes=(), dma=False, cost=None):
        if self.cap is not None:
            self.cap.append((eng, fn, tuple(reads), tuple(writes), dma, cost))
            return None
        op = Op()
        op.eng = eng
        op.fn = fn
        op.is_dma = dma
        op.needs_inc = dma
        op.sem = None
        op.val = 0
        op.idx = len(self.ops)
        deps = {}
        xr = [r for r in reads if r.excl]
        if xr:
            reads = [r for r in reads if not r.excl]
            writes = list(writes) + [r for r in xr if r not in writes]
        for r in reads:
            if r.w is not None:
                deps[r.w.idx] = r.w
        for w in writes:
            if w.w is not None:
                deps[w.w.idx] = w.w
            for o in w.r.values():
                deps[o.idx] = o
        op.deps = list(deps.values())
        for r in reads:
            key = (eng, op.idx) if dma else (eng, -1)
            r.r[key] = op
        for w in writes:
            w.w = op
            w.r = {}
        self.ops.append(op)
        return op

    def finalize(self, stack):
        nc = self.nc
        for op in self.ops:
            for d in op.deps:
                if d.is_dma or d.eng != op.eng or op.eng != "pe" or op.is_dma:
                    d.needs_inc = True
        csem = {e: stack.enter_context(nc.semaphore("cs_" + e)) for e in ("pe", "act", "dve", "pool")}
        dsem = {e: [stack.enter_context(nc.semaphore("ds_%s%d" % (e, i))) for i in range(self.n_dma_sems)]
                for e in ("sp", "pool", "act")}
        ccount = {e: 0 for e in csem}
        dstate = {e: [None] * self.n_dma_sems for e in dsem}
        duse = {e: [0] * self.n_dma_sems for e in dsem}
        drr = {e: 0 for e in dsem}
        waited = {e: {} for e in self.ENGS}
        streams = {e: [] for e in self.ENGS}
        for op in self.ops:
            waits = []
            e = op.eng
            extra = []
            if op.is_dma:
                k = drr[e] % self.n_dma_sems
                drr[e] += 1
                prev = dstate[e][k]
                if prev is not None:
                    extra.append(prev)
                duse[e][k] += 1
                op.sem = dsem[e][k]
                op.val = 16 * duse[e][k]
                dstate[e][k] = op
            elif op.needs_inc:
                ccount[e] += 1
                op.sem = csem[e]
                op.val = ccount[e]
            for d in op.deps + extra:
                if (not d.is_dma) and d.eng == e and e == "pe" and not op.is_dma:
                    continue
                key = id(d.sem)
                if waited[e].get(key, 0) >= d.val:
                    continue
                waited[e][key] = d.val
                waits.append((d.sem, d.val))
            streams[e].append((waits, op))
        fin = []
        for e in dsem:
            for k in range(self.n_dma_sems):
                if duse[e][k] and waited["sp"].get(id(dsem[e][k]), 0) < 16 * duse[e][k]:
                    fin.append((dsem[e][k], 16 * duse[e][k]))
        self.streams = streams
        self.fin = fin

    def run(self, block):
        streams = self.streams
        fin = self.fin

        def body(name):
            def f(eng):
                for waits, op in streams[name]:
                    for s, v in waits:
                        eng.wait_ge(s, v)
                    inst = op.fn(eng)
                    if op.sem is not None:
                        inst.then_inc(op.sem, 16 if op.is_dma else 1)
                if name == "sp":
                    for s, v in fin:
                        eng.wait_ge(s, v)
            return f
        block.tensor(body("pe"))
        block.scalar(body("act"))
        block.vector(body("dve"))
        block.gpsimd(body("pool"))
        block.sync(body("sp"))


def build_program(n_scan=NPRE // T, n_main=SEG // T, do_pre=True, do_sample=True, dumps=(), do_conv=True, init_stop=99):
    NPRE = n_scan * T
    NTOK_IN = NPRE + PRE_T + n_main * T
    nc = bass.Bass("TRN2", target_bir_lowering=False)
    st = ExitStack()
    S = Sched(nc)

    def din(name, shape):
        return nc.dram_tensor(name, list(shape), F32, kind="ExternalInput")

    def dout(name, shape):
        return nc.dram_tensor(name, list(shape), F32, kind="ExternalOutput")

    xin = din("xin", [NTOK_IN, D])
    hv_d = din("hv", [128, 1])
    xs_d = din("xs", [16, D])
    ckT_d = din("ckT", [128, 4, 512])
    cv_d = din("cv", [8, 512, 64])
    s0_d = din("s0", [4, 128, 128])
    cprev_d = din("cprev", [128, NFT, 2])
    w_in_d = din("w_in", [D, 5632])
    w_a_d = din("w_a", [512, D])
    w_b_d = din("w_b", [512, D])
    w_o_d = din("w_o", [D, D])
    w_g_d = din("w_g", [D, DFF])
    w_u_d = din("w_u", [D, DFF])
    w_d_d = din("w_d", [DFF, D])
    gmix_d = din("gmix", [128, 8])
    gffn_d = din("gffn", [128, 8])
    gfin_d = din("gfin", [D])
    rel_d = din("rel", [8, 192])
    lbl_d = din("lbl", [128, 2, 4])
    gn_d = din("gn", [128])
    cw_d = din("cw", [128, NFT, 3])
    cb_d = din("cb", [128, NFT])
    ident_d = din("ident", [128, 128])
    mask_d = din("mask", [128, 128])
    jmat_d = din("jmat", [128, 128])

    y_d = dout("y", [SEG, D])
    ys_d = dout("ys", [16, D])
    okT_d = dout("okT", [4, 128, 512])
    ov_d = dout("ov", [512, 512])
    oS_d = dout("oS", [4, 128, 128])
    oconv_d = dout("oconv", [128, NFT, 2])
    okTs_d = dout("okTs", [4, 128, 16])
    ovs_d = dout("ovs", [16, 512])
    oSs_d = dout("oSs", [4, 128, 128])
    oconvs_d = dout("oconvs", [128, NFT, 2])

    def scratch(name, shape, dt=BF16):
        return nc.dram_tensor(name, list(shape), dt, kind="Internal")

    wb_in = scratch("wb_in", [D, 5632])
    wb_a = scratch("wb_a", [512, D])
    wb_b = scratch("wb_b", [512, D])
    wb_o = scratch("wb_o", [D, D])
    wb_g = scratch("wb_g", [D, DFF])
    wb_u = scratch("wb_u", [D, DFF])
    wb_d = scratch("wb_d", [DFF, D])
    ext_d = scratch("ext_d", [8, 768], F32)

    def sb(name, shape, dt=F32):
        return st.enter_context(nc.sbuf_tensor(name, list(shape), dt))

    def R(name, excl=False):
        return Res(name, excl)

    def fsz(ap):
        n = 1
        for d_ in list(ap.shape)[1:]:
            n *= int(d_)
        return n

    def ecost(eng, ap):
        n = fsz(ap)
        return {"act": 0.25 + n / 1000.0, "dve": 0.1 + n / 850.0, "pool": 0.2 + n / 480.0}[eng]

    def act(out, in_, func, reads, writes, **kw):
        S.emit("act", lambda e: e.activation(out=out, in_=in_, func=func, **kw), reads, writes, cost=ecost("act", out))

    def tt(out, in0, in1, op, reads, writes, eng="dve"):
        S.emit(eng, lambda e: e.tensor_tensor(out=out, in0=in0, in1=in1, op=op), reads, writes, cost=ecost(eng, out))

    def ts(out, in0, s1, s2, op0, op1, reads, writes, eng="dve"):
        if op1 is None:
            S.emit(eng, lambda e: e.tensor_scalar(out=out, in0=in0, scalar1=s1, scalar2=None, op0=op0), reads, writes, cost=ecost(eng, out))
        else:
            S.emit(eng, lambda e: e.tensor_scalar(out=out, in0=in0, scalar1=s1, scalar2=s2, op0=op0, op1=op1), reads, writes, cost=ecost(eng, out))

    def stt(out, in0, scalar, in1, op0, op1, reads, writes):
        S.emit("dve", lambda e: e.scalar_tensor_tensor(out=out, in0=in0, scalar=scalar, in1=in1, op0=op0, op1=op1), reads, writes, cost=ecost("dve", out))

    def cp(out, in_, reads, writes, eng="dve"):
        if eng == "act":
            act(out, in_, AF.Copy, reads, writes)
        else:
            S.emit(eng, lambda e: e.tensor_copy(out=out, in_=in_), reads, writes, cost=ecost(eng, out))

    def recip(out, in_, reads, writes):
        S.emit("dve", lambda e: e.reciprocal(out=out, in_=in_), reads, writes)

    def mset(ap, val, writes, eng="pool"):
        S.emit(eng, lambda e: e.memset(ap, val), (), writes)

    def mm(out, lhsT, rhs, start, stop, reads, writes):
        S.emit("pe", lambda e: e.matmul(out, lhsT=lhsT, rhs=rhs, start=start, stop=stop), reads, writes, cost=0.05 + fsz(rhs) / 2000.0)

    def dma(eng, out, in_, reads, writes):
        S.emit(eng, lambda e: e.dma_start(out=out, in_=in_), reads, writes, dma=True)

    identb = sb("identb", [128, 128], BF16); r_ident = R("ident")
    maskb = sb("maskb", [128, 128], BF16); r_mask = R("mask")
    jb = sb("jb", [128, 128], BF16); r_j = R("j")
    epst = sb("epst", [128, 1]); r_eps = R("eps")
    onesT = sb("onesT", [128, 8]); r_ones = R("ones")
    hvt = sb("hvt", [128, 1]); r_hv = R("hv")
    gmix = sb("gmix_s", [128, 8]); r_gmix = R("gmix")
    gffn = sb("gffn_s", [128, 8]); r_gffn = R("gffn")
    gfin = sb("gfin_s", [128, D]); r_gfin = R("gfin")
    gnrep = sb("gnrep", [128, 4, 128]); r_gn = R("gn")
    cw = sb("cw_s", [128, NFT, 3]); r_cw = R("cw")
    cb = sb("cb_s", [128, NFT]); r_cb = R("cb")
    lbl = sb("lbl_s", [128, 2, 4]); r_lbl = R("lbl")
    lb = sb("lb_s", [128, 4]); oml = sb("oml_s", [128, 4]); r_lb = R("lb")
    ET = sb("ET", [128, 5, 8, 128], BF16); r_ET = R("ET")

    dma("pool", identb[:], ident_d.ap(), (), [r_ident])
    dma("pool", maskb[:], mask_d.ap(), (), [r_mask])
    dma("pool", jb[:], jmat_d.ap(), (), [r_j])
    mset(epst[:], EPS, [r_eps])
    nhalf = sb("nhalf", [128, 8]); r_nh = R("nhalf")
    mset(nhalf[:], -0.5, [r_nh])
    mset(onesT[:], 1.0, [r_ones])
    dma("sp", hvt[:], hv_d.ap(), (), [r_hv])
    dma("sp", gmix[:], gmix_d.ap(), (), [r_gmix])
    dma("sp", gffn[:], gffn_d.ap(), (), [r_gffn])
    dma("sp", gfin[:], AP(gfin_d, 0, [[0, 128], [1, D]]), (), [r_gfin])
    dma("sp", gnrep[:], AP(gn_d, 0, [[0, 128], [0, 4], [1, 128]]), (), [r_gn])
    dma("sp", cw[:], cw_d.ap(), (), [r_cw])
    dma("sp", cb[:], cb_d.ap(), (), [r_cb])
    dma("sp", lbl[:], lbl_d.ap(), (), [r_lbl])

    if init_stop <= 1:
        S.finalize(st)
        with nc.Block() as block:
            S.run(block)
        st.close()
        return nc
    ps_att = st.enter_context(nc.psum_tensor("ps_att", [128, 1536], F32))
    r_att = [R("att0", True), R("att1", True)]
    r_attB = R("attB", True)
    NGEN = 2
    ps_gen = [st.enter_context(nc.psum_tensor("ps_g%d" % i, [128, 512], F32)) for i in range(NGEN)]
    r_gen = [R("g%d" % i, True) for i in range(NGEN)]
    ps_o = st.enter_context(nc.psum_tensor("ps_o", [128, 512], F32))
    r_o = R("ps_o", True)
    ps_tr = [st.enter_context(nc.psum_tensor("ps_t%d" % i, [128, 1024], BF16)) for i in range(2)]
    r_tr = [R("t%d" % i, True) for i in range(2)]
    cnt = {"g": 0, "t": 0, "a": 0, "p": 0}

    gpool_full = [(ps_gen[i], r_gen[i]) for i in range(NGEN)]
    gpool_front = gpool_full + [(ps_o, r_o)]
    gpool_back = [(ps_att[:, 0:512], r_att[0]), (ps_att[:, 512:1024], r_att[1]), (ps_att[:, 1024:1536], r_attB)]
    gstate = {"pool": gpool_full, "tr": (0, 1)}

    def gbank():
        pool = gstate["pool"]
        i = cnt["g"] % len(pool)
        cnt["g"] += 1
        return pool[i]

    def tbank():
        sel = gstate["tr"]
        i = sel[cnt["t"] % len(sel)]
        cnt["t"] += 1
        return ps_tr[i], r_tr[i]

    lbt = sb("lbt", [128, 4])
    tt(lbt[:], lbl[:, 1, :], lbl[:, 0, :], ALU.subtract, [r_lbl], [r_lb])
    act(lbt[:], lbt[:], AF.Exp, [r_lb], [r_lb])
    ts(lbt[:], lbt[:], 1.0, None, ALU.add, None, [r_lb], [r_lb])
    recip(lb[:], lbt[:], [r_lb], [r_lb])
    ts(oml[:], lb[:], -1.0, 1.0, ALU.mult, ALU.add, [r_lb], [r_lb])
    omlh = sb("omlh_s", [128, 4]); lbh = sb("lbh_s", [128, 4])
    ts(omlh[:], oml[:], 0.5, None, ALU.mult, None, [r_lb], [r_lb])
    tt(lbh[:], lb[:], omlh[:], ALU.add, [r_lb], [r_lb])

    if init_stop <= 2:
        S.finalize(st)
        with nc.Block() as block:
            S.run(block)
        st.close()
        return nc
    if init_stop <= 3:
        S.finalize(st)
        with nc.Block() as block:
            S.run(block)
        st.close()
        return nc
    r_wb = {}
    conv_jobs = []
    for name, src, dst, rows in (("in", w_in_d, wb_in, D), ("a", w_a_d, wb_a, 512), ("b", w_b_d, wb_b, 512),
                                 ("o", w_o_d, wb_o, D), ("g", w_g_d, wb_g, D), ("u", w_u_d, wb_u, D),
                                 ("d", w_d_d, wb_d, DFF)):
        r_wb[name] = R("wb_" + name)
        for r0 in range(0, rows, 128):
            conv_jobs.append((name, dst.ap()[r0:r0 + 128, :], src.ap()[r0:r0 + 128, :]))

    def emit_conv(n):
        for _ in range(n):
            if conv_jobs:
                name, d_, s_ = conv_jobs.pop(0)
                dma("pool", d_, s_, (), [r_wb[name]])

    wslot = [sb("wslot%d" % i, [128, SLOT_E], BF16) for i in range(NSLOT)]
    r_wslot = [R("wslot%d" % i) for i in range(NSLOT)]
    W1v = []
    for i, c0 in enumerate((2048, 2560, 512, 1024)):
        v_ = wslot[i][:, :].rearrange("p (k n) -> p k n", k=8)
        dma("pool", v_, w_in_d.ap()[:, c0:c0 + 512].rearrange("(k p) n -> p k n", p=128), (), [r_wslot[i]])
        W1v.append(v_)

    def wgroups(mode):
        g = []
        for i in (3, 4, 5, 6, 0, 1, 2, 7, 8, 9, 10):
            g.append(("in%d" % i, "in", wb_in.ap()[:, 512 * i:512 * i + 512].rearrange("(k p) n -> p k n", p=128), 8, 512))
        g.append(("a", "a", wb_a.ap().rearrange("(k p) n -> p k n", p=128), 4, 1024))
        g.append(("b", "b", wb_b.ap().rearrange("(k p) n -> p k n", p=128), 4, 1024))
        for n in range(2):
            g.append(("o%d" % n, "o", wb_o.ap()[:, 512 * n:512 * n + 512].rearrange("(k p) n -> p k n", p=128), 8, 512))
        for gi in range(6):
            nc_ = 512 if gi < 5 else 256
            g.append(("g%d" % gi, "g", wb_g.ap()[:, 512 * gi:512 * gi + nc_].rearrange("(k p) n -> p k n", p=128), 8, nc_))
            if mode != "pre":
                g.append(("u%d" % gi, "u", wb_u.ap()[:, 512 * gi:512 * gi + nc_].rearrange("(k p) n -> p k n", p=128), 8, nc_))
        if mode != "pre":
            for n in range(2):
                for gk in range(3):
                    nk = 8 if gk < 2 else 6
                    g.append(("d%d_%d" % (n, gk), "d",
                              wb_d.ap()[1024 * gk:1024 * gk + 128 * nk, 512 * n:512 * n + 512].rearrange("(k p) n -> p k n", p=128), nk, 512))
        return g

    wq = []
    tiles_plan = ([("pre", 0)] if do_pre else []) + [("main", m) for m in range(n_main)] + ([("sample", 0)] if do_sample else [])
    for mode, _ in tiles_plan:
        wq.extend(wgroups(mode))
    wstate = {"issued": 0, "consumed": 0, "slot": {}}

    def w_issue(slot):
        i = wstate["issued"]
        if i >= len(wq):
            return
        key, wname, src, nk, ncol = wq[i]
        view = wslot[slot][:, 0:nk * ncol].rearrange("p (k n) -> p k n", k=nk)
        dma("sp", view, src, [r_wb[wname]], [r_wslot[slot]])
        wstate["slot"][i] = slot
        wstate["issued"] += 1

    def w_next(key):
        i = wstate["consumed"]
        assert wq[i][0] == key, (wq[i][0], key)
        slot = wstate["slot"][i]
        _, _, _, nk, ncol = wq[i]
        view = wslot[slot][:, 0:nk * ncol].rearrange("p (k n) -> p k n", k=nk)
        return view, r_wslot[slot], slot

    def w_done(slot):
        wstate["consumed"] += 1
        w_issue(slot)

    if init_stop <= 4:
        S.finalize(st)
        with nc.Block() as block:
            S.run(block)
        st.close()
        return nc
    X = [sb("X%d" % i, [128, 2, D]) for i in range(2)]
    r_X = [[R("X%d_%d" % (i, j)) for j in range(2)] for i in range(2)]
    XN = sb("XN", [128, 2, D], BF16); r_XN = [R("XN0"), R("XN1")]
    ss = sb("ss", [128, 8]); r_ss = R("ss")
    hTs = [sb("hTa", [128, 8, T], BF16), sb("hTb", [128, 8, T], BF16), sb("h2T", [128, 8, T], BF16)]
    r_hTs = [[R("hTa0"), R("hTa1")], [R("hTb0"), R("hTb1")], [R("h2T0"), R("h2T1")]]
    qT = sb("qT", [128, 4, T], BF16); r_qT = R("qT")
    kTr = sb("kTr", [128, 4, 6, 128], BF16); r_kT = [R("kT%d" % i) for i in range(6)]
    Vr = sb("Vr", [128, 6, 8, 65], BF16); r_V = [R("V%d" % i) for i in range(6)]
    Pb = [sb("Pb%d" % i, [128, 5, 128], BF16) for i in range(3)]; r_Pb = [R("Pb%d" % i) for i in range(3)]
    oa = sb("oa", [128, 2, 512], BF16); r_oa = [R("oa0"), R("oa1")]
    oaT = sb("oaT", [128, 4, T], BF16); r_oaT = [R("oaT0"), R("oaT1")]
    rec = sb("rec", [128, 8]); r_rec = R("rec")
    sg = sb("sg", [128, 4, T]); r_sg = [R("sg%d" % i) for i in range(4)]
    siluq = sb("siluq", [128, 4, T]); r_sq = [R("siluq%d" % i) for i in range(4)]
    gF = sb("gF", [128, 4 * T]); r_gF = R("gF")
    gL = sb("gL", [128, 4 * T]); r_gL = R("gL")
    gB = sb("gB", [128, 4 * T]); r_gB = R("gB")
    gK = sb("gK", [128, 4 * T]); r_gK = R("gK")
    gE = sb("gE", [128, 4 * T]); r_gE = R("gE")
    dec = sb("dec", [128, 3, 4, 4]); r_dec = R("dec")
    Zq = sb("Zq", [128, 4, 4, 128], BF16); r_Zq = [R("Zq%d" % i) for i in range(4)]
    ktT = sb("ktT", [128, 4, T], BF16); r_ktT = [R("ktT%d" % i) for i in range(4)]
    khT = sb("khT", [128, 4, T], BF16); r_khT = [R("khT%d" % i) for i in range(4)]
    khtm = sb("khtm", [128, 2, 512], BF16); r_khtm = [R("khtm0"), R("khtm1")]
    vh = sb("vh", [128, 2, 512], BF16); r_vh = [R("vh0"), R("vh1")]
    sgb = sb("sgb", [128, 2, 512]); r_sgb = [R("sgb0"), R("sgb1")]
    Sst = sb("Sst", [128, 4, 128]); r_S = R("S")
    Sp = sb("Sp", [128, 2, 4, 128], BF16); r_Sp = [R("Sp0"), R("Sp1")]
    AT = sb("AT", [128, 4, 128], BF16); r_AT = R("AT")
    sqb = sb("sqb", [128, 512]); r_sqb = R("sqb")
    ssq = sb("ssq", [128, 4]); r_ssq = R("ssq")
    t1 = sb("t1", [128, 512]); r_t1 = R("t1")
    gG = sb("gG", [128, 512]); r_gG = R("gG")
    ob = sb("ob", [128, 512], BF16); r_ob = R("ob")
    obT = sb("obT", [128, 4, T], BF16); r_obT = [R("obT0"), R("obT1")]
    zs = sb("zs", [128, 16, T], BF16); r_zs = [R("zs%d" % i) for i in range(16)]
    mT = sb("mT", [128, 8, T], BF16); r_mT = [R("mT%d" % i) for i in range(8)]
    aT = [sb("aT%d" % i, [128, T + 2]) for i in range(2)]; r_aT = [R("aT0"), R("aT1")]
    c1 = [sb("c1_%d" % i, [128, T]) for i in range(2)]; r_c1 = [R("c1_0"), R("c1_1")]
    c2 = [sb("c2_%d" % i, [128, T]) for i in range(2)]; r_c2 = [R("c2_0"), R("c2_1")]
    m1, r_m1, m2, r_m2 = c1[0], r_c1[0], c2[0], r_c2[0]
    gT = sb("gT", [128, NFT, T], BF16); r_gT = [R("gT%d" % i) for i in range(NFT)]
    cprev = sb("cprev_s", [128, NFT, 2]); r_cprev = R("cprev")

    ext_ = siluq[0:8].rearrange("p h t -> p (h t)")[:, 0:768]; r_ext = r_sq; r_extd = R("extd")
    dma("sp", ext_[:, 64:256], rel_d.ap(), (), r_ext)
    act(ext_[:, 0:64], ext_[:, 64:65].to_broadcast([8, 64]), AF.Identity, r_ext, r_ext)
    act(ext_[:, 256:768], ext_[:, 255:256].to_broadcast([8, 512]), AF.Identity, r_ext, r_ext)
    act(ext_[:, :], ext_[:, :], AF.Exp, r_ext, r_ext)
    dma("sp", ext_d.ap(), ext_[:, :], r_ext, [r_extd])
    hk = sg[:].rearrange("p h t -> p (h t)").rearrange("p (a b) -> p a b", a=8); r_hk = r_sg
    hkb = ktT[:].rearrange("p h t -> p (h t)").rearrange("p (a b) -> p a b", a=8); r_hkb = r_ktT
    for kt in range(5):
        dma("sp", hk, AP(ext_d, 512 - 128 * kt, [[1, 128], [768, 8], [1, 128]]), [r_extd], r_hk)
        cp(hkb, hk, r_hk, r_hkb)
        for n in range(2):
            pb, rb = gbank()
            mm(pb[:, :], jb[:], hkb[:, 4 * n:4 * n + 4, :].rearrange("p h q -> p (h q)"), True, True, [r_j] + r_hkb, [rb])
            cp(ET[:, kt, 4 * n:4 * n + 4, :].rearrange("p h q -> p (h q)"), pb[:, :], [rb], [r_ET])
    mset(ET[0:64, 0, :, 64:128], 0.0, [r_ET])
    mset(ET[64:128, 4, :, 0:64], 0.0, [r_ET])

    kst = gT[:, 0:8, :].rearrange("p a b -> p (a b)").bitcast(F32).rearrange("p (h t) -> p h t", h=4)
    vst = gT[:, 8:16, :].rearrange("p a b -> p (a b)").bitcast(F32).rearrange("p (j c) -> p j c", j=2)
    rl_kst = r_gT[0:8]
    rl_vst = [r_gT[8:12], r_gT[12:16]]
    sg2 = gT[:, 0:8, :].rearrange("p a b -> p (a b)").bitcast(F32).rearrange("p (h t) -> p h t", h=4)
    r_sg2 = [[r_gT[2 * i], r_gT[2 * i + 1]] for i in range(4)]
    vh2 = gT[:, 8:12, :].rearrange("p a b -> p (a b)").rearrange("p (j c) -> p j c", j=2)
    r_vh2 = [[r_gT[8], r_gT[9]], [r_gT[10], r_gT[11]]]
    vh3 = gT[:, 12:16, :].rearrange("p a b -> p (a b)").rearrange("p (j c) -> p j c", j=2)
    r_vh3 = [[r_gT[12], r_gT[13]], [r_gT[14], r_gT[15]]]
    ktT2 = gT[:, 16:20, :]
    r_ktT2 = r_gT[16:20]
    dec2 = sb("dec2", [128, 3, 4, 4]); r_dec2 = R("dec2")
    sgs = [(sg, [[r] for r in r_sg]), (sg2, r_sg2)]
    vhs = [(vh, [[r] for r in r_vh]), (vh2, r_vh2), (vh3, r_vh3)]
    kts = [(ktT, r_ktT, dec, r_dec), (ktT2, r_ktT2, dec2, r_dec2)]
    mset(Zq[:], 0.0, r_Zq)
    mset(Sst[:], 0.0, [r_S])
    mset(cprev[:], 0.0, [r_cprev])

    if init_stop <= 5:
        S.finalize(st)
        with nc.Block() as block:
            S.run(block)
        st.close()
        return nc
    def load_x(xsrc, ntok, xb):
        nsub = (ntok + 127) // 128
        for j in range(nsub):
            nt = min(128, ntok - 128 * j)
            dma("sp", X[xb][:nt, j, :], xsrc[j * 128:j * 128 + nt, :], (), [r_X[xb][j]])

    def norm_T(src_tile, rsrc, gcol, rg, col0, hsel, ntok):
        nsub = (ntok + 127) // 128
        dst, rdst = hTs[hsel], r_hTs[hsel]
        nts_ = [min(128, ntok - 128 * j) for j in range(nsub)]
        for j in range(nsub):
            nt = nts_[j]
            act(XN[:nt, j, :], src_tile[:nt, j, :], AF.Square, [rsrc[j]], [r_XN[j], r_ss], accum_out=ss[:nt, col0 + j:col0 + j + 1])
        for j in range(nsub):
            nt = nts_[j]
            ts(ss[:nt, col0 + j:col0 + j + 1], ss[:nt, col0 + j:col0 + j + 1], 1.0 / D, EPS, ALU.mult, ALU.add, [r_ss], [r_ss], eng="pool")
            tt(ss[:nt, col0 + j:col0 + j + 1], ss[:nt, col0 + j:col0 + j + 1], nhalf[:nt, 0:1], ALU.pow, [r_ss, r_nh], [r_ss], eng="pool")
        for j in range(nsub):
            nt = nts_[j]
            act(XN[:nt, j, :], src_tile[:nt, j, :], AF.Copy, [rsrc[j], r_ss], [r_XN[j]], scale=ss[:nt, col0 + j:col0 + j + 1])
            pt, rt = tbank()
            for kc in range(8):
                S.emit("pe", lambda e, kc=kc, j=j, nt=nt, pt=pt: e.transpose(out=pt[:, kc * 128:kc * 128 + nt], in_=XN[:nt, j, kc * 128:(kc + 1) * 128],
                                                                              identity=identb[:nt, :nt]), [r_XN[j], r_ident], [rt], cost=0.1)
            tt(dst[:, :, j * 128:j * 128 + nt], pt[:, :].rearrange("p (k t) -> p k t", k=8)[:, :, 0:nt],
               gcol[:, 0:8].unsqueeze(2).to_broadcast([128, 8, nt]), ALU.mult, [rt, rg], [rdst[j]])

    def macro_tile(mode, ntok, xb, gt0, hsel=0, normed=False, kv=False, out_row=None, final_kv=None, after_norm=None, prenorm=None,
                   part="all", bsel=0, vsel=0, ksel=0, deferred=None):
        sample = mode == "sample"
        nsub = (ntok + 127) // 128
        nts = [min(128, ntok - 128 * j) for j in range(nsub)]
        C = 16 if sample else (ntok if mode == "scan" else 64)
        nch = ntok // C
        ri = C - 1 if mode == "scan" else C // 2 - 1
        cps = 1 if sample else 2
        Xb = X[xb]
        rX = r_X[xb]
        hT = hTs[hsel]
        r_hT = r_hTs[hsel]
        rhT_all = r_hT[:nsub]
        cur["hT"] = hT
        cur["r_hT"] = r_hT
        sg, rl_sg = sgs[bsel]
        vh, rl_vh = vhs[vsel]
        ktT, r_ktT, dec, r_dec = kts[ksel]
        if mode == "scan":
            gstate["pool"] = gpool_front if part in ("f1", "f2") else gpool_back
            gstate["tr"] = (0,) if part in ("f1", "f2") else (1,)
        else:
            gstate["pool"] = gpool_front + gpool_back
            gstate["tr"] = (0, 1)

        if part in ("all", "f1"):
            if not normed:
                norm_T(Xb, rX, gmix, r_gmix, 0, hsel, ntok)
            if after_norm is not None and (deferred is None or not deferred):
                after_norm()
                after_norm = None

        def proj_fm(wv, rw, ct, evac):
            pb, rb = gbank()
            for kc in range(8):
                mm(pb[:, 0:ntok], wv[:, kc, ct * 128:(ct + 1) * 128], hT[:, kc, 0:ntok], kc == 0, kc == 7, [rw] + rhT_all, [rb])
            evac(pb, rb)

        def proj_tm(wv, rw, j, evac, ncol=512):
            pb, rb = gbank()
            nt = nts[j]
            for kc in range(8):
                mm(pb[:nt, 0:ncol], hT[:, kc, j * 128:j * 128 + nt], wv[:, kc, 0:ncol], kc == 0, kc == 7, [rw, r_hT[j]], [rb])
            evac(pb, rb, j, nt)

        def slot_of(j):
            return (gt0 + j) % 6 if not sample else 4

        def hgrn_gates_all():
            n4 = 4 * ntok
            nc4 = 4 * nch
            fl = lambda b: b[:, 0:n4]
            v3 = lambda b: b[:, 0:n4].rearrange("p (h t) -> p h t", h=4)
            ch = lambda b: b[:, 0:n4].rearrange("p (c t) -> p c t", t=C)
            hc = lambda b: b[:, 0:n4].rearrange("p (h c t) -> p h c t", h=4, t=C)
            rsg = [r for l_ in rl_sg for r in l_]
            sgf = sg.rearrange("p h t -> p (h t)")
            tt(v3(gF), v3(sgf), omlh[:, 0:4].unsqueeze(2).to_broadcast([128, 4, ntok]), ALU.mult, rsg + [r_lb], [r_gF])
            tt(v3(gF), v3(gF), lbh[:, 0:4].unsqueeze(2).to_broadcast([128, 4, ntok]), ALU.add, [r_gF, r_lb], [r_gF])
            act(fl(gL), fl(gF), AF.Ln, [r_gF], [r_gL])
            S.emit("dve", lambda e: e.tensor_tensor_scan(out=fl(gB), data0=fl(gL), data1=fl(gL), initial=0.0, op0=ALU.add, op1=ALU.min),
                   [r_gL], [r_gB], cost=0.1 + 2 * n4 / 900.0)
            ts(fl(gK), fl(gF), -1.0, 1.0, ALU.mult, ALU.add, [r_gF], [r_gK], eng="pool")
            tt(ch(gL), ch(gB), ch(gB)[:, :, ri:ri + 1].to_broadcast([128, nc4, C]), ALU.subtract, [r_gB, r_gL], [r_gL])
            act(fl(gE), fl(gL), AF.Exp, [r_gL], [r_gE], scale=-1.0)
            tt(ktT[:, :, 0:ntok], v3(gK), v3(gE), ALU.mult, [r_gK, r_gE], r_ktT, eng="pool")
            dsl = lambda i: dec[:, i, :, 0:nch]
            if mode != "scan":
                act(fl(gB), fl(gL), AF.Exp, [r_gL, r_gB], [r_gB])
                cp(dsl(2), hc(gB)[:, :, :, C - 1], [r_gB], [r_dec])
            tt(dsl(0), hc(gE)[:, :, :, 0], hc(gF)[:, :, :, 0], ALU.mult, [r_gE, r_gF], [r_dec])
            if mode == "scan":
                return
            tt(dsl(1), dsl(0), dsl(2), ALU.mult, [r_dec], [r_dec])
            k4 = lambda b: b[:, :, 0:ntok].rearrange("p h (c t) -> p h c t", t=C)
            tt(k4(khT), k4(ktT), dsl(2).unsqueeze(3).to_broadcast([128, 4, nch, C]), ALU.mult, r_ktT + [r_dec], r_khT)
            if mode != "scan":
                sqf = siluq.rearrange("p h t -> p (h t)")
                if sample:
                    tt(Zq[:, :, 0, 0:ntok], v3(sqf), v3(gB), ALU.mult, r_sq + [r_gB], r_Zq)
                else:
                    base = Zq[:, 0, 0, 0:64]
                    zout = AP(base.tensor, base.offset, [list(base.ap[0]), [512 // nsub, 4 * nsub], [192, 2], [1, 64]])
                    tt(zout, fl(sqf).rearrange("p (a c t) -> p a c t", c=2, t=64), fl(gB).rearrange("p (a c t) -> p a c t", c=2, t=64),
                       ALU.mult, r_sq + [r_gB], r_Zq)

        def khat_transpose(j):
            nt = nts[j]
            pt, rt = tbank()
            ksrc, rks = (ktT, r_ktT) if mode == "scan" else (khT, r_khT)
            for hb in range(4):
                S.emit("pe", lambda e, hb=hb, pt=pt: e.transpose(out=pt[:nt, hb * 128:(hb + 1) * 128], in_=ksrc[:, hb, j * 128:j * 128 + nt],
                                                                  identity=identb[:, :]), [rks[hb], r_ident], [rt])
            cp(khtm[:nt, j, :], pt[:nt, 0:512], [rt], [r_khtm[j]], eng="dve" if mode == "scan" else "act")

        def s_update(j, ci):
            p = ci % cps
            rows = slice(p * C, p * C + C)
            pb, rb = gbank()
            for hb in range(4):
                mm(pb[:, hb * 128:(hb + 1) * 128], khtm[rows, j, hb * 128:(hb + 1) * 128], vh[rows, j, hb * 128:(hb + 1) * 128],
                   True, True, [r_khtm[j]] + rl_vh[j], [rb])
            tt(Sst[:], Sst[:], dec[:, 1, :, ci:ci + 1].to_broadcast([128, 4, 128]), ALU.mult, [r_S, r_dec], [r_S])
            tt(Sst[:].rearrange("p h v -> p (h v)"), Sst[:].rearrange("p h v -> p (h v)"), pb[:, :], ALU.add, [r_S, rb], [r_S])

        if mode == "scan":
            wv_f, wv_i = W1v[0], W1v[1]
            if part == "f2":
                for hb in range(4):
                    proj_fm(wv_f, r_wslot[0], hb, lambda pb, rb, hb=hb: act(sg.rearrange("p h t -> p (h t)")[:, hb * ntok:(hb + 1) * ntok], pb[:, 0:ntok], AF.Tanh, [rb], rl_sg[hb], scale=0.5))
                for j in range(nsub):
                    proj_tm(wv_i, r_wslot[1], j, lambda pb, rb, j, nt: cp(vh[:nt, j, :], pb[:nt, :], [rb], rl_vh[j], eng="dve"))
                if kv:
                    kv_proj_only()
            if part == "gates":
                hgrn_gates_all()
            if part == "upd":
                for j in range(nsub):
                    khat_transpose(j)
                pb, rb = gbank()
                for hb in range(4):
                    for j in range(nsub):
                        mm(pb[:, hb * 128:(hb + 1) * 128], khtm[:, j, hb * 128:(hb + 1) * 128], vh[:, j, hb * 128:(hb + 1) * 128],
                           j == 0, j == nsub - 1, [r_khtm[j]] + rl_vh[j], [rb])
                tt(Sst[:], Sst[:], dec[:, 0, :, 0:1].to_broadcast([128, 4, 128]), ALU.mult, [r_S, r_dec], [r_S])
                tt(Sst[:].rearrange("p h v -> p (h v)"), Sst[:].rearrange("p h v -> p (h v)"), pb[:, :], ALU.add, [r_S, rb], [r_S])
            return

        def ev_q(ct):
            return lambda pb, rb: act(qT[:, ct, 0:ntok], pb[:, 0:ntok], AF.Copy, [rb], [r_qT], scale=0.125)

        def ev_k(ct):
            def f(pb, rb):
                for j in range(nsub):
                    cp(kTr[:, ct, slot_of(j), 0:nts[j]], pb[:, j * 128:j * 128 + nts[j]], [rb], [r_kT[slot_of(j)]], eng="act")
                if final_kv is not None:
                    cp(kst[:, ct, 0:ntok], pb[:, 0:ntok], [rb], rl_kst)
            return f

        def ev_v(pb, rb, j, nt):
            s_ = slot_of(j)
            cp(Vr[:nt, s_, :, 0:64], pb[:nt, :].rearrange("p (h d) -> p h d", h=8), [rb], [r_V[s_]], eng="act")
            if mode == "main" or sample:
                cp(Vr[:nt, s_, :, 64:65], onesT[:nt, 0:8].unsqueeze(2), [r_ones], [r_V[s_]], eng="pool")
            else:
                cp(Vr[:nt, s_, :, 64:65], hvt[:nt, 0:1].unsqueeze(2).to_broadcast([nt, 8, 1]), [r_hv], [r_V[s_]], eng="pool")
            if final_kv is not None:
                cp(vst[:nt, j, :], pb[:nt, :], [rb], rl_vst[j])
                dma("act", final_kv[1].ap()[final_kv[2] + j * 128:final_kv[2] + j * 128 + nt, :], vst[:nt, j, :], rl_vst[j], ())

        wv, rw, sl = w_next("in3")
        for hb in range(4):
            proj_fm(wv, rw, hb, lambda pb, rb, hb=hb: act(siluq.rearrange("p h t -> p (h t)")[:, hb * ntok:(hb + 1) * ntok], pb[:, 0:ntok], AF.Silu, [rb], [r_sq[hb]]))
        w_done(sl)
        wv, rw, sl = w_next("in4")
        for hb in range(4):
            proj_fm(wv, rw, hb, lambda pb, rb, hb=hb: act(sg.rearrange("p h t -> p (h t)")[:, hb * ntok:(hb + 1) * ntok], pb[:, 0:ntok], AF.Tanh, [rb], rl_sg[hb], scale=0.5))
        w_done(sl)
        wv, rw, sl = w_next("in5")
        for j in range(nsub):
            proj_tm(wv, rw, j, lambda pb, rb, j, nt: cp(vh[:nt, j, :], pb[:nt, :], [rb], rl_vh[j], eng="act"))
        w_done(sl)
        wv, rw, sl = w_next("in6")
        for j in range(nsub):
            proj_tm(wv, rw, j, lambda pb, rb, j, nt: act(sgb[:nt, j, :], pb[:nt, :], AF.Silu, [rb], [r_sgb[j]]))
        w_done(sl)
        wv, rw, sl = w_next("in0")
        for ct in range(4):
            proj_fm(wv, rw, ct, ev_q(ct))
        w_done(sl)
        wv, rw, sl = w_next("in1")
        for ct in range(4):
            proj_fm(wv, rw, ct, ev_k(ct))
        w_done(sl)
        if final_kv is not None:
            for ct in range(4):
                dma("act", final_kv[0].ap()[ct, :, final_kv[2]:final_kv[2] + ntok], kst[:, ct, 0:ntok], rl_kst, ())
        wv, rw, sl = w_next("in2")
        for j in range(nsub):
            proj_tm(wv, rw, j, ev_v)
        w_done(sl)

        d_ops = S.capture(deferred.pop(0)) if deferred else []

        att_steps, z_steps, h_steps = [], [], []

        zst = {}

        def z_step(zi):
            gi, ct = divmod(zi, 4)
            if ct == 0:
                zst["w"] = w_next("in%d" % (7 + gi))
            wv_, rw_, sl_ = zst["w"]
            proj_fm(wv_, rw_, ct, lambda pb, rb: act(zs[:, zi, 0:ntok], pb[:, 0:ntok], AF.Tanh, [rb], [r_zs[zi]], scale=0.5))
            if ct == 3:
                w_done(sl_)
        for zi in range(16):
            z_steps.append(lambda zi=zi: z_step(zi))

        def make_att(j):
            nq = nts[j]
            if sample:
                ktiles = [(0, 128), (1, 128), (2, 128), (3, 128), (4, 16)]
            else:
                ktiles = [((gt0 + j - 4 + kt) % 6, 128) for kt in range(5)]
            pend = []

            def pv(h, pslot):
                for kt, (s_, nk) in enumerate(ktiles):
                    mm(ps_o[:nq, (h % 4) * 65:(h % 4) * 65 + 65], Pb[pslot][:nk, kt, 0:nq], Vr[:nk, s_, h, :], kt == 0, kt == 4,
                       [r_Pb[pslot], r_V[s_]], [r_o])

            def normalize(half):
                o3 = ps_o[:nq, 0:260].rearrange("p (h d) -> p h d", h=4)
                ts(rec[:nq, half * 4:half * 4 + 4].unsqueeze(2), o3[:, :, 64:65], 1e-30, None, ALU.max, None, [r_o], [r_rec])
                recip(rec[:nq, half * 4:half * 4 + 4], rec[:nq, half * 4:half * 4 + 4], [r_rec], [r_rec])
                tt(oa[:nq, j, half * 256:half * 256 + 256].rearrange("p (h d) -> p h d", h=4), o3[:, :, 0:64],
                   rec[:nq, half * 4:half * 4 + 4].unsqueeze(2).to_broadcast([nq, 4, 64]), ALU.mult, [r_o, r_rec], [r_oa[j]])

            def head(h):
                hp, r0 = h // 2, (h % 2) * 64
                ai = cnt["a"] % 2
                cnt["a"] += 1
                offA = ai * 512
                offB = 1024 + ai * 128
                for kt in (4, 0, 1, 2, 3):
                    s_, nk = ktiles[kt]
                    o_ = offB if kt == 4 else offA + kt * 128
                    mm(ps_att[:nk, o_:o_ + nq], kTr[r0:r0 + 64, hp, s_, 0:nk], qT[r0:r0 + 64, hp, j * 128:j * 128 + nq],
                       True, True, [r_kT[s_], r_qT], [r_attB if kt == 4 else r_att[ai]])
                pi = cnt["p"] % 3
                cnt["p"] += 1
                sattA = ps_att[:, offA:offA + 512].rearrange("p (k q) -> p k q", k=4)
                sattB = ps_att[:, offB:offB + 128]
                if sample:
                    act(Pb[pi][:16, 4, 0:nq], sattB[:16, 0:nq], AF.Exp, [r_attB], [r_Pb[pi]])
                    act(Pb[pi][:, 0:4, 0:nq], sattA[:, :, 0:nq], AF.Exp, [r_att[ai]], [r_Pb[pi]])
                    tt(Pb[pi][:, 0:4, 0:nq], Pb[pi][:, 0:4, 0:nq], ET[:, 0:4, h, 0:nq], ALU.mult, [r_Pb[pi], r_ET], [r_Pb[pi]], eng="pool")
                    tt(Pb[pi][:16, 4, 0:nq], Pb[pi][:16, 4, 0:nq], ET[:16, 4, h, 0:nq], ALU.mult, [r_Pb[pi], r_ET], [r_Pb[pi]], eng="pool")
                else:
                    act(Pb[pi][:, 4, :], sattB, AF.Exp, [r_attB], [r_Pb[pi]])
                    act(Pb[pi][:, 0:4, :], sattA, AF.Exp, [r_att[ai]], [r_Pb[pi]])
                    tt(Pb[pi][:, :, :], Pb[pi][:, :, :], ET[:, :, h, :], ALU.mult, [r_Pb[pi], r_ET], [r_Pb[pi]], eng="dve")
                if len(pend) > 1:
                    ph, ppi = pend.pop(0)
                    pv(ph, ppi)
                    if ph == 3:
                        normalize(0)
                pend.append((h, pi))

            def tail():
                while pend:
                    ph, ppi = pend.pop(0)
                    pv(ph, ppi)
                    if ph == 3:
                        normalize(0)
                normalize(1)
                pt, rt = tbank()
                for kc in range(4):
                    S.emit("pe", lambda e, kc=kc, pt=pt: e.transpose(out=pt[:, kc * 128:kc * 128 + nq], in_=oa[:nq, j, kc * 128:(kc + 1) * 128],
                                                                      identity=identb[:nq, :nq]), [r_oa[j], r_ident], [rt])
                cp(oaT[:, :, j * 128:j * 128 + nq], pt[:, 0:512].rearrange("p (k t) -> p k t", k=4)[:, :, 0:nq], [rt], [r_oaT[j]], eng="act")
            for h in range(8):
                att_steps.append(lambda h=h: head(h))
            att_steps.append(tail)
        for j in range(nsub):
            make_att(j)

        def make_h(j):
            nt = nts[j]

            def h_at():
                pb, rb = gbank()
                for hb in range(4):
                    if sample:
                        qrhs = Zq[:, hb, 0, 0:nt]
                    else:
                        base = Zq[:, hb, 2 * j, 0:64]
                        qrhs = AP(base.tensor, base.offset, [list(base.ap[0]), [192, 2], [1, 64]])
                    mm(pb[:nt, hb * 128:hb * 128 + nt], ktT[:, hb, j * 128:j * 128 + nt], qrhs, True, True, [r_ktT[hb], r_Zq[hb]], [rb])
                tt(AT[:nt, :, 0:nt], pb[:nt, :].rearrange("p (h t) -> p h t", h=4)[:, :, 0:nt],
                   maskb[:nt, 0:nt].unsqueeze(1).to_broadcast([nt, 4, nt]), ALU.mult, [rb, r_mask], [r_AT])

            def h_chunk(p):
                ci = j * cps + p
                tt(Sp[:, p], Sst[:], dec[:, 0, :, ci:ci + 1].to_broadcast([128, 4, 128]), ALU.mult, [r_S, r_dec], [r_Sp[p]])
                s_update(j, ci)

            def h_out():
                ob_, rob = gbank()
                for hb in range(4):
                    for p in range(cps):
                        zl = Zq[:, hb, 0, 0:nt] if sample else Zq[:, hb, 2 * j + p, :]
                        mm(ob_[:nt, hb * 128:(hb + 1) * 128], zl, Sp[:, p, hb, :], p == 0, False, [r_Zq[hb], r_Sp[p]], [rob])
                    mm(ob_[:nt, hb * 128:(hb + 1) * 128], AT[:nt, hb, 0:nt], vh[:nt, j, hb * 128:(hb + 1) * 128], False, True, [r_AT] + rl_vh[j], [rob])
                act(sqb[:nt, :], ob_[:nt, :], AF.Square, [rob], [r_sqb])
                S.emit("dve", lambda e: e.tensor_reduce(out=ssq[:nt, 0:4], in_=sqb[:nt, :].rearrange("p (h v) -> p h v", h=4), axis=AX.X, op=ALU.add),
                       [r_sqb], [r_ssq])
                ts(ssq[:nt, :], ssq[:nt, :], 1.0 / 128, EPS, ALU.mult, ALU.add, [r_ssq], [r_ssq], eng="pool")
                tt(ssq[:nt, :], ssq[:nt, :], nhalf[:nt, 0:4], ALU.pow, [r_ssq, r_nh], [r_ssq], eng="pool")
                tt(t1[:nt, :].rearrange("p (h v) -> p h v", h=4), ob_[:nt, :].rearrange("p (h v) -> p h v", h=4),
                   ssq[:nt, 0:4].unsqueeze(2).to_broadcast([nt, 4, 128]), ALU.mult, [rob, r_ssq], [r_t1])
                tt(gG[:nt, :], sgb[:nt, j, :], gnrep[:nt].rearrange("p h v -> p (h v)"), ALU.mult, [r_sgb[j], r_gn], [r_gG], eng="pool")
                tt(ob[:nt, :], t1[:nt, :], gG[:nt, :], ALU.mult, [r_t1, r_gG], [r_ob])

            def h_tr():
                pt, rt = tbank()
                for hb in range(4):
                    S.emit("pe", lambda e, hb=hb, pt=pt: e.transpose(out=pt[:, hb * 128:hb * 128 + nt], in_=ob[:nt, hb * 128:(hb + 1) * 128],
                                                                      identity=identb[:nt, :nt]), [r_ob, r_ident], [rt])
                cp(obT[:, :, j * 128:j * 128 + nt], pt[:, 0:512].rearrange("p (k t) -> p k t", k=4)[:, :, 0:nt], [rt], [r_obT[j]], eng="act")
            h_steps.append(lambda: khat_transpose(j))
            h_steps.append(h_at)
            for p in range(cps):
                h_steps.append(lambda p=p: h_chunk(p))
            h_steps.append(h_out)
            h_steps.append(h_tr)
        for j in range(nsub):
            make_h(j)

        def run_steps(steps, pool, tr, pre=None):
            def f():
                gstate["pool"] = pool
                gstate["tr"] = tr
                if pre is not None:
                    pre()
                for st_ in steps:
                    st_()
            return f
        att_ops = S.capture(run_steps(att_steps, [], (0,)))
        z_ops = S.capture(run_steps(z_steps, gpool_full[0:1], (0,)))
        h_ops = S.capture(run_steps(h_steps, gpool_full[1:2], (1,), pre=hgrn_gates_all))
        S.emit_merged(att_ops, z_ops, h_ops, d_ops)
        gstate["tr"] = (0, 1)
        if after_norm is not None:
            after_norm()
            after_norm = None

        gstate["pool"] = gpool_front + gpool_back
        wva, rwa, sla = w_next("a")
        wstate["consumed"] += 1
        wvb, rwb, slb = w_next("b")
        wstate["consumed"] -= 1
        for ct in range(8):
            pb, rb = gbank()
            for kc in range(4):
                mm(pb[:, 0:ntok], wva[:, kc, ct * 128:(ct + 1) * 128], oaT[:, kc, 0:ntok], kc == 0, kc == 3, [rwa] + r_oaT[:nsub], [rb])
            for kc in range(4):
                mm(pb[:, 256:256 + ntok], wvb[:, kc, ct * 128:(ct + 1) * 128], obT[:, kc, 0:ntok], kc == 0, kc == 3, [rwb] + r_obT[:nsub], [rb])
            stt(m1[:, 0:ntok], zs[:, ct, 0:ntok], 1.0, pb[:, 0:ntok], ALU.add, ALU.mult, [rb, r_zs[ct]], [r_m1])
            stt(m2[:, 0:ntok], zs[:, 8 + ct, 0:ntok], 1.0, pb[:, 256:256 + ntok], ALU.add, ALU.mult, [rb, r_zs[8 + ct]], [r_m2])
            tt(mT[:, ct, 0:ntok], m1[:, 0:ntok], m2[:, 0:ntok], ALU.add, [r_m1, r_m2], [r_mT[ct]], eng="pool")
        w_done(sla)
        w_done(slb)

        for n in range(2):
            wv, rw, sl = w_next("o%d" % n)
            for j in range(nsub):
                nt = nts[j]
                pb, rb = gbank()
                for kc in range(8):
                    mm(pb[:nt, :], mT[:, kc, j * 128:j * 128 + nt], wv[:, kc, :], kc == 0, kc == 7, [rw, r_mT[kc]], [rb])
                stt(Xb[:nt, j, n * 512:(n + 1) * 512], pb[:nt, :], 0.5, Xb[:nt, j, n * 512:(n + 1) * 512], ALU.mult, ALU.add, [rX[j], rb], [rX[j]])
            w_done(sl)
        norm_T(Xb, rX, gffn, r_gffn, 2, 2, ntok)
        hT = hTs[2]
        r_hT = r_hTs[2]
        rhT_all = r_hT[:nsub]

        ffn_pend = []
        for gi in range(6):
            ntile = 4 if gi < 5 else 2
            if gi == 2 and prenorm is not None:
                prenorm()
            wvg, rwg, slg = w_next("g%d" % gi)
            if mode != "pre":
                wstate["consumed"] += 1
                wvu, rwu, slu = w_next("u%d" % gi)
                wstate["consumed"] -= 1
            for ct in range(ntile):
                ft = gi * 4 + ct
                pb, rb = gbank()
                if mode == "pre":
                    for kc in range(8):
                        mm(pb[:, 0:2], wvg[:, kc, ct * 128:(ct + 1) * 128], hT[:, kc, ntok - 2:ntok], kc == 0, kc == 7, [rwg] + rhT_all, [rb])
                    cp(cprev[:, ft, :], pb[:, 0:2], [rb], [r_cprev])
                    continue
                for kc in range(8):
                    mm(pb[:, 0:ntok], wvg[:, kc, ct * 128:(ct + 1) * 128], hT[:, kc, 0:ntok], kc == 0, kc == 7, [rwg] + rhT_all, [rb])
                for kc in range(8):
                    mm(pb[:, 256:256 + ntok], wvu[:, kc, ct * 128:(ct + 1) * 128], hT[:, kc, 0:ntok], kc == 0, kc == 7, [rwu] + rhT_all, [rb])
                bi = ft % 2
                a_ = aT[bi]
                cp(a_[:, 0:2], cprev[:, ft, :], [r_cprev], [r_aT[bi]], eng="pool")
                act(a_[:, 2:2 + ntok], pb[:, 0:ntok], AF.Copy, [rb], [r_aT[bi]])
                cp(cprev[:, ft, :], a_[:, ntok:ntok + 2], [r_aT[bi]], [r_cprev], eng="pool")
                act(c1[bi][:, 0:ntok], pb[:, 0:ntok], AF.Identity, [rb, r_cw, r_cb], [r_c1[bi]], scale=cw[:, ft, 2:3], bias=cb[:, ft:ft + 1])
                stt(c2[bi][:, 0:ntok], a_[:, 1:1 + ntok], cw[:, ft, 1:2], c1[bi][:, 0:ntok], ALU.mult, ALU.add, [r_aT[bi], r_c1[bi], r_cw], [r_c2[bi]])
                stt(c1[bi][:, 0:ntok], a_[:, 0:ntok], cw[:, ft, 0:1], c2[bi][:, 0:ntok], ALU.mult, ALU.add, [r_aT[bi], r_c2[bi], r_cw], [r_c1[bi]])
                def fin(bi=bi, ft=ft, pb=pb, rb=rb):
                    act(c2[bi][:, 0:ntok], c1[bi][:, 0:ntok], AF.Gelu_apprx_tanh, [r_c1[bi]], [r_c2[bi]])
                    tt(gT[:, ft, 0:ntok], c2[bi][:, 0:ntok], pb[:, 256:256 + ntok], ALU.mult, [r_c2[bi], rb], [r_gT[ft]])
                if ffn_pend:
                    ffn_pend.pop(0)()
                ffn_pend.append(fin)
            w_done(slg)
            if mode != "pre":
                w_done(slu)
        if mode == "pre":
            return
        while ffn_pend:
            ffn_pend.pop(0)()

        for n in range(2):
            banks = [gbank() for _ in range(nsub)]
            for gk in range(3):
                wv, rw, sl = w_next("d%d_%d" % (n, gk))
                nk = 8 if gk < 2 else 6
                for kl in range(nk):
                    kc = gk * 8 + kl
                    for j in range(nsub):
                        nt = nts[j]
                        mm(banks[j][0][:nt, :], gT[:, kc, j * 128:j * 128 + nt], wv[:, kl, :], kc == 0, kc == NFT - 1, [rw, r_gT[kc]], [banks[j][1]])
                w_done(sl)
            for j in range(nsub):
                nt = nts[j]
                tt(Xb[:nt, j, n * 512:(n + 1) * 512], Xb[:nt, j, n * 512:(n + 1) * 512], banks[j][0][:nt, :], ALU.add, [rX[j], banks[j][1]], [rX[j]])
        def final_norm():
            for j in range(nsub):
                nt = nts[j]
                act(XN[:nt, j, :], Xb[:nt, j, :], AF.Square, [rX[j]], [r_XN[j], r_ss], accum_out=ss[:nt, 4 + j:5 + j])
                ts(ss[:nt, 4 + j:5 + j], ss[:nt, 4 + j:5 + j], 1.0 / D, EPS, ALU.mult, ALU.add, [r_ss], [r_ss], eng="pool")
                tt(ss[:nt, 4 + j:5 + j], ss[:nt, 4 + j:5 + j], nhalf[:nt, 0:1], ALU.pow, [r_ss, r_nh], [r_ss], eng="pool")
                stt(Xb[:nt, j, :], Xb[:nt, j, :], ss[:nt, 4 + j:5 + j], gfin[:nt, :], ALU.mult, ALU.mult, [rX[j], r_ss, r_gfin], [rX[j]])
                dma("act", out_row[j * 128:j * 128 + nt, :], Xb[:nt, j, :], [rX[j]], ())
        if deferred is not None and mode == "main":
            deferred.append(final_norm)
        else:
            final_norm()

    cur = {}

    def kv_proj_only():
        ntok, gt0 = cur["ntok"], cur["gt0"]
        for ct in range(4):
            pb, rb = gbank()
            for kc in range(8):
                mm(pb[:, 0:ntok], W1v[2][:, kc, ct * 128:(ct + 1) * 128], cur["hT"][:, kc, 0:ntok], kc == 0, kc == 7, [r_wslot[2]] + cur["r_hT"], [rb])
            for j in range(2):
                s_ = (gt0 + j) % 6
                cp(kTr[:, ct, s_, :], pb[:, j * 128:(j + 1) * 128], [rb], [r_kT[s_]], eng="act")
        for j in range(2):
            s_ = (gt0 + j) % 6
            pb, rb = gbank()
            for kc in range(8):
                mm(pb[:, :], cur["hT"][:, kc, j * 128:(j + 1) * 128], W1v[3][:, kc, :], kc == 0, kc == 7, [r_wslot[3], cur["r_hT"][j]], [rb])
            cp(Vr[:, s_, :, 0:64], pb[:, :].rearrange("p (h d) -> p h d", h=8), [rb], [r_V[s_]], eng="act")
            cp(Vr[:, s_, :, 64:65], hvt[:, 0:1].unsqueeze(2).to_broadcast([128, 8, 1]), [r_hv], [r_V[s_]], eng="pool")


    xa = xin.ap()
    nscan = n_scan
    nmain = n_main
    plan = []
    for m in range(nscan):
        plan.append(("scan", xa[m * T:(m + 1) * T, :], T, dict(gt0=-5 + 2 * (m - (nscan - 2)) + 6, kv=m >= nscan - 2)))
    if do_pre:
        plan.append(("pre", xa[NPRE:NPRE + PRE_T, :], PRE_T, dict(gt0=5)))
    for m in range(nmain):
        fk = (okT_d, ov_d, (m - (nmain - 2)) * T) if (m >= nmain - 2 and not NOFK) else None
        r0 = NPRE + PRE_T + m * T
        plan.append(("main", xa[r0:r0 + T, :], T, dict(gt0=2 * m + 6, out_row=y_d.ap()[m * T:(m + 1) * T, :], final_kv=fk)))
    if do_sample:
        plan.append(("sample", xs_d.ap(), 16, dict(gt0=0, out_row=ys_d.ap(), final_kv=(okTs_d, ovs_d, 0))))
    if plan:
        load_x(plan[0][1], plan[0][2], 0)
    if not do_conv:
        conv_jobs.clear()
    dfr = []
    for i, (mode, xsrc, ntok, kw) in enumerate(plan):
        xb = i % 2
        nxt = None
        pren = None
        if i + 1 < len(plan):
            nxt = (lambda p=plan[i + 1], b=(i + 1) % 2: load_x(p[1], p[2], b))
            if mode == "main":
                pren = (lambda p=plan[i + 1], b=(i + 1) % 2: norm_T(X[b], r_X[b], gmix, r_gmix, 0, b, p[2]))
        normed = i > 0 and plan[i - 1][0] == "main"
        if mode == "scan":
            if i > 0:
                continue

            def stage(k, part):
                md, xs_, nt_, kw_ = plan[k]
                nx = None
                if part == "f1":
                    emit_conv(2)
                    nx = (lambda p=plan[k + 1], b=(k + 1) % 2: load_x(p[1], p[2], b)) if k + 1 < len(plan) else None
                if part == "f2":
                    cur["ntok"] = nt_
                    cur["gt0"] = kw_["gt0"]
                macro_tile("scan", nt_, k % 2, hsel=k % 2, after_norm=nx, part=part, bsel=k % 2, vsel=k % 3, ksel=k % 2, **kw_)
            for w in range(-3, nscan):
                lists = []
                for off, part in ((0, "upd"), (1, "gates"), (2, "f2"), (3, "f1")):
                    k = w + off
                    if 0 <= k < nscan:
                        lists.append(S.capture(lambda k=k, part=part: stage(k, part)))
                S.emit_merged(*lists)
            continue
        if i == nscan:
            emit_conv(len(conv_jobs))
            for s_ in range(NSLOT):
                w_issue(s_)
        if mode == "sample":
            dma("act", oS_d.ap().rearrange("h k v -> k h v"), Sst[:], [r_S], ())
            dma("act", oconv_d.ap(), cprev[:], [r_cprev], ())
            dma("pool", kTr[:, :, 0:4, :], ckT_d.ap().rearrange("p c (t k) -> p c t k", t=4), (), r_kT[0:4])
            for t_ in range(4):
                dma("pool", Vr[:, t_, :, 0:64], cv_d.ap()[:, t_ * 128:(t_ + 1) * 128, :].rearrange("h p d -> p h d"), (), [r_V[t_]])
                cp(Vr[:, t_, :, 64:65], onesT[:, 0:8].unsqueeze(2), [r_ones], [r_V[t_]], eng="pool")
            dma("sp", Sst[:], s0_d.ap().rearrange("h k v -> k h v"), (), [r_S])
            dma("sp", cprev[:], cprev_d.ap(), (), [r_cprev])
        macro_tile(mode, ntok, xb, hsel=xb, normed=normed, after_norm=nxt, prenorm=pren, deferred=dfr, **kw)
    while dfr:
        dfr.pop(0)()
    emit_conv(len(conv_jobs))
    if not do_sample:
        dma("act", oS_d.ap().rearrange("h k v -> k h v"), Sst[:], [r_S], ())
        dma("act", oconv_d.ap(), cprev[:], [r_cprev], ())
    else:
        dma("act", oSs_d.ap().rearrange("h k v -> k h v"), Sst[:], [r_S], ())
        dma("act", oconvs_d.ap(), cprev[:], [r_cprev], ())
    for nm, getter in dumps:
        ap_, res_ = getter(locals())
        d_ = nc.dram_tensor("dbg_" + nm, list(ap_.shape), ap_.dtype if hasattr(ap_, "dtype") else F32, kind="ExternalOutput")
        dma("pool", d_.ap(), ap_, res_, ())

    S.finalize(st)
    with nc.Block() as block:
        S.run(block)
    st.close()
    return nc


_NC_CACHE = {}


def kernel(x_prompt, x_sample, cache_attn_k, cache_attn_v, state_hgrn, state_ffn_conv,
           norm_mix_g, w_in, rel_bias, hgrn_lb_logits, hgrn_norm_g, w_branch_a, w_branch_b, w_out,
           norm_ffn_g, w_ffn_gate, w_ffn_up, ffn_conv_w, ffn_conv_b, w_ffn_down, norm_final_g):
    f32 = np.float32
    A = lambda a: np.ascontiguousarray(np.asarray(a, dtype=f32))
    x_prompt = A(x_prompt)
    if "nc" not in _NC_CACHE:
        _NC_CACHE["nc"] = build_program()
    nc = _NC_CACHE["nc"]
    s_idx = np.arange(128)
    mask = ((s_idx[:, None] // 64 == s_idx[None, :] // 64) & (s_idx[:, None] <= s_idx[None, :])).astype(f32)
    shared = {
        "w_in": A(w_in[0]), "w_a": A(w_branch_a[0]), "w_b": A(w_branch_b[0]), "w_o": A(w_out[0]),
        "w_g": A(w_ffn_gate[0]), "w_u": A(w_ffn_up[0]), "w_d": A(w_ffn_down[0]),
        "gmix": A(np.asarray(norm_mix_g[0]).reshape(8, 128).T), "gffn": A(np.asarray(norm_ffn_g[0]).reshape(8, 128).T),
        "gfin": A(norm_final_g), "rel": A(rel_bias[0]),
        "lbl": A(np.asarray(hgrn_lb_logits).reshape(2, 4, 128).transpose(2, 0, 1)),
        "gn": A(hgrn_norm_g[0]),
        "cw": A(np.asarray(ffn_conv_w[0]).reshape(3, NFT, 128).transpose(2, 1, 0)),
        "cb": A(np.asarray(ffn_conv_b[0]).reshape(NFT, 128).T),
        "ident": np.eye(128, dtype=f32), "mask": mask, "jmat": np.ascontiguousarray(np.eye(128, dtype=f32)[::-1]),
    }
    in_maps = []
    for c in range(8):
        b, j = divmod(c, 4)
        s = j * SEG
        lo = s - PRE_T - NPRE
        xin = np.zeros((NTOK_IN, D), f32)
        a0 = max(lo, 0)
        xin[a0 - lo:] = x_prompt[b, a0:s + SEG]
        m = dict(shared)
        m["xin"] = xin
        m["hv"] = np.full((128, 1), 1.0 if j > 0 else 0.0, f32)
        m["xs"] = A(x_sample[c])
        m["ckT"] = A(np.asarray(cache_attn_k[0, c]).transpose(0, 2, 1).reshape(4, 128, 512).transpose(1, 0, 2))
        m["cv"] = A(cache_attn_v[0, c])
        m["s0"] = A(state_hgrn[0, c])
        m["cprev"] = A(np.asarray(state_ffn_conv[0, c]).reshape(2, NFT, 128).transpose(2, 1, 0))
        in_maps.append(m)
    res = run_bass_kernel_spmd(nc, in_maps, core_ids=list(range(8)))
    R_ = res.results
    B = 2
    y_prompt = np.stack([np.concatenate([R_[b * 4 + j]["y"] for j in range(4)], axis=0) for b in range(B)])
    y_sample = np.stack([R_[c]["ys"] for c in range(8)])

    def kT_to_rows(a):
        n = a.shape[-1]
        return a.reshape(8, 64, n).transpose(0, 2, 1)

    def v_to_rows(a):
        n = a.shape[0]
        return a.reshape(n, 8, 64).transpose(1, 0, 2)

    def conv_rows(a):
        return a.transpose(2, 1, 0).reshape(2, DFF)

    last = [3, 7]
    new_k_p = np.stack([kT_to_rows(R_[c]["okT"]) for c in last])[None]
    new_v_p = np.stack([v_to_rows(R_[c]["ov"]) for c in last])[None]
    hg_p = np.stack([R_[c]["oS"] for c in last])[None]
    cv_p = np.stack([conv_rows(R_[c]["oconv"]) for c in last])[None]
    new_k_s = np.stack([kT_to_rows(R_[c]["okTs"]) for c in range(8)])[None]
    new_v_s = np.stack([v_to_rows(R_[c]["ovs"]) for c in range(8)])[None]
    hg_s = np.stack([R_[c]["oSs"] for c in range(8)])[None]
    cv_s = np.stack([conv_rows(R_[c]["oconvs"]) for c in range(8)])[None]
    outs = (y_prompt, y_sample, new_k_p, new_v_p, hg_p, cv_p, new_k_s, new_v_s, hg_s, cv_s)
    return tuple(np.ascontiguousarray(o, dtype=f32) for o in outs)
```
